# Optimizing a Trainium2 kernel written in Bass

```python
import math
import jax, jax.numpy as jnp
from jax import lax
import numpy as np

D_MODEL = 2048
BATCH = 2
SEQ = 8192
DEPTH = 1

D_RNN = D_MODEL
RG_HEADS = 16
RG_HEAD_DIM = D_RNN // RG_HEADS
CONV_WIDTH = 4
RG_C = 8.0
D_SSM = D_MODEL // 2
SSM_GROUP = 16
SSM_GROUPS = D_SSM // SSM_GROUP
SSM_STATE = 64
D_FF = 4 * D_MODEL
D_IN = 2 * D_RNN + D_SSM + 2 * D_MODEL
LN_EPS = 1e-5

kernel_name = "hybrid_rglru_s5_gated_deepnorm_block"


def _layernorm(x, g, b):
    xf = x.astype(jnp.float32)
    mu = jnp.mean(xf, axis=-1, keepdims=True)
    var = jnp.mean(jnp.square(xf - mu), axis=-1, keepdims=True)
    y = (xf - mu) * lax.rsqrt(var + LN_EPS)
    return (y * g.astype(jnp.float32) + b.astype(jnp.float32)).astype(x.dtype)


def _real_linear_scan(a, b):
    def combine(c1, c2):
        a1, b1 = c1
        a2, b2 = c2
        return a1 * a2, a2 * b1 + b2
    _, h = lax.associative_scan(combine, (a, b), axis=1)
    return h


def _complex_linear_scan(a_re, a_im, b_re, b_im):
    def combine(c1, c2):
        a1r, a1i, b1r, b1i = c1
        a2r, a2i, b2r, b2i = c2
        ar = a2r * a1r - a2i * a1i
        ai = a2r * a1i + a2i * a1r
        br = a2r * b1r - a2i * b1i + b2r
        bi = a2r * b1i + a2i * b1r + b2i
        return ar, ai, br, bi
    _, _, h_re, h_im = lax.associative_scan(combine, (a_re, a_im, b_re, b_im), axis=1)
    return h_re, h_im


def _causal_depthwise_conv(x, w, bias):
    c = x.shape[-1]
    y = lax.conv_general_dilated(
        x, w[:, None, :].astype(x.dtype), window_strides=(1,),
        padding=[(CONV_WIDTH - 1, 0)], dimension_numbers=("NWC", "WIO", "NWC"),
        feature_group_count=c)
    return y + bias


def _rglru_branch(xr, gate, conv_w, conv_b, wa, ba, wx, bx, lam, w_a_out):
    bsz, s, _ = xr.shape
    xc = _causal_depthwise_conv(xr, conv_w, conv_b)
    xh = xc.reshape(bsz, s, RG_HEADS, RG_HEAD_DIM)
    r = jax.nn.sigmoid(jnp.einsum("bshi,hij->bshj", xh, wa) + ba).reshape(bsz, s, D_RNN)
    i = jax.nn.sigmoid(jnp.einsum("bshi,hij->bshj", xh, wx) + bx).reshape(bsz, s, D_RNN)
    log_a = (-RG_C * r.astype(jnp.float32)) * jax.nn.softplus(-lam.astype(jnp.float32))
    a = jnp.exp(log_a)
    mult = jnp.sqrt(-jnp.expm1(2.0 * log_a))
    b = mult * (i.astype(jnp.float32) * xc.astype(jnp.float32))
    h = _real_linear_scan(a, b).astype(xr.dtype)
    return (h * jax.nn.gelu(gate)) @ w_a_out


def _s5_branch(u, a_re, a_im, log_dt, b_re, b_im, c_re, c_im, d, glu_w, glu_v):
    bsz, s, _ = u.shape
    uf = u.astype(jnp.float32).reshape(bsz, s, SSM_GROUPS, SSM_GROUP)
    dt = jnp.exp(log_dt.astype(jnp.float32))[:, None]
    lr = jnp.minimum(a_re.astype(jnp.float32), -1e-4)
    li = a_im.astype(jnp.float32)
    mag = jnp.exp(lr * dt)
    lbr = mag * jnp.cos(li * dt)
    lbi = mag * jnp.sin(li * dt)
    zr, zi = lbr - 1.0, lbi
    den = lr * lr + li * li
    fr = (zr * lr + zi * li) / den
    fi = (zi * lr - zr * li) / den
    br32, bi32 = b_re.astype(jnp.float32), b_im.astype(jnp.float32)
    bbr = fr[..., None] * br32 - fi[..., None] * bi32
    bbi = fr[..., None] * bi32 + fi[..., None] * br32
    bu_re = jnp.einsum("bsgh,gph->bsgp", uf, bbr)
    bu_im = jnp.einsum("bsgh,gph->bsgp", uf, bbi)
    shp = (1, s, SSM_GROUPS, SSM_STATE)
    h_re, h_im = _complex_linear_scan(jnp.broadcast_to(lbr, shp), jnp.broadcast_to(lbi, shp),
                                      bu_re, bu_im)
    y = (jnp.einsum("bsgp,ghp->bsgh", h_re, c_re.astype(jnp.float32))
         - jnp.einsum("bsgp,ghp->bsgh", h_im, c_im.astype(jnp.float32))
         + d.astype(jnp.float32) * uf)
    y = jax.nn.gelu(y.reshape(bsz, s, D_SSM)).astype(u.dtype)
    return (y @ glu_w) * jax.nn.sigmoid(y @ glu_v)


def setup_inputs(seed: int = 0) -> dict:
    key = jax.random.key(seed)
    ks = jax.random.split(key, 32)
    L = DEPTH
    beta = (8.0 * DEPTH) ** -0.25

    def nrm(k, shape, scale):
        return jax.random.normal(k, shape, jnp.float32) * scale

    x = nrm(ks[0], (BATCH, SEQ, D_MODEL), 1.0)
    w_in = nrm(ks[1], (L, D_MODEL, D_IN), D_MODEL ** -0.5)
    conv_w = nrm(ks[2], (L, CONV_WIDTH, D_RNN), CONV_WIDTH ** -0.5)
    conv_b = nrm(ks[3], (L, D_RNN), 0.01)
    rg_wa = nrm(ks[4], (L, RG_HEADS, RG_HEAD_DIM, RG_HEAD_DIM), RG_HEAD_DIM ** -0.5)
    rg_ba = nrm(ks[5], (L, RG_HEADS, RG_HEAD_DIM), 0.01)
    rg_wx = nrm(ks[6], (L, RG_HEADS, RG_HEAD_DIM, RG_HEAD_DIM), RG_HEAD_DIM ** -0.5)
    rg_bx = nrm(ks[7], (L, RG_HEADS, RG_HEAD_DIM), 0.01)
    a_c = jax.random.uniform(ks[8], (L, D_RNN), jnp.float32, 0.9, 0.999)
    a0 = a_c ** (1.0 / RG_C)
    rg_lambda = jnp.log(a0) - jnp.log1p(-a0)
    w_a_out = nrm(ks[9], (L, D_RNN, D_MODEL), D_RNN ** -0.5)
    n = jnp.arange(SSM_STATE, dtype=jnp.float32)
    ssm_a_re = -0.5 + nrm(ks[10], (L, SSM_GROUPS, SSM_STATE), 0.01)
    ssm_a_im = math.pi * n + nrm(ks[11], (L, SSM_GROUPS, SSM_STATE), 0.01)
    ssm_log_dt = jax.random.uniform(ks[12], (L, SSM_GROUPS), jnp.float32,
                                    math.log(1e-3), math.log(1e-1))
    ssm_b_re = nrm(ks[13], (L, SSM_GROUPS, SSM_STATE, SSM_GROUP), (2.0 * SSM_GROUP) ** -0.5)
    ssm_b_im = nrm(ks[14], (L, SSM_GROUPS, SSM_STATE, SSM_GROUP), (2.0 * SSM_GROUP) ** -0.5)
    ssm_c_re = nrm(ks[15], (L, SSM_GROUPS, SSM_GROUP, SSM_STATE), (0.5 * SSM_STATE) ** -0.5)
    ssm_c_im = nrm(ks[16], (L, SSM_GROUPS, SSM_GROUP, SSM_STATE), (0.5 * SSM_STATE) ** -0.5)
    ssm_d = nrm(ks[17], (L, SSM_GROUPS, SSM_GROUP), 1.0)
    glu_w = nrm(ks[18], (L, D_SSM, D_MODEL), D_SSM ** -0.5)
    glu_v = nrm(ks[19], (L, D_SSM, D_MODEL), D_SSM ** -0.5)
    w_out = nrm(ks[20], (L, D_MODEL, D_MODEL), beta * D_MODEL ** -0.5)
    ln1_g = 1.0 + nrm(ks[21], (L, D_MODEL), 0.02)
    ln1_b = nrm(ks[22], (L, D_MODEL), 0.02)
    mlp_w_up = nrm(ks[23], (L, D_MODEL, D_FF), beta * D_MODEL ** -0.5)
    mlp_b_up = nrm(ks[24], (L, D_FF), 0.01)
    mlp_w_down = nrm(ks[25], (L, D_FF, D_MODEL), beta * D_FF ** -0.5)
    mlp_b_down = nrm(ks[26], (L, D_MODEL), 0.01)
    ln2_g = 1.0 + nrm(ks[27], (L, D_MODEL), 0.02)
    ln2_b = nrm(ks[28], (L, D_MODEL), 0.02)
    return {"x": x, "w_in": w_in, "conv_w": conv_w, "conv_b": conv_b,
            "rg_wa": rg_wa, "rg_ba": rg_ba, "rg_wx": rg_wx, "rg_bx": rg_bx,
            "rg_lambda": rg_lambda, "w_a_out": w_a_out,
            "ssm_a_re": ssm_a_re, "ssm_a_im": ssm_a_im, "ssm_log_dt": ssm_log_dt,
            "ssm_b_re": ssm_b_re, "ssm_b_im": ssm_b_im, "ssm_c_re": ssm_c_re,
            "ssm_c_im": ssm_c_im, "ssm_d": ssm_d, "glu_w": glu_w, "glu_v": glu_v,
            "w_out": w_out, "ln1_g": ln1_g, "ln1_b": ln1_b,
            "mlp_w_up": mlp_w_up, "mlp_b_up": mlp_b_up, "mlp_w_down": mlp_w_down,
            "mlp_b_down": mlp_b_down, "ln2_g": ln2_g, "ln2_b": ln2_b}


def reference(x, w_in, conv_w, conv_b, rg_wa, rg_ba, rg_wx, rg_bx, rg_lambda, w_a_out,
              ssm_a_re, ssm_a_im, ssm_log_dt, ssm_b_re, ssm_b_im, ssm_c_re, ssm_c_im,
              ssm_d, glu_w, glu_v, w_out, ln1_g, ln1_b, mlp_w_up, mlp_b_up,
              mlp_w_down, mlp_b_down, ln2_g, ln2_b):
    alpha = (2.0 * DEPTH) ** 0.25
    splits = [D_RNN, 2 * D_RNN, 2 * D_RNN + D_SSM, 2 * D_RNN + D_SSM + D_MODEL]
    for l in range(DEPTH):
        z = x @ w_in[l]
        xr, gate_r, u_s, g_a, g_b = jnp.split(z, splits, axis=-1)
        y_a = _rglru_branch(xr, gate_r, conv_w[l], conv_b[l], rg_wa[l], rg_ba[l],
                            rg_wx[l], rg_bx[l], rg_lambda[l], w_a_out[l])
        y_b = _s5_branch(u_s, ssm_a_re[l], ssm_a_im[l], ssm_log_dt[l], ssm_b_re[l],
                         ssm_b_im[l], ssm_c_re[l], ssm_c_im[l], ssm_d[l], glu_w[l], glu_v[l])
        mix = jax.nn.sigmoid(g_a) * y_a + jax.nn.sigmoid(g_b) * y_b
        x = _layernorm(alpha * x + mix @ w_out[l], ln1_g[l], ln1_b[l])
        h = jnp.square(jax.nn.relu(x @ mlp_w_up[l] + mlp_b_up[l])) @ mlp_w_down[l] + mlp_b_down[l]
        x = _layernorm(alpha * x + h, ln2_g[l], ln2_b[l])
    return x
```

```python
import numpy as np
from contextlib import ExitStack
import concourse.bass as bass
import concourse.mybir as mybir
from concourse.bass_utils import run_bass_kernel_spmd

F32 = mybir.dt.float32
BF16 = mybir.dt.bfloat16
I32 = mybir.dt.int32
AF = mybir.ActivationFunctionType
ALU = mybir.AluOpType

NCORE = 8
TOK = 2048
HALO = 3
D = 2048
DIN = 9216
DFF = 8192
ALPHA = 2.0 ** 0.25
EPS = 1e-5
TWO_PI = 6.283185307179586
PI = 3.141592653589793


class Sched:
    ENGS = ("pe", "act", "dve", "pool", "sp")

    def __init__(self, nc, stack):
        self.nc = nc
        self.stack = stack
        self.eng = {"pe": nc.tensor, "act": nc.scalar, "dve": nc.vector, "pool": nc.gpsimd, "sp": nc.sync}
        self.sem = {e: stack.enter_context(nc.semaphore("s_" + e)) for e in self.ENGS}
        self.cnt = {e: 0 for e in self.ENGS}
        self.waited = {e: {} for e in self.ENGS}
        self.last_w = {}
        self.readers = {}
        self.dsem = {}
        self.dcnt = {}

    def _h(self, s):
        return self.sem[s[1]] if s[0] == "e" else self.dsem[s[1]]

    def _wait(self, eng, s, v, raw=False):
        if s == ("e", eng) and (eng == "pe" or not raw):
            return
        if self.waited[eng].get(s, 0) >= v:
            return
        self.waited[eng][s] = v
        self.eng[eng].wait_ge(self._h(s), v)

    def _deps(self, eng, reads, writes):
        need = {}
        own = ("e", eng)

        def add(s, v, raw):
            if s == own and not raw:
                return
            if v > need.get(s, 0):
                need[s] = v
        for b in reads:
            w = self.last_w.get(b)
            if w is not None:
                add(w[0], w[1], True)
        for b in writes:
            w = self.last_w.get(b)
            if w is not None:
                add(w[0], w[1], False)
            for s, v in self.readers.get(b, {}).items():
                add(s, v, False)
        for s, v in need.items():
            self._wait(eng, s, v, raw=True)

    def _mark(self, tok, reads, writes):
        for b in writes:
            self.last_w[b] = tok
            self.readers[b] = {}
        for b in reads:
            d = self.readers.setdefault(b, {})
            if tok[1] > d.get(tok[0], 0):
                d[tok[0]] = tok[1]

    def op(self, eng, fn, reads=(), writes=(), inc=True):
        self._deps(eng, reads, writes)
        ins = fn(self.eng[eng])
        if inc:
            self.cnt[eng] += 1
            ins.then_inc(self.sem[eng], 1)
            tok = (("e", eng), self.cnt[eng])
        else:
            tok = (("e", eng), self.cnt[eng] + 1)
        self._mark(tok, reads, writes)
        return tok

    def dma(self, eng, key, fn, reads=(), writes=(), incv=16):
        if key not in self.dsem:
            self.dsem[key] = self.stack.enter_context(self.nc.semaphore("d_" + str(key)))
            self.dcnt[key] = 0
        self._deps(eng, reads, writes)
        self.dcnt[key] += incv
        fn(self.eng[eng]).then_inc(self.dsem[key], incv)
        tok = (("d", key), self.dcnt[key])
        self._mark(tok, reads, writes)
        return tok

    def wait_tok(self, eng, tok):
        self._wait(eng, tok[0], tok[1])

    def barrier(self):
        for e in self.ENGS:
            for e2 in self.ENGS:
                if e2 != e and self.cnt[e2] > 0:
                    self._wait(e, ("e", e2), self.cnt[e2])
            for k, v in self.dcnt.items():
                self._wait(e, ("d", k), v)
        self.last_w.clear()
        self.readers.clear()


_DTSZ = {F32: 4, I32: 4, BF16: 2}
SB_BYTES = 207 * 1024


class Arena:
    def __init__(self, handle, size):
        self.h = handle
        self.lo = 0
        self.hi = size

    def alloc(self, shape, dt, top=False):
        n = 1
        for d in shape[1:]:
            n *= d
        nb = (n * _DTSZ[dt] + 63) // 64 * 64
        if top:
            self.hi -= nb
            off = self.hi
        else:
            off = self.lo
            self.lo += nb
        assert self.lo <= self.hi, ("SBUF arena overflow", self.lo, self.hi)
        v = self.h[:, off:off + n * _DTSZ[dt]].bitcast(dt)
        names = "abcdefg"[:len(shape) - 1]
        if len(shape) > 2:
            v = v.rearrange("p (%s) -> p %s" % (" ".join(names), " ".join(names)),
                            **{k: d for k, d in zip(names[:-1], shape[1:-1])})
        if shape[0] < 128:
            v = v[0:shape[0]]
        return v

    def view(self, off, shape, dt):
        n = 1
        for d in shape[1:]:
            n *= d
        v = self.h[:, off:off + n * _DTSZ[dt]].bitcast(dt)
        names = "abcdefg"[:len(shape) - 1]
        if len(shape) > 2:
            v = v.rearrange("p (%s) -> p %s" % (" ".join(names), " ".join(names)),
                            **{k: d for k, d in zip(names[:-1], shape[1:-1])})
        return v

    def mark(self):
        return (self.lo, self.hi)

    def release(self, m):
        self.lo, self.hi = m


def build(dbg=False, stop_after=None):
    nc = bass.Bass("TRN2", target_bir_lowering=False)

    small = stop_after in ("p0", "a1", "a3", "a4")
    BIG = ("w_a_out", "glu_w", "glu_v", "w_out", "mlp_w_up", "mlp_w_down")

    def din(name, shape):
        if small and name in BIG:
            shape = [128, 128]
        return nc.dram_tensor(name, list(shape), F32, kind="ExternalInput").ap()

    def dscr(name, shape, dt):
        if dbg:
            return nc.dram_tensor(name, list(shape), dt, kind="ExternalOutput").ap()
        return nc.dram_tensor(name, list(shape), dt).ap()

    x_d = din("x", [4 * TOK + HALO, D])
    segm_d = din("segm", [128, 4])
    w_in_d = din("w_in", [D, DIN])
    conv_w_d = din("conv_w", [4, D])
    conv_b_d = din("conv_b", [D])
    rg_wa_d = din("rg_wa", [16, 128, 128])
    rg_ba_d = din("rg_ba", [16, 128])
    rg_wx_d = din("rg_wx", [16, 128, 128])
    rg_bx_d = din("rg_bx", [16, 128])
    rg_lam_d = din("rg_lambda", [D])
    w_a_d = din("w_a_out", [D, D])
    a_re_d = din("ssm_a_re", [64, 64])
    a_im_d = din("ssm_a_im", [64, 64])
    ldt_d = din("ssm_log_dt", [64])
    b_re_d = din("ssm_b_re", [64, 64, 16])
    b_im_d = din("ssm_b_im", [64, 64, 16])
    c_re_d = din("ssm_c_re", [64, 16, 64])
    c_im_d = din("ssm_c_im", [64, 16, 64])
    ssm_d_d = din("ssm_d", [64, 16])
    glu_w_d = din("glu_w", [1024, D])
    glu_v_d = din("glu_v", [1024, D])
    w_out_d = din("w_out", [D, D])
    ln1_g_d = din("ln1_g", [D])
    ln1_b_d = din("ln1_b", [D])
    w_up_d = din("mlp_w_up", [D, DFF])
    b_up_d = din("mlp_b_up", [DFF])
    w_dn_d = din("mlp_w_down", [DFF, D])
    b_dn_d = din("mlp_b_down", [D])
    ln2_g_d = din("ln2_g", [D])
    ln2_b_d = din("ln2_b", [D])
    out_d = nc.dram_tensor("out", [TOK, D], F32, kind="ExternalOutput").ap()

    ab_d = dscr("ab_d", [2, 16, 128, TOK], F32)
    xT_d = dscr("xT_d", [128, 16, TOK + HALO], BF16)
    yS_d = dscr("yS_d", [128, 8, TOK], BF16)
    mix_d = dscr("mix_d", [128, 16, TOK], BF16)
    W1_d = dscr("W1_d", [8, 128, 8192], BF16)
    W3_d = dscr("W3_d", [8, 128, 8192], BF16)
    KT_d = dscr("KT_d", [8, 128, 1024], BF16)
    S_d = nc.dram_tensor("S_d", [128, 16384], F32).ap()
    ccin_d = nc.dram_tensor("ccin_d", [128, 96], F32).ap()
    ccout_d = nc.dram_tensor("ccout_d", [NCORE * 128, 96], F32).ap()
    if dbg:
        dbg_small = nc.dram_tensor("dbg_small", [128, 256], F32, kind="ExternalOutput").ap()
        dbg_x1 = nc.dram_tensor("dbg_x1", [1024, D], F32, kind="ExternalOutput").ap()
        dbg_sc = nc.dram_tensor("dbg_sc", [128, 24 * 32], F32, kind="ExternalOutput").ap()
        dbg_lp = nc.dram_tensor("dbg_lp", [128, 9 * 4 * 32], F32, kind="ExternalOutput").ap()
        dbg_p3 = nc.dram_tensor("dbg_p3", [128, 104], F32, kind="ExternalOutput").ap()
        dbg_p1 = nc.dram_tensor("dbg_p1", [128, 128], F32, kind="ExternalOutput").ap()

    w_in_v = w_in_d.rearrange("(kt p) c -> p kt c", p=128)
    if not small:
        w_a_v = w_a_d.rearrange("(kt p) c -> p kt c", p=128)
        glu_w_v = glu_w_d.rearrange("(kt p) c -> p kt c", p=128)
        glu_v_v = glu_v_d.rearrange("(kt p) c -> p kt c", p=128)
        w_out_v = w_out_d.rearrange("(kt p) c -> p kt c", p=128)
        w_up_v = w_up_d.rearrange("(kt p) c -> p kt c", p=128)
        w_dn_v = w_dn_d.rearrange("(ft p) c -> p ft c", p=128)

    with ExitStack() as top:
        S = Sched(nc, top)
        ccsem = top.enter_context(nc.semaphore("ccsem"))
        arena_t = top.enter_context(nc.sbuf_tensor("arena", [128, SB_BYTES], mybir.dt.uint8))
        A = Arena(arena_t, SB_BYTES)

        def sbt(name, shape, dt, top_=False):
            return A.alloc(list(shape), dt, top=top_)

        def end_phase(m):
            S.barrier()
            A.release(m)

        ps = [top.enter_context(nc.psum_tensor(f"ps{i}", [128, 512], F32)) for i in range(8)]
        bank_ctr = [0]

        def nextbank():
            b = bank_ctr[0] % 8
            bank_ctr[0] += 1
            return b

        ev_ctr = [0]

        def ev_eng():
            ev_ctr[0] += 1
            return "act" if ev_ctr[0] % 2 == 0 else "dve"

        def copy_op(eng, out, in_):
            if eng == "act":
                return lambda e: e.copy(out, in_)
            return lambda e: e.tensor_copy(out, in_)

        ident = sbt("ident", [128, 128], F32)
        P1 = sbt("P1", [128, 128], F32)
        bup = sbt("bup", [128, 64], F32)
        rgc = sbt("rgc", [128, 8, 16], F32)
        wa_sb = sbt("wa_sb", [128, 16, 128], BF16)
        wx_sb = sbt("wx_sb", [128, 16, 128], BF16)
        Ecur = sbt("Ecur", [128, 16], F32)
        sumth = sbt("sumth", [128, 16, 4], F32)
        hcar = sbt("hcar", [128, 16], F32)
        Hin = sbt("Hin", [128, 2, 32], F32)
        LL8 = sbt("LL8", [128, 4, 32], F32)
        D2k = sbt("D2k", [128, 2, 32], F32)
        ccin = sbt("ccin", [128, 96], F32)
        segm = sbt("segm", [128, 4], F32)
        Hs = sbt("Hs", [128, 2, 32], F32)
        Xp = sbt("Xp", [128, 2, 32], F32)
        Yp = sbt("Yp", [128, 2, 32], F32)
        Xd = sbt("Xd", [128, 2, 32], F32)
        Yd = sbt("Yd", [128, 2, 32], F32)
        Ha = sbt("Ha", [128, 2, 32], F32)
        Hb = sbt("Hb", [128, 2, 32], F32)
        D1k = sbt("D1k", [128, 4, 32], F32)
        stats = sbt("stats", [128, 8, 4, 6], F32)
        mv = sbt("mv", [128, 8, 4], F32)

        S.op("pool", lambda e: e.memset(ident[:], 1.0), writes=["ident"])
        S.op("pool", lambda e: e.affine_select(out=ident[:], in_=ident[:], pattern=[[-1, 128]],
                                               compare_op=ALU.is_equal, fill=0.0, base=0, channel_multiplier=1),
             reads=["ident"], writes=["ident"])
        S.dma("sp", "segm", lambda e: e.dma_start(out=segm[:], in_=segm_d), writes=["segm"])
        S.dma("pool", "wa_sb", lambda e: e.dma_start(out=wa_sb[:], in_=rg_wa_d.rearrange("h i j -> i h j")), writes=["wa_sb"])
        S.dma("pool", "wx_sb", lambda e: e.dma_start(out=wx_sb[:], in_=rg_wx_d.rearrange("h i j -> i h j")), writes=["wx_sb"])

        m0 = A.mark()
        if True:
            st1 = sbt("st1", [128, 128], F32)
            st2 = sbt("st2", [64, 128], F32)
            st3 = sbt("st3", [104, 128], F32)
            ld2 = sbt("ld2", [32, 2], F32)
            P3 = sbt("P3", [128, 104], F32)
            S.dma("sp", "st1a", lambda e: e.dma_start(out=st1[0:64, :], in_=conv_w_d.rearrange("k (h p) -> (k h) p", p=128)), writes=["st1a"])
            S.dma("sp", "st1b", lambda e: e.dma_start(out=st1[64:80, :], in_=conv_b_d.rearrange("(h p) -> h p", p=128)), writes=["st1b"])
            S.dma("sp", "st1c", lambda e: e.dma_start(out=st1[80:96, :], in_=rg_ba_d), writes=["st1c"])
            S.dma("sp", "st1d", lambda e: e.dma_start(out=st1[96:112, :], in_=rg_bx_d), writes=["st1d"])
            S.dma("sp", "st1e", lambda e: e.dma_start(out=st1[112:128, :], in_=rg_lam_d.rearrange("(h p) -> h p", p=128)), writes=["st1e"])
            S.dma("sp", "st2", lambda e: e.dma_start(out=st2[:], in_=b_up_d.rearrange("(f p) -> f p", p=128)), writes=["st2"])
            S.dma("sp", "ld2", lambda e: e.dma_start(out=ld2[:], in_=ldt_d.rearrange("(j g) -> j g", g=2)), writes=["ld2"])
            S.dma("sp", "st3b", lambda e: e.dma_start(out=st3[32:64, :], in_=a_re_d.rearrange("(j g) p -> j (g p)", g=2)), writes=["st3b"])
            S.dma("sp", "st3c", lambda e: e.dma_start(out=st3[64:96, :], in_=a_im_d.rearrange("(j g) p -> j (g p)", g=2)), writes=["st3c"])
            S.dma("sp", "st3d", lambda e: e.dma_start(out=st3[96:104, :], in_=ssm_d_d.rearrange("(kt g) h -> kt (g h)", g=8)), writes=["st3d"])
            S.op("dve", lambda e: e.tensor_copy(st3[0:32, :].rearrange("j (g p) -> j g p", g=2),
                                                ld2[:, :].unsqueeze(2).to_broadcast([32, 2, 64])),
                 reads=["ld2"], writes=["st3a"])
            b = nextbank()
            S.op("pe", lambda e: e.transpose(ps[b][:, 0:128], st1[:, :], ident[:, :]),
                 reads=["st1a", "st1b", "st1c", "st1d", "st1e", "ident"], writes=[f"ps{b}"])
            S.op("dve", lambda e: e.tensor_copy(P1[:], ps[b][:, 0:128]), reads=[f"ps{b}"], writes=["P1"])
            b = nextbank()
            S.op("pe", lambda e: e.transpose(ps[b][:, 0:64], st2[:, :], ident[0:64, 0:64]),
                 reads=["st2", "ident"], writes=[f"ps{b}"])
            S.op("dve", lambda e: e.tensor_copy(bup[:], ps[b][:, 0:64]), reads=[f"ps{b}"], writes=["bup"])
            b = nextbank()
            S.op("pe", lambda e: e.transpose(ps[b][:, 0:104], st3[:, :], ident[0:104, 0:104]),
                 reads=["st3a", "st3b", "st3c", "st3d", "ident"], writes=[f"ps{b}"])
            S.op("dve", lambda e: e.tensor_copy(P3[:], ps[b][:, 0:104]), reads=[f"ps{b}"], writes=["P3"])

            ba_v = P1[:, 80:96]
            bx_v = P1[:, 96:112]
            lam_v = P1[:, 112:128]
            S.op("dve", lambda e: e.tensor_scalar(rgc[:, 0, :], ba_v, 0.5, None, ALU.mult), reads=["P1"], writes=["rgc0"])
            S.op("dve", lambda e: e.tensor_scalar(rgc[:, 1, :], bx_v, 0.5, None, ALU.mult), reads=["P1"], writes=["rgc1"])
            S.op("act", lambda e: e.activation(rgc[:, 5, :], lam_v, AF.Exp, scale=-1.0), reads=["P1"], writes=["rgc5"])
            S.op("act", lambda e: e.activation(rgc[:, 5, :], rgc[:, 5, :], AF.Ln, bias=1.0), reads=["rgc5"], writes=["rgc5"])
            S.op("dve", lambda e: e.tensor_scalar(rgc[:, 2, :], rgc[:, 5, :], -8.0, None, ALU.mult), reads=["rgc5"], writes=["rgc2"])
            S.op("dve", lambda e: e.tensor_scalar(rgc[:, 3, :], rgc[:, 5, :], -4.0, None, ALU.mult), reads=["rgc5"], writes=["rgc3"])
            S.op("dve", lambda e: e.tensor_scalar(rgc[:, 4, :], rgc[:, 5, :], 4.0, None, ALU.mult), reads=["rgc5"], writes=["rgc4"])

            sc = sbt("sc", [128, 24, 32], F32)
            Lp = sbt("Lp", [128, 9, 4, 32], F32)
            isc = sbt("isc", [128, 32], I32)

            def V(i):
                return sc[:, i, :]

            def dv(fn):
                S.op("dve", fn, reads=["sc", "P3"], writes=["sc"])

            def av(fn):
                S.op("act", fn, reads=["sc", "P3"], writes=["sc"])
            ldt_v, are_v, aim_v = P3[:, 0:32], P3[:, 32:64], P3[:, 64:96]
            av(lambda e: e.activation(V(0), ldt_v, AF.Exp))
            dv(lambda e: e.tensor_scalar(V(1), are_v, -1e-4, None, ALU.min))
            dv(lambda e: e.tensor_tensor(V(2), V(1), V(0), ALU.mult))
            av(lambda e: e.activation(V(3), V(2), AF.Exp))
            dv(lambda e: e.tensor_tensor(V(4), aim_v, V(0), ALU.mult))

            def reduced_sin(dst, shift):
                dv(lambda e: e.tensor_scalar(V(5), V(4), shift, None, ALU.add))
                dv(lambda e: e.tensor_scalar(V(6), V(5), 1.0 / TWO_PI, None, ALU.mult))
                dv(lambda e: e.tensor_copy(isc[:], V(6)))
                dv(lambda e: e.tensor_copy(V(6), isc[:]))
                dv(lambda e: e.scalar_tensor_tensor(V(5), V(6), -TWO_PI, V(5), ALU.mult, ALU.add))
                dv(lambda e: e.tensor_scalar(V(7), V(5), PI, None, ALU.is_gt))
                dv(lambda e: e.scalar_tensor_tensor(V(5), V(7), -TWO_PI, V(5), ALU.mult, ALU.add))
                dv(lambda e: e.tensor_scalar(V(7), V(5), -PI, None, ALU.is_lt))
                dv(lambda e: e.scalar_tensor_tensor(V(5), V(7), TWO_PI, V(5), ALU.mult, ALU.add))
                dv(lambda e: e.tensor_scalar(V(5), V(5), PI, -PI, ALU.min, ALU.max))
                av(lambda e: e.activation(dst, V(5), AF.Sin))
            reduced_sin(V(9), 0.0)
            reduced_sin(V(10), PI / 2)
            dv(lambda e: e.tensor_tensor(V(11), V(3), V(10), ALU.mult))
            dv(lambda e: e.tensor_tensor(V(12), V(3), V(9), ALU.mult))
            dv(lambda e: e.tensor_tensor(V(13), V(1), V(1), ALU.mult))
            dv(lambda e: e.tensor_tensor(V(5), aim_v, aim_v, ALU.mult))
            dv(lambda e: e.tensor_tensor(V(13), V(13), V(5), ALU.add))
            dv(lambda e: e.reciprocal(V(13), V(13)))
            dv(lambda e: e.tensor_scalar(V(14), V(11), -1.0, None, ALU.add))
            dv(lambda e: e.tensor_tensor(V(5), V(14), V(1), ALU.mult))
            dv(lambda e: e.tensor_tensor(V(6), V(12), aim_v, ALU.mult))
            dv(lambda e: e.tensor_tensor(V(5), V(5), V(6), ALU.add))
            dv(lambda e: e.tensor_tensor(V(15), V(5), V(13), ALU.mult))
            dv(lambda e: e.tensor_tensor(V(5), V(12), V(1), ALU.mult))
            dv(lambda e: e.tensor_tensor(V(6), V(14), aim_v, ALU.mult))
            dv(lambda e: e.tensor_tensor(V(5), V(5), V(6), ALU.subtract))
            dv(lambda e: e.tensor_tensor(V(16), V(5), V(13), ALU.mult))

            def lp(fn):
                S.op("dve", fn, reads=["sc", "Lp"], writes=["Lp", "sc"])
            lp(lambda e: e.memset(Lp[:, 0, 0, :], 1.0))
            lp(lambda e: e.memset(Lp[:, 0, 1, :], 0.0))
            lp(lambda e: e.tensor_copy(Lp[:, 1, 0, :], V(11)))
            lp(lambda e: e.tensor_copy(Lp[:, 1, 1, :], V(12)))

            def cmul(o_re, o_im, a_re_, a_im_, b_re_, b_im_, t1, t2, t3):
                lp(lambda e: e.tensor_tensor(t1, a_re_, b_re_, ALU.mult))
                lp(lambda e: e.tensor_tensor(t2, a_im_, b_im_, ALU.mult))
                lp(lambda e: e.tensor_tensor(t1, t1, t2, ALU.subtract))
                lp(lambda e: e.tensor_tensor(t2, a_re_, b_im_, ALU.mult))
                lp(lambda e: e.tensor_tensor(t3, a_im_, b_re_, ALU.mult))
                lp(lambda e: e.tensor_tensor(o_im, t3, t2, ALU.add))
                lp(lambda e: e.tensor_copy(o_re, t1))
            for n in range(1, 8):
                cmul(Lp[:, n + 1, 0, :], Lp[:, n + 1, 1, :], Lp[:, n, 0, :], Lp[:, n, 1, :],
                     Lp[:, 1, 0, :], Lp[:, 1, 1, :], V(17), V(18), V(21))
            for n in range(9):
                lp(lambda e, n=n: e.tensor_scalar(Lp[:, n, 2:4, :], Lp[:, n, 0:2, :], -1.0, None, ALU.mult))
            S.op("dve", lambda e: e.tensor_copy(LL8[:, 0, :], Lp[:, 8, 0, :]), reads=["Lp"], writes=["LL8"])
            S.op("dve", lambda e: e.tensor_copy(LL8[:, 1, :], Lp[:, 8, 0, :]), reads=["Lp"], writes=["LL8"])
            S.op("dve", lambda e: e.tensor_copy(LL8[:, 2, :], Lp[:, 8, 3, :]), reads=["Lp"], writes=["LL8"])
            S.op("dve", lambda e: e.tensor_copy(LL8[:, 3, :], Lp[:, 8, 1, :]), reads=["Lp"], writes=["LL8"])
            lp(lambda e: e.tensor_copy(V(19), Lp[:, 8, 0, :]))
            lp(lambda e: e.tensor_copy(V(20), Lp[:, 8, 1, :]))
            for it_ in range(8):
                cmul(V(19), V(20), V(19), V(20), V(19), V(20), V(17), V(18), V(21))
                if it_ == 6:
                    S.op("dve", lambda e: e.tensor_copy(D1k[:, 0, :], V(19)), reads=["Lp", "sc"], writes=["D1k"])
                    S.op("dve", lambda e: e.tensor_copy(D1k[:, 1, :], V(19)), reads=["Lp", "sc"], writes=["D1k"])
                    S.op("dve", lambda e: e.tensor_scalar(D1k[:, 2, :], V(20), -1.0, None, ALU.mult), reads=["Lp", "sc"], writes=["D1k"])
                    S.op("dve", lambda e: e.tensor_copy(D1k[:, 3, :], V(20)), reads=["Lp", "sc"], writes=["D1k"])
            S.op("dve", lambda e: e.tensor_copy(D2k[:, 0, :], V(19)), reads=["Lp", "sc"], writes=["D2k"])
            S.op("dve", lambda e: e.tensor_copy(D2k[:, 1, :], V(20)), reads=["Lp", "sc"], writes=["D2k"])

            if dbg:
                S.dma("sp", "dbgsc", lambda e: e.dma_start(out=dbg_sc, in_=sc[:].rearrange("p a b -> p (a b)")), reads=["sc", "Lp"])
                S.dma("sp", "dbglp", lambda e: e.dma_start(out=dbg_lp, in_=Lp[:].rearrange("p a b c -> p (a b c)")), reads=["sc", "Lp"])
                S.dma("sp", "dbgp3", lambda e: e.dma_start(out=dbg_p3, in_=P3[:]), reads=["P3"])
                S.dma("sp", "dbgp1", lambda e: e.dma_start(out=dbg_p1, in_=P1[:]), reads=["P1"])
            Bn = sbt("Bn", [128, 2, 32, 16], F32)
            Bb = sbt("Bb", [128, 2, 32, 16], F32)
            Vn = sbt("Vn", [128, 8, 2, 32, 16], F32)
            tb = sbt("tb", [128, 32, 16], F32)
            S.dma("sp", "Bn0", lambda e: e.dma_start(out=Bn[:, 0, :, :], in_=b_re_d.rearrange("(j g) p h -> (g p) j h", g=2)), writes=["Bn0"])
            S.dma("sp", "Bn1", lambda e: e.dma_start(out=Bn[:, 1, :, :], in_=b_im_d.rearrange("(j g) p h -> (g p) j h", g=2)), writes=["Bn1"])

            def bc(v):
                return v.unsqueeze(2).to_broadcast([128, 32, 16])

            def bb(fn):
                S.op("dve", fn, reads=["sc", "Lp", "Bn0", "Bn1", "Bb", "tb"], writes=["Bb", "tb"])
            bb(lambda e: e.tensor_tensor(Bb[:, 0], Bn[:, 0], bc(V(15)), ALU.mult))
            bb(lambda e: e.tensor_tensor(tb[:], Bn[:, 1], bc(V(16)), ALU.mult))
            bb(lambda e: e.tensor_tensor(Bb[:, 0], Bb[:, 0], tb[:], ALU.subtract))
            bb(lambda e: e.tensor_tensor(Bb[:, 1], Bn[:, 1], bc(V(15)), ALU.mult))
            bb(lambda e: e.tensor_tensor(tb[:], Bn[:, 0], bc(V(16)), ALU.mult))
            bb(lambda e: e.tensor_tensor(Bb[:, 1], Bb[:, 1], tb[:], ALU.add))

            def vv(fn):
                S.op("dve", fn, reads=["Bb", "Lp", "Vn", "tb"], writes=["Vn", "tb"])
            for n in range(8):
                vv(lambda e, n=n: e.tensor_tensor(Vn[:, n, 0], Bb[:, 0], bc(Lp[:, n, 0, :]), ALU.mult))
                vv(lambda e, n=n: e.tensor_tensor(tb[:], Bb[:, 1], bc(Lp[:, n, 1, :]), ALU.mult))
                vv(lambda e, n=n: e.tensor_tensor(Vn[:, n, 0], Vn[:, n, 0], tb[:], ALU.subtract))
                vv(lambda e, n=n: e.tensor_tensor(Vn[:, n, 1], Bb[:, 1], bc(Lp[:, n, 0, :]), ALU.mult))
                vv(lambda e, n=n: e.tensor_tensor(tb[:], Bb[:, 0], bc(Lp[:, n, 1, :]), ALU.mult))
                vv(lambda e, n=n: e.tensor_tensor(Vn[:, n, 1], Vn[:, n, 1], tb[:], ALU.add))

            Cn = sbt("Cn", [128, 2, 8, 64], F32)
            par = sbt("par", [128, 2], F32)
            Xc = sbt("Xc", [128, 2, 128], F32)
            Yc = sbt("Yc", [128, 3, 8, 128], F32)
            S.dma("sp", "Cn0", lambda e: e.dma_start(out=Cn[:, 0, :, :], in_=c_re_d.rearrange("(kt g) h p -> (g h) kt p", g=8)), writes=["Cn0"])
            S.dma("sp", "Cn1", lambda e: e.dma_start(out=Cn[:, 1, :, :], in_=c_im_d.rearrange("(kt g) h p -> (g h) kt p", g=8)), writes=["Cn1"])
            Mp = sbt("Mp", [128, 128], F32)
            S.op("dve", lambda e: e.memset(Mp[:], 0.0), writes=["Mp"])
            for blk_ in range(4):
                S.op("dve", lambda e, blk_=blk_: e.memset(Mp[:, 32 * blk_ + 16:32 * blk_ + 32], 1.0), reads=["Mp"], writes=["Mp"])
            b = nextbank()
            S.op("pe", lambda e: e.transpose(ps[b][:, 0:128], Mp[:, :], ident[:, :]), reads=["Mp", "ident"], writes=[f"ps{b}"])
            S.op("dve", lambda e: e.tensor_copy(par[:, 0:1], ps[b][:, 0:1]), reads=[f"ps{b}"], writes=["par"])
            S.op("dve", lambda e: e.tensor_scalar(par[:, 1:2], par[:, 0:1], -1.0, 1.0, ALU.mult, ALU.add), reads=["par"], writes=["par"])
            for kt in range(8):
                for ri in range(2):
                    S.op("dve", lambda e, kt=kt, ri=ri: e.tensor_scalar(Xc[:, ri, 0:64], Cn[:, ri, kt, :], par[:, 1:2], None, ALU.mult),
                         reads=["Cn0", "Cn1", "par", "Xc%d" % ri], writes=["Xc%d" % ri])
                    S.op("dve", lambda e, kt=kt, ri=ri: e.tensor_scalar(Xc[:, ri, 64:128], Cn[:, ri, kt, :], par[:, 0:1], None, ALU.mult),
                         reads=["Cn0", "Cn1", "par", "Xc%d" % ri], writes=["Xc%d" % ri])
                    b = nextbank()
                    S.op("pe", lambda e, ri=ri, b=b: e.transpose(ps[b][:, 0:128], Xc[:, ri, :], ident[:, :]),
                         reads=["Xc%d" % ri, "ident"], writes=[f"ps{b}"])
                    S.op("act", lambda e, kt=kt, ri=ri, b=b: e.copy(Yc[:, ri, kt, :], ps[b][:, 0:128]),
                         reads=[f"ps{b}"], writes=["Yc"])
                    if ri == 1:
                        S.op("act", lambda e, kt=kt, b=b: e.mul(Yc[:, 2, kt, :], ps[b][:, 0:128], -1.0),
                             reads=[f"ps{b}"], writes=["Yc"])

            W3b = [sbt(f"W3b{i}", [128, 4, 8, 2, 128], BF16) for i in range(2)]
            W1b = [sbt(f"W1b{i}", [128, 4, 8, 2, 128], BF16) for i in range(2)]
            KTb = [sbt(f"KTb{i}", [128, 8, 128], BF16) for i in range(2)]
            Zf = [sbt(f"Zb{i}", [128, 1280], F32) for i in range(2)]
            Zb = [z[:, 0:1024].rearrange("p (q r c) -> p q r c", q=4, r=2) for z in Zf]
            Zd = [z[:, 0:1152].rearrange("p (q x) -> p q x", x=288)[:, :, 0:256].rearrange("p q (r c) -> p q r c", r=2) for z in Zf]
            w3t = sbt("w3t", [128, 32], F32)
            dI = sbt("dI", [128, 128], F32)
            for i in range(2):
                S.op("pool", lambda e, i=i: e.memset(W3b[i][:], 0.0), writes=[f"W3b{i}", f"W3b{i}a"])
                S.op("pool", lambda e, i=i: e.memset(Zf[i][:], 0.0), writes=[f"Zb{i}"])
            zc = 0
            for kt in range(8):
                w3 = W3b[kt % 2]
                w1 = W1b[kt % 2]
                ktb = KTb[kt % 2]
                k3, k1, kk = f"W3b{kt%2}", f"W1b{kt%2}", f"KTb{kt%2}"
                for q in range(4):
                    j = 4 * kt + q
                    cs = slice(32 * q, 32 * q + 32)
                    for tau in range(8):
                        n = tau + 1
                        S.op("dve", lambda e, n=n, j=j, cs=cs, w3=w3, q=q, tau=tau: e.tensor_scalar(
                            w3[:, q, tau, 0, cs], Yc[:, 0, kt, cs], Lp[:, n, 0, j:j + 1], None, ALU.mult), reads=["Yc", "Lp"], writes=[k3 + "a"])
                        S.op("dve", lambda e, n=n, j=j, cs=cs, w3=w3, q=q, tau=tau: e.tensor_scalar(
                            w3[:, q, tau, 1, cs], Yc[:, 0, kt, cs], Lp[:, n, 3, j:j + 1], None, ALU.mult), reads=["Yc", "Lp"], writes=[k3 + "a"])
                for q in range(4):
                    j = 4 * kt + q
                    cs = slice(32 * q, 32 * q + 32)
                    for tau in range(8):
                        n = tau + 1
                        S.op("dve", lambda e, n=n, j=j, cs=cs, w3=w3, q=q, tau=tau: e.scalar_tensor_tensor(
                            w3[:, q, tau, 0, cs], Yc[:, 1, kt, cs], Lp[:, n, 3, j:j + 1], w3[:, q, tau, 0, cs], ALU.mult, ALU.add),
                            reads=["Yc", "Lp", k3 + "a"], writes=[k3])
                        S.op("dve", lambda e, n=n, j=j, cs=cs, w3=w3, q=q, tau=tau: e.scalar_tensor_tensor(
                            w3[:, q, tau, 1, cs], Yc[:, 1, kt, cs], Lp[:, n, 2, j:j + 1], w3[:, q, tau, 1, cs], ALU.mult, ALU.add),
                            reads=["Yc", "Lp", k3 + "a"], writes=[k3])
                S.dma("sp", k3 + "st", lambda e, kt=kt, w3=w3: e.dma_start(out=W3_d[kt], in_=w3[:].rearrange("p a b c d -> p (a b c d)")),
                      reads=[k3, k3 + "a"], writes=[f"W3d{kt}"])
                S.op("dve", lambda e, kt=kt: e.tensor_scalar(dI[:], ident[:], P3[:, 96 + kt:97 + kt], None, ALU.mult),
                     reads=["ident", "P3", "dI"], writes=["dI"])
                for n in range(8):
                    zi = zc % 2
                    zb, zd = Zb[zi], Zd[zi]
                    kz = f"Zb{zi}"
                    zc += 1
                    S.op("pool", lambda e, zd=zd, n=n: e.tensor_copy(
                        zd[0:64, :, :, 0:16], Vn[0:64, n, :, 4 * kt:4 * kt + 4, :].rearrange("p r q h -> p q r h")),
                        reads=["Vn", kz], writes=[kz])
                    S.op("pool", lambda e, zd=zd, n=n: e.tensor_copy(
                        zd[64:128, :, :, 16:32], Vn[64:128, n, :, 4 * kt:4 * kt + 4, :].rearrange("p r q h -> p q r h")),
                        reads=["Vn", kz], writes=[kz])
                    for q in range(4):
                        for ri in range(2):
                            b = nextbank()
                            S.op("pe", lambda e, zb=zb, q=q, ri=ri, b=b: e.transpose(ps[b][:, 0:128], zb[:, q, ri, :], ident[:, :]),
                                 reads=[kz, "ident"], writes=[f"ps{b}"])
                            eng = ev_eng()
                            S.op(eng, copy_op(eng, w1[:, q, 7 - n, ri, :], ps[b][:, 0:128]), reads=[f"ps{b}"], writes=[k1])
                    b = nextbank()
                    for q in range(4):
                        cs = slice(32 * q, 32 * q + 32)
                        S.op("pe", lambda e, zb=zb, q=q, cs=cs, b=b: e.matmul(ps[b][:, cs], zb[:, q, 0, :], Yc[:, 0, kt, cs], start=True, stop=False),
                             reads=[kz, "Yc"], writes=[f"ps{b}"], inc=False)
                        S.op("pe", lambda e, zb=zb, q=q, cs=cs, b=b: e.matmul(ps[b][:, cs], zb[:, q, 1, :], Yc[:, 2, kt, cs], start=False, stop=True),
                             reads=[kz, "Yc"], writes=[f"ps{b}"], inc=(q == 3))
                    if n == 0:
                        S.op("dve", lambda e, ktb=ktb, b=b: e.tensor_tensor(ktb[:, 0, :], ps[b][:, 0:128], dI[:], ALU.add),
                             reads=[f"ps{b}", "dI"], writes=[kk])
                    else:
                        S.op("act", lambda e, ktb=ktb, b=b, n=n: e.copy(ktb[:, n, :], ps[b][:, 0:128]),
                             reads=[f"ps{b}"], writes=[kk])
                S.dma("sp", k1 + "st", lambda e, kt=kt, w1=w1: e.dma_start(out=W1_d[kt], in_=w1[:].rearrange("p a b c d -> p (a b c d)")),
                      reads=[k1], writes=[f"W1d{kt}"])
                S.dma("sp", kk + "st", lambda e, kt=kt, ktb=ktb: e.dma_start(out=KT_d[kt], in_=ktb[:].rearrange("p a b -> p (a b)")),
                      reads=[kk], writes=[f"KTd{kt}"])
        end_phase(m0)
        if stop_after == "p0":
            return nc

        mA = A.mark()
        u_sb = sbt("u_sb", [128, 8, 8, 256], BF16, top_=True)
        S.op("dve", lambda e: e.memset(Ecur[:], 0.0), writes=["Ecur"])
        S.op("dve", lambda e: e.memset(Hs[:], 0.0), writes=["Hs"])
        for seg in range(4):
            own = (seg == 3)
            xo = seg * TOK
            S.op("dve", lambda e: e.tensor_scalar(Ecur[:], Ecur[:], segm[:, seg:seg + 1], None, ALU.mult), reads=["Ecur", "segm"], writes=["Ecur"])
            S.op("dve", lambda e: e.tensor_scalar(Hs[:].rearrange("p a b -> p (a b)"), Hs[:].rearrange("p a b -> p (a b)"), segm[:, seg:seg + 1], None, ALU.mult),
                 reads=["Hs", "segm"], writes=["Hs"])
            if own:
                S.op("dve", lambda e: e.tensor_copy(hcar[:], Ecur[:]), reads=["Ecur"], writes=["hcar"])
                S.op("dve", lambda e: e.tensor_copy(Hin[:], Hs[:]), reads=["Hs"], writes=["Hin"])
            mA1 = A.mark()
            xT = sbt("xT", [128, 16, TOK + HALO], BF16)
            mx = A.mark()
            xin = [sbt(f"xin{i}", [128, D], F32) for i in range(3)]
            tiles = [(0, 3, 0)] + [(3 + 128 * t, 128, 3 + 128 * t) for t in range(16)]
            for ti, (r0, nr, c0) in enumerate(tiles):
                xb = xin[ti % 3]
                key = f"xin{ti%3}"
                S.dma("sp", key, lambda e, xb=xb, r0=r0, nr=nr: e.dma_start(out=xb[0:nr, :], in_=x_d[xo + r0:xo + r0 + nr, :]), writes=[key])
                for b4 in range(4):
                    b = nextbank()
                    for jj in range(4):
                        kt = 4 * b4 + jj
                        S.op("pe", lambda e, xb=xb, nr=nr, kt=kt, jj=jj, b=b: e.transpose(
                            ps[b][:, jj * 128:jj * 128 + nr], xb[0:nr, kt * 128:(kt + 1) * 128], ident[0:nr, 0:nr]),
                            reads=[key, "ident"], writes=[f"ps{b}"], inc=(jj == 3))
                    eng = ev_eng()
                    S.op(eng, copy_op(eng, xT[:, 4 * b4:4 * b4 + 4, c0:c0 + nr],
                                      ps[b][:, 0:512].rearrange("p (j t) -> p j t", j=4)[:, :, 0:nr]),
                         reads=[f"ps{b}"], writes=[f"xT{ti}_{b4}"])
            allx = [f"xT{ti}_{b4}" for ti in range(17) for b4 in range(4)]
            if own:
                S.dma("sp", "xTst", lambda e: e.dma_start(out=xT_d, in_=xT[:]), reads=allx, writes=["xTd"])
            S.barrier()
            A.release(mx)

            def alloc_wb():
                return [sbt(f"wb{i}", [128, 16, 512], BF16) for i in range(2)]

            def do_A1(scan_emit):
                NH = 1024
                bufs = []
                for i in range(2):
                    bufs.append(dict(
                        xr=sbt(f"xr{i}", [128, NH + 3], F32), xc=sbt(f"xc{i}", [128, NH], F32),
                        xcb=sbt(f"xcb{i}", [128, NH], BF16), thr=sbt(f"thr{i}", [128, NH], F32),
                        thi=sbt(f"thi{i}", [128, NH], F32), at=sbt(f"at{i}", [128, NH], F32),
                        tt=sbt(f"tt{i}", [128, NH], F32)))
                for B__ in bufs:
                    B__["hl"] = B__["tt"]
                un = 0
                for h in range(16):
                    if h % 4 == 0:
                        load_w((h // 4) % 2, w_in_v, 512 * (h // 4))
                    wbi = (h // 4) % 2
                    hh = h % 4
                    for hf in range(2):
                        B_ = bufs[un % 2]
                        sx = str(un % 2)
                        un += 1
                        cb0 = NH * hf
                        pieces = [(cb0, 3, 0), (cb0 + 3, 512, 3), (cb0 + 515, 512, 515)]
                        for (c0, n, off) in pieces:
                            b = nextbank()
                            for kt in range(16):
                                S.op("pe", lambda e, b=b, n=n, kt=kt, c0=c0: e.matmul(
                                    ps[b][:, 0:n], wb[wbi][:, kt, hh * 128:(hh + 1) * 128], xT[:, kt, c0:c0 + n],
                                    start=(kt == 0), stop=(kt == 15)),
                                    reads=[f"wb{wbi}"], writes=[f"ps{b}"], inc=(kt == 15))
                            S.op("act", lambda e, b=b, n=n, off=off, B_=B_: e.copy(B_["xr"][:, off:off + n], ps[b][:, 0:n]),
                                 reads=[f"ps{b}"], writes=["xr" + sx + "_%d" % off])
                        xrk = ["xr" + sx + "_%d" % o for o in (0, 3, 515)]
                        S.op("dve", lambda e, B_=B_: e.tensor_scalar(B_["xc"][:], B_["xr"][:, 0:NH], P1[:, h:h + 1], P1[:, 64 + h:65 + h], ALU.mult, ALU.add),
                             reads=xrk + ["xc" + sx, "xcb" + sx], writes=["xc" + sx])
                        for k in range(1, 4):
                            S.op("dve", lambda e, B_=B_, k=k: e.scalar_tensor_tensor(
                                B_["xc"][:], B_["xr"][:, k:k + NH], P1[:, k * 16 + h:k * 16 + h + 1], B_["xc"][:], ALU.mult, ALU.add),
                                reads=xrk + ["xc" + sx], writes=["xc" + sx])
                        S.op("act", lambda e, B_=B_: e.copy(B_["xcb"][:], B_["xc"][:]), reads=["xc" + sx], writes=["xcb" + sx])
                        for g, (wsb, wk, dst) in enumerate(((wa_sb, "wa_sb", "thr"), (wx_sb, "wx_sb", "thi"))):
                            for t2 in range(2):
                                b = nextbank()
                                S.op("pe", lambda e, b=b, wsb=wsb, t2=t2, B_=B_: e.matmul(
                                    ps[b][:, :], wsb[:, h, :], B_["xcb"][:, t2 * 512:(t2 + 1) * 512], start=True, stop=True),
                                    reads=[wk, "xcb" + sx], writes=[f"ps{b}"])
                                if g == 0:
                                    S.op("act", lambda e, b=b, t2=t2, B_=B_, dst=dst: e.activation(
                                        B_[dst][:, t2 * 512:(t2 + 1) * 512], ps[b][:, :], AF.Tanh, bias=rgc[:, 0, h:h + 1], scale=0.5),
                                        reads=[f"ps{b}"], writes=[dst + sx + "_%d" % t2])
                                else:
                                    S.op("act", lambda e, b=b, t2=t2, B_=B_, dst=dst: e.activation(
                                        B_[dst][:, t2 * 512:(t2 + 1) * 512], ps[b][:, :], AF.Tanh, bias=rgc[:, 1, h:h + 1], scale=0.5),
                                        reads=[f"ps{b}"], writes=[dst + sx + "_%d" % t2])
                        thrk = ["thr" + sx + "_0", "thr" + sx + "_1"]
                        thik = ["thi" + sx + "_0", "thi" + sx + "_1"]
                        S.op("act", lambda e, B_=B_: e.activation(B_["at"][:], B_["thr"][:], AF.Exp, bias=rgc[:, 3, h:h + 1], scale=rgc[:, 3, h:h + 1]),
                             reads=thrk, writes=["at" + sx])
                        S.op("act", lambda e, B_=B_: e.activation(B_["tt"][:], B_["thr"][:], AF.Tanh, bias=rgc[:, 4, h:h + 1], scale=rgc[:, 4, h:h + 1]),
                             reads=thrk, writes=["tt" + sx])
                        S.op("act", lambda e, B_=B_: e.activation(B_["thr"][:], B_["thr"][:], AF.Exp, bias=rgc[:, 2, h:h + 1], scale=rgc[:, 2, h:h + 1]),
                             reads=thrk, writes=thrk)
                        S.op("dve", lambda e, B_=B_: e.scalar_tensor_tensor(B_["tt"][:], B_["thr"][:], 1.0, B_["tt"][:], ALU.add, ALU.mult),
                             reads=thrk + ["tt" + sx], writes=["tt" + sx])
                        S.op("act", lambda e, B_=B_: e.activation(B_["tt"][:], B_["tt"][:], AF.Sqrt), reads=["tt" + sx], writes=["tt" + sx])
                        S.op("dve", lambda e, B_=B_: e.scalar_tensor_tensor(B_["thi"][:], B_["thi"][:], 1.0, B_["xc"][:], ALU.add, ALU.mult),
                             reads=thik + ["xc" + sx], writes=thik)
                        S.op("dve", lambda e, B_=B_: e.scalar_tensor_tensor(B_["thi"][:], B_["thi"][:], 0.5, B_["tt"][:], ALU.mult, ALU.mult),
                             reads=thik + ["tt" + sx], writes=thik)
                        if not own:
                            S.op("dve", lambda e, B_=B_: e.tensor_tensor_scan(B_["hl"][:], B_["at"][:], B_["thi"][:], Ecur[:, h:h + 1], ALU.mult, ALU.add),
                                 reads=["at" + sx, "Ecur", "tt" + sx] + thik, writes=["tt" + sx])
                            S.op("dve", lambda e, B_=B_: e.tensor_copy(Ecur[:, h:h + 1], B_["hl"][:, NH - 1:NH]), reads=["tt" + sx], writes=["Ecur"])
                        else:
                            S.dma("act", "ast" + sx, lambda e, B_=B_, hf=hf: e.dma_start(out=ab_d[0, h, :, hf * NH:(hf + 1) * NH], in_=B_["at"][:]),
                                  reads=["at" + sx], writes=[f"abd0_{h}_{hf}"])
                            S.dma("sp", "bst" + sx, lambda e, B_=B_, hf=hf: e.dma_start(out=ab_d[1, h, :, hf * NH:(hf + 1) * NH], in_=B_["thi"][:]),
                                  reads=thik, writes=[f"abd1_{h}_{hf}"])
                    if scan_emit is not None:
                        scan_emit(8)

            def do_uproj():
                for g in range(2):
                    load_w(g, w_in_v, 4096 + 512 * g)
                for kt in range(8):
                    wbi, hh = kt // 4, kt % 4
                    for tq in range(4):
                        b = nextbank()
                        c0 = 3 + 512 * tq
                        for k in range(16):
                            S.op("pe", lambda e, b=b, k=k, c0=c0: e.matmul(
                                ps[b][:, :], wb[wbi][:, k, hh * 128:(hh + 1) * 128], xT[:, k, c0:c0 + 512], start=(k == 0), stop=(k == 15)),
                                reads=[f"wb{wbi}"], writes=[f"ps{b}"], inc=(k == 15))
                        eng = ev_eng()
                        S.op(eng, copy_op(eng, u_sb[:, kt, :, 64 * tq:64 * tq + 64], ps[b][:, :].rearrange("p (c s) -> p s c", s=8)),
                             reads=[f"ps{b}"], writes=[f"u{kt}_{tq}"])

            def do_S():
                W1s = [sbt(f"W1s{i}", [128, 8192], BF16) for i in range(2)]
                for kt in range(8):
                    w1 = W1s[kt % 2]
                    k1 = f"W1s{kt%2}"
                    S.dma("sp", k1, lambda e, w1=w1, kt=kt: e.dma_start(out=w1[:], in_=W1_d[kt]), writes=[k1])
                    for q in range(4):
                        j = 4 * kt + q
                        for ri in range(2):
                            b = nextbank()
                            for s in range(8):
                                o = ((q * 8 + s) * 2 + ri) * 128
                                S.op("pe", lambda e, b=b, s=s, o=o, w1=w1: e.matmul(
                                    ps[b][:, 0:256], w1[:, o:o + 128], u_sb[:, kt, s, :], start=(s == 0), stop=(s == 7)),
                                    reads=[k1], writes=[f"ps{b}"], inc=(s == 7))
                            eng = ev_eng()
                            S.op(eng, copy_op(eng, S_sb[:, :, ri, j], ps[b][:, 0:256]), reads=[f"ps{b}"], writes=[f"S_{j}_{ri}"])
                allS = [f"S_{j}_{ri}" for j in range(32) for ri in range(2)]
                return allS

            if own:
                wb = alloc_wb()
                def load_w(i, src_v, c0, ncol=512, nk=16):
                    S.dma("pool", f"wb{i}", lambda e: e.dma_start(out=wb[i][:, 0:nk, 0:ncol], in_=src_v[:, :, c0:c0 + ncol]), writes=[f"wb{i}"])

                do_A1(None)
                do_uproj()
                end_phase(mA1)
                mS = A.mark()
                S_sb = sbt("S_sb", [128, 256, 2, 32], F32)
                mA2 = A.mark()
                do_S()
                end_phase(mA2)
            else:
                mW = A.mark()
                wb = alloc_wb()
                def load_w(i, src_v, c0, ncol=512, nk=16):
                    S.dma("pool", f"wb{i}", lambda e: e.dma_start(out=wb[i][:, 0:nk, 0:ncol], in_=src_v[:, :, c0:c0 + ncol]), writes=[f"wb{i}"])

                do_uproj()
                end_phase(mW)
                mS = A.mark()
                S_sb = sbt("S_sb", [128, 256, 2, 32], F32)
                allS = do_S()
                S.dma("sp", "Sspill", lambda e: e.dma_start(out=S_d, in_=S_sb[:].rearrange("p c r j -> p (c r j)")), reads=allS, writes=["S_d"])
                end_phase(mS)
                wb = alloc_wb()
                def load_w(i, src_v, c0, ncol=512, nk=16):
                    S.dma("pool", f"wb{i}", lambda e: e.dma_start(out=wb[i][:, 0:nk, 0:ncol], in_=src_v[:, :, c0:c0 + ncol]), writes=[f"wb{i}"])

                SstA = [sbt(f"SstA{i}", [128, 8, 2, 32], F32) for i in range(2)]
                SstB = [sbt(f"SstB{i}", [128, 8, 2, 32], F32) for i in range(2)]
                sstate = {"c": 0}

                def load_piece(half, i):
                    bufs_ = SstA if half == 0 else SstB
                    kk_ = ("SstA%d" if half == 0 else "SstB%d") % (i % 2)
                    c0_ = (128 * half + 8 * i) * 64
                    S.dma("sp", kk_, lambda e: e.dma_start(out=bufs_[i % 2][:].rearrange("p c r j -> p (c r j)"), in_=S_d[:, c0_:c0_ + 512]), writes=[kk_])

                def one_step(eng, LLm, prev, sc_ap, out_ap, X_, Y_, xk, yk, rk, wk):
                    S.op(eng, lambda e: e.tensor_tensor(X_[:], LLm[:, 0:2, :], prev, ALU.mult), reads=rk, writes=[xk])
                    S.op(eng, lambda e: e.tensor_tensor(Y_[:, 0, :], LLm[:, 2, :], prev[:, 1, :], ALU.mult), reads=rk, writes=[yk])
                    S.op(eng, lambda e: e.tensor_tensor(Y_[:, 1, :], LLm[:, 3, :], prev[:, 0, :], ALU.mult), reads=rk, writes=[yk])
                    S.op(eng, lambda e: e.tensor_tensor(X_[:], X_[:], Y_[:], ALU.add), reads=[xk, yk], writes=[xk])
                    S.op(eng, lambda e: e.tensor_tensor(out_ap, X_[:], sc_ap, ALU.add), reads=[xk] + rk, writes=wk)

                def scan_emit(n):
                    for _ in range(n):
                        c = sstate["c"]
                        if c >= 128:
                            return
                        i, cc = c // 8, c % 8
                        one_step("pool", LL8, Ha[:], SstA[i % 2][:, cc, :, :], Ha[:], Xp, Yp, "Xp", "Yp", ["Ha", "SstA%d" % (i % 2)], ["Ha"])
                        one_step("dve", LL8, Hb[:], SstB[i % 2][:, cc, :, :], Hb[:], Xd, Yd, "Xd", "Yd", ["Hb", "SstB%d" % (i % 2)], ["Hb"])
                        sstate["c"] = c + 1
                        if cc == 7 and i + 2 < 16:
                            load_piece(0, i + 2)
                            load_piece(1, i + 2)
                S.op("pool", lambda e: e.tensor_copy(Ha[:], Hs[:]), reads=["Hs"], writes=["Ha"])
                S.op("dve", lambda e: e.memset(Hb[:], 0.0), writes=["Hb"])
                for i_ in range(2):
                    load_piece(0, i_)
                    load_piece(1, i_)
                do_A1(scan_emit)
                scan_emit(256)
                one_step("dve", D1k, Ha[:], Hb[:], Hs[:], Xd, Yd, "Xd", "Yd", ["Ha", "Hb"], ["Hs"])
                end_phase(mA1)
        if True:
            Hbf = sbt("Hbf", [128, 32, 2, 258], BF16)
            W3s = [sbt(f"W3s{i}", [128, 8192], BF16) for i in range(2)]
            KTs = [sbt(f"KTs{i}", [128, 8, 128], BF16) for i in range(2)]
            ysb = [sbt(f"ysb{i}", [128, TOK], BF16) for i in range(2)]
            Xs = sbt("Xs4", [128, 2, 32], F32)
            Ys = sbt("Ys4", [128, 2, 32], F32)

            def chunk_step4(prev, c):
                S.op("dve", lambda e: e.tensor_tensor(Xs[:], LL8[:, 0:2, :], prev, ALU.mult), reads=["Ssb"], writes=["Xs"])
                S.op("dve", lambda e: e.tensor_tensor(Ys[:, 0, :], LL8[:, 2, :], prev[:, 1, :], ALU.mult), reads=["Ssb"], writes=["Ys"])
                S.op("dve", lambda e: e.tensor_tensor(Ys[:, 1, :], LL8[:, 3, :], prev[:, 0, :], ALU.mult), reads=["Ssb"], writes=["Ys"])
                S.op("dve", lambda e: e.tensor_tensor(Xs[:], Xs[:], Ys[:], ALU.add), reads=["Xs", "Ys"], writes=["Xs"])
                S.op("dve", lambda e: e.tensor_tensor(S_sb[:, c, :, :], Xs[:], S_sb[:, c, :, :], ALU.add), reads=["Xs", "Ssb"], writes=["Ssb"])
            for c in range(256):
                chunk_step4(Hin[:] if c == 0 else S_sb[:, c - 1, :, :], c)
            S.op("act", lambda e: e.copy(Hbf[:, :, :, 0], Hin[:].rearrange("p a b -> p b a")), reads=[], writes=["Hbf_0"])
            S.op("act", lambda e: e.copy(Hbf[:, :, 0, 1:257], S_sb[:, :, 0, :].rearrange("p c j -> p j c")), reads=["Ssb"], writes=["Hbf_1"])
            S.op("dve", lambda e: e.tensor_copy(Hbf[:, :, 1, 1:257], S_sb[:, :, 1, :].rearrange("p c j -> p j c")), reads=["Ssb"], writes=["Hbf_2"])
            hbk = ["Hbf_0", "Hbf_1", "Hbf_2"]
            for kt in range(8):
                w3 = W3s[kt % 2]
                ktb = KTs[kt % 2]
                k3, kk = f"W3s{kt%2}", f"KTs{kt%2}"
                ys = ysb[kt % 2]
                ky = f"ysb{kt%2}"
                S.dma("sp", k3, lambda e, w3=w3, kt=kt: e.dma_start(out=w3[:], in_=W3_d[kt]), writes=[k3])
                S.dma("sp", kk, lambda e, ktb=ktb, kt=kt: e.dma_start(out=ktb[:].rearrange("p a b -> p (a b)"), in_=KT_d[kt]), writes=[kk])
                for tau in range(8):
                    b = nextbank()
                    mm = []
                    for s in range(tau + 1):
                        mm.append((ktb[:, tau - s, :], u_sb[:, kt, s, :], [kk]))
                    for q in range(4):
                        for ri in range(2):
                            o = ((q * 8 + tau) * 2 + ri) * 128
                            mm.append((w3[:, o:o + 128], Hbf[:, 4 * kt + q, ri, 0:256], [k3] + hbk))
                    for i, (l_, r_, rk) in enumerate(mm):
                        S.op("pe", lambda e, b=b, l_=l_, r_=r_, i=i, n=len(mm): e.matmul(
                            ps[b][:, 0:256], l_, r_, start=(i == 0), stop=(i == n - 1)),
                            reads=rk, writes=[f"ps{b}"], inc=(i == len(mm) - 1))
                    S.op("act", lambda e, b=b, ys=ys, tau=tau: e.activation(ys[:, tau::8], ps[b][:, 0:256], AF.Gelu_apprx_tanh),
                         reads=[f"ps{b}"], writes=[ky + "_%d" % tau])
                S.dma("act", ky + "st", lambda e, ys=ys, kt=kt: e.dma_start(out=yS_d[:, kt, :], in_=ys[:]),
                      reads=[ky + "_%d" % t for t in range(8)], writes=[f"ySd{kt}"])
        end_phase(mA)
        if stop_after == "a4":
            return nc

        NB = 1024
        for blk in range(2):
            t0 = blk * NB
            mB = A.mark()
            xTb = sbt("xTb", [128, 16, NB], BF16)
            ySb = sbt("ySb", [128, 8, NB], BF16)
            hg = sbt("hg", [128, 16, NB], BF16)
            S.dma("sp", "xTb", lambda e: e.dma_start(out=xTb[:], in_=xT_d[:, :, 3 + t0:3 + t0 + NB]), writes=["xTb"])
            S.dma("sp", "ySb", lambda e: e.dma_start(out=ySb[:], in_=yS_d[:, :, t0:t0 + NB]), writes=["ySb"])
            mG = A.mark()
            if True:
                wg = [sbt(f"wg{i}", [128, 16, 512], BF16) for i in range(2)]
                gsb = [sbt(f"gsb{i}", [128, NB], F32) for i in range(2)]
                ab = [sbt(f"abl{i}", [128, 2, NB], F32) for i in range(2)]
                hs = [sbt(f"hs{i}", [128, NB], F32) for i in range(2)]
                def load_wg(g):
                    S.dma("pool", f"wg{g%2}", lambda e: e.dma_start(out=wg[g % 2][:], in_=w_in_v[:, :, 2048 + 512 * g:2048 + 512 * g + 512]), writes=[f"wg{g%2}"])
                load_wg(0)
                for h in range(16):
                    if h % 4 == 0 and h // 4 + 1 < 4:
                        load_wg(h // 4 + 1)
                    wgi, hh, sx = (h // 4) % 2, h % 4, str(h % 2)
                    S.dma("sp", "abl" + sx, lambda e, h=h: e.dma_start(out=ab[h % 2][:], in_=ab_d[:, h, :, t0:t0 + NB].rearrange("a p t -> p a t")),
                          writes=["abl" + sx])
                    for t2 in range(2):
                        b = nextbank()
                        for kt in range(16):
                            S.op("pe", lambda e, b=b, kt=kt, t2=t2: e.matmul(
                                ps[b][:, :], wg[wgi][:, kt, hh * 128:(hh + 1) * 128], xTb[:, kt, t2 * 512:(t2 + 1) * 512], start=(kt == 0), stop=(kt == 15)),
                                reads=[f"wg{wgi}", "xTb"], writes=[f"ps{b}"], inc=(kt == 15))
                        S.op("act", lambda e, b=b, t2=t2, h=h: e.activation(gsb[h % 2][:, t2 * 512:(t2 + 1) * 512], ps[b][:, :], AF.Gelu_apprx_tanh),
                             reads=[f"ps{b}"], writes=["gsb" + sx + "_%d" % t2])
                    S.op("dve", lambda e, h=h: e.tensor_tensor_scan(hs[h % 2][:], ab[h % 2][:, 0, :], ab[h % 2][:, 1, :], hcar[:, h:h + 1], ALU.mult, ALU.add),
                         reads=["abl" + sx, "hcar"], writes=["hs" + sx])
                    S.op("dve", lambda e, h=h: e.tensor_copy(hcar[:, h:h + 1], hs[h % 2][:, NB - 1:NB]), reads=["hs" + sx], writes=["hcar"])
                    S.op("dve", lambda e, h=h: e.tensor_tensor(hg[:, h, :], hs[h % 2][:], gsb[h % 2][:], ALU.mult),
                         reads=["hs" + sx, "gsb" + sx + "_0", "gsb" + sx + "_1"], writes=[f"hg{h}"])
            end_phase(mG)
            if True:
                wA = [sbt(f"wA{i}", [128, 16, 256], BF16) for i in range(2)]
                wGa = [sbt(f"wGa{i}", [128, 16, 256], BF16) for i in range(2)]
                wGb = [sbt(f"wGb{i}", [128, 16, 256], BF16) for i in range(2)]
                wLw = [sbt(f"wLw{i}", [128, 8, 256], BF16) for i in range(2)]
                wLv = [sbt(f"wLv{i}", [128, 8, 256], BF16) for i in range(2)]
                tmp = [sbt(f"mt{i}", [128, 4, 512], F32) for i in range(2)]
                mixs = [sbt(f"mixs{i}", [128, 512], BF16) for i in range(2)]
                mc = 0

                def load_mix(jg):
                    i = jg % 2
                    c0 = 256 * jg
                    S.dma("pool", f"wGa{i}", lambda e: e.dma_start(out=wGa[i][:], in_=w_in_v[:, :, 5120 + c0:5120 + c0 + 256]), writes=[f"wGa{i}"])
                    S.dma("pool", f"wGb{i}", lambda e: e.dma_start(out=wGb[i][:], in_=w_in_v[:, :, 7168 + c0:7168 + c0 + 256]), writes=[f"wGb{i}"])
                    S.dma("pool", f"wLv{i}", lambda e: e.dma_start(out=wLv[i][:], in_=glu_v_v[:, :, c0:c0 + 256]), writes=[f"wLv{i}"])
                    S.dma("pool", f"wLw{i}", lambda e: e.dma_start(out=wLw[i][:], in_=glu_w_v[:, :, c0:c0 + 256]), writes=[f"wLw{i}"])
                    S.dma("pool", f"wA{i}", lambda e: e.dma_start(out=wA[i][:], in_=w_a_v[:, :, c0:c0 + 256]), writes=[f"wA{i}"])
                load_mix(0)
                for jg in range(8):
                    i = jg % 2
                    c0 = 256 * jg
                    if jg + 1 < 8:
                        load_mix(jg + 1)
                    for jj in range(2):
                        j = 2 * jg + jj
                        cs = slice(128 * jj, 128 * jj + 128)
                        for t2 in range(2):
                            ts = slice(t2 * 512, (t2 + 1) * 512)
                            T_ = tmp[mc % 2]
                            tk = f"mt{mc%2}"
                            ms = mixs[mc % 2]
                            mk = f"mixs{mc%2}"
                            mc += 1

                            def group(wt, wk, act, ak, nk):
                                b = nextbank()
                                for kt in range(nk):
                                    S.op("pe", lambda e, b=b, kt=kt: e.matmul(ps[b][:, :], wt[:, kt, cs], act[:, kt, ts], start=(kt == 0), stop=(kt == nk - 1)),
                                         reads=[wk] + ak, writes=[f"ps{b}"], inc=(kt == nk - 1))
                                return b
                            bA = group(wGa[i], f"wGa{i}", xTb, ["xTb"], 16)
                            S.op("act", lambda e, b=bA, T_=T_: e.activation(T_[:, 0, :], ps[b][:, :], AF.Sigmoid), reads=[f"ps{bA}"], writes=[tk + "a"])
                            bB = group(wGb[i], f"wGb{i}", xTb, ["xTb"], 16)
                            S.op("act", lambda e, b=bB, T_=T_: e.activation(T_[:, 1, :], ps[b][:, :], AF.Sigmoid), reads=[f"ps{bB}"], writes=[tk + "b"])
                            bV = group(wLv[i], f"wLv{i}", ySb, ["ySb"], 8)
                            S.op("act", lambda e, b=bV, T_=T_: e.activation(T_[:, 2, :], ps[b][:, :], AF.Sigmoid), reads=[f"ps{bV}"], writes=[tk + "v"])
                            bW = group(wLw[i], f"wLw{i}", ySb, ["ySb"], 8)
                            S.op("dve", lambda e, b=bW, T_=T_: e.tensor_tensor(T_[:, 2, :], ps[b][:, :], T_[:, 2, :], ALU.mult), reads=[f"ps{bW}", tk + "v"], writes=[tk + "v"])
                            S.op("dve", lambda e, T_=T_: e.tensor_tensor(T_[:, 2, :], T_[:, 2, :], T_[:, 1, :], ALU.mult), reads=[tk + "v", tk + "b"], writes=[tk + "v"])
                            bY = group(wA[i], f"wA{i}", hg, [], 16)
                            S.op("dve", lambda e, b=bY, T_=T_: e.tensor_tensor(T_[:, 0, :], ps[b][:, :], T_[:, 0, :], ALU.mult), reads=[f"ps{bY}", tk + "a"], writes=[tk + "a"])
                            S.op("dve", lambda e, T_=T_, ms=ms: e.tensor_tensor(ms[:], T_[:, 0, :], T_[:, 2, :], ALU.add), reads=[tk + "a", tk + "v"], writes=[mk])
                            S.dma("sp", mk + "st", lambda e, ms=ms, j=j, t2=t2: e.dma_start(out=mix_d[:, j, t0 + t2 * 512:t0 + (t2 + 1) * 512], in_=ms[:]),
                                  reads=[mk], writes=[f"mixd{j}_{t2}"])
            end_phase(mB)
            if stop_after == "mix":
                continue

            mF = A.mark()
            acc = sbt("acc", [128, 8, D], F32)
            lnp_off = A.lo
            lnp = sbt("lnp", [128, 2, D], F32)
            mO = A.mark()
            if True:
                mixT = sbt("mixT", [128, 16, NB], BF16)
                wo = [sbt(f"wo{i}", [128, 16, 512], BF16) for i in range(2)]
                xres = [sbt(f"xres{i}", [128, 512], F32) for i in range(3)]
                S.dma("sp", "mixT", lambda e: e.dma_start(out=mixT[:], in_=mix_d[:, :, t0:t0 + NB]), writes=["mixT"])
                S.dma("sp", "lnp0", lambda e: e.dma_start(out=lnp[:, 0, :], in_=ln1_g_d.partition_broadcast(128)), writes=["lnp0"])
                S.dma("sp", "lnp1", lambda e: e.dma_start(out=lnp[:, 1, :], in_=ln1_b_d.partition_broadcast(128)), writes=["lnp1"])
                xc_ = 0
                def load_wo(cb):
                    S.dma("pool", f"wo{cb%2}", lambda e: e.dma_start(out=wo[cb % 2][:], in_=w_out_v[:, :, 512 * cb:512 * cb + 512]), writes=[f"wo{cb%2}"])
                load_wo(0)
                for cb in range(4):
                    i = cb % 2
                    if cb + 1 < 4:
                        load_wo(cb + 1)
                    for tt in range(8):
                        xr_ = xres[xc_ % 3]
                        xk = f"xres{xc_%3}"
                        xc_ += 1
                        r0 = 3 * TOK + 3 + t0 + 128 * tt
                        S.dma("sp", xk, lambda e, xr_=xr_, r0=r0, cb=cb: e.dma_start(out=xr_[:], in_=x_d[r0:r0 + 128, 512 * cb:512 * cb + 512]), writes=[xk])
                        b = nextbank()
                        for kt in range(16):
                            S.op("pe", lambda e, b=b, kt=kt, tt=tt, i=i: e.matmul(
                                ps[b][:, :], mixT[:, kt, 128 * tt:128 * tt + 128], wo[i][:, kt, :], start=(kt == 0), stop=(kt == 15)),
                                reads=["mixT", f"wo{i}"], writes=[f"ps{b}"], inc=(kt == 15))
                        S.op("dve", lambda e, b=b, xr_=xr_, tt=tt, cb=cb: e.scalar_tensor_tensor(
                            acc[:, tt, 512 * cb:512 * cb + 512], xr_[:], ALPHA, ps[b][:, :], ALU.mult, ALU.add),
                            reads=[xk, f"ps{b}"], writes=[f"acc{tt}_{cb}"])
                        S.op("dve", lambda e, tt=tt, cb=cb: e.bn_stats(stats[:, tt, cb, :], acc[:, tt, 512 * cb:512 * cb + 512]),
                             reads=[f"acc{tt}_{cb}"], writes=[f"stats{tt}_{cb}"])
            end_phase(mO)
            x1T = sbt("x1T", [128, 16, NB], BF16)

            def layernorm(tt, outk):
                S.op("dve", lambda e: e.bn_aggr(mv[:, tt, 0:2], stats[:, tt, :, :].rearrange("p a b -> p (a b)")),
                     reads=[f"stats{tt}_{c_}" for c_ in range(4)], writes=[f"mv{tt}"])
                S.op("act", lambda e: e.activation(mv[:, tt, 2:3], mv[:, tt, 1:2], AF.Sqrt, bias=EPS), reads=[f"mv{tt}"], writes=[f"mv{tt}"])
                S.op("dve", lambda e: e.reciprocal(mv[:, tt, 2:3], mv[:, tt, 2:3]), reads=[f"mv{tt}"], writes=[f"mv{tt}"])
                S.op("dve", lambda e: e.scalar_tensor_tensor(mv[:, tt, 3:4], mv[:, tt, 0:1], -1.0, mv[:, tt, 2:3], ALU.mult, ALU.mult),
                     reads=[f"mv{tt}"], writes=[f"mv{tt}"])
                S.op("act", lambda e: e.activation(acc[:, tt, :], acc[:, tt, :], AF.Identity, bias=mv[:, tt, 3:4], scale=mv[:, tt, 2:3]),
                     reads=[f"mv{tt}"], writes=[outk])
                S.op("dve", lambda e: e.tensor_tensor(acc[:, tt, :], acc[:, tt, :], lnp[:, 0, :], ALU.mult), reads=[outk, "lnp0"], writes=[outk])
                S.op("dve", lambda e: e.tensor_tensor(acc[:, tt, :], acc[:, tt, :], lnp[:, 1, :], ALU.add), reads=[outk, "lnp1"], writes=[outk])

            for tt in range(8):
                layernorm(tt, f"x1_{tt}")
                for b4 in range(4):
                    b = nextbank()
                    for jj in range(4):
                        kt = 4 * b4 + jj
                        S.op("pe", lambda e, b=b, jj=jj, kt=kt, tt=tt: e.transpose(
                            ps[b][:, jj * 128:(jj + 1) * 128], acc[:, tt, kt * 128:(kt + 1) * 128], ident[:, :]),
                            reads=[f"x1_{tt}"], writes=[f"ps{b}"], inc=(jj == 3))
                    S.op("act", lambda e, b=b, b4=b4, tt=tt: e.copy(x1T[:, 4 * b4:4 * b4 + 4, 128 * tt:128 * tt + 128],
                                                                   ps[b][:, :].rearrange("p (j t) -> p j t", j=4)),
                         reads=[f"ps{b}"], writes=[f"x1T{tt}_{b4}"])
            if dbg and blk == 0:
                S.dma("sp", "dbgx1", lambda e: e.dma_start(out=dbg_x1.rearrange("(t p) c -> p t c", p=128), in_=acc[:]),
                      reads=[f"x1_{tt}" for tt in range(8)])
            S.dma("sp", "lnp0", lambda e: e.dma_start(out=lnp[:, 0, :], in_=b_dn_d.partition_broadcast(128)),
                  reads=[f"x1_{tt}" for tt in range(8)], writes=["lnp0"])
            for tt in range(8):
                S.op("dve", lambda e, tt=tt: e.scalar_tensor_tensor(acc[:, tt, :], acc[:, tt, :], ALPHA, lnp[:, 0, :], ALU.mult, ALU.add),
                     reads=[f"x1_{tt}", "lnp0"], writes=[f"x1_{tt}"])
            S.barrier()

            if True:
                FC = 8
                hTb = [sbt("hT", [128, FC, NB], BF16), A.view(lnp_off, [128, FC, NB], BF16)]
                wu = [sbt(f"wu{i}", [128, 16, 256], BF16) for i in range(2)]
                wd = sbt("wd", [128, FC, D], BF16)
                rl = [sbt(f"rl{i}", [128, 512], F32) for i in range(3)]
                wuc = 0
                rc = 0
                NFC = DFF // 128 // FC
                NG = NFC * (FC // 2)

                def load_wu(g):
                    c0 = g * 256
                    S.dma("pool", f"wu{g%2}", lambda e: e.dma_start(out=wu[g % 2][:], in_=w_up_v[:, :, c0:c0 + 256]), writes=[f"wu{g%2}"])

                def load_wd(fc):
                    for f2 in range(FC // 2):
                        f0 = fc * FC + f2 * 2
                        S.dma("pool", f"wd{f2}", lambda e, f2=f2, f0=f0: e.dma_start(out=wd[:, 2 * f2:2 * f2 + 2, :], in_=w_dn_v[:, f0:f0 + 2, :]), writes=[f"wd{f2}"])
                load_wu(0)
                for fc in range(NFC):
                    hT = hTb[fc % 2]
                    hk = "hT%d_" % (fc % 2)
                    for f2 in range(FC // 2):
                        g = fc * (FC // 2) + f2
                        i = g % 2
                        if g + 1 < NG:
                            load_wu(g + 1)
                        if f2 == 0:
                            load_wd(fc)
                        for f in range(2):
                            fl = f2 * 2 + f
                            ft = fc * FC + fl
                            for t2 in range(2):
                                b = nextbank()
                                for kt in range(16):
                                    S.op("pe", lambda e, b=b, kt=kt, i=i, f=f, t2=t2: e.matmul(
                                        ps[b][:, :], wu[i][:, kt, f * 128:(f + 1) * 128], x1T[:, kt, t2 * 512:(t2 + 1) * 512], start=(kt == 0), stop=(kt == 15)),
                                        reads=[f"wu{i}"], writes=[f"ps{b}"], inc=(kt == 15))
                                r_ = rl[rc % 3]
                                rk = f"rl{rc%3}"
                                rc += 1
                                S.op("act", lambda e, b=b, r_=r_, ft=ft: e.activation(r_[:], ps[b][:, :], AF.Relu, bias=bup[:, ft:ft + 1]),
                                     reads=[f"ps{b}"], writes=[rk])
                                S.op("act", lambda e, r_=r_, fl=fl, t2=t2: e.activation(hT[:, fl, t2 * 512:(t2 + 1) * 512], r_[:], AF.Square),
                                     reads=[rk], writes=[hk + f"{fl}_{t2}"])
                    for tt in range(8):
                        for cb in range(4):
                            b = nextbank()
                            for fl in range(FC):
                                S.op("pe", lambda e, b=b, fl=fl, tt=tt, cb=cb: e.matmul(
                                    ps[b][:, :], hT[:, fl, 128 * tt:128 * tt + 128], wd[:, fl, 512 * cb:512 * cb + 512], start=(fl == 0), stop=(fl == FC - 1)),
                                    reads=[f"wd{fl//2}", hk + f"{fl}_{(128*tt)//512}"], writes=[f"ps{b}"], inc=(fl == FC - 1))
                            S.op("dve", lambda e, b=b, tt=tt, cb=cb: e.tensor_tensor(
                                acc[:, tt, 512 * cb:512 * cb + 512], acc[:, tt, 512 * cb:512 * cb + 512], ps[b][:, :], ALU.add),
                                reads=[f"ps{b}", f"accf{tt}_{cb}"], writes=[f"accf{tt}_{cb}"])
                hk1 = [f"hT1_{fl}_{t2}" for fl in range(FC) for t2 in range(2)]
                S.dma("sp", "lnp0", lambda e: e.dma_start(out=lnp[:, 0, :], in_=ln2_g_d.partition_broadcast(128)), writes=["lnp0"] + hk1)
                S.dma("sp", "lnp1", lambda e: e.dma_start(out=lnp[:, 1, :], in_=ln2_b_d.partition_broadcast(128)), writes=["lnp1"] + hk1)
                for tt in range(8):
                    for cb in range(4):
                        S.op("dve", lambda e, tt=tt, cb=cb: e.bn_stats(stats[:, tt, cb, :], acc[:, tt, 512 * cb:512 * cb + 512]),
                             reads=[f"accf{tt}_{cb}"], writes=[f"stats{tt}_{cb}"])
                    layernorm(tt, f"x2_{tt}")
                    S.dma("sp", f"ost{tt%2}", lambda e, tt=tt: e.dma_start(out=out_d[t0 + 128 * tt:t0 + 128 * tt + 128, :], in_=acc[:, tt, :]),
                          reads=[f"x2_{tt}"], writes=[f"outd{tt}"])
            end_phase(mF)
        S.barrier()
    return nc


_CACHE = {}


def _prep_inputs(inputs, small=False):
    x = np.ascontiguousarray(np.asarray(inputs["x"], dtype=np.float32))
    names = ["w_in", "conv_w", "conv_b", "rg_wa", "rg_ba", "rg_wx", "rg_bx", "rg_lambda", "w_a_out",
             "ssm_a_re", "ssm_a_im", "ssm_log_dt", "ssm_b_re", "ssm_b_im", "ssm_c_re", "ssm_c_im", "ssm_d",
             "glu_w", "glu_v", "w_out", "ln1_g", "ln1_b", "mlp_w_up", "mlp_b_up", "mlp_w_down", "mlp_b_down",
             "ln2_g", "ln2_b"]
    shared = {n: np.ascontiguousarray(np.asarray(inputs[n], dtype=np.float32)[0]) for n in names}
    in_maps = []
    for r in range(NCORE):
        b, k = r // 4, r % 4
        xs = np.zeros((4 * TOK + HALO, D), np.float32)
        n_real = TOK * (k + 1)
        xs[4 * TOK + HALO - n_real:] = x[b, 0:n_real]
        segm = np.ones((128, 4), np.float32)
        segm[:, 3 - k] = 0.0
        m = {"x": xs, "segm": segm}
        m.update(shared)
        if small:
            for n in ("w_a_out", "glu_w", "glu_v", "w_out", "mlp_w_up", "mlp_w_down"):
                m[n] = np.zeros((128, 128), np.float32)
        in_maps.append(m)
    return in_maps


def kernel(**inputs):
    if "nc" not in _CACHE:
        _CACHE["nc"] = build()
    nc = _CACHE["nc"]
    in_maps = _prep_inputs(inputs)
    res = run_bass_kernel_spmd(nc, in_maps, core_ids=list(range(NCORE)))
    out = np.empty((2, 4 * TOK, D), np.float32)
    for r in range(NCORE):
        b, k = r // 4, r % 4
        out[b, TOK * k:TOK * (k + 1)] = res.results[r]["out"]
    return out
```

```python
import numpy as np
from contextlib import ExitStack
import concourse.bass as bass
import concourse.mybir as mybir
from concourse.bass_utils import run_bass_kernel_spmd

F32 = mybir.dt.float32
BF16 = mybir.dt.bfloat16
I32 = mybir.dt.int32
AF = mybir.ActivationFunctionType
ALU = mybir.AluOpType

NCORE = 8
TOK = 2048
HALO = 3
D = 2048
DIN = 9216
DFF = 8192
ALPHA = 2.0 ** 0.25
EPS = 1e-5
TWO_PI = 6.283185307179586
PI = 3.141592653589793


class Sched:
    ENGS = ("pe", "act", "dve", "pool", "sp")

    def __init__(self, nc, stack):
        self.nc = nc
        self.stack = stack
        self.eng = {"pe": nc.tensor, "act": nc.scalar, "dve": nc.vector, "pool": nc.gpsimd, "sp": nc.sync}
        self.sem = {e: stack.enter_context(nc.semaphore("s_" + e)) for e in self.ENGS}
        self.cnt = {e: 0 for e in self.ENGS}
        self.waited = {e: {} for e in self.ENGS}
        self.last_w = {}
        self.readers = {}
        self.dsem = {}
        self.dcnt = {}

    def _h(self, s):
        return self.sem[s[1]] if s[0] == "e" else self.dsem[s[1]]

    def _wait(self, eng, s, v, raw=False):
        if s == ("e", eng) and (eng == "pe" or not raw):
            return
        if self.waited[eng].get(s, 0) >= v:
            return
        self.waited[eng][s] = v
        self.eng[eng].wait_ge(self._h(s), v)

    def _deps(self, eng, reads, writes):
        need = {}
        own = ("e", eng)

        def add(s, v, raw):
            if s == own and not raw:
                return
            if v > need.get(s, 0):
                need[s] = v
        for b in reads:
            w = self.last_w.get(b)
            if w is not None:
                add(w[0], w[1], True)
        for b in writes:
            w = self.last_w.get(b)
            if w is not None:
                add(w[0], w[1], False)
            for s, v in self.readers.get(b, {}).items():
                add(s, v, False)
        for s, v in need.items():
            self._wait(eng, s, v, raw=True)

    def _mark(self, tok, reads, writes):
        for b in writes:
            self.last_w[b] = tok
            self.readers[b] = {}
        for b in reads:
            d = self.readers.setdefault(b, {})
            if tok[1] > d.get(tok[0], 0):
                d[tok[0]] = tok[1]

    def op(self, eng, fn, reads=(), writes=(), inc=True):
        self._deps(eng, reads, writes)
        ins = fn(self.eng[eng])
        if inc:
            self.cnt[eng] += 1
            ins.then_inc(self.sem[eng], 1)
            tok = (("e", eng), self.cnt[eng])
        else:
            tok = (("e", eng), self.cnt[eng] + 1)
        self._mark(tok, reads, writes)
        return tok

    def dma(self, eng, key, fn, reads=(), writes=(), incv=16):
        if key not in self.dsem:
            self.dsem[key] = self.stack.enter_context(self.nc.semaphore("d_" + str(key)))
            self.dcnt[key] = 0
        self._deps(eng, reads, writes)
        self.dcnt[key] += incv
        fn(self.eng[eng]).then_inc(self.dsem[key], incv)
        tok = (("d", key), self.dcnt[key])
        self._mark(tok, reads, writes)
        return tok

    def wait_tok(self, eng, tok):
        self._wait(eng, tok[0], tok[1])

    def barrier(self):
        for e in self.ENGS:
            for e2 in self.ENGS:
                if e2 != e and self.cnt[e2] > 0:
                    self._wait(e, ("e", e2), self.cnt[e2])
            for k, v in self.dcnt.items():
                self._wait(e, ("d", k), v)
        self.last_w.clear()
        self.readers.clear()


_DTSZ = {F32: 4, I32: 4, BF16: 2}
SB_BYTES = 206 * 1024


class Arena:
    def __init__(self, handle, size):
        self.h = handle
        self.lo = 0
        self.hi = size

    def alloc(self, shape, dt, top=False):
        n = 1
        for d in shape[1:]:
            n *= d
        nb = (n * _DTSZ[dt] + 63) // 64 * 64
        if top:
            self.hi -= nb
            off = self.hi
        else:
            off = self.lo
            self.lo += nb
        assert self.lo <= self.hi, ("SBUF arena overflow", self.lo, self.hi)
        v = self.h[:, off:off + n * _DTSZ[dt]].bitcast(dt)
        names = "abcdefg"[:len(shape) - 1]
        if len(shape) > 2:
            v = v.rearrange("p (%s) -> p %s" % (" ".join(names), " ".join(names)),
                            **{k: d for k, d in zip(names[:-1], shape[1:-1])})
        if shape[0] < 128:
            v = v[0:shape[0]]
        return v

    def view(self, off, shape, dt):
        n = 1
        for d in shape[1:]:
            n *= d
        v = self.h[:, off:off + n * _DTSZ[dt]].bitcast(dt)
        names = "abcdefg"[:len(shape) - 1]
        if len(shape) > 2:
            v = v.rearrange("p (%s) -> p %s" % (" ".join(names), " ".join(names)),
                            **{k: d for k, d in zip(names[:-1], shape[1:-1])})
        return v

    def mark(self):
        return (self.lo, self.hi)

    def release(self, m):
        self.lo, self.hi = m


def build(dbg=False, stop_after=None):
    nc = bass.Bass("TRN2", target_bir_lowering=False)

    small = stop_after in ("p0", "a1", "a3", "a4")
    BIG = ("w_a_out", "glu_w", "glu_v", "w_out", "mlp_w_up", "mlp_w_down")

    def din(name, shape):
        if small and name in BIG:
            shape = [128, 128]
        return nc.dram_tensor(name, list(shape), F32, kind="ExternalInput").ap()

    def dscr(name, shape, dt):
        if dbg:
            return nc.dram_tensor(name, list(shape), dt, kind="ExternalOutput").ap()
        return nc.dram_tensor(name, list(shape), dt).ap()

    x_d = din("x", [4 * TOK + HALO, D])
    segm_d = din("segm", [128, 4])
    w_in_d = din("w_in", [D, DIN])
    conv_w_d = din("conv_w", [4, D])
    conv_b_d = din("conv_b", [D])
    rg_wa_d = din("rg_wa", [16, 128, 128])
    rg_ba_d = din("rg_ba", [16, 128])
    rg_wx_d = din("rg_wx", [16, 128, 128])
    rg_bx_d = din("rg_bx", [16, 128])
    rg_lam_d = din("rg_lambda", [D])
    w_a_d = din("w_a_out", [D, D])
    a_re_d = din("ssm_a_re", [64, 64])
    a_im_d = din("ssm_a_im", [64, 64])
    ldt_d = din("ssm_log_dt", [64])
    b_re_d = din("ssm_b_re", [64, 64, 16])
    b_im_d = din("ssm_b_im", [64, 64, 16])
    c_re_d = din("ssm_c_re", [64, 16, 64])
    c_im_d = din("ssm_c_im", [64, 16, 64])
    ssm_d_d = din("ssm_d", [64, 16])
    glu_w_d = din("glu_w", [1024, D])
    glu_v_d = din("glu_v", [1024, D])
    w_out_d = din("w_out", [D, D])
    ln1_g_d = din("ln1_g", [D])
    ln1_b_d = din("ln1_b", [D])
    w_up_d = din("mlp_w_up", [D, DFF])
    b_up_d = din("mlp_b_up", [DFF])
    w_dn_d = din("mlp_w_down", [DFF, D])
    b_dn_d = din("mlp_b_down", [D])
    ln2_g_d = din("ln2_g", [D])
    ln2_b_d = din("ln2_b", [D])
    out_d = nc.dram_tensor("out", [TOK, D], F32, kind="ExternalOutput").ap()

    ab_d = dscr("ab_d", [2, 16, 128, TOK], F32)
    xT_d = dscr("xT_d", [128, 16, TOK + HALO], BF16)
    yS_d = dscr("yS_d", [128, 8, TOK], BF16)
    mix_d = dscr("mix_d", [128, 16, TOK], BF16)
    W1_d = dscr("W1_d", [8, 128, 8192], BF16)
    W3_d = dscr("W3_d", [8, 128, 8192], BF16)
    KT_d = dscr("KT_d", [8, 128, 1024], BF16)
    S_d3 = nc.dram_tensor("S_d3", [3, 128, 16384], F32).ap()
    ccin_d = nc.dram_tensor("ccin_d", [128, 96], F32).ap()
    ccout_d = nc.dram_tensor("ccout_d", [NCORE * 128, 96], F32).ap()
    if dbg:
        dbg_small = nc.dram_tensor("dbg_small", [128, 256], F32, kind="ExternalOutput").ap()
        dbg_x1 = nc.dram_tensor("dbg_x1", [1024, D], F32, kind="ExternalOutput").ap()
        dbg_sc = nc.dram_tensor("dbg_sc", [128, 24 * 32], F32, kind="ExternalOutput").ap()
        dbg_lp = nc.dram_tensor("dbg_lp", [128, 9 * 4 * 32], F32, kind="ExternalOutput").ap()
        dbg_p3 = nc.dram_tensor("dbg_p3", [128, 104], F32, kind="ExternalOutput").ap()
        dbg_p1 = nc.dram_tensor("dbg_p1", [128, 128], F32, kind="ExternalOutput").ap()

    w_in_v = w_in_d.rearrange("(kt p) c -> p kt c", p=128)
    if not small:
        w_a_v = w_a_d.rearrange("(kt p) c -> p kt c", p=128)
        glu_w_v = glu_w_d.rearrange("(kt p) c -> p kt c", p=128)
        glu_v_v = glu_v_d.rearrange("(kt p) c -> p kt c", p=128)
        w_out_v = w_out_d.rearrange("(kt p) c -> p kt c", p=128)
        w_up_v = w_up_d.rearrange("(kt p) c -> p kt c", p=128)
        w_dn_v = w_dn_d.rearrange("(ft p) c -> p ft c", p=128)

    with ExitStack() as top:
        S = Sched(nc, top)
        ccsem = top.enter_context(nc.semaphore("ccsem"))
        arena_t = top.enter_context(nc.sbuf_tensor("arena", [128, SB_BYTES], mybir.dt.uint8))
        A = Arena(arena_t, SB_BYTES)

        def sbt(name, shape, dt, top_=False):
            return A.alloc(list(shape), dt, top=top_)

        def end_phase(m):
            S.barrier()
            A.release(m)

        ps = [top.enter_context(nc.psum_tensor(f"ps{i}", [128, 512], F32)) for i in range(8)]
        bank_ctr = [0]

        def nextbank():
            b = bank_ctr[0] % 8
            bank_ctr[0] += 1
            return b

        ev_ctr = [0]

        def ev_eng():
            ev_ctr[0] += 1
            return "act" if ev_ctr[0] % 2 == 0 else "dve"

        def copy_op(eng, out, in_):
            if eng == "act":
                return lambda e: e.copy(out, in_)
            return lambda e: e.tensor_copy(out, in_)

        ident = sbt("ident", [128, 128], F32)
        P1 = sbt("P1", [128, 128], F32)
        bup = sbt("bup", [128, 64], F32)
        rgc = sbt("rgc", [128, 8, 16], F32)
        wa_sb = sbt("wa_sb", [128, 16, 128], BF16)
        wx_sb = sbt("wx_sb", [128, 16, 128], BF16)
        Ecur = sbt("Ecur", [128, 16], F32)
        sumth = sbt("sumth", [128, 16, 4], F32)
        hcar = sbt("hcar", [128, 16], F32)
        Hin = sbt("Hin", [128, 2, 32], F32)
        LL8 = sbt("LL8", [128, 4, 32], F32)
        D2k = sbt("D2k", [128, 2, 32], F32)
        ccin = sbt("ccin", [128, 96], F32)
        segm = sbt("segm", [128, 4], F32)
        Hs = sbt("Hs", [128, 2, 32], F32)
        Xp = sbt("Xp", [128, 2, 32], F32)
        Yp = sbt("Yp", [128, 2, 32], F32)
        Xd = sbt("Xd", [128, 2, 32], F32)
        Yd = sbt("Yd", [128, 2, 32], F32)
        Ha = sbt("Ha", [128, 2, 32], F32)
        H3 = sbt("H3", [128, 3, 2, 32], F32)
        X3 = sbt("X3", [128, 3, 2, 32], F32)
        Y3 = sbt("Y3", [128, 3, 2, 32], F32)
        D2kL = sbt("D2kL", [128, 4, 32], F32)
        stats = sbt("stats", [128, 8, 4, 6], F32)
        mv = sbt("mv", [128, 8, 4], F32)

        S.op("pool", lambda e: e.memset(ident[:], 1.0), writes=["ident"])
        S.op("pool", lambda e: e.affine_select(out=ident[:], in_=ident[:], pattern=[[-1, 128]],
                                               compare_op=ALU.is_equal, fill=0.0, base=0, channel_multiplier=1),
             reads=["ident"], writes=["ident"])
        S.dma("sp", "segm", lambda e: e.dma_start(out=segm[:], in_=segm_d), writes=["segm"])
        S.dma("pool", "wa_sb", lambda e: e.dma_start(out=wa_sb[:], in_=rg_wa_d.rearrange("h i j -> i h j")), writes=["wa_sb"])
        S.dma("pool", "wx_sb", lambda e: e.dma_start(out=wx_sb[:], in_=rg_wx_d.rearrange("h i j -> i h j")), writes=["wx_sb"])

        m0 = A.mark()
        if True:
            st1 = sbt("st1", [128, 128], F32)
            st2 = sbt("st2", [64, 128], F32)
            st3 = sbt("st3", [104, 128], F32)
            ld2 = sbt("ld2", [32, 2], F32)
            P3 = sbt("P3", [128, 104], F32)
            S.dma("sp", "st1a", lambda e: e.dma_start(out=st1[0:64, :], in_=conv_w_d.rearrange("k (h p) -> (k h) p", p=128)), writes=["st1a"])
            S.dma("sp", "st1b", lambda e: e.dma_start(out=st1[64:80, :], in_=conv_b_d.rearrange("(h p) -> h p", p=128)), writes=["st1b"])
            S.dma("sp", "st1c", lambda e: e.dma_start(out=st1[80:96, :], in_=rg_ba_d), writes=["st1c"])
            S.dma("sp", "st1d", lambda e: e.dma_start(out=st1[96:112, :], in_=rg_bx_d), writes=["st1d"])
            S.dma("sp", "st1e", lambda e: e.dma_start(out=st1[112:128, :], in_=rg_lam_d.rearrange("(h p) -> h p", p=128)), writes=["st1e"])
            S.dma("sp", "st2", lambda e: e.dma_start(out=st2[:], in_=b_up_d.rearrange("(f p) -> f p", p=128)), writes=["st2"])
            S.dma("sp", "ld2", lambda e: e.dma_start(out=ld2[:], in_=ldt_d.rearrange("(j g) -> j g", g=2)), writes=["ld2"])
            S.dma("sp", "st3b", lambda e: e.dma_start(out=st3[32:64, :], in_=a_re_d.rearrange("(j g) p -> j (g p)", g=2)), writes=["st3b"])
            S.dma("sp", "st3c", lambda e: e.dma_start(out=st3[64:96, :], in_=a_im_d.rearrange("(j g) p -> j (g p)", g=2)), writes=["st3c"])
            S.dma("sp", "st3d", lambda e: e.dma_start(out=st3[96:104, :], in_=ssm_d_d.rearrange("(kt g) h -> kt (g h)", g=8)), writes=["st3d"])
            S.op("dve", lambda e: e.tensor_copy(st3[0:32, :].rearrange("j (g p) -> j g p", g=2),
                                                ld2[:, :].unsqueeze(2).to_broadcast([32, 2, 64])),
                 reads=["ld2"], writes=["st3a"])
            b = nextbank()
            S.op("pe", lambda e: e.transpose(ps[b][:, 0:128], st1[:, :], ident[:, :]),
                 reads=["st1a", "st1b", "st1c", "st1d", "st1e", "ident"], writes=[f"ps{b}"])
            S.op("dve", lambda e: e.tensor_copy(P1[:], ps[b][:, 0:128]), reads=[f"ps{b}"], writes=["P1"])
            b = nextbank()
            S.op("pe", lambda e: e.transpose(ps[b][:, 0:64], st2[:, :], ident[0:64, 0:64]),
                 reads=["st2", "ident"], writes=[f"ps{b}"])
            S.op("dve", lambda e: e.tensor_copy(bup[:], ps[b][:, 0:64]), reads=[f"ps{b}"], writes=["bup"])
            b = nextbank()
            S.op("pe", lambda e: e.transpose(ps[b][:, 0:104], st3[:, :], ident[0:104, 0:104]),
                 reads=["st3a", "st3b", "st3c", "st3d", "ident"], writes=[f"ps{b}"])
            S.op("dve", lambda e: e.tensor_copy(P3[:], ps[b][:, 0:104]), reads=[f"ps{b}"], writes=["P3"])

            ba_v = P1[:, 80:96]
            bx_v = P1[:, 96:112]
            lam_v = P1[:, 112:128]
            S.op("dve", lambda e: e.tensor_scalar(rgc[:, 0, :], ba_v, 0.5, None, ALU.mult), reads=["P1"], writes=["rgc0"])
            S.op("dve", lambda e: e.tensor_scalar(rgc[:, 1, :], bx_v, 0.5, None, ALU.mult), reads=["P1"], writes=["rgc1"])
            S.op("act", lambda e: e.activation(rgc[:, 5, :], lam_v, AF.Exp, scale=-1.0), reads=["P1"], writes=["rgc5"])
            S.op("act", lambda e: e.activation(rgc[:, 5, :], rgc[:, 5, :], AF.Ln, bias=1.0), reads=["rgc5"], writes=["rgc5"])
            S.op("dve", lambda e: e.tensor_scalar(rgc[:, 2, :], rgc[:, 5, :], -8.0, None, ALU.mult), reads=["rgc5"], writes=["rgc2"])
            S.op("dve", lambda e: e.tensor_scalar(rgc[:, 3, :], rgc[:, 5, :], -4.0, None, ALU.mult), reads=["rgc5"], writes=["rgc3"])
            S.op("dve", lambda e: e.tensor_scalar(rgc[:, 4, :], rgc[:, 5, :], 4.0, None, ALU.mult), reads=["rgc5"], writes=["rgc4"])

            sc = sbt("sc", [128, 24, 32], F32)
            Lp = sbt("Lp", [128, 9, 4, 32], F32)
            isc = sbt("isc", [128, 32], I32)

            def V(i):
                return sc[:, i, :]

            def dv(fn):
                S.op("dve", fn, reads=["sc", "P3"], writes=["sc"])

            def av(fn):
                S.op("act", fn, reads=["sc", "P3"], writes=["sc"])
            ldt_v, are_v, aim_v = P3[:, 0:32], P3[:, 32:64], P3[:, 64:96]
            av(lambda e: e.activation(V(0), ldt_v, AF.Exp))
            dv(lambda e: e.tensor_scalar(V(1), are_v, -1e-4, None, ALU.min))
            dv(lambda e: e.tensor_tensor(V(2), V(1), V(0), ALU.mult))
            av(lambda e: e.activation(V(3), V(2), AF.Exp))
            dv(lambda e: e.tensor_tensor(V(4), aim_v, V(0), ALU.mult))

            def reduced_sin(dst, shift):
                dv(lambda e: e.tensor_scalar(V(5), V(4), shift, None, ALU.add))
                dv(lambda e: e.tensor_scalar(V(6), V(5), 1.0 / TWO_PI, None, ALU.mult))
                dv(lambda e: e.tensor_copy(isc[:], V(6)))
                dv(lambda e: e.tensor_copy(V(6), isc[:]))
                dv(lambda e: e.scalar_tensor_tensor(V(5), V(6), -TWO_PI, V(5), ALU.mult, ALU.add))
                dv(lambda e: e.tensor_scalar(V(7), V(5), PI, None, ALU.is_gt))
                dv(lambda e: e.scalar_tensor_tensor(V(5), V(7), -TWO_PI, V(5), ALU.mult, ALU.add))
                dv(lambda e: e.tensor_scalar(V(7), V(5), -PI, None, ALU.is_lt))
                dv(lambda e: e.scalar_tensor_tensor(V(5), V(7), TWO_PI, V(5), ALU.mult, ALU.add))
                dv(lambda e: e.tensor_scalar(V(5), V(5), PI, -PI, ALU.min, ALU.max))
                av(lambda e: e.activation(dst, V(5), AF.Sin))
            reduced_sin(V(9), 0.0)
            reduced_sin(V(10), PI / 2)
            dv(lambda e: e.tensor_tensor(V(11), V(3), V(10), ALU.mult))
            dv(lambda e: e.tensor_tensor(V(12), V(3), V(9), ALU.mult))
            dv(lambda e: e.tensor_tensor(V(13), V(1), V(1), ALU.mult))
            dv(lambda e: e.tensor_tensor(V(5), aim_v, aim_v, ALU.mult))
            dv(lambda e: e.tensor_tensor(V(13), V(13), V(5), ALU.add))
            dv(lambda e: e.reciprocal(V(13), V(13)))
            dv(lambda e: e.tensor_scalar(V(14), V(11), -1.0, None, ALU.add))
            dv(lambda e: e.tensor_tensor(V(5), V(14), V(1), ALU.mult))
            dv(lambda e: e.tensor_tensor(V(6), V(12), aim_v, ALU.mult))
            dv(lambda e: e.tensor_tensor(V(5), V(5), V(6), ALU.add))
            dv(lambda e: e.tensor_tensor(V(15), V(5), V(13), ALU.mult))
            dv(lambda e: e.tensor_tensor(V(5), V(12), V(1), ALU.mult))
            dv(lambda e: e.tensor_tensor(V(6), V(14), aim_v, ALU.mult))
            dv(lambda e: e.tensor_tensor(V(5), V(5), V(6), ALU.subtract))
            dv(lambda e: e.tensor_tensor(V(16), V(5), V(13), ALU.mult))

            def lp(fn):
                S.op("dve", fn, reads=["sc", "Lp"], writes=["Lp", "sc"])
            lp(lambda e: e.memset(Lp[:, 0, 0, :], 1.0))
            lp(lambda e: e.memset(Lp[:, 0, 1, :], 0.0))
            lp(lambda e: e.tensor_copy(Lp[:, 1, 0, :], V(11)))
            lp(lambda e: e.tensor_copy(Lp[:, 1, 1, :], V(12)))

            def cmul(o_re, o_im, a_re_, a_im_, b_re_, b_im_, t1, t2, t3):
                lp(lambda e: e.tensor_tensor(t1, a_re_, b_re_, ALU.mult))
                lp(lambda e: e.tensor_tensor(t2, a_im_, b_im_, ALU.mult))
                lp(lambda e: e.tensor_tensor(t1, t1, t2, ALU.subtract))
                lp(lambda e: e.tensor_tensor(t2, a_re_, b_im_, ALU.mult))
                lp(lambda e: e.tensor_tensor(t3, a_im_, b_re_, ALU.mult))
                lp(lambda e: e.tensor_tensor(o_im, t3, t2, ALU.add))
                lp(lambda e: e.tensor_copy(o_re, t1))
            for n in range(1, 8):
                cmul(Lp[:, n + 1, 0, :], Lp[:, n + 1, 1, :], Lp[:, n, 0, :], Lp[:, n, 1, :],
                     Lp[:, 1, 0, :], Lp[:, 1, 1, :], V(17), V(18), V(21))
            for n in range(9):
                lp(lambda e, n=n: e.tensor_scalar(Lp[:, n, 2:4, :], Lp[:, n, 0:2, :], -1.0, None, ALU.mult))
            S.op("dve", lambda e: e.tensor_copy(LL8[:, 0, :], Lp[:, 8, 0, :]), reads=["Lp"], writes=["LL8"])
            S.op("dve", lambda e: e.tensor_copy(LL8[:, 1, :], Lp[:, 8, 0, :]), reads=["Lp"], writes=["LL8"])
            S.op("dve", lambda e: e.tensor_copy(LL8[:, 2, :], Lp[:, 8, 3, :]), reads=["Lp"], writes=["LL8"])
            S.op("dve", lambda e: e.tensor_copy(LL8[:, 3, :], Lp[:, 8, 1, :]), reads=["Lp"], writes=["LL8"])
            lp(lambda e: e.tensor_copy(V(19), Lp[:, 8, 0, :]))
            lp(lambda e: e.tensor_copy(V(20), Lp[:, 8, 1, :]))
            for _ in range(8):
                cmul(V(19), V(20), V(19), V(20), V(19), V(20), V(17), V(18), V(21))
            S.op("dve", lambda e: e.tensor_copy(D2kL[:, 0, :], V(19)), reads=["Lp", "sc"], writes=["D2kL"])
            S.op("dve", lambda e: e.tensor_copy(D2kL[:, 1, :], V(19)), reads=["Lp", "sc"], writes=["D2kL"])
            S.op("dve", lambda e: e.tensor_scalar(D2kL[:, 2, :], V(20), -1.0, None, ALU.mult), reads=["Lp", "sc"], writes=["D2kL"])
            S.op("dve", lambda e: e.tensor_copy(D2kL[:, 3, :], V(20)), reads=["Lp", "sc"], writes=["D2kL"])
            S.op("dve", lambda e: e.tensor_copy(D2k[:, 0, :], V(19)), reads=["Lp", "sc"], writes=["D2k"])
            S.op("dve", lambda e: e.tensor_copy(D2k[:, 1, :], V(20)), reads=["Lp", "sc"], writes=["D2k"])

            if dbg:
                S.dma("sp", "dbgsc", lambda e: e.dma_start(out=dbg_sc, in_=sc[:].rearrange("p a b -> p (a b)")), reads=["sc", "Lp"])
                S.dma("sp", "dbglp", lambda e: e.dma_start(out=dbg_lp, in_=Lp[:].rearrange("p a b c -> p (a b c)")), reads=["sc", "Lp"])
                S.dma("sp", "dbgp3", lambda e: e.dma_start(out=dbg_p3, in_=P3[:]), reads=["P3"])
                S.dma("sp", "dbgp1", lambda e: e.dma_start(out=dbg_p1, in_=P1[:]), reads=["P1"])
            Bn = sbt("Bn", [128, 2, 32, 16], F32)
            Bb = sbt("Bb", [128, 2, 32, 16], F32)
            Vn = sbt("Vn", [128, 8, 2, 32, 16], F32)
            tb = sbt("tb", [128, 32, 16], F32)
            S.dma("sp", "Bn0", lambda e: e.dma_start(out=Bn[:, 0, :, :], in_=b_re_d.rearrange("(j g) p h -> (g p) j h", g=2)), writes=["Bn0"])
            S.dma("sp", "Bn1", lambda e: e.dma_start(out=Bn[:, 1, :, :], in_=b_im_d.rearrange("(j g) p h -> (g p) j h", g=2)), writes=["Bn1"])

            def bc(v):
                return v.unsqueeze(2).to_broadcast([128, 32, 16])

            def bb(fn):
                S.op("dve", fn, reads=["sc", "Lp", "Bn0", "Bn1", "Bb", "tb"], writes=["Bb", "tb"])
            bb(lambda e: e.tensor_tensor(Bb[:, 0], Bn[:, 0], bc(V(15)), ALU.mult))
            bb(lambda e: e.tensor_tensor(tb[:], Bn[:, 1], bc(V(16)), ALU.mult))
            bb(lambda e: e.tensor_tensor(Bb[:, 0], Bb[:, 0], tb[:], ALU.subtract))
            bb(lambda e: e.tensor_tensor(Bb[:, 1], Bn[:, 1], bc(V(15)), ALU.mult))
            bb(lambda e: e.tensor_tensor(tb[:], Bn[:, 0], bc(V(16)), ALU.mult))
            bb(lambda e: e.tensor_tensor(Bb[:, 1], Bb[:, 1], tb[:], ALU.add))

            def vv(fn):
                S.op("dve", fn, reads=["Bb", "Lp", "Vn", "tb"], writes=["Vn", "tb"])
            for n in range(8):
                vv(lambda e, n=n: e.tensor_tensor(Vn[:, n, 0], Bb[:, 0], bc(Lp[:, n, 0, :]), ALU.mult))
                vv(lambda e, n=n: e.tensor_tensor(tb[:], Bb[:, 1], bc(Lp[:, n, 1, :]), ALU.mult))
                vv(lambda e, n=n: e.tensor_tensor(Vn[:, n, 0], Vn[:, n, 0], tb[:], ALU.subtract))
                vv(lambda e, n=n: e.tensor_tensor(Vn[:, n, 1], Bb[:, 1], bc(Lp[:, n, 0, :]), ALU.mult))
                vv(lambda e, n=n: e.tensor_tensor(tb[:], Bb[:, 0], bc(Lp[:, n, 1, :]), ALU.mult))
                vv(lambda e, n=n: e.tensor_tensor(Vn[:, n, 1], Vn[:, n, 1], tb[:], ALU.add))

            Cn = sbt("Cn", [128, 2, 8, 64], F32)
            par = sbt("par", [128, 2], F32)
            Xc = sbt("Xc", [128, 2, 128], F32)
            Yc = sbt("Yc", [128, 3, 8, 128], F32)
            S.dma("sp", "Cn0", lambda e: e.dma_start(out=Cn[:, 0, :, :], in_=c_re_d.rearrange("(kt g) h p -> (g h) kt p", g=8)), writes=["Cn0"])
            S.dma("sp", "Cn1", lambda e: e.dma_start(out=Cn[:, 1, :, :], in_=c_im_d.rearrange("(kt g) h p -> (g h) kt p", g=8)), writes=["Cn1"])
            Mp = sbt("Mp", [128, 128], F32)
            S.op("dve", lambda e: e.memset(Mp[:], 0.0), writes=["Mp"])
            for blk_ in range(4):
                S.op("dve", lambda e, blk_=blk_: e.memset(Mp[:, 32 * blk_ + 16:32 * blk_ + 32], 1.0), reads=["Mp"], writes=["Mp"])
            b = nextbank()
            S.op("pe", lambda e: e.transpose(ps[b][:, 0:128], Mp[:, :], ident[:, :]), reads=["Mp", "ident"], writes=[f"ps{b}"])
            S.op("dve", lambda e: e.tensor_copy(par[:, 0:1], ps[b][:, 0:1]), reads=[f"ps{b}"], writes=["par"])
            S.op("dve", lambda e: e.tensor_scalar(par[:, 1:2], par[:, 0:1], -1.0, 1.0, ALU.mult, ALU.add), reads=["par"], writes=["par"])
            for kt in range(8):
                for ri in range(2):
                    S.op("dve", lambda e, kt=kt, ri=ri: e.tensor_scalar(Xc[:, ri, 0:64], Cn[:, ri, kt, :], par[:, 1:2], None, ALU.mult),
                         reads=["Cn0", "Cn1", "par", "Xc%d" % ri], writes=["Xc%d" % ri])
                    S.op("dve", lambda e, kt=kt, ri=ri: e.tensor_scalar(Xc[:, ri, 64:128], Cn[:, ri, kt, :], par[:, 0:1], None, ALU.mult),
                         reads=["Cn0", "Cn1", "par", "Xc%d" % ri], writes=["Xc%d" % ri])
                    b = nextbank()
                    S.op("pe", lambda e, ri=ri, b=b: e.transpose(ps[b][:, 0:128], Xc[:, ri, :], ident[:, :]),
                         reads=["Xc%d" % ri, "ident"], writes=[f"ps{b}"])
                    S.op("act", lambda e, kt=kt, ri=ri, b=b: e.copy(Yc[:, ri, kt, :], ps[b][:, 0:128]),
                         reads=[f"ps{b}"], writes=["Yc"])
                    if ri == 1:
                        S.op("act", lambda e, kt=kt, b=b: e.mul(Yc[:, 2, kt, :], ps[b][:, 0:128], -1.0),
                             reads=[f"ps{b}"], writes=["Yc"])

            W3b = [sbt(f"W3b{i}", [128, 4, 8, 2, 128], BF16) for i in range(2)]
            W1b = [sbt(f"W1b{i}", [128, 4, 8, 2, 128], BF16) for i in range(2)]
            KTb = [sbt(f"KTb{i}", [128, 8, 128], BF16) for i in range(2)]
            Zf = [sbt(f"Zb{i}", [128, 1280], F32) for i in range(2)]
            Zb = [z[:, 0:1024].rearrange("p (q r c) -> p q r c", q=4, r=2) for z in Zf]
            Zd = [z[:, 0:1152].rearrange("p (q x) -> p q x", x=288)[:, :, 0:256].rearrange("p q (r c) -> p q r c", r=2) for z in Zf]
            w3t = sbt("w3t", [128, 32], F32)
            dI = sbt("dI", [128, 128], F32)
            for i in range(2):
                S.op("pool", lambda e, i=i: e.memset(W3b[i][:], 0.0), writes=[f"W3b{i}", f"W3b{i}a"])
                S.op("pool", lambda e, i=i: e.memset(Zf[i][:], 0.0), writes=[f"Zb{i}"])
            zc = 0
            for kt in range(8):
                w3 = W3b[kt % 2]
                w1 = W1b[kt % 2]
                ktb = KTb[kt % 2]
                k3, k1, kk = f"W3b{kt%2}", f"W1b{kt%2}", f"KTb{kt%2}"
                for q in range(4):
                    j = 4 * kt + q
                    cs = slice(32 * q, 32 * q + 32)
                    for tau in range(8):
                        n = tau + 1
                        S.op("dve", lambda e, n=n, j=j, cs=cs, w3=w3, q=q, tau=tau: e.tensor_scalar(
                            w3[:, q, tau, 0, cs], Yc[:, 0, kt, cs], Lp[:, n, 0, j:j + 1], None, ALU.mult), reads=["Yc", "Lp"], writes=[k3 + "a"])
                        S.op("dve", lambda e, n=n, j=j, cs=cs, w3=w3, q=q, tau=tau: e.tensor_scalar(
                            w3[:, q, tau, 1, cs], Yc[:, 0, kt, cs], Lp[:, n, 3, j:j + 1], None, ALU.mult), reads=["Yc", "Lp"], writes=[k3 + "a"])
                for q in range(4):
                    j = 4 * kt + q
                    cs = slice(32 * q, 32 * q + 32)
                    for tau in range(8):
                        n = tau + 1
                        S.op("dve", lambda e, n=n, j=j, cs=cs, w3=w3, q=q, tau=tau: e.scalar_tensor_tensor(
                            w3[:, q, tau, 0, cs], Yc[:, 1, kt, cs], Lp[:, n, 3, j:j + 1], w3[:, q, tau, 0, cs], ALU.mult, ALU.add),
                            reads=["Yc", "Lp", k3 + "a"], writes=[k3])
                        S.op("dve", lambda e, n=n, j=j, cs=cs, w3=w3, q=q, tau=tau: e.scalar_tensor_tensor(
                            w3[:, q, tau, 1, cs], Yc[:, 1, kt, cs], Lp[:, n, 2, j:j + 1], w3[:, q, tau, 1, cs], ALU.mult, ALU.add),
                            reads=["Yc", "Lp", k3 + "a"], writes=[k3])
                S.dma("sp", k3 + "st", lambda e, kt=kt, w3=w3: e.dma_start(out=W3_d[kt], in_=w3[:].rearrange("p a b c d -> p (a b c d)")),
                      reads=[k3, k3 + "a"], writes=[f"W3d{kt}"])
                S.op("dve", lambda e, kt=kt: e.tensor_scalar(dI[:], ident[:], P3[:, 96 + kt:97 + kt], None, ALU.mult),
                     reads=["ident", "P3", "dI"], writes=["dI"])
                for n in range(8):
                    zi = zc % 2
                    zb, zd = Zb[zi], Zd[zi]
                    kz = f"Zb{zi}"
                    zc += 1
                    S.op("pool", lambda e, zd=zd, n=n: e.tensor_copy(
                        zd[0:64, :, :, 0:16], Vn[0:64, n, :, 4 * kt:4 * kt + 4, :].rearrange("p r q h -> p q r h")),
                        reads=["Vn", kz], writes=[kz])
                    S.op("pool", lambda e, zd=zd, n=n: e.tensor_copy(
                        zd[64:128, :, :, 16:32], Vn[64:128, n, :, 4 * kt:4 * kt + 4, :].rearrange("p r q h -> p q r h")),
                        reads=["Vn", kz], writes=[kz])
                    for q in range(4):
                        for ri in range(2):
                            b = nextbank()
                            S.op("pe", lambda e, zb=zb, q=q, ri=ri, b=b: e.transpose(ps[b][:, 0:128], zb[:, q, ri, :], ident[:, :]),
                                 reads=[kz, "ident"], writes=[f"ps{b}"])
                            eng = ev_eng()
                            S.op(eng, copy_op(eng, w1[:, q, 7 - n, ri, :], ps[b][:, 0:128]), reads=[f"ps{b}"], writes=[k1])
                    b = nextbank()
                    for q in range(4):
                        cs = slice(32 * q, 32 * q + 32)
                        S.op("pe", lambda e, zb=zb, q=q, cs=cs, b=b: e.matmul(ps[b][:, cs], zb[:, q, 0, :], Yc[:, 0, kt, cs], start=True, stop=False),
                             reads=[kz, "Yc"], writes=[f"ps{b}"], inc=False)
                        S.op("pe", lambda e, zb=zb, q=q, cs=cs, b=b: e.matmul(ps[b][:, cs], zb[:, q, 1, :], Yc[:, 2, kt, cs], start=False, stop=True),
                             reads=[kz, "Yc"], writes=[f"ps{b}"], inc=(q == 3))
                    if n == 0:
                        S.op("dve", lambda e, ktb=ktb, b=b: e.tensor_tensor(ktb[:, 0, :], ps[b][:, 0:128], dI[:], ALU.add),
                             reads=[f"ps{b}", "dI"], writes=[kk])
                    else:
                        S.op("act", lambda e, ktb=ktb, b=b, n=n: e.copy(ktb[:, n, :], ps[b][:, 0:128]),
                             reads=[f"ps{b}"], writes=[kk])
                S.dma("sp", k1 + "st", lambda e, kt=kt, w1=w1: e.dma_start(out=W1_d[kt], in_=w1[:].rearrange("p a b c d -> p (a b c d)")),
                      reads=[k1], writes=[f"W1d{kt}"])
                S.dma("sp", kk + "st", lambda e, kt=kt, ktb=ktb: e.dma_start(out=KT_d[kt], in_=ktb[:].rearrange("p a b -> p (a b)")),
                      reads=[kk], writes=[f"KTd{kt}"])
        end_phase(m0)
        if stop_after == "p0":
            return nc

        mA = A.mark()
        S.op("dve", lambda e: e.memset(Ecur[:], 0.0), writes=["Ecur"])
        u_holder = {}
        scan_hook = {"f": None}

        def run_segment(seg, mode):
            own = (mode == "own")
            xo = seg * TOK
            if mode in ("rg", "own"):
                S.op("dve", lambda e: e.tensor_scalar(Ecur[:], Ecur[:], segm[:, seg:seg + 1], None, ALU.mult), reads=["Ecur", "segm"], writes=["Ecur"])
            if own:
                S.op("dve", lambda e: e.tensor_copy(hcar[:], Ecur[:]), reads=["Ecur"], writes=["hcar"])
            mA1 = A.mark()
            if mode == "s5prep":
                u_sb = sbt("u_sb", [128, 8, 8, 256], BF16)
            else:
                u_sb = u_holder.get("u")
            mU = A.mark()
            xT = sbt("xT", [128, 16, TOK + HALO], BF16)
            mx = A.mark()
            xin = [sbt(f"xin{i}", [128, D], F32) for i in range(3)]
            tiles = [(0, 3, 0)] + [(3 + 128 * t, 128, 3 + 128 * t) for t in range(16)]
            for ti, (r0, nr, c0) in enumerate(tiles):
                xb = xin[ti % 3]
                key = f"xin{ti%3}"
                S.dma("sp", key, lambda e, xb=xb, r0=r0, nr=nr: e.dma_start(out=xb[0:nr, :], in_=x_d[xo + r0:xo + r0 + nr, :]), writes=[key])
                for b4 in range(4):
                    b = nextbank()
                    for jj in range(4):
                        kt = 4 * b4 + jj
                        S.op("pe", lambda e, xb=xb, nr=nr, kt=kt, jj=jj, b=b: e.transpose(
                            ps[b][:, jj * 128:jj * 128 + nr], xb[0:nr, kt * 128:(kt + 1) * 128], ident[0:nr, 0:nr]),
                            reads=[key, "ident"], writes=[f"ps{b}"], inc=(jj == 3))
                    eng = ev_eng()
                    S.op(eng, copy_op(eng, xT[:, 4 * b4:4 * b4 + 4, c0:c0 + nr],
                                      ps[b][:, 0:512].rearrange("p (j t) -> p j t", j=4)[:, :, 0:nr]),
                         reads=[f"ps{b}"], writes=[f"xT{ti}_{b4}"])
            allx = [f"xT{ti}_{b4}" for ti in range(17) for b4 in range(4)]
            if own:
                S.dma("sp", "xTst", lambda e: e.dma_start(out=xT_d, in_=xT[:]), reads=allx, writes=["xTd"])
            S.barrier()
            A.release(mx)

            def alloc_wb():
                return [sbt(f"wb{i}", [128, 16, 512], BF16) for i in range(2)]

            def do_A1(scan_emit):
                NH = 1024
                bufs = []
                for i in range(2):
                    bufs.append(dict(
                        xr=sbt(f"xr{i}", [128, NH + 3], F32), xc=sbt(f"xc{i}", [128, NH], F32),
                        xcb=sbt(f"xcb{i}", [128, NH], BF16), thr=sbt(f"thr{i}", [128, NH], F32),
                        thi=sbt(f"thi{i}", [128, NH], F32), at=sbt(f"at{i}", [128, NH], F32),
                        tt=sbt(f"tt{i}", [128, NH], F32)))
                for B__ in bufs:
                    B__["hl"] = B__["tt"]
                un = 0
                load_w(0, w_in_v, 0)
                for h in range(16):
                    if h % 4 == 0 and h // 4 + 1 < 4:
                        load_w((h // 4 + 1) % 2, w_in_v, 512 * (h // 4 + 1))
                    wbi = (h // 4) % 2
                    hh = h % 4
                    for hf in range(2):
                        B_ = bufs[un % 2]
                        sx = str(un % 2)
                        un += 1
                        cb0 = NH * hf
                        pieces = [(cb0, 3, 0), (cb0 + 3, 512, 3), (cb0 + 515, 512, 515)]
                        for (c0, n, off) in pieces:
                            b = nextbank()
                            for kt in range(16):
                                S.op("pe", lambda e, b=b, n=n, kt=kt, c0=c0: e.matmul(
                                    ps[b][:, 0:n], wb[wbi][:, kt, hh * 128:(hh + 1) * 128], xT[:, kt, c0:c0 + n],
                                    start=(kt == 0), stop=(kt == 15)),
                                    reads=[f"wb{wbi}"], writes=[f"ps{b}"], inc=(kt == 15))
                            S.op("act", lambda e, b=b, n=n, off=off, B_=B_: e.copy(B_["xr"][:, off:off + n], ps[b][:, 0:n]),
                                 reads=[f"ps{b}"], writes=["xr" + sx + "_%d" % off])
                        xrk = ["xr" + sx + "_%d" % o for o in (0, 3, 515)]
                        S.op("dve", lambda e, B_=B_: e.tensor_scalar(B_["xc"][:], B_["xr"][:, 0:NH], P1[:, h:h + 1], P1[:, 64 + h:65 + h], ALU.mult, ALU.add),
                             reads=xrk + ["xc" + sx, "xcb" + sx], writes=["xc" + sx])
                        for k in range(1, 4):
                            S.op("dve", lambda e, B_=B_, k=k: e.scalar_tensor_tensor(
                                B_["xc"][:], B_["xr"][:, k:k + NH], P1[:, k * 16 + h:k * 16 + h + 1], B_["xc"][:], ALU.mult, ALU.add),
                                reads=xrk + ["xc" + sx], writes=["xc" + sx])
                        S.op("act", lambda e, B_=B_: e.copy(B_["xcb"][:], B_["xc"][:]), reads=["xc" + sx], writes=["xcb" + sx])
                        for g, (wsb, wk, dst) in enumerate(((wa_sb, "wa_sb", "thr"), (wx_sb, "wx_sb", "thi"))):
                            for t2 in range(2):
                                b = nextbank()
                                S.op("pe", lambda e, b=b, wsb=wsb, t2=t2, B_=B_: e.matmul(
                                    ps[b][:, :], wsb[:, h, :], B_["xcb"][:, t2 * 512:(t2 + 1) * 512], start=True, stop=True),
                                    reads=[wk, "xcb" + sx], writes=[f"ps{b}"])
                                if g == 0:
                                    S.op("act", lambda e, b=b, t2=t2, B_=B_, dst=dst: e.activation(
                                        B_[dst][:, t2 * 512:(t2 + 1) * 512], ps[b][:, :], AF.Tanh, bias=rgc[:, 0, h:h + 1], scale=0.5),
                                        reads=[f"ps{b}"], writes=[dst + sx + "_%d" % t2])
                                else:
                                    S.op("act", lambda e, b=b, t2=t2, B_=B_, dst=dst: e.activation(
                                        B_[dst][:, t2 * 512:(t2 + 1) * 512], ps[b][:, :], AF.Tanh, bias=rgc[:, 1, h:h + 1], scale=0.5),
                                        reads=[f"ps{b}"], writes=[dst + sx + "_%d" % t2])
                        thrk = ["thr" + sx + "_0", "thr" + sx + "_1"]
                        thik = ["thi" + sx + "_0", "thi" + sx + "_1"]
                        S.op("act", lambda e, B_=B_: e.activation(B_["at"][:], B_["thr"][:], AF.Exp, bias=rgc[:, 3, h:h + 1], scale=rgc[:, 3, h:h + 1]),
                             reads=thrk, writes=["at" + sx])
                        S.op("act", lambda e, B_=B_: e.activation(B_["tt"][:], B_["thr"][:], AF.Tanh, bias=rgc[:, 4, h:h + 1], scale=rgc[:, 4, h:h + 1]),
                             reads=thrk, writes=["tt" + sx])
                        S.op("act", lambda e, B_=B_: e.activation(B_["thr"][:], B_["thr"][:], AF.Exp, bias=rgc[:, 2, h:h + 1], scale=rgc[:, 2, h:h + 1]),
                             reads=thrk, writes=thrk)
                        S.op("dve", lambda e, B_=B_: e.scalar_tensor_tensor(B_["tt"][:], B_["thr"][:], 1.0, B_["tt"][:], ALU.add, ALU.mult),
                             reads=thrk + ["tt" + sx], writes=["tt" + sx])
                        S.op("act", lambda e, B_=B_: e.activation(B_["tt"][:], B_["tt"][:], AF.Sqrt), reads=["tt" + sx], writes=["tt" + sx])
                        S.op("dve", lambda e, B_=B_: e.scalar_tensor_tensor(B_["thi"][:], B_["thi"][:], 1.0, B_["xc"][:], ALU.add, ALU.mult),
                             reads=thik + ["xc" + sx], writes=thik)
                        S.op("dve", lambda e, B_=B_: e.scalar_tensor_tensor(B_["thi"][:], B_["thi"][:], 0.5, B_["tt"][:], ALU.mult, ALU.mult),
                             reads=thik + ["tt" + sx], writes=thik)
                        if not own:
                            S.op("dve", lambda e, B_=B_: e.tensor_tensor_scan(B_["hl"][:], B_["at"][:], B_["thi"][:], Ecur[:, h:h + 1], ALU.mult, ALU.add),
                                 reads=["at" + sx, "Ecur", "tt" + sx] + thik, writes=["tt" + sx])
                            S.op("dve", lambda e, B_=B_: e.tensor_copy(Ecur[:, h:h + 1], B_["hl"][:, NH - 1:NH]), reads=["tt" + sx], writes=["Ecur"])
                        else:
                            S.dma("act", "ast" + sx, lambda e, B_=B_, hf=hf: e.dma_start(out=ab_d[0, h, :, hf * NH:(hf + 1) * NH], in_=B_["at"][:]),
                                  reads=["at" + sx], writes=[f"abd0_{h}_{hf}"])
                            S.dma("sp", "bst" + sx, lambda e, B_=B_, hf=hf: e.dma_start(out=ab_d[1, h, :, hf * NH:(hf + 1) * NH], in_=B_["thi"][:]),
                                  reads=thik, writes=[f"abd1_{h}_{hf}"])
                    if scan_emit is not None:
                        scan_emit(8)

            def do_uproj():
                for g in range(2):
                    load_w(g, w_in_v, 4096 + 512 * g)
                for kt in range(8):
                    wbi, hh = kt // 4, kt % 4
                    for tq in range(4):
                        b = nextbank()
                        c0 = 3 + 512 * tq
                        for k in range(16):
                            S.op("pe", lambda e, b=b, k=k, c0=c0: e.matmul(
                                ps[b][:, :], wb[wbi][:, k, hh * 128:(hh + 1) * 128], xT[:, k, c0:c0 + 512], start=(k == 0), stop=(k == 15)),
                                reads=[f"wb{wbi}"], writes=[f"ps{b}"], inc=(k == 15))
                        eng = ev_eng()
                        S.op(eng, copy_op(eng, u_sb[:, kt, :, 64 * tq:64 * tq + 64], ps[b][:, :].rearrange("p (c s) -> p s c", s=8)),
                             reads=[f"ps{b}"], writes=[f"u{kt}_{tq}"])

            def do_S():
                W1s = [sbt(f"W1s{i}", [128, 8192], BF16) for i in range(2)]
                for kt in range(8):
                    w1 = W1s[kt % 2]
                    k1 = f"W1s{kt%2}"
                    S.dma("sp", k1, lambda e, w1=w1, kt=kt: e.dma_start(out=w1[:], in_=W1_d[kt]), writes=[k1])
                    for q in range(4):
                        j = 4 * kt + q
                        for ri in range(2):
                            b = nextbank()
                            for s in range(8):
                                o = ((q * 8 + s) * 2 + ri) * 128
                                S.op("pe", lambda e, b=b, s=s, o=o, w1=w1: e.matmul(
                                    ps[b][:, 0:256], w1[:, o:o + 128], u_sb[:, kt, s, :], start=(s == 0), stop=(s == 7)),
                                    reads=[k1], writes=[f"ps{b}"], inc=(s == 7))
                            eng = ev_eng()
                            S.op(eng, copy_op(eng, S_sb[:, :, ri, j], ps[b][:, 0:256]), reads=[f"ps{b}"], writes=[f"S_{j}_{ri}"])
                allS = [f"S_{j}_{ri}" for j in range(32) for ri in range(2)]
                return allS

            if own:
                wb = alloc_wb()
                def load_w(i, src_v, c0, ncol=512, nk=16):
                    S.dma("pool", f"wb{i}", lambda e: e.dma_start(out=wb[i][:, 0:nk, 0:ncol], in_=src_v[:, :, c0:c0 + ncol]), writes=[f"wb{i}"])

                do_A1(None)
                do_uproj()
                end_phase(mA1)
                mS = A.mark()
                S_sb = sbt("S_sb", [128, 256, 2, 32], F32)
                u_holder["S_sb"] = S_sb
                mA2 = A.mark()
                do_S()
                end_phase(mA2)
            elif mode == "s5prep":
                mW = A.mark()
                wb = alloc_wb()
                def load_w(i, src_v, c0, ncol=512, nk=16):
                    S.dma("pool", f"wb{i}", lambda e: e.dma_start(out=wb[i][:, 0:nk, 0:ncol], in_=src_v[:, :, c0:c0 + ncol]), writes=[f"wb{i}"])

                do_uproj()
                end_phase(mU)
                S_sb = sbt("S_sb", [128, 256, 2, 32], F32)
                allS = do_S()
                S.dma("sp", "Sspill", lambda e: e.dma_start(out=S_d3[seg], in_=S_sb[:].rearrange("p c r j -> p (c r j)")), reads=allS, writes=[f"S_d3_{seg}"])
                end_phase(mA1)
            else:
                wb = alloc_wb()
                def load_w(i, src_v, c0, ncol=512, nk=16):
                    S.dma("pool", f"wb{i}", lambda e: e.dma_start(out=wb[i][:, 0:nk, 0:ncol], in_=src_v[:, :, c0:c0 + ncol]), writes=[f"wb{i}"])

                do_A1(scan_hook["f"])
                end_phase(mA1)

        for seg in range(3):
            run_segment(seg, "s5prep")
        mW3 = A.mark()
        Sst3 = [sbt(f"Sst3_{i}", [128, 3, 8, 2, 32], F32) for i in range(2)]
        sstate = {"c": 0}
        S_d3v = S_d3.rearrange("s p f -> p s f")

        def load_piece(i):
            S.dma("sp", f"Sst3_{i%2}", lambda e: e.dma_start(out=Sst3[i % 2][:].rearrange("p s c r j -> p s (c r j)"), in_=S_d3v[:, :, 512 * i:512 * (i + 1)]),
                  writes=[f"Sst3_{i%2}"])

        def scan_emit(n):
            for _ in range(n):
                c = sstate["c"]
                if c >= 256:
                    return
                i, cc = c // 8, c % 8
                sk = f"Sst3_{i%2}"
                sc_ap = Sst3[i % 2][:, :, cc, :, :]
                S.op("pool", lambda e: e.tensor_tensor(X3[:], H3[:], LL8[:, 0:2, :].unsqueeze(1).to_broadcast([128, 3, 2, 32]), ALU.mult), reads=["H3"], writes=["X3"])
                S.op("pool", lambda e: e.tensor_tensor(Y3[:, :, 0, :], H3[:, :, 1, :], LL8[:, 2, :].unsqueeze(1).to_broadcast([128, 3, 32]), ALU.mult), reads=["H3"], writes=["Y3"])
                S.op("pool", lambda e: e.tensor_tensor(Y3[:, :, 1, :], H3[:, :, 0, :], LL8[:, 3, :].unsqueeze(1).to_broadcast([128, 3, 32]), ALU.mult), reads=["H3"], writes=["Y3"])
                S.op("pool", lambda e: e.tensor_tensor(X3[:], X3[:], Y3[:], ALU.add), reads=["X3", "Y3"], writes=["X3"])
                S.op("pool", lambda e: e.tensor_tensor(H3[:], X3[:], sc_ap, ALU.add), reads=["X3", sk], writes=["H3"])
                sstate["c"] = c + 1
                if cc == 7 and i + 2 < 32:
                    load_piece(i + 2)
        scan_hook["f"] = scan_emit
        S.op("pool", lambda e: e.memset(H3[:], 0.0), writes=["H3"])
        load_piece(0)
        load_piece(1)
        for seg in range(3):
            run_segment(seg, "rg")
        scan_emit(256)

        def cstep(LLm, prev, add_ap, out_ap, rk, wk):
            S.op("dve", lambda e: e.tensor_tensor(Xd[:], LLm[:, 0:2, :], prev, ALU.mult), reads=rk, writes=["Xd"])
            S.op("dve", lambda e: e.tensor_tensor(Yd[:, 0, :], LLm[:, 2, :], prev[:, 1, :], ALU.mult), reads=rk, writes=["Yd"])
            S.op("dve", lambda e: e.tensor_tensor(Yd[:, 1, :], LLm[:, 3, :], prev[:, 0, :], ALU.mult), reads=rk, writes=["Yd"])
            S.op("dve", lambda e: e.tensor_tensor(Xd[:], Xd[:], Yd[:], ALU.add), reads=["Xd", "Yd"], writes=["Xd"])
            S.op("dve", lambda e: e.tensor_tensor(out_ap, Xd[:], add_ap, ALU.add), reads=["Xd"] + rk, writes=wk)
        cstep(D2kL, H3[:, 0, :, :], H3[:, 1, :, :], Ha[:], ["H3"], ["Ha"])
        cstep(D2kL, Ha[:], H3[:, 2, :, :], Hin[:], ["Ha", "H3"], ["Hin"])
        end_phase(mW3)
        u_holder["u"] = sbt("u_sb", [128, 8, 8, 256], BF16, top_=True)
        run_segment(3, "own")
        u_sb = u_holder["u"]
        S_sb = u_holder["S_sb"]
        if True:
            Hbf = sbt("Hbf", [128, 32, 2, 258], BF16)
            W3s = [sbt(f"W3s{i}", [128, 8192], BF16) for i in range(2)]
            KTs = [sbt(f"KTs{i}", [128, 8, 128], BF16) for i in range(2)]
            ysb = [sbt(f"ysb{i}", [128, TOK], BF16) for i in range(2)]
            Xs = sbt("Xs4", [128, 2, 32], F32)
            Ys = sbt("Ys4", [128, 2, 32], F32)

            def chunk_step4(prev, c):
                S.op("dve", lambda e: e.tensor_tensor(Xs[:], LL8[:, 0:2, :], prev, ALU.mult), reads=["Ssb"], writes=["Xs"])
                S.op("dve", lambda e: e.tensor_tensor(Ys[:, 0, :], LL8[:, 2, :], prev[:, 1, :], ALU.mult), reads=["Ssb"], writes=["Ys"])
                S.op("dve", lambda e: e.tensor_tensor(Ys[:, 1, :], LL8[:, 3, :], prev[:, 0, :], ALU.mult), reads=["Ssb"], writes=["Ys"])
                S.op("dve", lambda e: e.tensor_tensor(Xs[:], Xs[:], Ys[:], ALU.add), reads=["Xs", "Ys"], writes=["Xs"])
                S.op("dve", lambda e: e.tensor_tensor(S_sb[:, c, :, :], Xs[:], S_sb[:, c, :, :], ALU.add), reads=["Xs", "Ssb"], writes=["Ssb"])
            for c in range(256):
                chunk_step4(Hin[:] if c == 0 else S_sb[:, c - 1, :, :], c)
            S.op("act", lambda e: e.copy(Hbf[:, :, :, 0], Hin[:].rearrange("p a b -> p b a")), reads=[], writes=["Hbf_0"])
            S.op("act", lambda e: e.copy(Hbf[:, :, 0, 1:257], S_sb[:, :, 0, :].rearrange("p c j -> p j c")), reads=["Ssb"], writes=["Hbf_1"])
            S.op("dve", lambda e: e.tensor_copy(Hbf[:, :, 1, 1:257], S_sb[:, :, 1, :].rearrange("p c j -> p j c")), reads=["Ssb"], writes=["Hbf_2"])
            hbk = ["Hbf_0", "Hbf_1", "Hbf_2"]
            for kt in range(8):
                w3 = W3s[kt % 2]
                ktb = KTs[kt % 2]
                k3, kk = f"W3s{kt%2}", f"KTs{kt%2}"
                ys = ysb[kt % 2]
                ky = f"ysb{kt%2}"
                S.dma("sp", k3, lambda e, w3=w3, kt=kt: e.dma_start(out=w3[:], in_=W3_d[kt]), writes=[k3])
                S.dma("sp", kk, lambda e, ktb=ktb, kt=kt: e.dma_start(out=ktb[:].rearrange("p a b -> p (a b)"), in_=KT_d[kt]), writes=[kk])
                for tau in range(8):
                    b = nextbank()
                    mm = []
                    for s in range(tau + 1):
                        mm.append((ktb[:, tau - s, :], u_sb[:, kt, s, :], [kk]))
                    for q in range(4):
                        for ri in range(2):
                            o = ((q * 8 + tau) * 2 + ri) * 128
                            mm.append((w3[:, o:o + 128], Hbf[:, 4 * kt + q, ri, 0:256], [k3] + hbk))
                    for i, (l_, r_, rk) in enumerate(mm):
                        S.op("pe", lambda e, b=b, l_=l_, r_=r_, i=i, n=len(mm): e.matmul(
                            ps[b][:, 0:256], l_, r_, start=(i == 0), stop=(i == n - 1)),
                            reads=rk, writes=[f"ps{b}"], inc=(i == len(mm) - 1))
                    S.op("act", lambda e, b=b, ys=ys, tau=tau: e.activation(ys[:, tau::8], ps[b][:, 0:256], AF.Gelu_apprx_tanh),
                         reads=[f"ps{b}"], writes=[ky + "_%d" % tau])
                S.dma("act", ky + "st", lambda e, ys=ys, kt=kt: e.dma_start(out=yS_d[:, kt, :], in_=ys[:]),
                      reads=[ky + "_%d" % t for t in range(8)], writes=[f"ySd{kt}"])
        end_phase(mA)
        if stop_after == "a4":
            return nc

        NB = 1024
        for blk in range(2):
            t0 = blk * NB
            mB = A.mark()
            xTb = sbt("xTb", [128, 16, NB], BF16)
            ySb = sbt("ySb", [128, 8, NB], BF16)
            hg = sbt("hg", [128, 16, NB], BF16)
            S.dma("sp", "xTb", lambda e: e.dma_start(out=xTb[:], in_=xT_d[:, :, 3 + t0:3 + t0 + NB]), writes=["xTb"])
            S.dma("sp", "ySb", lambda e: e.dma_start(out=ySb[:], in_=yS_d[:, :, t0:t0 + NB]), writes=["ySb"])
            mG = A.mark()
            if True:
                wg = [sbt(f"wg{i}", [128, 16, 512], BF16) for i in range(2)]
                gsb = [sbt(f"gsb{i}", [128, NB], F32) for i in range(2)]
                ab = [sbt(f"abl{i}", [128, 2, NB], F32) for i in range(2)]
                hs = [sbt(f"hs{i}", [128, NB], F32) for i in range(2)]
                def load_wg(g):
                    S.dma("pool", f"wg{g%2}", lambda e: e.dma_start(out=wg[g % 2][:], in_=w_in_v[:, :, 2048 + 512 * g:2048 + 512 * g + 512]), writes=[f"wg{g%2}"])
                load_wg(0)
                for h in range(16):
                    if h % 4 == 0 and h // 4 + 1 < 4:
                        load_wg(h // 4 + 1)
                    wgi, hh, sx = (h // 4) % 2, h % 4, str(h % 2)
                    S.dma("sp", "abl" + sx, lambda e, h=h: e.dma_start(out=ab[h % 2][:], in_=ab_d[:, h, :, t0:t0 + NB].rearrange("a p t -> p a t")),
                          writes=["abl" + sx])
                    for t2 in range(2):
                        b = nextbank()
                        for kt in range(16):
                            S.op("pe", lambda e, b=b, kt=kt, t2=t2: e.matmul(
                                ps[b][:, :], wg[wgi][:, kt, hh * 128:(hh + 1) * 128], xTb[:, kt, t2 * 512:(t2 + 1) * 512], start=(kt == 0), stop=(kt == 15)),
                                reads=[f"wg{wgi}", "xTb"], writes=[f"ps{b}"], inc=(kt == 15))
                        S.op("act", lambda e, b=b, t2=t2, h=h: e.activation(gsb[h % 2][:, t2 * 512:(t2 + 1) * 512], ps[b][:, :], AF.Gelu_apprx_tanh),
                             reads=[f"ps{b}"], writes=["gsb" + sx + "_%d" % t2])
                    S.op("dve", lambda e, h=h: e.tensor_tensor_scan(hs[h % 2][:], ab[h % 2][:, 0, :], ab[h % 2][:, 1, :], hcar[:, h:h + 1], ALU.mult, ALU.add),
                         reads=["abl" + sx, "hcar"], writes=["hs" + sx])
                    S.op("dve", lambda e, h=h: e.tensor_copy(hcar[:, h:h + 1], hs[h % 2][:, NB - 1:NB]), reads=["hs" + sx], writes=["hcar"])
                    S.op("dve", lambda e, h=h: e.tensor_tensor(hg[:, h, :], hs[h % 2][:], gsb[h % 2][:], ALU.mult),
                         reads=["hs" + sx, "gsb" + sx + "_0", "gsb" + sx + "_1"], writes=[f"hg{h}"])
            end_phase(mG)
            if True:
                wA = [sbt(f"wA{i}", [128, 16, 256], BF16) for i in range(2)]
                wGa = [sbt(f"wGa{i}", [128, 16, 256], BF16) for i in range(2)]
                wGb = [sbt(f"wGb{i}", [128, 16, 256], BF16) for i in range(2)]
                wLw = [sbt(f"wLw{i}", [128, 8, 256], BF16) for i in range(2)]
                wLv = [sbt(f"wLv{i}", [128, 8, 256], BF16) for i in range(2)]
                tmp = [sbt(f"mt{i}", [128, 4, 512], F32) for i in range(2)]
                mixs = [sbt(f"mixs{i}", [128, 512], BF16) for i in range(2)]
                mc = 0

                def load_mix(jg):
                    i = jg % 2
                    c0 = 256 * jg
                    S.dma("pool", f"wGa{i}", lambda e: e.dma_start(out=wGa[i][:], in_=w_in_v[:, :, 5120 + c0:5120 + c0 + 256]), writes=[f"wGa{i}"])
                    S.dma("pool", f"wGb{i}", lambda e: e.dma_start(out=wGb[i][:], in_=w_in_v[:, :, 7168 + c0:7168 + c0 + 256]), writes=[f"wGb{i}"])
                    S.dma("pool", f"wLv{i}", lambda e: e.dma_start(out=wLv[i][:], in_=glu_v_v[:, :, c0:c0 + 256]), writes=[f"wLv{i}"])
                    S.dma("pool", f"wLw{i}", lambda e: e.dma_start(out=wLw[i][:], in_=glu_w_v[:, :, c0:c0 + 256]), writes=[f"wLw{i}"])
                    S.dma("pool", f"wA{i}", lambda e: e.dma_start(out=wA[i][:], in_=w_a_v[:, :, c0:c0 + 256]), writes=[f"wA{i}"])
                load_mix(0)
                for jg in range(8):
                    i = jg % 2
                    c0 = 256 * jg
                    if jg + 1 < 8:
                        load_mix(jg + 1)
                    for jj in range(2):
                        j = 2 * jg + jj
                        cs = slice(128 * jj, 128 * jj + 128)
                        for t2 in range(2):
                            ts = slice(t2 * 512, (t2 + 1) * 512)
                            T_ = tmp[mc % 2]
                            tk = f"mt{mc%2}"
                            ms = mixs[mc % 2]
                            mk = f"mixs{mc%2}"
                            mc += 1

                            def group(wt, wk, act, ak, nk):
                                b = nextbank()
                                for kt in range(nk):
                                    S.op("pe", lambda e, b=b, kt=kt: e.matmul(ps[b][:, :], wt[:, kt, cs], act[:, kt, ts], start=(kt == 0), stop=(kt == nk - 1)),
                                         reads=[wk] + ak, writes=[f"ps{b}"], inc=(kt == nk - 1))
                                return b
                            bA = group(wGa[i], f"wGa{i}", xTb, ["xTb"], 16)
                            S.op("act", lambda e, b=bA, T_=T_: e.activation(T_[:, 0, :], ps[b][:, :], AF.Sigmoid), reads=[f"ps{bA}"], writes=[tk + "a"])
                            bB = group(wGb[i], f"wGb{i}", xTb, ["xTb"], 16)
                            S.op("act", lambda e, b=bB, T_=T_: e.activation(T_[:, 1, :], ps[b][:, :], AF.Sigmoid), reads=[f"ps{bB}"], writes=[tk + "b"])
                            bV = group(wLv[i], f"wLv{i}", ySb, ["ySb"], 8)
                            S.op("act", lambda e, b=bV, T_=T_: e.activation(T_[:, 2, :], ps[b][:, :], AF.Sigmoid), reads=[f"ps{bV}"], writes=[tk + "v"])
                            bW = group(wLw[i], f"wLw{i}", ySb, ["ySb"], 8)
                            S.op("dve", lambda e, b=bW, T_=T_: e.tensor_tensor(T_[:, 2, :], ps[b][:, :], T_[:, 2, :], ALU.mult), reads=[f"ps{bW}", tk + "v"], writes=[tk + "v"])
                            S.op("dve", lambda e, T_=T_: e.tensor_tensor(T_[:, 2, :], T_[:, 2, :], T_[:, 1, :], ALU.mult), reads=[tk + "v", tk + "b"], writes=[tk + "v"])
                            bY = group(wA[i], f"wA{i}", hg, [], 16)
                            S.op("dve", lambda e, b=bY, T_=T_: e.tensor_tensor(T_[:, 0, :], ps[b][:, :], T_[:, 0, :], ALU.mult), reads=[f"ps{bY}", tk + "a"], writes=[tk + "a"])
                            S.op("dve", lambda e, T_=T_, ms=ms: e.tensor_tensor(ms[:], T_[:, 0, :], T_[:, 2, :], ALU.add), reads=[tk + "a", tk + "v"], writes=[mk])
                            S.dma("sp", mk + "st", lambda e, ms=ms, j=j, t2=t2: e.dma_start(out=mix_d[:, j, t0 + t2 * 512:t0 + (t2 + 1) * 512], in_=ms[:]),
                                  reads=[mk], writes=[f"mixd{j}_{t2}"])
            end_phase(mB)
            if stop_after == "mix":
                continue

            mF = A.mark()
            acc = sbt("acc", [128, 8, D], F32)
            lnp_off = A.lo
            lnp = sbt("lnp", [128, 2, D], F32)
            mO = A.mark()
            if True:
                mixT = sbt("mixT", [128, 16, NB], BF16)
                wo = [sbt(f"wo{i}", [128, 16, 512], BF16) for i in range(2)]
                xres = [sbt(f"xres{i}", [128, 512], F32) for i in range(3)]
                S.dma("sp", "mixT", lambda e: e.dma_start(out=mixT[:], in_=mix_d[:, :, t0:t0 + NB]), writes=["mixT"])
                S.dma("sp", "lnp0", lambda e: e.dma_start(out=lnp[:, 0, :], in_=ln1_g_d.partition_broadcast(128)), writes=["lnp0"])
                S.dma("sp", "lnp1", lambda e: e.dma_start(out=lnp[:, 1, :], in_=ln1_b_d.partition_broadcast(128)), writes=["lnp1"])
                xc_ = 0
                def load_wo(cb):
                    S.dma("pool", f"wo{cb%2}", lambda e: e.dma_start(out=wo[cb % 2][:], in_=w_out_v[:, :, 512 * cb:512 * cb + 512]), writes=[f"wo{cb%2}"])
                load_wo(0)
                for cb in range(4):
                    i = cb % 2
                    if cb + 1 < 4:
                        load_wo(cb + 1)
                    for tt in range(8):
                        xr_ = xres[xc_ % 3]
                        xk = f"xres{xc_%3}"
                        xc_ += 1
                        r0 = 3 * TOK + 3 + t0 + 128 * tt
                        S.dma("sp", xk, lambda e, xr_=xr_, r0=r0, cb=cb: e.dma_start(out=xr_[:], in_=x_d[r0:r0 + 128, 512 * cb:512 * cb + 512]), writes=[xk])
                        b = nextbank()
                        for kt in range(16):
                            S.op("pe", lambda e, b=b, kt=kt, tt=tt, i=i: e.matmul(
                                ps[b][:, :], mixT[:, kt, 128 * tt:128 * tt + 128], wo[i][:, kt, :], start=(kt == 0), stop=(kt == 15)),
                                reads=["mixT", f"wo{i}"], writes=[f"ps{b}"], inc=(kt == 15))
                        S.op("dve", lambda e, b=b, xr_=xr_, tt=tt, cb=cb: e.scalar_tensor_tensor(
                            acc[:, tt, 512 * cb:512 * cb + 512], xr_[:], ALPHA, ps[b][:, :], ALU.mult, ALU.add),
                            reads=[xk, f"ps{b}"], writes=[f"acc{tt}_{cb}"])
                        S.op("dve", lambda e, tt=tt, cb=cb: e.bn_stats(stats[:, tt, cb, :], acc[:, tt, 512 * cb:512 * cb + 512]),
                             reads=[f"acc{tt}_{cb}"], writes=[f"stats{tt}_{cb}"])
            end_phase(mO)
            x1T = sbt("x1T", [128, 16, NB], BF16)

            def layernorm(tt, outk):
                S.op("dve", lambda e: e.bn_aggr(mv[:, tt, 0:2], stats[:, tt, :, :].rearrange("p a b -> p (a b)")),
                     reads=[f"stats{tt}_{c_}" for c_ in range(4)], writes=[f"mv{tt}"])
                S.op("act", lambda e: e.activation(mv[:, tt, 2:3], mv[:, tt, 1:2], AF.Sqrt, bias=EPS), reads=[f"mv{tt}"], writes=[f"mv{tt}"])
                S.op("dve", lambda e: e.reciprocal(mv[:, tt, 2:3], mv[:, tt, 2:3]), reads=[f"mv{tt}"], writes=[f"mv{tt}"])
                S.op("dve", lambda e: e.scalar_tensor_tensor(mv[:, tt, 3:4], mv[:, tt, 0:1], -1.0, mv[:, tt, 2:3], ALU.mult, ALU.mult),
                     reads=[f"mv{tt}"], writes=[f"mv{tt}"])
                S.op("act", lambda e: e.activation(acc[:, tt, :], acc[:, tt, :], AF.Identity, bias=mv[:, tt, 3:4], scale=mv[:, tt, 2:3]),
                     reads=[f"mv{tt}"], writes=[outk])
                S.op("dve", lambda e: e.tensor_tensor(acc[:, tt, :], acc[:, tt, :], lnp[:, 0, :], ALU.mult), reads=[outk, "lnp0"], writes=[outk])
                S.op("dve", lambda e: e.tensor_tensor(acc[:, tt, :], acc[:, tt, :], lnp[:, 1, :], ALU.add), reads=[outk, "lnp1"], writes=[outk])

            for tt in range(8):
                layernorm(tt, f"x1_{tt}")
                for b4 in range(4):
                    b = nextbank()
                    for jj in range(4):
                        kt = 4 * b4 + jj
                        S.op("pe", lambda e, b=b, jj=jj, kt=kt, tt=tt: e.transpose(
                            ps[b][:, jj * 128:(jj + 1) * 128], acc[:, tt, kt * 128:(kt + 1) * 128], ident[:, :]),
                            reads=[f"x1_{tt}"], writes=[f"ps{b}"], inc=(jj == 3))
                    S.op("act", lambda e, b=b, b4=b4, tt=tt: e.copy(x1T[:, 4 * b4:4 * b4 + 4, 128 * tt:128 * tt + 128],
                                                                   ps[b][:, :].rearrange("p (j t) -> p j t", j=4)),
                         reads=[f"ps{b}"], writes=[f"x1T{tt}_{b4}"])
            if dbg and blk == 0:
                S.dma("sp", "dbgx1", lambda e: e.dma_start(out=dbg_x1.rearrange("(t p) c -> p t c", p=128), in_=acc[:]),
                      reads=[f"x1_{tt}" for tt in range(8)])
            S.dma("sp", "lnp0", lambda e: e.dma_start(out=lnp[:, 0, :], in_=b_dn_d.partition_broadcast(128)),
                  reads=[f"x1_{tt}" for tt in range(8)], writes=["lnp0"])
            for tt in range(8):
                S.op("dve", lambda e, tt=tt: e.scalar_tensor_tensor(acc[:, tt, :], acc[:, tt, :], ALPHA, lnp[:, 0, :], ALU.mult, ALU.add),
                     reads=[f"x1_{tt}", "lnp0"], writes=[f"x1_{tt}"])
            S.barrier()

            if True:
                FC = 8
                hTb = [sbt("hT", [128, FC, NB], BF16), A.view(lnp_off, [128, FC, NB], BF16)]
                wu = [sbt(f"wu{i}", [128, 16, 256], BF16) for i in range(2)]
                wd = sbt("wd", [128, FC, D], BF16)
                rl = [sbt(f"rl{i}", [128, 512], F32) for i in range(3)]
                wuc = 0
                rc = 0
                NFC = DFF // 128 // FC
                NG = NFC * (FC // 2)

                def load_wu(g):
                    c0 = g * 256
                    S.dma("pool", f"wu{g%2}", lambda e: e.dma_start(out=wu[g % 2][:], in_=w_up_v[:, :, c0:c0 + 256]), writes=[f"wu{g%2}"])

                def load_wd(fc):
                    for f2 in range(FC // 2):
                        f0 = fc * FC + f2 * 2
                        S.dma("pool", f"wd{f2}", lambda e, f2=f2, f0=f0: e.dma_start(out=wd[:, 2 * f2:2 * f2 + 2, :], in_=w_dn_v[:, f0:f0 + 2, :]), writes=[f"wd{f2}"])
                load_wu(0)
                for fc in range(NFC):
                    hT = hTb[fc % 2]
                    hk = "hT%d_" % (fc % 2)
                    for f2 in range(FC // 2):
                        g = fc * (FC // 2) + f2
                        i = g % 2
                        if g + 1 < NG:
                            load_wu(g + 1)
                        if f2 == 0:
                            load_wd(fc)
                        for f in range(2):
                            fl = f2 * 2 + f
                            ft = fc * FC + fl
                            for t2 in range(2):
                                b = nextbank()
                                for kt in range(16):
                                    S.op("pe", lambda e, b=b, kt=kt, i=i, f=f, t2=t2: e.matmul(
                                        ps[b][:, :], wu[i][:, kt, f * 128:(f + 1) * 128], x1T[:, kt, t2 * 512:(t2 + 1) * 512], start=(kt == 0), stop=(kt == 15)),
                                        reads=[f"wu{i}"], writes=[f"ps{b}"], inc=(kt == 15))
                                r_ = rl[rc % 3]
                                rk = f"rl{rc%3}"
                                rc += 1
                                S.op("act", lambda e, b=b, r_=r_, ft=ft: e.activation(r_[:], ps[b][:, :], AF.Relu, bias=bup[:, ft:ft + 1]),
                                     reads=[f"ps{b}"], writes=[rk])
                                S.op("act", lambda e, r_=r_, fl=fl, t2=t2: e.activation(hT[:, fl, t2 * 512:(t2 + 1) * 512], r_[:], AF.Square),
                                     reads=[rk], writes=[hk + f"{fl}_{t2}"])
                    for tt in range(8):
                        for cb in range(4):
                            b = nextbank()
                            for fl in range(FC):
                                S.op("pe", lambda e, b=b, fl=fl, tt=tt, cb=cb: e.matmul(
                                    ps[b][:, :], hT[:, fl, 128 * tt:128 * tt + 128], wd[:, fl, 512 * cb:512 * cb + 512], start=(fl == 0), stop=(fl == FC - 1)),
                                    reads=[f"wd{fl//2}", hk + f"{fl}_{(128*tt)//512}"], writes=[f"ps{b}"], inc=(fl == FC - 1))
                            S.op("dve", lambda e, b=b, tt=tt, cb=cb: e.tensor_tensor(
                                acc[:, tt, 512 * cb:512 * cb + 512], acc[:, tt, 512 * cb:512 * cb + 512], ps[b][:, :], ALU.add),
                                reads=[f"ps{b}", f"accf{tt}_{cb}"], writes=[f"accf{tt}_{cb}"])
                hk1 = [f"hT1_{fl}_{t2}" for fl in range(FC) for t2 in range(2)]
                S.dma("sp", "lnp0", lambda e: e.dma_start(out=lnp[:, 0, :], in_=ln2_g_d.partition_broadcast(128)), writes=["lnp0"] + hk1)
                S.dma("sp", "lnp1", lambda e: e.dma_start(out=lnp[:, 1, :], in_=ln2_b_d.partition_broadcast(128)), writes=["lnp1"] + hk1)
                for tt in range(8):
                    for cb in range(4):
                        S.op("dve", lambda e, tt=tt, cb=cb: e.bn_stats(stats[:, tt, cb, :], acc[:, tt, 512 * cb:512 * cb + 512]),
                             reads=[f"accf{tt}_{cb}"], writes=[f"stats{tt}_{cb}"])
                    layernorm(tt, f"x2_{tt}")
                    S.dma("sp", f"ost{tt%2}", lambda e, tt=tt: e.dma_start(out=out_d[t0 + 128 * tt:t0 + 128 * tt + 128, :], in_=acc[:, tt, :]),
                          reads=[f"x2_{tt}"], writes=[f"outd{tt}"])
            end_phase(mF)
        S.barrier()
    return nc


_CACHE = {}


def _prep_inputs(inputs, small=False):
    x = np.ascontiguousarray(np.asarray(inputs["x"], dtype=np.float32))
    names = ["w_in", "conv_w", "conv_b", "rg_wa", "rg_ba", "rg_wx", "rg_bx", "rg_lambda", "w_a_out",
             "ssm_a_re", "ssm_a_im", "ssm_log_dt", "ssm_b_re", "ssm_b_im", "ssm_c_re", "ssm_c_im", "ssm_d",
             "glu_w", "glu_v", "w_out", "ln1_g", "ln1_b", "mlp_w_up", "mlp_b_up", "mlp_w_down", "mlp_b_down",
             "ln2_g", "ln2_b"]
    shared = {n: np.ascontiguousarray(np.asarray(inputs[n], dtype=np.float32)[0]) for n in names}
    in_maps = []
    for r in range(NCORE):
        b, k = r // 4, r % 4
        xs = np.zeros((4 * TOK + HALO, D), np.float32)
        n_real = TOK * (k + 1)
        xs[4 * TOK + HALO - n_real:] = x[b, 0:n_real]
        segm = np.ones((128, 4), np.float32)
        segm[:, 3 - k] = 0.0
        m = {"x": xs, "segm": segm}
        m.update(shared)
        if small:
            for n in ("w_a_out", "glu_w", "glu_v", "w_out", "mlp_w_up", "mlp_w_down"):
                m[n] = np.zeros((128, 128), np.float32)
        in_maps.append(m)
    return in_maps


def kernel(**inputs):
    if "nc" not in _CACHE:
        _CACHE["nc"] = build()
    nc = _CACHE["nc"]
    in_maps = _prep_inputs(inputs)
    res = run_bass_kernel_spmd(nc, in_maps, core_ids=list(range(NCORE)))
    out = np.empty((2, 4 * TOK, D), np.float32)
    for r in range(NCORE):
        b, k = r // 4, r % 4
        out[b, TOK * k:TOK * (k + 1)] = res.results[r]["out"]
    return out
```

```python
import numpy as np
from contextlib import ExitStack
import concourse.bass as bass
import concourse.mybir as mybir
from concourse.bass_utils import run_bass_kernel_spmd

F32 = mybir.dt.float32
BF16 = mybir.dt.bfloat16
I32 = mybir.dt.int32
AF = mybir.ActivationFunctionType
ALU = mybir.AluOpType

NCORE = 8
TOK = 2048
HALO = 3
D = 2048
DIN = 9216
DFF = 8192
ALPHA = 2.0 ** 0.25
EPS = 1e-5
TWO_PI = 6.283185307179586
PI = 3.141592653589793


class Sched:
    ENGS = ("pe", "act", "dve", "pool", "sp")

    def __init__(self, nc, stack):
        self.nc = nc
        self.stack = stack
        self.eng = {"pe": nc.tensor, "act": nc.scalar, "dve": nc.vector, "pool": nc.gpsimd, "sp": nc.sync}
        self.sem = {e: stack.enter_context(nc.semaphore("s_" + e)) for e in self.ENGS}
        self.cnt = {e: 0 for e in self.ENGS}
        self.waited = {e: {} for e in self.ENGS}
        self.last_w = {}
        self.readers = {}
        self.dsem = {}
        self.dcnt = {}

    def _h(self, s):
        return self.sem[s[1]] if s[0] == "e" else self.dsem[s[1]]

    def _wait(self, eng, s, v, raw=False):
        if s == ("e", eng) and (eng == "pe" or not raw):
            return
        if self.waited[eng].get(s, 0) >= v:
            return
        self.waited[eng][s] = v
        self.eng[eng].wait_ge(self._h(s), v)

    def _deps(self, eng, reads, writes):
        need = {}
        own = ("e", eng)

        def add(s, v, raw):
            if s == own and not raw:
                return
            if v > need.get(s, 0):
                need[s] = v
        for b in reads:
            w = self.last_w.get(b)
            if w is not None:
                add(w[0], w[1], True)
        for b in writes:
            w = self.last_w.get(b)
            if w is not None:
                add(w[0], w[1], False)
            for s, v in self.readers.get(b, {}).items():
                add(s, v, False)
        for s, v in need.items():
            self._wait(eng, s, v, raw=True)

    def _mark(self, tok, reads, writes):
        for b in writes:
            self.last_w[b] = tok
            self.readers[b] = {}
        for b in reads:
            d = self.readers.setdefault(b, {})
            if tok[1] > d.get(tok[0], 0):
                d[tok[0]] = tok[1]

    def op(self, eng, fn, reads=(), writes=(), inc=True):
        self._deps(eng, reads, writes)
        ins = fn(self.eng[eng])
        if inc:
            self.cnt[eng] += 1
            ins.then_inc(self.sem[eng], 1)
            tok = (("e", eng), self.cnt[eng])
        else:
            tok = (("e", eng), self.cnt[eng] + 1)
        self._mark(tok, reads, writes)
        return tok

    def dma(self, eng, key, fn, reads=(), writes=(), incv=16):
        if key not in self.dsem:
            self.dsem[key] = self.stack.enter_context(self.nc.semaphore("d_" + str(key)))
            self.dcnt[key] = 0
        self._deps(eng, reads, writes)
        self.dcnt[key] += incv
        fn(self.eng[eng]).then_inc(self.dsem[key], incv)
        tok = (("d", key), self.dcnt[key])
        self._mark(tok, reads, writes)
        return tok

    def wait_tok(self, eng, tok):
        self._wait(eng, tok[0], tok[1])

    def barrier(self):
        for e in self.ENGS:
            for e2 in self.ENGS:
                if e2 != e and self.cnt[e2] > 0:
                    self._wait(e, ("e", e2), self.cnt[e2])
            for k, v in self.dcnt.items():
                self._wait(e, ("d", k), v)
        self.last_w.clear()
        self.readers.clear()


_DTSZ = {F32: 4, I32: 4, BF16: 2}
SB_BYTES = 206 * 1024


class Arena:
    def __init__(self, handle, size):
        self.h = handle
        self.lo = 0
        self.hi = size

    def alloc(self, shape, dt, top=False):
        n = 1
        for d in shape[1:]:
            n *= d
        nb = (n * _DTSZ[dt] + 63) // 64 * 64
        if top:
            self.hi -= nb
            off = self.hi
        else:
            off = self.lo
            self.lo += nb
        assert self.lo <= self.hi, ("SBUF arena overflow", self.lo, self.hi)
        v = self.h[:, off:off + n * _DTSZ[dt]].bitcast(dt)
        names = "abcdefg"[:len(shape) - 1]
        if len(shape) > 2:
            v = v.rearrange("p (%s) -> p %s" % (" ".join(names), " ".join(names)),
                            **{k: d for k, d in zip(names[:-1], shape[1:-1])})
        if shape[0] < 128:
            v = v[0:shape[0]]
        return v

    def view(self, off, shape, dt):
        n = 1
        for d in shape[1:]:
            n *= d
        v = self.h[:, off:off + n * _DTSZ[dt]].bitcast(dt)
        names = "abcdefg"[:len(shape) - 1]
        if len(shape) > 2:
            v = v.rearrange("p (%s) -> p %s" % (" ".join(names), " ".join(names)),
                            **{k: d for k, d in zip(names[:-1], shape[1:-1])})
        return v

    def mark(self):
        return (self.lo, self.hi)

    def release(self, m):
        self.lo, self.hi = m


def build(dbg=False, stop_after=None):
    nc = bass.Bass("TRN2", target_bir_lowering=False)

    small = stop_after in ("p0", "a1", "a3", "a4")
    BIG = ("w_a_out", "glu_w", "glu_v", "w_out", "mlp_w_up", "mlp_w_down")

    def din(name, shape):
        if small and name in BIG:
            shape = [128, 128]
        return nc.dram_tensor(name, list(shape), F32, kind="ExternalInput").ap()

    def dscr(name, shape, dt):
        if dbg:
            return nc.dram_tensor(name, list(shape), dt, kind="ExternalOutput").ap()
        return nc.dram_tensor(name, list(shape), dt).ap()

    x_d = din("x", [4 * TOK + HALO, D])
    segm_d = din("segm", [128, 4])
    w_in_d = din("w_in", [D, DIN])
    conv_w_d = din("conv_w", [4, D])
    conv_b_d = din("conv_b", [D])
    rg_wa_d = din("rg_wa", [16, 128, 128])
    rg_ba_d = din("rg_ba", [16, 128])
    rg_wx_d = din("rg_wx", [16, 128, 128])
    rg_bx_d = din("rg_bx", [16, 128])
    rg_lam_d = din("rg_lambda", [D])
    w_a_d = din("w_a_out", [D, D])
    a_re_d = din("ssm_a_re", [64, 64])
    a_im_d = din("ssm_a_im", [64, 64])
    ldt_d = din("ssm_log_dt", [64])
    b_re_d = din("ssm_b_re", [64, 64, 16])
    b_im_d = din("ssm_b_im", [64, 64, 16])
    c_re_d = din("ssm_c_re", [64, 16, 64])
    c_im_d = din("ssm_c_im", [64, 16, 64])
    ssm_d_d = din("ssm_d", [64, 16])
    glu_w_d = din("glu_w", [1024, D])
    glu_v_d = din("glu_v", [1024, D])
    w_out_d = din("w_out", [D, D])
    ln1_g_d = din("ln1_g", [D])
    ln1_b_d = din("ln1_b", [D])
    w_up_d = din("mlp_w_up", [D, DFF])
    b_up_d = din("mlp_b_up", [DFF])
    w_dn_d = din("mlp_w_down", [DFF, D])
    b_dn_d = din("mlp_b_down", [D])
    ln2_g_d = din("ln2_g", [D])
    ln2_b_d = din("ln2_b", [D])
    out_d = nc.dram_tensor("out", [TOK, D], F32, kind="ExternalOutput").ap()

    ab_d = dscr("ab_d", [2, 16, 128, TOK], F32)
    xT_d = dscr("xT_d", [128, 16, TOK + HALO], BF16)
    yS_d = dscr("yS_d", [128, 8, TOK], BF16)
    mix_d = dscr("mix_d", [128, 16, TOK], BF16)
    W1_d = dscr("W1_d", [8, 128, 8192], BF16)
    W3_d = dscr("W3_d", [8, 128, 8192], BF16)
    KT_d = dscr("KT_d", [8, 128, 1024], BF16)
    S_d3 = nc.dram_tensor("S_d3", [3, 128, 16384], F32).ap()
    ccin_d = nc.dram_tensor("ccin_d", [128, 96], F32).ap()
    ccout_d = nc.dram_tensor("ccout_d", [NCORE * 128, 96], F32).ap()
    if dbg:
        dbg_small = nc.dram_tensor("dbg_small", [128, 256], F32, kind="ExternalOutput").ap()
        dbg_x1 = nc.dram_tensor("dbg_x1", [1024, D], F32, kind="ExternalOutput").ap()
        dbg_sc = nc.dram_tensor("dbg_sc", [128, 24 * 32], F32, kind="ExternalOutput").ap()
        dbg_lp = nc.dram_tensor("dbg_lp", [128, 9 * 4 * 32], F32, kind="ExternalOutput").ap()
        dbg_p3 = nc.dram_tensor("dbg_p3", [128, 104], F32, kind="ExternalOutput").ap()
        dbg_p1 = nc.dram_tensor("dbg_p1", [128, 128], F32, kind="ExternalOutput").ap()

    w_in_v = w_in_d.rearrange("(kt p) c -> p kt c", p=128)
    if not small:
        w_a_v = w_a_d.rearrange("(kt p) c -> p kt c", p=128)
        glu_w_v = glu_w_d.rearrange("(kt p) c -> p kt c", p=128)
        glu_v_v = glu_v_d.rearrange("(kt p) c -> p kt c", p=128)
        w_out_v = w_out_d.rearrange("(kt p) c -> p kt c", p=128)
        w_up_v = w_up_d.rearrange("(kt p) c -> p kt c", p=128)
        w_dn_v = w_dn_d.rearrange("(ft p) c -> p ft c", p=128)

    with ExitStack() as top:
        S = Sched(nc, top)
        ccsem = top.enter_context(nc.semaphore("ccsem"))
        arena_t = top.enter_context(nc.sbuf_tensor("arena", [128, SB_BYTES], mybir.dt.uint8))
        A = Arena(arena_t, SB_BYTES)

        def sbt(name, shape, dt, top_=False):
            return A.alloc(list(shape), dt, top=top_)

        def end_phase(m):
            S.barrier()
            A.release(m)

        ps = [top.enter_context(nc.psum_tensor(f"ps{i}", [128, 512], F32)) for i in range(8)]
        bank_ctr = [0]

        def nextbank():
            b = bank_ctr[0] % 8
            bank_ctr[0] += 1
            return b

        ev_ctr = [0]

        def ev_eng():
            ev_ctr[0] += 1
            return "act" if ev_ctr[0] % 2 == 0 else "dve"

        def copy_op(eng, out, in_):
            if eng == "act":
                return lambda e: e.copy(out, in_)
            return lambda e: e.tensor_copy(out, in_)

        ident = sbt("ident", [128, 128], F32)
        P1 = sbt("P1", [128, 128], F32)
        bup = sbt("bup", [128, 64], F32)
        rgc = sbt("rgc", [128, 8, 16], F32)
        wa_sb = sbt("wa_sb", [128, 16, 128], BF16)
        wx_sb = sbt("wx_sb", [128, 16, 128], BF16)
        Ecur = sbt("Ecur", [128, 16], F32)
        sumth = sbt("sumth", [128, 16, 4], F32)
        hcar = sbt("hcar", [128, 16], F32)
        Hin = sbt("Hin", [128, 2, 32], F32)
        LL8 = sbt("LL8", [128, 4, 32], F32)
        D2k = sbt("D2k", [128, 2, 32], F32)
        ccin = sbt("ccin", [128, 96], F32)
        segm = sbt("segm", [128, 4], F32)
        Hs = sbt("Hs", [128, 2, 32], F32)
        Xp = sbt("Xp", [128, 2, 32], F32)
        Yp = sbt("Yp", [128, 2, 32], F32)
        Xd = sbt("Xd", [128, 2, 32], F32)
        Yd = sbt("Yd", [128, 2, 32], F32)
        Ha = sbt("Ha", [128, 2, 32], F32)
        H3 = sbt("H3", [128, 3, 2, 32], F32)
        X3 = sbt("X3", [128, 3, 2, 32], F32)
        Y3 = sbt("Y3", [128, 3, 2, 32], F32)
        D2kL = sbt("D2kL", [128, 4, 32], F32)
        stats = sbt("stats", [128, 8, 4, 6], F32)
        mv = sbt("mv", [128, 8, 4], F32)

        S.op("pool", lambda e: e.memset(ident[:], 1.0), writes=["ident"])
        S.op("pool", lambda e: e.affine_select(out=ident[:], in_=ident[:], pattern=[[-1, 128]],
                                               compare_op=ALU.is_equal, fill=0.0, base=0, channel_multiplier=1),
             reads=["ident"], writes=["ident"])
        S.dma("sp", "segm", lambda e: e.dma_start(out=segm[:], in_=segm_d), writes=["segm"])
        S.dma("pool", "wa_sb", lambda e: e.dma_start(out=wa_sb[:], in_=rg_wa_d.rearrange("h i j -> i h j")), writes=["wa_sb"])
        S.dma("pool", "wx_sb", lambda e: e.dma_start(out=wx_sb[:], in_=rg_wx_d.rearrange("h i j -> i h j")), writes=["wx_sb"])

        m0 = A.mark()
        if True:
            st1 = sbt("st1", [128, 128], F32)
            st2 = sbt("st2", [64, 128], F32)
            st3 = sbt("st3", [104, 128], F32)
            ld2 = sbt("ld2", [32, 2], F32)
            P3 = sbt("P3", [128, 104], F32)
            S.dma("sp", "st1a", lambda e: e.dma_start(out=st1[0:64, :], in_=conv_w_d.rearrange("k (h p) -> (k h) p", p=128)), writes=["st1a"])
            S.dma("sp", "st1b", lambda e: e.dma_start(out=st1[64:80, :], in_=conv_b_d.rearrange("(h p) -> h p", p=128)), writes=["st1b"])
            S.dma("sp", "st1c", lambda e: e.dma_start(out=st1[80:96, :], in_=rg_ba_d), writes=["st1c"])
            S.dma("sp", "st1d", lambda e: e.dma_start(out=st1[96:112, :], in_=rg_bx_d), writes=["st1d"])
            S.dma("sp", "st1e", lambda e: e.dma_start(out=st1[112:128, :], in_=rg_lam_d.rearrange("(h p) -> h p", p=128)), writes=["st1e"])
            S.dma("sp", "st2", lambda e: e.dma_start(out=st2[:], in_=b_up_d.rearrange("(f p) -> f p", p=128)), writes=["st2"])
            S.dma("sp", "ld2", lambda e: e.dma_start(out=ld2[:], in_=ldt_d.rearrange("(j g) -> j g", g=2)), writes=["ld2"])
            S.dma("sp", "st3b", lambda e: e.dma_start(out=st3[32:64, :], in_=a_re_d.rearrange("(j g) p -> j (g p)", g=2)), writes=["st3b"])
            S.dma("sp", "st3c", lambda e: e.dma_start(out=st3[64:96, :], in_=a_im_d.rearrange("(j g) p -> j (g p)", g=2)), writes=["st3c"])
            S.dma("sp", "st3d", lambda e: e.dma_start(out=st3[96:104, :], in_=ssm_d_d.rearrange("(kt g) h -> kt (g h)", g=8)), writes=["st3d"])
            S.op("dve", lambda e: e.tensor_copy(st3[0:32, :].rearrange("j (g p) -> j g p", g=2),
                                                ld2[:, :].unsqueeze(2).to_broadcast([32, 2, 64])),
                 reads=["ld2"], writes=["st3a"])
            b = nextbank()
            S.op("pe", lambda e: e.transpose(ps[b][:, 0:128], st1[:, :], ident[:, :]),
                 reads=["st1a", "st1b", "st1c", "st1d", "st1e", "ident"], writes=[f"ps{b}"])
            S.op("dve", lambda e: e.tensor_copy(P1[:], ps[b][:, 0:128]), reads=[f"ps{b}"], writes=["P1"])
            b = nextbank()
            S.op("pe", lambda e: e.transpose(ps[b][:, 0:64], st2[:, :], ident[0:64, 0:64]),
                 reads=["st2", "ident"], writes=[f"ps{b}"])
            S.op("dve", lambda e: e.tensor_copy(bup[:], ps[b][:, 0:64]), reads=[f"ps{b}"], writes=["bup"])
            b = nextbank()
            S.op("pe", lambda e: e.transpose(ps[b][:, 0:104], st3[:, :], ident[0:104, 0:104]),
                 reads=["st3a", "st3b", "st3c", "st3d", "ident"], writes=[f"ps{b}"])
            S.op("dve", lambda e: e.tensor_copy(P3[:], ps[b][:, 0:104]), reads=[f"ps{b}"], writes=["P3"])

            ba_v = P1[:, 80:96]
            bx_v = P1[:, 96:112]
            lam_v = P1[:, 112:128]
            S.op("dve", lambda e: e.tensor_scalar(rgc[:, 0, :], ba_v, 0.5, None, ALU.mult), reads=["P1"], writes=["rgc0"])
            S.op("dve", lambda e: e.tensor_scalar(rgc[:, 1, :], bx_v, 0.5, None, ALU.mult), reads=["P1"], writes=["rgc1"])
            S.op("act", lambda e: e.activation(rgc[:, 5, :], lam_v, AF.Exp, scale=-1.0), reads=["P1"], writes=["rgc5"])
            S.op("act", lambda e: e.activation(rgc[:, 5, :], rgc[:, 5, :], AF.Ln, bias=1.0), reads=["rgc5"], writes=["rgc5"])
            S.op("dve", lambda e: e.tensor_scalar(rgc[:, 2, :], rgc[:, 5, :], -8.0, None, ALU.mult), reads=["rgc5"], writes=["rgc2"])
            S.op("dve", lambda e: e.tensor_scalar(rgc[:, 3, :], rgc[:, 5, :], -4.0, None, ALU.mult), reads=["rgc5"], writes=["rgc3"])
            S.op("dve", lambda e: e.tensor_scalar(rgc[:, 4, :], rgc[:, 5, :], 4.0, None, ALU.mult), reads=["rgc5"], writes=["rgc4"])

            sc = sbt("sc", [128, 24, 32], F32)
            Lp = sbt("Lp", [128, 9, 4, 32], F32)
            isc = sbt("isc", [128, 32], I32)

            def V(i):
                return sc[:, i, :]

            def dv(fn):
                S.op("dve", fn, reads=["sc", "P3"], writes=["sc"])

            def av(fn):
                S.op("act", fn, reads=["sc", "P3"], writes=["sc"])
            ldt_v, are_v, aim_v = P3[:, 0:32], P3[:, 32:64], P3[:, 64:96]
            av(lambda e: e.activation(V(0), ldt_v, AF.Exp))
            dv(lambda e: e.tensor_scalar(V(1), are_v, -1e-4, None, ALU.min))
            dv(lambda e: e.tensor_tensor(V(2), V(1), V(0), ALU.mult))
            av(lambda e: e.activation(V(3), V(2), AF.Exp))
            dv(lambda e: e.tensor_tensor(V(4), aim_v, V(0), ALU.mult))

            def reduced_sin(dst, shift):
                dv(lambda e: e.tensor_scalar(V(5), V(4), shift, None, ALU.add))
                dv(lambda e: e.tensor_scalar(V(6), V(5), 1.0 / TWO_PI, None, ALU.mult))
                dv(lambda e: e.tensor_copy(isc[:], V(6)))
                dv(lambda e: e.tensor_copy(V(6), isc[:]))
                dv(lambda e: e.scalar_tensor_tensor(V(5), V(6), -TWO_PI, V(5), ALU.mult, ALU.add))
                dv(lambda e: e.tensor_scalar(V(7), V(5), PI, None, ALU.is_gt))
                dv(lambda e: e.scalar_tensor_tensor(V(5), V(7), -TWO_PI, V(5), ALU.mult, ALU.add))
                dv(lambda e: e.tensor_scalar(V(7), V(5), -PI, None, ALU.is_lt))
                dv(lambda e: e.scalar_tensor_tensor(V(5), V(7), TWO_PI, V(5), ALU.mult, ALU.add))
                dv(lambda e: e.tensor_scalar(V(5), V(5), PI, -PI, ALU.min, ALU.max))
                av(lambda e: e.activation(dst, V(5), AF.Sin))
            reduced_sin(V(9), 0.0)
            reduced_sin(V(10), PI / 2)
            dv(lambda e: e.tensor_tensor(V(11), V(3), V(10), ALU.mult))
            dv(lambda e: e.tensor_tensor(V(12), V(3), V(9), ALU.mult))
            dv(lambda e: e.tensor_tensor(V(13), V(1), V(1), ALU.mult))
            dv(lambda e: e.tensor_tensor(V(5), aim_v, aim_v, ALU.mult))
            dv(lambda e: e.tensor_tensor(V(13), V(13), V(5), ALU.add))
            dv(lambda e: e.reciprocal(V(13), V(13)))
            dv(lambda e: e.tensor_scalar(V(14), V(11), -1.0, None, ALU.add))
            dv(lambda e: e.tensor_tensor(V(5), V(14), V(1), ALU.mult))
            dv(lambda e: e.tensor_tensor(V(6), V(12), aim_v, ALU.mult))
            dv(lambda e: e.tensor_tensor(V(5), V(5), V(6), ALU.add))
            dv(lambda e: e.tensor_tensor(V(15), V(5), V(13), ALU.mult))
            dv(lambda e: e.tensor_tensor(V(5), V(12), V(1), ALU.mult))
            dv(lambda e: e.tensor_tensor(V(6), V(14), aim_v, ALU.mult))
            dv(lambda e: e.tensor_tensor(V(5), V(5), V(6), ALU.subtract))
            dv(lambda e: e.tensor_tensor(V(16), V(5), V(13), ALU.mult))

            def lp(fn):
                S.op("dve", fn, reads=["sc", "Lp"], writes=["Lp", "sc"])
            lp(lambda e: e.memset(Lp[:, 0, 0, :], 1.0))
            lp(lambda e: e.memset(Lp[:, 0, 1, :], 0.0))
            lp(lambda e: e.tensor_copy(Lp[:, 1, 0, :], V(11)))
            lp(lambda e: e.tensor_copy(Lp[:, 1, 1, :], V(12)))

            def cmul(o_re, o_im, a_re_, a_im_, b_re_, b_im_, t1, t2, t3):
                lp(lambda e: e.tensor_tensor(t1, a_re_, b_re_, ALU.mult))
                lp(lambda e: e.tensor_tensor(t2, a_im_, b_im_, ALU.mult))
                lp(lambda e: e.tensor_tensor(t1, t1, t2, ALU.subtract))
                lp(lambda e: e.tensor_tensor(t2, a_re_, b_im_, ALU.mult))
                lp(lambda e: e.tensor_tensor(t3, a_im_, b_re_, ALU.mult))
                lp(lambda e: e.tensor_tensor(o_im, t3, t2, ALU.add))
                lp(lambda e: e.tensor_copy(o_re, t1))
            for n in range(1, 8):
                cmul(Lp[:, n + 1, 0, :], Lp[:, n + 1, 1, :], Lp[:, n, 0, :], Lp[:, n, 1, :],
                     Lp[:, 1, 0, :], Lp[:, 1, 1, :], V(17), V(18), V(21))
            for n in range(9):
                lp(lambda e, n=n: e.tensor_scalar(Lp[:, n, 2:4, :], Lp[:, n, 0:2, :], -1.0, None, ALU.mult))
            S.op("dve", lambda e: e.tensor_copy(LL8[:, 0, :], Lp[:, 8, 0, :]), reads=["Lp"], writes=["LL8"])
            S.op("dve", lambda e: e.tensor_copy(LL8[:, 1, :], Lp[:, 8, 0, :]), reads=["Lp"], writes=["LL8"])
            S.op("dve", lambda e: e.tensor_copy(LL8[:, 2, :], Lp[:, 8, 3, :]), reads=["Lp"], writes=["LL8"])
            S.op("dve", lambda e: e.tensor_copy(LL8[:, 3, :], Lp[:, 8, 1, :]), reads=["Lp"], writes=["LL8"])
            lp(lambda e: e.tensor_copy(V(19), Lp[:, 8, 0, :]))
            lp(lambda e: e.tensor_copy(V(20), Lp[:, 8, 1, :]))
            for _ in range(8):
                cmul(V(19), V(20), V(19), V(20), V(19), V(20), V(17), V(18), V(21))
            S.op("dve", lambda e: e.tensor_copy(D2kL[:, 0, :], V(19)), reads=["Lp", "sc"], writes=["D2kL"])
            S.op("dve", lambda e: e.tensor_copy(D2kL[:, 1, :], V(19)), reads=["Lp", "sc"], writes=["D2kL"])
            S.op("dve", lambda e: e.tensor_scalar(D2kL[:, 2, :], V(20), -1.0, None, ALU.mult), reads=["Lp", "sc"], writes=["D2kL"])
            S.op("dve", lambda e: e.tensor_copy(D2kL[:, 3, :], V(20)), reads=["Lp", "sc"], writes=["D2kL"])
            S.op("dve", lambda e: e.tensor_copy(D2k[:, 0, :], V(19)), reads=["Lp", "sc"], writes=["D2k"])
            S.op("dve", lambda e: e.tensor_copy(D2k[:, 1, :], V(20)), reads=["Lp", "sc"], writes=["D2k"])

            if dbg:
                S.dma("sp", "dbgsc", lambda e: e.dma_start(out=dbg_sc, in_=sc[:].rearrange("p a b -> p (a b)")), reads=["sc", "Lp"])
                S.dma("sp", "dbglp", lambda e: e.dma_start(out=dbg_lp, in_=Lp[:].rearrange("p a b c -> p (a b c)")), reads=["sc", "Lp"])
                S.dma("sp", "dbgp3", lambda e: e.dma_start(out=dbg_p3, in_=P3[:]), reads=["P3"])
                S.dma("sp", "dbgp1", lambda e: e.dma_start(out=dbg_p1, in_=P1[:]), reads=["P1"])
            Bn = sbt("Bn", [128, 2, 32, 16], F32)
            Bb = sbt("Bb", [128, 2, 32, 16], F32)
            Vn = sbt("Vn", [128, 8, 2, 32, 16], F32)
            tb = sbt("tb", [128, 32, 16], F32)
            S.dma("sp", "Bn0", lambda e: e.dma_start(out=Bn[:, 0, :, :], in_=b_re_d.rearrange("(j g) p h -> (g p) j h", g=2)), writes=["Bn0"])
            S.dma("sp", "Bn1", lambda e: e.dma_start(out=Bn[:, 1, :, :], in_=b_im_d.rearrange("(j g) p h -> (g p) j h", g=2)), writes=["Bn1"])

            def bc(v):
                return v.unsqueeze(2).to_broadcast([128, 32, 16])

            def bb(fn):
                S.op("dve", fn, reads=["sc", "Lp", "Bn0", "Bn1", "Bb", "tb"], writes=["Bb", "tb"])
            bb(lambda e: e.tensor_tensor(Bb[:, 0], Bn[:, 0], bc(V(15)), ALU.mult))
            bb(lambda e: e.tensor_tensor(tb[:], Bn[:, 1], bc(V(16)), ALU.mult))
            bb(lambda e: e.tensor_tensor(Bb[:, 0], Bb[:, 0], tb[:], ALU.subtract))
            bb(lambda e: e.tensor_tensor(Bb[:, 1], Bn[:, 1], bc(V(15)), ALU.mult))
            bb(lambda e: e.tensor_tensor(tb[:], Bn[:, 0], bc(V(16)), ALU.mult))
            bb(lambda e: e.tensor_tensor(Bb[:, 1], Bb[:, 1], tb[:], ALU.add))

            def vv(fn):
                S.op("dve", fn, reads=["Bb", "Lp", "Vn", "tb"], writes=["Vn", "tb"])
            for n in range(8):
                vv(lambda e, n=n: e.tensor_tensor(Vn[:, n, 0], Bb[:, 0], bc(Lp[:, n, 0, :]), ALU.mult))
                vv(lambda e, n=n: e.tensor_tensor(tb[:], Bb[:, 1], bc(Lp[:, n, 1, :]), ALU.mult))
                vv(lambda e, n=n: e.tensor_tensor(Vn[:, n, 0], Vn[:, n, 0], tb[:], ALU.subtract))
                vv(lambda e, n=n: e.tensor_tensor(Vn[:, n, 1], Bb[:, 1], bc(Lp[:, n, 0, :]), ALU.mult))
                vv(lambda e, n=n: e.tensor_tensor(tb[:], Bb[:, 0], bc(Lp[:, n, 1, :]), ALU.mult))
                vv(lambda e, n=n: e.tensor_tensor(Vn[:, n, 1], Vn[:, n, 1], tb[:], ALU.add))

            Cn = sbt("Cn", [128, 2, 8, 64], F32)
            par = sbt("par", [128, 2], F32)
            Xc = sbt("Xc", [128, 2, 128], F32)
            Yc = sbt("Yc", [128, 3, 8, 128], F32)
            S.dma("sp", "Cn0", lambda e: e.dma_start(out=Cn[:, 0, :, :], in_=c_re_d.rearrange("(kt g) h p -> (g h) kt p", g=8)), writes=["Cn0"])
            S.dma("sp", "Cn1", lambda e: e.dma_start(out=Cn[:, 1, :, :], in_=c_im_d.rearrange("(kt g) h p -> (g h) kt p", g=8)), writes=["Cn1"])
            Mp = sbt("Mp", [128, 128], F32)
            S.op("dve", lambda e: e.memset(Mp[:], 0.0), writes=["Mp"])
            for blk_ in range(4):
                S.op("dve", lambda e, blk_=blk_: e.memset(Mp[:, 32 * blk_ + 16:32 * blk_ + 32], 1.0), reads=["Mp"], writes=["Mp"])
            b = nextbank()
            S.op("pe", lambda e: e.transpose(ps[b][:, 0:128], Mp[:, :], ident[:, :]), reads=["Mp", "ident"], writes=[f"ps{b}"])
            S.op("dve", lambda e: e.tensor_copy(par[:, 0:1], ps[b][:, 0:1]), reads=[f"ps{b}"], writes=["par"])
            S.op("dve", lambda e: e.tensor_scalar(par[:, 1:2], par[:, 0:1], -1.0, 1.0, ALU.mult, ALU.add), reads=["par"], writes=["par"])
            for kt in range(8):
                for ri in range(2):
                    S.op("dve", lambda e, kt=kt, ri=ri: e.tensor_scalar(Xc[:, ri, 0:64], Cn[:, ri, kt, :], par[:, 1:2], None, ALU.mult),
                         reads=["Cn0", "Cn1", "par", "Xc%d" % ri], writes=["Xc%d" % ri])
                    S.op("dve", lambda e, kt=kt, ri=ri: e.tensor_scalar(Xc[:, ri, 64:128], Cn[:, ri, kt, :], par[:, 0:1], None, ALU.mult),
                         reads=["Cn0", "Cn1", "par", "Xc%d" % ri], writes=["Xc%d" % ri])
                    b = nextbank()
                    S.op("pe", lambda e, ri=ri, b=b: e.transpose(ps[b][:, 0:128], Xc[:, ri, :], ident[:, :]),
                         reads=["Xc%d" % ri, "ident"], writes=[f"ps{b}"])
                    S.op("act", lambda e, kt=kt, ri=ri, b=b: e.copy(Yc[:, ri, kt, :], ps[b][:, 0:128]),
                         reads=[f"ps{b}"], writes=["Yc"])
                    if ri == 1:
                        S.op("act", lambda e, kt=kt, b=b: e.mul(Yc[:, 2, kt, :], ps[b][:, 0:128], -1.0),
                             reads=[f"ps{b}"], writes=["Yc"])

            W3b = [sbt(f"W3b{i}", [128, 4, 8, 2, 128], BF16) for i in range(2)]
            W1b = [sbt(f"W1b{i}", [128, 4, 8, 2, 128], BF16) for i in range(2)]
            KTb = [sbt(f"KTb{i}", [128, 8, 128], BF16) for i in range(2)]
            Zf = [sbt(f"Zb{i}", [128, 1280], F32) for i in range(2)]
            Zb = [z[:, 0:1024].rearrange("p (q r c) -> p q r c", q=4, r=2) for z in Zf]
            Zd = [z[:, 0:1152].rearrange("p (q x) -> p q x", x=288)[:, :, 0:256].rearrange("p q (r c) -> p q r c", r=2) for z in Zf]
            w3t = sbt("w3t", [128, 32], F32)
            dI = sbt("dI", [128, 128], F32)
            for i in range(2):
                S.op("pool", lambda e, i=i: e.memset(W3b[i][:], 0.0), writes=[f"W3b{i}", f"W3b{i}a"])
                S.op("pool", lambda e, i=i: e.memset(Zf[i][:], 0.0), writes=[f"Zb{i}"])
            zc = 0
            for kt in range(8):
                w3 = W3b[kt % 2]
                w1 = W1b[kt % 2]
                ktb = KTb[kt % 2]
                k3, k1, kk = f"W3b{kt%2}", f"W1b{kt%2}", f"KTb{kt%2}"
                for q in range(4):
                    j = 4 * kt + q
                    cs = slice(32 * q, 32 * q + 32)
                    for tau in range(8):
                        n = tau + 1
                        S.op("dve", lambda e, n=n, j=j, cs=cs, w3=w3, q=q, tau=tau: e.tensor_scalar(
                            w3[:, q, tau, 0, cs], Yc[:, 0, kt, cs], Lp[:, n, 0, j:j + 1], None, ALU.mult), reads=["Yc", "Lp"], writes=[k3 + "a"])
                        S.op("dve", lambda e, n=n, j=j, cs=cs, w3=w3, q=q, tau=tau: e.tensor_scalar(
                            w3[:, q, tau, 1, cs], Yc[:, 0, kt, cs], Lp[:, n, 3, j:j + 1], None, ALU.mult), reads=["Yc", "Lp"], writes=[k3 + "a"])
                for q in range(4):
                    j = 4 * kt + q
                    cs = slice(32 * q, 32 * q + 32)
                    for tau in range(8):
                        n = tau + 1
                        S.op("dve", lambda e, n=n, j=j, cs=cs, w3=w3, q=q, tau=tau: e.scalar_tensor_tensor(
                            w3[:, q, tau, 0, cs], Yc[:, 1, kt, cs], Lp[:, n, 3, j:j + 1], w3[:, q, tau, 0, cs], ALU.mult, ALU.add),
                            reads=["Yc", "Lp", k3 + "a"], writes=[k3])
                        S.op("dve", lambda e, n=n, j=j, cs=cs, w3=w3, q=q, tau=tau: e.scalar_tensor_tensor(
                            w3[:, q, tau, 1, cs], Yc[:, 1, kt, cs], Lp[:, n, 2, j:j + 1], w3[:, q, tau, 1, cs], ALU.mult, ALU.add),
                            reads=["Yc", "Lp", k3 + "a"], writes=[k3])
                S.dma("sp", k3 + "st", lambda e, kt=kt, w3=w3: e.dma_start(out=W3_d[kt], in_=w3[:].rearrange("p a b c d -> p (a b c d)")),
                      reads=[k3, k3 + "a"], writes=[f"W3d{kt}"])
                S.op("dve", lambda e, kt=kt: e.tensor_scalar(dI[:], ident[:], P3[:, 96 + kt:97 + kt], None, ALU.mult),
                     reads=["ident", "P3", "dI"], writes=["dI"])
                for n in range(8):
                    zi = zc % 2
                    zb, zd = Zb[zi], Zd[zi]
                    kz = f"Zb{zi}"
                    zc += 1
                    S.op("pool", lambda e, zd=zd, n=n: e.tensor_copy(
                        zd[0:64, :, :, 0:16], Vn[0:64, n, :, 4 * kt:4 * kt + 4, :].rearrange("p r q h -> p q r h")),
                        reads=["Vn", kz], writes=[kz])
                    S.op("pool", lambda e, zd=zd, n=n: e.tensor_copy(
                        zd[64:128, :, :, 16:32], Vn[64:128, n, :, 4 * kt:4 * kt + 4, :].rearrange("p r q h -> p q r h")),
                        reads=["Vn", kz], writes=[kz])
                    for q in range(4):
                        for ri in range(2):
                            b = nextbank()
                            S.op("pe", lambda e, zb=zb, q=q, ri=ri, b=b: e.transpose(ps[b][:, 0:128], zb[:, q, ri, :], ident[:, :]),
                                 reads=[kz, "ident"], writes=[f"ps{b}"])
                            eng = ev_eng()
                            S.op(eng, copy_op(eng, w1[:, q, 7 - n, ri, :], ps[b][:, 0:128]), reads=[f"ps{b}"], writes=[k1])
                    b = nextbank()
                    for q in range(4):
                        cs = slice(32 * q, 32 * q + 32)
                        S.op("pe", lambda e, zb=zb, q=q, cs=cs, b=b: e.matmul(ps[b][:, cs], zb[:, q, 0, :], Yc[:, 0, kt, cs], start=True, stop=False),
                             reads=[kz, "Yc"], writes=[f"ps{b}"], inc=False)
                        S.op("pe", lambda e, zb=zb, q=q, cs=cs, b=b: e.matmul(ps[b][:, cs], zb[:, q, 1, :], Yc[:, 2, kt, cs], start=False, stop=True),
                             reads=[kz, "Yc"], writes=[f"ps{b}"], inc=(q == 3))
                    if n == 0:
                        S.op("dve", lambda e, ktb=ktb, b=b: e.tensor_tensor(ktb[:, 0, :], ps[b][:, 0:128], dI[:], ALU.add),
                             reads=[f"ps{b}", "dI"], writes=[kk])
                    else:
                        S.op("act", lambda e, ktb=ktb, b=b, n=n: e.copy(ktb[:, n, :], ps[b][:, 0:128]),
                             reads=[f"ps{b}"], writes=[kk])
                S.dma("sp", k1 + "st", lambda e, kt=kt, w1=w1: e.dma_start(out=W1_d[kt], in_=w1[:].rearrange("p a b c d -> p (a b c d)")),
                      reads=[k1], writes=[f"W1d{kt}"])
                S.dma("sp", kk + "st", lambda e, kt=kt, ktb=ktb: e.dma_start(out=KT_d[kt], in_=ktb[:].rearrange("p a b -> p (a b)")),
                      reads=[kk], writes=[f"KTd{kt}"])
        end_phase(m0)
        if stop_after == "p0":
            return nc

        mA = A.mark()
        S.op("dve", lambda e: e.memset(Ecur[:], 0.0), writes=["Ecur"])
        u_holder = {}
        scan_hook = {"f": None}

        def run_segment(seg, mode):
            own = (mode == "own")
            xo = seg * TOK
            if mode in ("rg", "own"):
                S.op("dve", lambda e: e.tensor_scalar(Ecur[:], Ecur[:], segm[:, seg:seg + 1], None, ALU.mult), reads=["Ecur", "segm"], writes=["Ecur"])
            if own:
                S.op("dve", lambda e: e.tensor_copy(hcar[:], Ecur[:]), reads=["Ecur"], writes=["hcar"])
            mA1 = A.mark()
            if mode == "s5prep":
                u_sb = sbt("u_sb", [128, 8, 8, 256], BF16)
            else:
                u_sb = u_holder.get("u")
            mU = A.mark()
            xT = sbt("xT", [128, 16, TOK + HALO], BF16)
            mx = A.mark()
            xin = [sbt(f"xin{i}", [128, D], F32) for i in range(3)]
            tiles = [(0, 3, 0)] + [(3 + 128 * t, 128, 3 + 128 * t) for t in range(16)]
            for ti, (r0, nr, c0) in enumerate(tiles):
                xb = xin[ti % 3]
                key = f"xin{ti%3}"
                S.dma("sp", key, lambda e, xb=xb, r0=r0, nr=nr: e.dma_start(out=xb[0:nr, :], in_=x_d[xo + r0:xo + r0 + nr, :]), writes=[key])
                for b4 in range(4):
                    b = nextbank()
                    for jj in range(4):
                        kt = 4 * b4 + jj
                        S.op("pe", lambda e, xb=xb, nr=nr, kt=kt, jj=jj, b=b: e.transpose(
                            ps[b][:, jj * 128:jj * 128 + nr], xb[0:nr, kt * 128:(kt + 1) * 128], ident[0:nr, 0:nr]),
                            reads=[key, "ident"], writes=[f"ps{b}"], inc=(jj == 3))
                    eng = ev_eng()
                    S.op(eng, copy_op(eng, xT[:, 4 * b4:4 * b4 + 4, c0:c0 + nr],
                                      ps[b][:, 0:512].rearrange("p (j t) -> p j t", j=4)[:, :, 0:nr]),
                         reads=[f"ps{b}"], writes=[f"xT{ti}_{b4}"])
            allx = [f"xT{ti}_{b4}" for ti in range(17) for b4 in range(4)]
            if own:
                S.dma("sp", "xTst", lambda e: e.dma_start(out=xT_d, in_=xT[:]), reads=allx, writes=["xTd"])
            S.barrier()
            A.release(mx)

            def alloc_wb():
                return [sbt(f"wb{i}", [128, 16, 512], BF16) for i in range(2)]

            def do_A1(scan_emit):
                NH = 1024
                bufs = []
                for i in range(2):
                    bufs.append(dict(
                        xr=sbt(f"xr{i}", [128, NH + 3], F32), xc=sbt(f"xc{i}", [128, NH], F32),
                        xcb=sbt(f"xcb{i}", [128, NH], BF16), thr=sbt(f"thr{i}", [128, NH], F32),
                        thi=sbt(f"thi{i}", [128, NH], F32), at=sbt(f"at{i}", [128, NH], F32),
                        tt=sbt(f"tt{i}", [128, NH], F32)))

                def stageA(u):
                    h, hf = u // 2, u % 2
                    B_ = bufs[u % 2]
                    sx = str(u % 2)
                    if hf == 0 and h % 4 == 0 and h // 4 + 1 < 4:
                        load_w((h // 4 + 1) % 2, w_in_v, 512 * (h // 4 + 1))
                    wbi = (h // 4) % 2
                    hh = h % 4
                    cb0 = NH * hf
                    pieces = [(cb0, 3, 0), (cb0 + 3, 512, 3), (cb0 + 515, 512, 515)]
                    for (c0, n, off) in pieces:
                        b = nextbank()
                        for kt in range(16):
                            S.op("pe", lambda e: e.matmul(
                                ps[b][:, 0:n], wb[wbi][:, kt, hh * 128:(hh + 1) * 128], xT[:, kt, c0:c0 + n],
                                start=(kt == 0), stop=(kt == 15)),
                                reads=[f"wb{wbi}"], writes=[f"ps{b}"], inc=(kt == 15))
                        S.op("act", lambda e: e.copy(B_["xr"][:, off:off + n], ps[b][:, 0:n]),
                             reads=[f"ps{b}"], writes=["xr" + sx + "_%d" % off])
                    xrk = ["xr" + sx + "_%d" % o for o in (0, 3, 515)]
                    S.op("dve", lambda e: e.tensor_scalar(B_["xc"][:], B_["xr"][:, 0:NH], P1[:, h:h + 1], P1[:, 64 + h:65 + h], ALU.mult, ALU.add),
                         reads=xrk + ["xc" + sx, "xcb" + sx], writes=["xc" + sx])
                    for k in range(1, 4):
                        S.op("dve", lambda e: e.scalar_tensor_tensor(
                            B_["xc"][:], B_["xr"][:, k:k + NH], P1[:, k * 16 + h:k * 16 + h + 1], B_["xc"][:], ALU.mult, ALU.add),
                            reads=xrk + ["xc" + sx], writes=["xc" + sx])
                    S.op("act", lambda e: e.copy(B_["xcb"][:], B_["xc"][:]), reads=["xc" + sx], writes=["xcb" + sx])
                    for g, (wsb, wk, dst) in enumerate(((wa_sb, "wa_sb", "thr"), (wx_sb, "wx_sb", "thi"))):
                        for t2 in range(2):
                            b = nextbank()
                            S.op("pe", lambda e: e.matmul(
                                ps[b][:, :], wsb[:, h, :], B_["xcb"][:, t2 * 512:(t2 + 1) * 512], start=True, stop=True),
                                reads=[wk, "xcb" + sx], writes=[f"ps{b}"])
                            S.op("act", lambda e: e.activation(
                                B_[dst][:, t2 * 512:(t2 + 1) * 512], ps[b][:, :], AF.Tanh, bias=rgc[:, g, h:h + 1], scale=0.5),
                                reads=[f"ps{b}"], writes=[dst + sx + "_%d" % t2])

                def stageB(u):
                    h, hf = u // 2, u % 2
                    B_ = bufs[u % 2]
                    sx = str(u % 2)
                    thrk = ["thr" + sx + "_0", "thr" + sx + "_1"]
                    thik = ["thi" + sx + "_0", "thi" + sx + "_1"]
                    S.op("act", lambda e: e.activation(B_["at"][:], B_["thr"][:], AF.Exp, bias=rgc[:, 3, h:h + 1], scale=rgc[:, 3, h:h + 1]),
                         reads=thrk, writes=["at" + sx])
                    S.op("act", lambda e: e.activation(B_["tt"][:], B_["thr"][:], AF.Tanh, bias=rgc[:, 4, h:h + 1], scale=rgc[:, 4, h:h + 1]),
                         reads=thrk, writes=["tt" + sx])
                    S.op("act", lambda e: e.activation(B_["thr"][:], B_["thr"][:], AF.Exp, bias=rgc[:, 2, h:h + 1], scale=rgc[:, 2, h:h + 1]),
                         reads=thrk, writes=thrk)
                    S.op("dve", lambda e: e.scalar_tensor_tensor(B_["tt"][:], B_["thr"][:], 1.0, B_["tt"][:], ALU.add, ALU.mult),
                         reads=thrk + ["tt" + sx], writes=["tt" + sx])
                    S.op("act", lambda e: e.activation(B_["tt"][:], B_["tt"][:], AF.Sqrt), reads=["tt" + sx], writes=["tt" + sx])
                    S.op("dve", lambda e: e.scalar_tensor_tensor(B_["thi"][:], B_["thi"][:], 1.0, B_["xc"][:], ALU.add, ALU.mult),
                         reads=thik + ["xc" + sx], writes=thik)
                    S.op("dve", lambda e: e.scalar_tensor_tensor(B_["thi"][:], B_["thi"][:], 0.5, B_["tt"][:], ALU.mult, ALU.mult),
                         reads=thik + ["tt" + sx], writes=thik)
                    if not own:
                        S.op("dve", lambda e: e.tensor_tensor_scan(B_["tt"][:], B_["at"][:], B_["thi"][:], Ecur[:, h:h + 1], ALU.mult, ALU.add),
                             reads=["at" + sx, "Ecur", "tt" + sx] + thik, writes=["tt" + sx])
                        S.op("dve", lambda e: e.tensor_copy(Ecur[:, h:h + 1], B_["tt"][:, NH - 1:NH]), reads=["tt" + sx], writes=["Ecur"])
                    else:
                        S.dma("act", "ast" + sx, lambda e: e.dma_start(out=ab_d[0, h, :, hf * NH:(hf + 1) * NH], in_=B_["at"][:]),
                              reads=["at" + sx], writes=[f"abd0_{h}_{hf}"])
                        S.dma("sp", "bst" + sx, lambda e: e.dma_start(out=ab_d[1, h, :, hf * NH:(hf + 1) * NH], in_=B_["thi"][:]),
                              reads=thik, writes=[f"abd1_{h}_{hf}"])
                    if scan_emit is not None:
                        scan_emit(3)

                load_w(0, w_in_v, 0)
                stageA(0)
                for u in range(32):
                    if u + 1 < 32:
                        stageA(u + 1)
                    stageB(u)

            def do_uproj():
                for g in range(2):
                    load_w(g, w_in_v, 4096 + 512 * g)
                for kt in range(8):
                    wbi, hh = kt // 4, kt % 4
                    for tq in range(4):
                        b = nextbank()
                        c0 = 3 + 512 * tq
                        for k in range(16):
                            S.op("pe", lambda e, b=b, k=k, c0=c0: e.matmul(
                                ps[b][:, :], wb[wbi][:, k, hh * 128:(hh + 1) * 128], xT[:, k, c0:c0 + 512], start=(k == 0), stop=(k == 15)),
                                reads=[f"wb{wbi}"], writes=[f"ps{b}"], inc=(k == 15))
                        eng = ev_eng()
                        S.op(eng, copy_op(eng, u_sb[:, kt, :, 64 * tq:64 * tq + 64], ps[b][:, :].rearrange("p (c s) -> p s c", s=8)),
                             reads=[f"ps{b}"], writes=[f"u{kt}_{tq}"])

            def do_S():
                W1s = [sbt(f"W1s{i}", [128, 8192], BF16) for i in range(2)]
                for kt in range(8):
                    w1 = W1s[kt % 2]
                    k1 = f"W1s{kt%2}"
                    S.dma("sp", k1, lambda e, w1=w1, kt=kt: e.dma_start(out=w1[:], in_=W1_d[kt]), writes=[k1])
                    for q in range(4):
                        j = 4 * kt + q
                        for ri in range(2):
                            b = nextbank()
                            for s in range(8):
                                o = ((q * 8 + s) * 2 + ri) * 128
                                S.op("pe", lambda e, b=b, s=s, o=o, w1=w1: e.matmul(
                                    ps[b][:, 0:256], w1[:, o:o + 128], u_sb[:, kt, s, :], start=(s == 0), stop=(s == 7)),
                                    reads=[k1], writes=[f"ps{b}"], inc=(s == 7))
                            eng = ev_eng()
                            S.op(eng, copy_op(eng, S_sb[:, :, ri, j], ps[b][:, 0:256]), reads=[f"ps{b}"], writes=[f"S_{j}_{ri}"])
                allS = [f"S_{j}_{ri}" for j in range(32) for ri in range(2)]
                return allS

            if own:
                wb = alloc_wb()
                def load_w(i, src_v, c0, ncol=512, nk=16):
                    S.dma("pool", f"wb{i}", lambda e: e.dma_start(out=wb[i][:, 0:nk, 0:ncol], in_=src_v[:, :, c0:c0 + ncol]), writes=[f"wb{i}"])

                do_A1(None)
                do_uproj()
                end_phase(mA1)
                mS = A.mark()
                S_sb = sbt("S_sb", [128, 256, 2, 32], F32)
                u_holder["S_sb"] = S_sb
                mA2 = A.mark()
                do_S()
                end_phase(mA2)
            elif mode == "s5prep":
                mW = A.mark()
                wb = alloc_wb()
                def load_w(i, src_v, c0, ncol=512, nk=16):
                    S.dma("pool", f"wb{i}", lambda e: e.dma_start(out=wb[i][:, 0:nk, 0:ncol], in_=src_v[:, :, c0:c0 + ncol]), writes=[f"wb{i}"])

                do_uproj()
                end_phase(mU)
                S_sb = sbt("S_sb", [128, 256, 2, 32], F32)
                allS = do_S()
                S.dma("sp", "Sspill", lambda e: e.dma_start(out=S_d3[seg], in_=S_sb[:].rearrange("p c r j -> p (c r j)")), reads=allS, writes=[f"S_d3_{seg}"])
                end_phase(mA1)
            else:
                wb = alloc_wb()
                def load_w(i, src_v, c0, ncol=512, nk=16):
                    S.dma("pool", f"wb{i}", lambda e: e.dma_start(out=wb[i][:, 0:nk, 0:ncol], in_=src_v[:, :, c0:c0 + ncol]), writes=[f"wb{i}"])

                do_A1(scan_hook["f"])
                end_phase(mA1)

        for seg in range(3):
            run_segment(seg, "s5prep")
        mW3 = A.mark()
        Sst3 = [sbt(f"Sst3_{i}", [128, 3, 8, 2, 32], F32) for i in range(2)]
        sstate = {"c": 0}
        S_d3v = S_d3.rearrange("s p f -> p s f")

        def load_piece(i):
            S.dma("sp", f"Sst3_{i%2}", lambda e: e.dma_start(out=Sst3[i % 2][:].rearrange("p s c r j -> p s (c r j)"), in_=S_d3v[:, :, 512 * i:512 * (i + 1)]),
                  writes=[f"Sst3_{i%2}"])

        def scan_emit(n):
            for _ in range(n):
                c = sstate["c"]
                if c >= 256:
                    return
                i, cc = c // 8, c % 8
                sk = f"Sst3_{i%2}"
                sc_ap = Sst3[i % 2][:, :, cc, :, :]
                S.op("pool", lambda e: e.tensor_tensor(X3[:], H3[:], LL8[:, 0:2, :].unsqueeze(1).to_broadcast([128, 3, 2, 32]), ALU.mult), reads=["H3"], writes=["X3"])
                S.op("pool", lambda e: e.tensor_tensor(Y3[:, :, 0, :], H3[:, :, 1, :], LL8[:, 2, :].unsqueeze(1).to_broadcast([128, 3, 32]), ALU.mult), reads=["H3"], writes=["Y3"])
                S.op("pool", lambda e: e.tensor_tensor(Y3[:, :, 1, :], H3[:, :, 0, :], LL8[:, 3, :].unsqueeze(1).to_broadcast([128, 3, 32]), ALU.mult), reads=["H3"], writes=["Y3"])
                S.op("pool", lambda e: e.tensor_tensor(X3[:], X3[:], Y3[:], ALU.add), reads=["X3", "Y3"], writes=["X3"])
                S.op("pool", lambda e: e.tensor_tensor(H3[:], X3[:], sc_ap, ALU.add), reads=["X3", sk], writes=["H3"])
                sstate["c"] = c + 1
                if cc == 7 and i + 2 < 32:
                    load_piece(i + 2)
        scan_hook["f"] = scan_emit
        S.op("pool", lambda e: e.memset(H3[:], 0.0), writes=["H3"])
        load_piece(0)
        load_piece(1)
        for seg in range(3):
            run_segment(seg, "rg")
        scan_emit(256)

        def cstep(LLm, prev, add_ap, out_ap, rk, wk):
            S.op("dve", lambda e: e.tensor_tensor(Xd[:], LLm[:, 0:2, :], prev, ALU.mult), reads=rk, writes=["Xd"])
            S.op("dve", lambda e: e.tensor_tensor(Yd[:, 0, :], LLm[:, 2, :], prev[:, 1, :], ALU.mult), reads=rk, writes=["Yd"])
            S.op("dve", lambda e: e.tensor_tensor(Yd[:, 1, :], LLm[:, 3, :], prev[:, 0, :], ALU.mult), reads=rk, writes=["Yd"])
            S.op("dve", lambda e: e.tensor_tensor(Xd[:], Xd[:], Yd[:], ALU.add), reads=["Xd", "Yd"], writes=["Xd"])
            S.op("dve", lambda e: e.tensor_tensor(out_ap, Xd[:], add_ap, ALU.add), reads=["Xd"] + rk, writes=wk)
        cstep(D2kL, H3[:, 0, :, :], H3[:, 1, :, :], Ha[:], ["H3"], ["Ha"])
        cstep(D2kL, Ha[:], H3[:, 2, :, :], Hin[:], ["Ha", "H3"], ["Hin"])
        end_phase(mW3)
        u_holder["u"] = sbt("u_sb", [128, 8, 8, 256], BF16, top_=True)
        run_segment(3, "own")
        u_sb = u_holder["u"]
        S_sb = u_holder["S_sb"]
        if True:
            Hbf = sbt("Hbf", [128, 32, 2, 258], BF16)
            W3s = [sbt(f"W3s{i}", [128, 8192], BF16) for i in range(2)]
            KTs = [sbt(f"KTs{i}", [128, 8, 128], BF16) for i in range(2)]
            ysb = [sbt(f"ysb{i}", [128, TOK], BF16) for i in range(2)]
            Xs = sbt("Xs4", [128, 2, 32], F32)
            Ys = sbt("Ys4", [128, 2, 32], F32)

            def chunk_step4(prev, c):
                S.op("dve", lambda e: e.tensor_tensor(Xs[:], LL8[:, 0:2, :], prev, ALU.mult), reads=["Ssb"], writes=["Xs"])
                S.op("dve", lambda e: e.tensor_tensor(Ys[:, 0, :], LL8[:, 2, :], prev[:, 1, :], ALU.mult), reads=["Ssb"], writes=["Ys"])
                S.op("dve", lambda e: e.tensor_tensor(Ys[:, 1, :], LL8[:, 3, :], prev[:, 0, :], ALU.mult), reads=["Ssb"], writes=["Ys"])
                S.op("dve", lambda e: e.tensor_tensor(Xs[:], Xs[:], Ys[:], ALU.add), reads=["Xs", "Ys"], writes=["Xs"])
                S.op("dve", lambda e: e.tensor_tensor(S_sb[:, c, :, :], Xs[:], S_sb[:, c, :, :], ALU.add), reads=["Xs", "Ssb"], writes=["Ssb"])
            for c in range(256):
                chunk_step4(Hin[:] if c == 0 else S_sb[:, c - 1, :, :], c)
            S.op("act", lambda e: e.copy(Hbf[:, :, :, 0], Hin[:].rearrange("p a b -> p b a")), reads=[], writes=["Hbf_0"])
            S.op("act", lambda e: e.copy(Hbf[:, :, 0, 1:257], S_sb[:, :, 0, :].rearrange("p c j -> p j c")), reads=["Ssb"], writes=["Hbf_1"])
            S.op("dve", lambda e: e.tensor_copy(Hbf[:, :, 1, 1:257], S_sb[:, :, 1, :].rearrange("p c j -> p j c")), reads=["Ssb"], writes=["Hbf_2"])
            hbk = ["Hbf_0", "Hbf_1", "Hbf_2"]
            for kt in range(8):
                w3 = W3s[kt % 2]
                ktb = KTs[kt % 2]
                k3, kk = f"W3s{kt%2}", f"KTs{kt%2}"
                ys = ysb[kt % 2]
                ky = f"ysb{kt%2}"
                S.dma("sp", k3, lambda e, w3=w3, kt=kt: e.dma_start(out=w3[:], in_=W3_d[kt]), writes=[k3])
                S.dma("sp", kk, lambda e, ktb=ktb, kt=kt: e.dma_start(out=ktb[:].rearrange("p a b -> p (a b)"), in_=KT_d[kt]), writes=[kk])
                for tau in range(8):
                    b = nextbank()
                    mm = []
                    for s in range(tau + 1):
                        mm.append((ktb[:, tau - s, :], u_sb[:, kt, s, :], [kk]))
                    for q in range(4):
                        for ri in range(2):
                            o = ((q * 8 + tau) * 2 + ri) * 128
                            mm.append((w3[:, o:o + 128], Hbf[:, 4 * kt + q, ri, 0:256], [k3] + hbk))
                    for i, (l_, r_, rk) in enumerate(mm):
                        S.op("pe", lambda e, b=b, l_=l_, r_=r_, i=i, n=len(mm): e.matmul(
                            ps[b][:, 0:256], l_, r_, start=(i == 0), stop=(i == n - 1)),
                            reads=rk, writes=[f"ps{b}"], inc=(i == len(mm) - 1))
                    S.op("act", lambda e, b=b, ys=ys, tau=tau: e.activation(ys[:, tau::8], ps[b][:, 0:256], AF.Gelu_apprx_tanh),
                         reads=[f"ps{b}"], writes=[ky + "_%d" % tau])
                S.dma("act", ky + "st", lambda e, ys=ys, kt=kt: e.dma_start(out=yS_d[:, kt, :], in_=ys[:]),
                      reads=[ky + "_%d" % t for t in range(8)], writes=[f"ySd{kt}"])
        end_phase(mA)
        if stop_after == "a4":
            return nc

        NB = 1024
        for blk in range(2):
            t0 = blk * NB
            mB = A.mark()
            xTb = sbt("xTb", [128, 16, NB], BF16)
            ySb = sbt("ySb", [128, 8, NB], BF16)
            hg = sbt("hg", [128, 16, NB], BF16)
            S.dma("sp", "xTb", lambda e: e.dma_start(out=xTb[:], in_=xT_d[:, :, 3 + t0:3 + t0 + NB]), writes=["xTb"])
            S.dma("sp", "ySb", lambda e: e.dma_start(out=ySb[:], in_=yS_d[:, :, t0:t0 + NB]), writes=["ySb"])
            mG = A.mark()
            if True:
                wg = [sbt(f"wg{i}", [128, 16, 512], BF16) for i in range(2)]
                gsb = [sbt(f"gsb{i}", [128, NB], F32) for i in range(2)]
                ab = [sbt(f"abl{i}", [128, 2, NB], F32) for i in range(2)]
                hs = [sbt(f"hs{i}", [128, NB], F32) for i in range(2)]
                def load_wg(g):
                    S.dma("pool", f"wg{g%2}", lambda e: e.dma_start(out=wg[g % 2][:], in_=w_in_v[:, :, 2048 + 512 * g:2048 + 512 * g + 512]), writes=[f"wg{g%2}"])
                load_wg(0)
                for h in range(16):
                    if h % 4 == 0 and h // 4 + 1 < 4:
                        load_wg(h // 4 + 1)
                    wgi, hh, sx = (h // 4) % 2, h % 4, str(h % 2)
                    S.dma("sp", "abl" + sx, lambda e, h=h: e.dma_start(out=ab[h % 2][:], in_=ab_d[:, h, :, t0:t0 + NB].rearrange("a p t -> p a t")),
                          writes=["abl" + sx])
                    for t2 in range(2):
                        b = nextbank()
                        for kt in range(16):
                            S.op("pe", lambda e, b=b, kt=kt, t2=t2: e.matmul(
                                ps[b][:, :], wg[wgi][:, kt, hh * 128:(hh + 1) * 128], xTb[:, kt, t2 * 512:(t2 + 1) * 512], start=(kt == 0), stop=(kt == 15)),
                                reads=[f"wg{wgi}", "xTb"], writes=[f"ps{b}"], inc=(kt == 15))
                        S.op("act", lambda e, b=b, t2=t2, h=h: e.activation(gsb[h % 2][:, t2 * 512:(t2 + 1) * 512], ps[b][:, :], AF.Gelu_apprx_tanh),
                             reads=[f"ps{b}"], writes=["gsb" + sx + "_%d" % t2])
                    S.op("dve", lambda e, h=h: e.tensor_tensor_scan(hs[h % 2][:], ab[h % 2][:, 0, :], ab[h % 2][:, 1, :], hcar[:, h:h + 1], ALU.mult, ALU.add),
                         reads=["abl" + sx, "hcar"], writes=["hs" + sx])
                    S.op("dve", lambda e, h=h: e.tensor_copy(hcar[:, h:h + 1], hs[h % 2][:, NB - 1:NB]), reads=["hs" + sx], writes=["hcar"])
                    S.op("dve", lambda e, h=h: e.tensor_tensor(hg[:, h, :], hs[h % 2][:], gsb[h % 2][:], ALU.mult),
                         reads=["hs" + sx, "gsb" + sx + "_0", "gsb" + sx + "_1"], writes=[f"hg{h}"])
            end_phase(mG)
            if True:
                wA = [sbt(f"wA{i}", [128, 16, 256], BF16) for i in range(2)]
                wGa = [sbt(f"wGa{i}", [128, 16, 256], BF16) for i in range(2)]
                wGb = [sbt(f"wGb{i}", [128, 16, 256], BF16) for i in range(2)]
                wLw = [sbt(f"wLw{i}", [128, 8, 256], BF16) for i in range(2)]
                wLv = [sbt(f"wLv{i}", [128, 8, 256], BF16) for i in range(2)]
                tmp = [sbt(f"mt{i}", [128, 4, 512], F32) for i in range(2)]
                mixs = [sbt(f"mixs{i}", [128, 512], BF16) for i in range(2)]
                mc = 0

                def load_mix(jg):
                    i = jg % 2
                    c0 = 256 * jg
                    S.dma("pool", f"wGa{i}", lambda e: e.dma_start(out=wGa[i][:], in_=w_in_v[:, :, 5120 + c0:5120 + c0 + 256]), writes=[f"wGa{i}"])
                    S.dma("pool", f"wGb{i}", lambda e: e.dma_start(out=wGb[i][:], in_=w_in_v[:, :, 7168 + c0:7168 + c0 + 256]), writes=[f"wGb{i}"])
                    S.dma("pool", f"wLv{i}", lambda e: e.dma_start(out=wLv[i][:], in_=glu_v_v[:, :, c0:c0 + 256]), writes=[f"wLv{i}"])
                    S.dma("pool", f"wLw{i}", lambda e: e.dma_start(out=wLw[i][:], in_=glu_w_v[:, :, c0:c0 + 256]), writes=[f"wLw{i}"])
                    S.dma("pool", f"wA{i}", lambda e: e.dma_start(out=wA[i][:], in_=w_a_v[:, :, c0:c0 + 256]), writes=[f"wA{i}"])
                load_mix(0)
                for jg in range(8):
                    i = jg % 2
                    c0 = 256 * jg
                    if jg + 1 < 8:
                        load_mix(jg + 1)
                    for jj in range(2):
                        j = 2 * jg + jj
                        cs = slice(128 * jj, 128 * jj + 128)
                        for t2 in range(2):
                            ts = slice(t2 * 512, (t2 + 1) * 512)
                            T_ = tmp[mc % 2]
                            tk = f"mt{mc%2}"
                            ms = mixs[mc % 2]
                            mk = f"mixs{mc%2}"
                            mc += 1

                            def group(wt, wk, act, ak, nk):
                                b = nextbank()
                                for kt in range(nk):
                                    S.op("pe", lambda e, b=b, kt=kt: e.matmul(ps[b][:, :], wt[:, kt, cs], act[:, kt, ts], start=(kt == 0), stop=(kt == nk - 1)),
                                         reads=[wk] + ak, writes=[f"ps{b}"], inc=(kt == nk - 1))
                                return b
                            bA = group(wGa[i], f"wGa{i}", xTb, ["xTb"], 16)
                            S.op("act", lambda e, b=bA, T_=T_: e.activation(T_[:, 0, :], ps[b][:, :], AF.Sigmoid), reads=[f"ps{bA}"], writes=[tk + "a"])
                            bB = group(wGb[i], f"wGb{i}", xTb, ["xTb"], 16)
                            S.op("act", lambda e, b=bB, T_=T_: e.activation(T_[:, 1, :], ps[b][:, :], AF.Sigmoid), reads=[f"ps{bB}"], writes=[tk + "b"])
                            bV = group(wLv[i], f"wLv{i}", ySb, ["ySb"], 8)
                            S.op("act", lambda e, b=bV, T_=T_: e.activation(T_[:, 2, :], ps[b][:, :], AF.Sigmoid), reads=[f"ps{bV}"], writes=[tk + "v"])
                            bW = group(wLw[i], f"wLw{i}", ySb, ["ySb"], 8)
                            S.op("dve", lambda e, b=bW, T_=T_: e.tensor_tensor(T_[:, 2, :], ps[b][:, :], T_[:, 2, :], ALU.mult), reads=[f"ps{bW}", tk + "v"], writes=[tk + "v"])
                            S.op("dve", lambda e, T_=T_: e.tensor_tensor(T_[:, 2, :], T_[:, 2, :], T_[:, 1, :], ALU.mult), reads=[tk + "v", tk + "b"], writes=[tk + "v"])
                            bY = group(wA[i], f"wA{i}", hg, [], 16)
                            S.op("dve", lambda e, b=bY, T_=T_: e.tensor_tensor(T_[:, 0, :], ps[b][:, :], T_[:, 0, :], ALU.mult), reads=[f"ps{bY}", tk + "a"], writes=[tk + "a"])
                            S.op("dve", lambda e, T_=T_, ms=ms: e.tensor_tensor(ms[:], T_[:, 0, :], T_[:, 2, :], ALU.add), reads=[tk + "a", tk + "v"], writes=[mk])
                            S.dma("sp", mk + "st", lambda e, ms=ms, j=j, t2=t2: e.dma_start(out=mix_d[:, j, t0 + t2 * 512:t0 + (t2 + 1) * 512], in_=ms[:]),
                                  reads=[mk], writes=[f"mixd{j}_{t2}"])
            end_phase(mB)
            if stop_after == "mix":
                continue

            mF = A.mark()
            acc = sbt("acc", [128, 8, D], F32)
            lnp_off = A.lo
            lnp = sbt("lnp", [128, 2, D], F32)
            mO = A.mark()
            if True:
                mixT = sbt("mixT", [128, 16, NB], BF16)
                wo = [sbt(f"wo{i}", [128, 16, 512], BF16) for i in range(2)]
                xres = [sbt(f"xres{i}", [128, 512], F32) for i in range(3)]
                S.dma("sp", "mixT", lambda e: e.dma_start(out=mixT[:], in_=mix_d[:, :, t0:t0 + NB]), writes=["mixT"])
                S.dma("sp", "lnp0", lambda e: e.dma_start(out=lnp[:, 0, :], in_=ln1_g_d.partition_broadcast(128)), writes=["lnp0"])
                S.dma("sp", "lnp1", lambda e: e.dma_start(out=lnp[:, 1, :], in_=ln1_b_d.partition_broadcast(128)), writes=["lnp1"])
                xc_ = 0
                def load_wo(cb):
                    S.dma("pool", f"wo{cb%2}", lambda e: e.dma_start(out=wo[cb % 2][:], in_=w_out_v[:, :, 512 * cb:512 * cb + 512]), writes=[f"wo{cb%2}"])
                load_wo(0)
                for cb in range(4):
                    i = cb % 2
                    if cb + 1 < 4:
                        load_wo(cb + 1)
                    for tt in range(8):
                        xr_ = xres[xc_ % 3]
                        xk = f"xres{xc_%3}"
                        xc_ += 1
                        r0 = 3 * TOK + 3 + t0 + 128 * tt
                        S.dma("sp", xk, lambda e, xr_=xr_, r0=r0, cb=cb: e.dma_start(out=xr_[:], in_=x_d[r0:r0 + 128, 512 * cb:512 * cb + 512]), writes=[xk])
                        b = nextbank()
                        for kt in range(16):
                            S.op("pe", lambda e, b=b, kt=kt, tt=tt, i=i: e.matmul(
                                ps[b][:, :], mixT[:, kt, 128 * tt:128 * tt + 128], wo[i][:, kt, :], start=(kt == 0), stop=(kt == 15)),
                                reads=["mixT", f"wo{i}"], writes=[f"ps{b}"], inc=(kt == 15))
                        S.op("dve", lambda e, b=b, xr_=xr_, tt=tt, cb=cb: e.scalar_tensor_tensor(
                            acc[:, tt, 512 * cb:512 * cb + 512], xr_[:], ALPHA, ps[b][:, :], ALU.mult, ALU.add),
                            reads=[xk, f"ps{b}"], writes=[f"acc{tt}_{cb}"])
                        S.op("dve", lambda e, tt=tt, cb=cb: e.bn_stats(stats[:, tt, cb, :], acc[:, tt, 512 * cb:512 * cb + 512]),
                             reads=[f"acc{tt}_{cb}"], writes=[f"stats{tt}_{cb}"])
            end_phase(mO)
            x1T = sbt("x1T", [128, 16, NB], BF16)

            def layernorm(tt, outk):
                S.op("dve", lambda e: e.bn_aggr(mv[:, tt, 0:2], stats[:, tt, :, :].rearrange("p a b -> p (a b)")),
                     reads=[f"stats{tt}_{c_}" for c_ in range(4)], writes=[f"mv{tt}"])
                S.op("act", lambda e: e.activation(mv[:, tt, 2:3], mv[:, tt, 1:2], AF.Sqrt, bias=EPS), reads=[f"mv{tt}"], writes=[f"mv{tt}"])
                S.op("dve", lambda e: e.reciprocal(mv[:, tt, 2:3], mv[:, tt, 2:3]), reads=[f"mv{tt}"], writes=[f"mv{tt}"])
                S.op("dve", lambda e: e.scalar_tensor_tensor(mv[:, tt, 3:4], mv[:, tt, 0:1], -1.0, mv[:, tt, 2:3], ALU.mult, ALU.mult),
                     reads=[f"mv{tt}"], writes=[f"mv{tt}"])
                S.op("act", lambda e: e.activation(acc[:, tt, :], acc[:, tt, :], AF.Identity, bias=mv[:, tt, 3:4], scale=mv[:, tt, 2:3]),
                     reads=[f"mv{tt}"], writes=[outk])
                S.op("dve", lambda e: e.tensor_tensor(acc[:, tt, :], acc[:, tt, :], lnp[:, 0, :], ALU.mult), reads=[outk, "lnp0"], writes=[outk])
                S.op("dve", lambda e: e.tensor_tensor(acc[:, tt, :], acc[:, tt, :], lnp[:, 1, :], ALU.add), reads=[outk, "lnp1"], writes=[outk])

            for tt in range(8):
                layernorm(tt, f"x1_{tt}")
                for b4 in range(4):
                    b = nextbank()
                    for jj in range(4):
                        kt = 4 * b4 + jj
                        S.op("pe", lambda e, b=b, jj=jj, kt=kt, tt=tt: e.transpose(
                            ps[b][:, jj * 128:(jj + 1) * 128], acc[:, tt, kt * 128:(kt + 1) * 128], ident[:, :]),
                            reads=[f"x1_{tt}"], writes=[f"ps{b}"], inc=(jj == 3))
                    S.op("act", lambda e, b=b, b4=b4, tt=tt: e.copy(x1T[:, 4 * b4:4 * b4 + 4, 128 * tt:128 * tt + 128],
                                                                   ps[b][:, :].rearrange("p (j t) -> p j t", j=4)),
                         reads=[f"ps{b}"], writes=[f"x1T{tt}_{b4}"])
            if dbg and blk == 0:
                S.dma("sp", "dbgx1", lambda e: e.dma_start(out=dbg_x1.rearrange("(t p) c -> p t c", p=128), in_=acc[:]),
                      reads=[f"x1_{tt}" for tt in range(8)])
            S.dma("sp", "lnp0", lambda e: e.dma_start(out=lnp[:, 0, :], in_=b_dn_d.partition_broadcast(128)),
                  reads=[f"x1_{tt}" for tt in range(8)], writes=["lnp0"])
            for tt in range(8):
                S.op("dve", lambda e, tt=tt: e.scalar_tensor_tensor(acc[:, tt, :], acc[:, tt, :], ALPHA, lnp[:, 0, :], ALU.mult, ALU.add),
                     reads=[f"x1_{tt}", "lnp0"], writes=[f"x1_{tt}"])
            S.barrier()

            if True:
                FC = 8
                hTb = [sbt("hT", [128, FC, NB], BF16), A.view(lnp_off, [128, FC, NB], BF16)]
                wu = [sbt(f"wu{i}", [128, 16, 256], BF16) for i in range(2)]
                wd = sbt("wd", [128, FC, D], BF16)
                rl = [sbt(f"rl{i}", [128, 512], F32) for i in range(3)]
                wuc = 0
                rc = 0
                NFC = DFF // 128 // FC
                NG = NFC * (FC // 2)

                def load_wu(g):
                    c0 = g * 256
                    S.dma("pool", f"wu{g%2}", lambda e: e.dma_start(out=wu[g % 2][:], in_=w_up_v[:, :, c0:c0 + 256]), writes=[f"wu{g%2}"])

                def load_wd(fc):
                    for f2 in range(FC // 2):
                        f0 = fc * FC + f2 * 2
                        S.dma("pool", f"wd{f2}", lambda e, f2=f2, f0=f0: e.dma_start(out=wd[:, 2 * f2:2 * f2 + 2, :], in_=w_dn_v[:, f0:f0 + 2, :]), writes=[f"wd{f2}"])
                load_wu(0)
                for fc in range(NFC):
                    hT = hTb[fc % 2]
                    hk = "hT%d_" % (fc % 2)
                    for f2 in range(FC // 2):
                        g = fc * (FC // 2) + f2
                        i = g % 2
                        if g + 1 < NG:
                            load_wu(g + 1)
                        if f2 == 0:
                            load_wd(fc)
                        for f in range(2):
                            fl = f2 * 2 + f
                            ft = fc * FC + fl
                            for t2 in range(2):
                                b = nextbank()
                                for kt in range(16):
                                    S.op("pe", lambda e, b=b, kt=kt, i=i, f=f, t2=t2: e.matmul(
                                        ps[b][:, :], wu[i][:, kt, f * 128:(f + 1) * 128], x1T[:, kt, t2 * 512:(t2 + 1) * 512], start=(kt == 0), stop=(kt == 15)),
                                        reads=[f"wu{i}"], writes=[f"ps{b}"], inc=(kt == 15))
                                r_ = rl[rc % 3]
                                rk = f"rl{rc%3}"
                                rc += 1
                                S.op("act", lambda e, b=b, r_=r_, ft=ft: e.activation(r_[:], ps[b][:, :], AF.Relu, bias=bup[:, ft:ft + 1]),
                                     reads=[f"ps{b}"], writes=[rk])
                                S.op("act", lambda e, r_=r_, fl=fl, t2=t2: e.activation(hT[:, fl, t2 * 512:(t2 + 1) * 512], r_[:], AF.Square),
                                     reads=[rk], writes=[hk + f"{fl}_{t2}"])
                    for tt in range(8):
                        for cb in range(4):
                            b = nextbank()
                            for fl in range(FC):
                                S.op("pe", lambda e, b=b, fl=fl, tt=tt, cb=cb: e.matmul(
                                    ps[b][:, :], hT[:, fl, 128 * tt:128 * tt + 128], wd[:, fl, 512 * cb:512 * cb + 512], start=(fl == 0), stop=(fl == FC - 1)),
                                    reads=[f"wd{fl//2}", hk + f"{fl}_{(128*tt)//512}"], writes=[f"ps{b}"], inc=(fl == FC - 1))
                            S.op("dve", lambda e, b=b, tt=tt, cb=cb: e.tensor_tensor(
                                acc[:, tt, 512 * cb:512 * cb + 512], acc[:, tt, 512 * cb:512 * cb + 512], ps[b][:, :], ALU.add),
                                reads=[f"ps{b}", f"accf{tt}_{cb}"], writes=[f"accf{tt}_{cb}"])
                hk1 = [f"hT1_{fl}_{t2}" for fl in range(FC) for t2 in range(2)]
                S.dma("sp", "lnp0", lambda e: e.dma_start(out=lnp[:, 0, :], in_=ln2_g_d.partition_broadcast(128)), writes=["lnp0"] + hk1)
                S.dma("sp", "lnp1", lambda e: e.dma_start(out=lnp[:, 1, :], in_=ln2_b_d.partition_broadcast(128)), writes=["lnp1"] + hk1)
                for tt in range(8):
                    for cb in range(4):
                        S.op("dve", lambda e, tt=tt, cb=cb: e.bn_stats(stats[:, tt, cb, :], acc[:, tt, 512 * cb:512 * cb + 512]),
                             reads=[f"accf{tt}_{cb}"], writes=[f"stats{tt}_{cb}"])
                    layernorm(tt, f"x2_{tt}")
                    S.dma("sp", f"ost{tt%2}", lambda e, tt=tt: e.dma_start(out=out_d[t0 + 128 * tt:t0 + 128 * tt + 128, :], in_=acc[:, tt, :]),
                          reads=[f"x2_{tt}"], writes=[f"outd{tt}"])
            end_phase(mF)
        S.barrier()
    return nc


_CACHE = {}


def _prep_inputs(inputs, small=False):
    x = np.ascontiguousarray(np.asarray(inputs["x"], dtype=np.float32))
    names = ["w_in", "conv_w", "conv_b", "rg_wa", "rg_ba", "rg_wx", "rg_bx", "rg_lambda", "w_a_out",
             "ssm_a_re", "ssm_a_im", "ssm_log_dt", "ssm_b_re", "ssm_b_im", "ssm_c_re", "ssm_c_im", "ssm_d",
             "glu_w", "glu_v", "w_out", "ln1_g", "ln1_b", "mlp_w_up", "mlp_b_up", "mlp_w_down", "mlp_b_down",
             "ln2_g", "ln2_b"]
    shared = {n: np.ascontiguousarray(np.asarray(inputs[n], dtype=np.float32)[0]) for n in names}
    in_maps = []
    for r in range(NCORE):
        b, k = r // 4, r % 4
        xs = np.zeros((4 * TOK + HALO, D), np.float32)
        n_real = TOK * (k + 1)
        xs[4 * TOK + HALO - n_real:] = x[b, 0:n_real]
        segm = np.ones((128, 4), np.float32)
        segm[:, 3 - k] = 0.0
        m = {"x": xs, "segm": segm}
        m.update(shared)
        if small:
            for n in ("w_a_out", "glu_w", "glu_v", "w_out", "mlp_w_up", "mlp_w_down"):
                m[n] = np.zeros((128, 128), np.float32)
        in_maps.append(m)
    return in_maps


def kernel(**inputs):
    if "nc" not in _CACHE:
        _CACHE["nc"] = build()
    nc = _CACHE["nc"]
    in_maps = _prep_inputs(inputs)
    res = run_bass_kernel_spmd(nc, in_maps, core_ids=list(range(NCORE)))
    out = np.empty((2, 4 * TOK, D), np.float32)
    for r in range(NCORE):
        b, k = r // 4, r % 4
        out[b, TOK * k:TOK * (k + 1)] = res.results[r]["out"]
    return out
```

```python
import numpy as np
from contextlib import ExitStack
import concourse.bass as bass
import concourse.mybir as mybir
from concourse.bass_utils import run_bass_kernel_spmd

F32 = mybir.dt.float32
BF16 = mybir.dt.bfloat16
I32 = mybir.dt.int32
AF = mybir.ActivationFunctionType
ALU = mybir.AluOpType

NCORE = 8
TOK = 2048
HALO = 3
D = 2048
DIN = 9216
DFF = 8192
ALPHA = 2.0 ** 0.25
EPS = 1e-5
TWO_PI = 6.283185307179586
PI = 3.141592653589793


class Sched:
    ENGS = ("pe", "act", "dve", "pool", "sp")

    def __init__(self, nc, stack):
        self.nc = nc
        self.stack = stack
        self.eng = {"pe": nc.tensor, "act": nc.scalar, "dve": nc.vector, "pool": nc.gpsimd, "sp": nc.sync}
        self.sem = {e: stack.enter_context(nc.semaphore("s_" + e)) for e in self.ENGS}
        self.cnt = {e: 0 for e in self.ENGS}
        self.waited = {e: {} for e in self.ENGS}
        self.last_w = {}
        self.readers = {}
        self.dsem = {}
        self.dcnt = {}

    def _h(self, s):
        return self.sem[s[1]] if s[0] == "e" else self.dsem[s[1]]

    def _wait(self, eng, s, v, raw=False):
        if s == ("e", eng) and (eng == "pe" or not raw):
            return
        if self.waited[eng].get(s, 0) >= v:
            return
        self.waited[eng][s] = v
        self.eng[eng].wait_ge(self._h(s), v)

    def _deps(self, eng, reads, writes):
        need = {}
        own = ("e", eng)

        def add(s, v, raw):
            if s == own and not raw:
                return
            if v > need.get(s, 0):
                need[s] = v
        for b in reads:
            w = self.last_w.get(b)
            if w is not None:
                add(w[0], w[1], True)
        for b in writes:
            w = self.last_w.get(b)
            if w is not None:
                add(w[0], w[1], False)
            for s, v in self.readers.get(b, {}).items():
                add(s, v, False)
        for s, v in need.items():
            self._wait(eng, s, v, raw=True)

    def _mark(self, tok, reads, writes):
        for b in writes:
            self.last_w[b] = tok
            self.readers[b] = {}
        for b in reads:
            d = self.readers.setdefault(b, {})
            if tok[1] > d.get(tok[0], 0):
                d[tok[0]] = tok[1]

    def op(self, eng, fn, reads=(), writes=(), inc=True):
        self._deps(eng, reads, writes)
        ins = fn(self.eng[eng])
        if inc:
            self.cnt[eng] += 1
            ins.then_inc(self.sem[eng], 1)
            tok = (("e", eng), self.cnt[eng])
        else:
            tok = (("e", eng), self.cnt[eng] + 1)
        self._mark(tok, reads, writes)
        return tok

    def dma(self, eng, key, fn, reads=(), writes=(), incv=16):
        if key not in self.dsem:
            self.dsem[key] = self.stack.enter_context(self.nc.semaphore("d_" + str(key)))
            self.dcnt[key] = 0
        self._deps(eng, reads, writes)
        self.dcnt[key] += incv
        fn(self.eng[eng]).then_inc(self.dsem[key], incv)
        tok = (("d", key), self.dcnt[key])
        self._mark(tok, reads, writes)
        return tok

    def wait_tok(self, eng, tok):
        self._wait(eng, tok[0], tok[1])

    def barrier(self):
        for e in self.ENGS:
            for e2 in self.ENGS:
                if e2 != e and self.cnt[e2] > 0:
                    self._wait(e, ("e", e2), self.cnt[e2])
            for k, v in self.dcnt.items():
                self._wait(e, ("d", k), v)
        self.last_w.clear()
        self.readers.clear()


_DTSZ = {F32: 4, I32: 4, BF16: 2}
SB_BYTES = 206 * 1024


class Arena:
    def __init__(self, handle, size):
        self.h = handle
        self.lo = 0
        self.hi = size

    def alloc(self, shape, dt, top=False):
        n = 1
        for d in shape[1:]:
            n *= d
        nb = (n * _DTSZ[dt] + 63) // 64 * 64
        if top:
            self.hi -= nb
            off = self.hi
        else:
            off = self.lo
            self.lo += nb
        assert self.lo <= self.hi, ("SBUF arena overflow", self.lo, self.hi)
        v = self.h[:, off:off + n * _DTSZ[dt]].bitcast(dt)
        names = "abcdefg"[:len(shape) - 1]
        if len(shape) > 2:
            v = v.rearrange("p (%s) -> p %s" % (" ".join(names), " ".join(names)),
                            **{k: d for k, d in zip(names[:-1], shape[1:-1])})
        if shape[0] < 128:
            v = v[0:shape[0]]
        return v

    def view(self, off, shape, dt):
        n = 1
        for d in shape[1:]:
            n *= d
        v = self.h[:, off:off + n * _DTSZ[dt]].bitcast(dt)
        names = "abcdefg"[:len(shape) - 1]
        if len(shape) > 2:
            v = v.rearrange("p (%s) -> p %s" % (" ".join(names), " ".join(names)),
                            **{k: d for k, d in zip(names[:-1], shape[1:-1])})
        return v

    def mark(self):
        return (self.lo, self.hi)

    def release(self, m):
        self.lo, self.hi = m


def build(dbg=False, stop_after=None):
    nc = bass.Bass("TRN2", target_bir_lowering=False)

    small = stop_after in ("p0", "a1", "a3", "a4")
    BIG = ("w_a_out", "glu_w", "glu_v", "w_out", "mlp_w_up", "mlp_w_down")

    def din(name, shape):
        if small and name in BIG:
            shape = [128, 128]
        return nc.dram_tensor(name, list(shape), F32, kind="ExternalInput").ap()

    def dscr(name, shape, dt):
        if dbg:
            return nc.dram_tensor(name, list(shape), dt, kind="ExternalOutput").ap()
        return nc.dram_tensor(name, list(shape), dt).ap()

    x_d = din("x", [4 * TOK + HALO, D])
    segm_d = din("segm", [128, 4])
    w_in_d = din("w_in", [D, DIN])
    conv_w_d = din("conv_w", [4, D])
    conv_b_d = din("conv_b", [D])
    rg_wa_d = din("rg_wa", [16, 128, 128])
    rg_ba_d = din("rg_ba", [16, 128])
    rg_wx_d = din("rg_wx", [16, 128, 128])
    rg_bx_d = din("rg_bx", [16, 128])
    rg_lam_d = din("rg_lambda", [D])
    w_a_d = din("w_a_out", [D, D])
    a_re_d = din("ssm_a_re", [64, 64])
    a_im_d = din("ssm_a_im", [64, 64])
    ldt_d = din("ssm_log_dt", [64])
    b_re_d = din("ssm_b_re", [64, 64, 16])
    b_im_d = din("ssm_b_im", [64, 64, 16])
    c_re_d = din("ssm_c_re", [64, 16, 64])
    c_im_d = din("ssm_c_im", [64, 16, 64])
    ssm_d_d = din("ssm_d", [64, 16])
    glu_w_d = din("glu_w", [1024, D])
    glu_v_d = din("glu_v", [1024, D])
    w_out_d = din("w_out", [D, D])
    ln1_g_d = din("ln1_g", [D])
    ln1_b_d = din("ln1_b", [D])
    w_up_d = din("mlp_w_up", [D, DFF])
    b_up_d = din("mlp_b_up", [DFF])
    w_dn_d = din("mlp_w_down", [DFF, D])
    b_dn_d = din("mlp_b_down", [D])
    ln2_g_d = din("ln2_g", [D])
    ln2_b_d = din("ln2_b", [D])
    out_d = nc.dram_tensor("out", [TOK, D], F32, kind="ExternalOutput").ap()

    ab_d = dscr("ab_d", [2, 16, 128, TOK], F32)
    xT_d = dscr("xT_d", [128, 16, TOK + HALO], BF16)
    yS_d = dscr("yS_d", [128, 8, TOK], BF16)
    mix_d = dscr("mix_d", [128, 16, TOK], BF16)
    W1_d = dscr("W1_d", [8, 128, 8192], BF16)
    W3_d = dscr("W3_d", [8, 128, 8192], BF16)
    KT_d = dscr("KT_d", [8, 128, 1024], BF16)
    S_d3 = nc.dram_tensor("S_d3", [3, 128, 16384], F32).ap()
    ccin_d = nc.dram_tensor("ccin_d", [128, 96], F32).ap()
    ccout_d = nc.dram_tensor("ccout_d", [NCORE * 128, 96], F32).ap()
    if dbg:
        dbg_small = nc.dram_tensor("dbg_small", [128, 256], F32, kind="ExternalOutput").ap()
        dbg_x1 = nc.dram_tensor("dbg_x1", [1024, D], F32, kind="ExternalOutput").ap()
        dbg_sc = nc.dram_tensor("dbg_sc", [128, 24 * 32], F32, kind="ExternalOutput").ap()
        dbg_lp = nc.dram_tensor("dbg_lp", [128, 9 * 4 * 32], F32, kind="ExternalOutput").ap()
        dbg_p3 = nc.dram_tensor("dbg_p3", [128, 104], F32, kind="ExternalOutput").ap()
        dbg_p1 = nc.dram_tensor("dbg_p1", [128, 128], F32, kind="ExternalOutput").ap()

    w_in_v = w_in_d.rearrange("(kt p) c -> p kt c", p=128)
    if not small:
        w_a_v = w_a_d.rearrange("(kt p) c -> p kt c", p=128)
        glu_w_v = glu_w_d.rearrange("(kt p) c -> p kt c", p=128)
        glu_v_v = glu_v_d.rearrange("(kt p) c -> p kt c", p=128)
        w_out_v = w_out_d.rearrange("(kt p) c -> p kt c", p=128)
        w_up_v = w_up_d.rearrange("(kt p) c -> p kt c", p=128)
        w_dn_v = w_dn_d.rearrange("(ft p) c -> p ft c", p=128)

    with ExitStack() as top:
        S = Sched(nc, top)
        ccsem = top.enter_context(nc.semaphore("ccsem"))
        arena_t = top.enter_context(nc.sbuf_tensor("arena", [128, SB_BYTES], mybir.dt.uint8))
        A = Arena(arena_t, SB_BYTES)

        def sbt(name, shape, dt, top_=False):
            return A.alloc(list(shape), dt, top=top_)

        def end_phase(m):
            S.barrier()
            A.release(m)

        ps = [top.enter_context(nc.psum_tensor(f"ps{i}", [128, 512], F32)) for i in range(8)]
        bank_ctr = [0]

        def nextbank():
            b = bank_ctr[0] % 8
            bank_ctr[0] += 1
            return b

        ev_ctr = [0]

        def ev_eng():
            ev_ctr[0] += 1
            return "act" if ev_ctr[0] % 2 == 0 else "dve"

        def copy_op(eng, out, in_):
            if eng == "act":
                return lambda e: e.copy(out, in_)
            return lambda e: e.tensor_copy(out, in_)

        ident = sbt("ident", [128, 128], F32)
        P1 = sbt("P1", [128, 128], F32)
        bup = sbt("bup", [128, 64], F32)
        rgc = sbt("rgc", [128, 8, 16], F32)
        wa_sb = sbt("wa_sb", [128, 16, 128], BF16)
        wx_sb = sbt("wx_sb", [128, 16, 128], BF16)
        Ecur = sbt("Ecur", [128, 16], F32)
        sumth = sbt("sumth", [128, 16, 4], F32)
        hcar = sbt("hcar", [128, 16], F32)
        Hin = sbt("Hin", [128, 2, 32], F32)
        LL8 = sbt("LL8", [128, 4, 32], F32)
        D2k = sbt("D2k", [128, 2, 32], F32)
        ccin = sbt("ccin", [128, 96], F32)
        segm = sbt("segm", [128, 4], F32)
        Hs = sbt("Hs", [128, 2, 32], F32)
        Xp = sbt("Xp", [128, 2, 32], F32)
        Yp = sbt("Yp", [128, 2, 32], F32)
        Xd = sbt("Xd", [128, 2, 32], F32)
        Yd = sbt("Yd", [128, 2, 32], F32)
        Ha = sbt("Ha", [128, 2, 32], F32)
        H3 = sbt("H3", [128, 3, 2, 32], F32)
        X3 = sbt("X3", [128, 3, 2, 32], F32)
        Y3 = sbt("Y3", [128, 3, 2, 32], F32)
        D2kL = sbt("D2kL", [128, 4, 32], F32)
        stats = sbt("stats", [128, 8, 4, 6], F32)
        mv = sbt("mv", [128, 8, 4], F32)

        S.op("pool", lambda e: e.memset(ident[:], 1.0), writes=["ident"])
        S.op("pool", lambda e: e.affine_select(out=ident[:], in_=ident[:], pattern=[[-1, 128]],
                                               compare_op=ALU.is_equal, fill=0.0, base=0, channel_multiplier=1),
             reads=["ident"], writes=["ident"])
        S.dma("sp", "segm", lambda e: e.dma_start(out=segm[:], in_=segm_d), writes=["segm"])
        S.dma("pool", "wa_sb", lambda e: e.dma_start(out=wa_sb[:], in_=rg_wa_d.rearrange("h i j -> i h j")), writes=["wa_sb"])
        S.dma("pool", "wx_sb", lambda e: e.dma_start(out=wx_sb[:], in_=rg_wx_d.rearrange("h i j -> i h j")), writes=["wx_sb"])

        m0 = A.mark()
        if True:
            st1 = sbt("st1", [128, 128], F32)
            st2 = sbt("st2", [64, 128], F32)
            st3 = sbt("st3", [104, 128], F32)
            ld2 = sbt("ld2", [32, 2], F32)
            P3 = sbt("P3", [128, 104], F32)
            S.dma("sp", "st1a", lambda e: e.dma_start(out=st1[0:64, :], in_=conv_w_d.rearrange("k (h p) -> (k h) p", p=128)), writes=["st1a"])
            S.dma("sp", "st1b", lambda e: e.dma_start(out=st1[64:80, :], in_=conv_b_d.rearrange("(h p) -> h p", p=128)), writes=["st1b"])
            S.dma("sp", "st1c", lambda e: e.dma_start(out=st1[80:96, :], in_=rg_ba_d), writes=["st1c"])
            S.dma("sp", "st1d", lambda e: e.dma_start(out=st1[96:112, :], in_=rg_bx_d), writes=["st1d"])
            S.dma("sp", "st1e", lambda e: e.dma_start(out=st1[112:128, :], in_=rg_lam_d.rearrange("(h p) -> h p", p=128)), writes=["st1e"])
            S.dma("sp", "st2", lambda e: e.dma_start(out=st2[:], in_=b_up_d.rearrange("(f p) -> f p", p=128)), writes=["st2"])
            S.dma("sp", "ld2", lambda e: e.dma_start(out=ld2[:], in_=ldt_d.rearrange("(j g) -> j g", g=2)), writes=["ld2"])
            S.dma("sp", "st3b", lambda e: e.dma_start(out=st3[32:64, :], in_=a_re_d.rearrange("(j g) p -> j (g p)", g=2)), writes=["st3b"])
            S.dma("sp", "st3c", lambda e: e.dma_start(out=st3[64:96, :], in_=a_im_d.rearrange("(j g) p -> j (g p)", g=2)), writes=["st3c"])
            S.dma("sp", "st3d", lambda e: e.dma_start(out=st3[96:104, :], in_=ssm_d_d.rearrange("(kt g) h -> kt (g h)", g=8)), writes=["st3d"])
            S.op("dve", lambda e: e.tensor_copy(st3[0:32, :].rearrange("j (g p) -> j g p", g=2),
                                                ld2[:, :].unsqueeze(2).to_broadcast([32, 2, 64])),
                 reads=["ld2"], writes=["st3a"])
            b = nextbank()
            S.op("pe", lambda e: e.transpose(ps[b][:, 0:128], st1[:, :], ident[:, :]),
                 reads=["st1a", "st1b", "st1c", "st1d", "st1e", "ident"], writes=[f"ps{b}"])
            S.op("dve", lambda e: e.tensor_copy(P1[:], ps[b][:, 0:128]), reads=[f"ps{b}"], writes=["P1"])
            b = nextbank()
            S.op("pe", lambda e: e.transpose(ps[b][:, 0:64], st2[:, :], ident[0:64, 0:64]),
                 reads=["st2", "ident"], writes=[f"ps{b}"])
            S.op("dve", lambda e: e.tensor_copy(bup[:], ps[b][:, 0:64]), reads=[f"ps{b}"], writes=["bup"])
            b = nextbank()
            S.op("pe", lambda e: e.transpose(ps[b][:, 0:104], st3[:, :], ident[0:104, 0:104]),
                 reads=["st3a", "st3b", "st3c", "st3d", "ident"], writes=[f"ps{b}"])
            S.op("dve", lambda e: e.tensor_copy(P3[:], ps[b][:, 0:104]), reads=[f"ps{b}"], writes=["P3"])

            ba_v = P1[:, 80:96]
            bx_v = P1[:, 96:112]
            lam_v = P1[:, 112:128]
            S.op("dve", lambda e: e.tensor_scalar(rgc[:, 0, :], ba_v, 0.5, None, ALU.mult), reads=["P1"], writes=["rgc0"])
            S.op("dve", lambda e: e.tensor_scalar(rgc[:, 1, :], bx_v, 0.5, None, ALU.mult), reads=["P1"], writes=["rgc1"])
            S.op("act", lambda e: e.activation(rgc[:, 5, :], lam_v, AF.Exp, scale=-1.0), reads=["P1"], writes=["rgc5"])
            S.op("act", lambda e: e.activation(rgc[:, 5, :], rgc[:, 5, :], AF.Ln, bias=1.0), reads=["rgc5"], writes=["rgc5"])
            S.op("dve", lambda e: e.tensor_scalar(rgc[:, 2, :], rgc[:, 5, :], -8.0, None, ALU.mult), reads=["rgc5"], writes=["rgc2"])
            S.op("dve", lambda e: e.tensor_scalar(rgc[:, 3, :], rgc[:, 5, :], -4.0, None, ALU.mult), reads=["rgc5"], writes=["rgc3"])
            S.op("dve", lambda e: e.tensor_scalar(rgc[:, 4, :], rgc[:, 5, :], 4.0, None, ALU.mult), reads=["rgc5"], writes=["rgc4"])

            sc = sbt("sc", [128, 24, 32], F32)
            Lp = sbt("Lp", [128, 9, 4, 32], F32)
            isc = sbt("isc", [128, 32], I32)

            def V(i):
                return sc[:, i, :]

            def dv(fn):
                S.op("dve", fn, reads=["sc", "P3"], writes=["sc"])

            def av(fn):
                S.op("act", fn, reads=["sc", "P3"], writes=["sc"])
            ldt_v, are_v, aim_v = P3[:, 0:32], P3[:, 32:64], P3[:, 64:96]
            av(lambda e: e.activation(V(0), ldt_v, AF.Exp))
            dv(lambda e: e.tensor_scalar(V(1), are_v, -1e-4, None, ALU.min))
            dv(lambda e: e.tensor_tensor(V(2), V(1), V(0), ALU.mult))
            av(lambda e: e.activation(V(3), V(2), AF.Exp))
            dv(lambda e: e.tensor_tensor(V(4), aim_v, V(0), ALU.mult))

            def reduced_sin(dst, shift):
                dv(lambda e: e.tensor_scalar(V(5), V(4), shift, None, ALU.add))
                dv(lambda e: e.tensor_scalar(V(6), V(5), 1.0 / TWO_PI, None, ALU.mult))
                dv(lambda e: e.tensor_copy(isc[:], V(6)))
                dv(lambda e: e.tensor_copy(V(6), isc[:]))
                dv(lambda e: e.scalar_tensor_tensor(V(5), V(6), -TWO_PI, V(5), ALU.mult, ALU.add))
                dv(lambda e: e.tensor_scalar(V(7), V(5), PI, None, ALU.is_gt))
                dv(lambda e: e.scalar_tensor_tensor(V(5), V(7), -TWO_PI, V(5), ALU.mult, ALU.add))
                dv(lambda e: e.tensor_scalar(V(7), V(5), -PI, None, ALU.is_lt))
                dv(lambda e: e.scalar_tensor_tensor(V(5), V(7), TWO_PI, V(5), ALU.mult, ALU.add))
                dv(lambda e: e.tensor_scalar(V(5), V(5), PI, -PI, ALU.min, ALU.max))
                av(lambda e: e.activation(dst, V(5), AF.Sin))
            reduced_sin(V(9), 0.0)
            reduced_sin(V(10), PI / 2)
            dv(lambda e: e.tensor_tensor(V(11), V(3), V(10), ALU.mult))
            dv(lambda e: e.tensor_tensor(V(12), V(3), V(9), ALU.mult))
            dv(lambda e: e.tensor_tensor(V(13), V(1), V(1), ALU.mult))
            dv(lambda e: e.tensor_tensor(V(5), aim_v, aim_v, ALU.mult))
            dv(lambda e: e.tensor_tensor(V(13), V(13), V(5), ALU.add))
            dv(lambda e: e.reciprocal(V(13), V(13)))
            dv(lambda e: e.tensor_scalar(V(14), V(11), -1.0, None, ALU.add))
            dv(lambda e: e.tensor_tensor(V(5), V(14), V(1), ALU.mult))
            dv(lambda e: e.tensor_tensor(V(6), V(12), aim_v, ALU.mult))
            dv(lambda e: e.tensor_tensor(V(5), V(5), V(6), ALU.add))
            dv(lambda e: e.tensor_tensor(V(15), V(5), V(13), ALU.mult))
            dv(lambda e: e.tensor_tensor(V(5), V(12), V(1), ALU.mult))
            dv(lambda e: e.tensor_tensor(V(6), V(14), aim_v, ALU.mult))
            dv(lambda e: e.tensor_tensor(V(5), V(5), V(6), ALU.subtract))
            dv(lambda e: e.tensor_tensor(V(16), V(5), V(13), ALU.mult))

            def lp(fn):
                S.op("dve", fn, reads=["sc", "Lp"], writes=["Lp", "sc"])
            lp(lambda e: e.memset(Lp[:, 0, 0, :], 1.0))
            lp(lambda e: e.memset(Lp[:, 0, 1, :], 0.0))
            lp(lambda e: e.tensor_copy(Lp[:, 1, 0, :], V(11)))
            lp(lambda e: e.tensor_copy(Lp[:, 1, 1, :], V(12)))

            def cmul(o_re, o_im, a_re_, a_im_, b_re_, b_im_, t1, t2, t3):
                lp(lambda e: e.tensor_tensor(t1, a_re_, b_re_, ALU.mult))
                lp(lambda e: e.tensor_tensor(t2, a_im_, b_im_, ALU.mult))
                lp(lambda e: e.tensor_tensor(t1, t1, t2, ALU.subtract))
                lp(lambda e: e.tensor_tensor(t2, a_re_, b_im_, ALU.mult))
                lp(lambda e: e.tensor_tensor(t3, a_im_, b_re_, ALU.mult))
                lp(lambda e: e.tensor_tensor(o_im, t3, t2, ALU.add))
                lp(lambda e: e.tensor_copy(o_re, t1))
            for n in range(1, 8):
                cmul(Lp[:, n + 1, 0, :], Lp[:, n + 1, 1, :], Lp[:, n, 0, :], Lp[:, n, 1, :],
                     Lp[:, 1, 0, :], Lp[:, 1, 1, :], V(17), V(18), V(21))
            for n in range(9):
                lp(lambda e, n=n: e.tensor_scalar(Lp[:, n, 2:4, :], Lp[:, n, 0:2, :], -1.0, None, ALU.mult))
            S.op("dve", lambda e: e.tensor_copy(LL8[:, 0, :], Lp[:, 8, 0, :]), reads=["Lp"], writes=["LL8"])
            S.op("dve", lambda e: e.tensor_copy(LL8[:, 1, :], Lp[:, 8, 0, :]), reads=["Lp"], writes=["LL8"])
            S.op("dve", lambda e: e.tensor_copy(LL8[:, 2, :], Lp[:, 8, 3, :]), reads=["Lp"], writes=["LL8"])
            S.op("dve", lambda e: e.tensor_copy(LL8[:, 3, :], Lp[:, 8, 1, :]), reads=["Lp"], writes=["LL8"])
            lp(lambda e: e.tensor_copy(V(19), Lp[:, 8, 0, :]))
            lp(lambda e: e.tensor_copy(V(20), Lp[:, 8, 1, :]))
            for _ in range(8):
                cmul(V(19), V(20), V(19), V(20), V(19), V(20), V(17), V(18), V(21))
            S.op("dve", lambda e: e.tensor_copy(D2kL[:, 0, :], V(19)), reads=["Lp", "sc"], writes=["D2kL"])
            S.op("dve", lambda e: e.tensor_copy(D2kL[:, 1, :], V(19)), reads=["Lp", "sc"], writes=["D2kL"])
            S.op("dve", lambda e: e.tensor_scalar(D2kL[:, 2, :], V(20), -1.0, None, ALU.mult), reads=["Lp", "sc"], writes=["D2kL"])
            S.op("dve", lambda e: e.tensor_copy(D2kL[:, 3, :], V(20)), reads=["Lp", "sc"], writes=["D2kL"])
            S.op("dve", lambda e: e.tensor_copy(D2k[:, 0, :], V(19)), reads=["Lp", "sc"], writes=["D2k"])
            S.op("dve", lambda e: e.tensor_copy(D2k[:, 1, :], V(20)), reads=["Lp", "sc"], writes=["D2k"])

            if dbg:
                S.dma("sp", "dbgsc", lambda e: e.dma_start(out=dbg_sc, in_=sc[:].rearrange("p a b -> p (a b)")), reads=["sc", "Lp"])
                S.dma("sp", "dbglp", lambda e: e.dma_start(out=dbg_lp, in_=Lp[:].rearrange("p a b c -> p (a b c)")), reads=["sc", "Lp"])
                S.dma("sp", "dbgp3", lambda e: e.dma_start(out=dbg_p3, in_=P3[:]), reads=["P3"])
                S.dma("sp", "dbgp1", lambda e: e.dma_start(out=dbg_p1, in_=P1[:]), reads=["P1"])
            Bn = sbt("Bn", [128, 2, 32, 16], F32)
            Bb = sbt("Bb", [128, 2, 32, 16], F32)
            Vn = sbt("Vn", [128, 8, 2, 32, 16], F32)
            tb = sbt("tb", [128, 32, 16], F32)
            S.dma("sp", "Bn0", lambda e: e.dma_start(out=Bn[:, 0, :, :], in_=b_re_d.rearrange("(j g) p h -> (g p) j h", g=2)), writes=["Bn0"])
            S.dma("sp", "Bn1", lambda e: e.dma_start(out=Bn[:, 1, :, :], in_=b_im_d.rearrange("(j g) p h -> (g p) j h", g=2)), writes=["Bn1"])

            def bc(v):
                return v.unsqueeze(2).to_broadcast([128, 32, 16])

            def bb(fn):
                S.op("dve", fn, reads=["sc", "Lp", "Bn0", "Bn1", "Bb", "tb"], writes=["Bb", "tb"])
            bb(lambda e: e.tensor_tensor(Bb[:, 0], Bn[:, 0], bc(V(15)), ALU.mult))
            bb(lambda e: e.tensor_tensor(tb[:], Bn[:, 1], bc(V(16)), ALU.mult))
            bb(lambda e: e.tensor_tensor(Bb[:, 0], Bb[:, 0], tb[:], ALU.subtract))
            bb(lambda e: e.tensor_tensor(Bb[:, 1], Bn[:, 1], bc(V(15)), ALU.mult))
            bb(lambda e: e.tensor_tensor(tb[:], Bn[:, 0], bc(V(16)), ALU.mult))
            bb(lambda e: e.tensor_tensor(Bb[:, 1], Bb[:, 1], tb[:], ALU.add))

            def vv(fn):
                S.op("dve", fn, reads=["Bb", "Lp", "Vn", "tb"], writes=["Vn", "tb"])
            for n in range(8):
                vv(lambda e, n=n: e.tensor_tensor(Vn[:, n, 0], Bb[:, 0], bc(Lp[:, n, 0, :]), ALU.mult))
                vv(lambda e, n=n: e.tensor_tensor(tb[:], Bb[:, 1], bc(Lp[:, n, 1, :]), ALU.mult))
                vv(lambda e, n=n: e.tensor_tensor(Vn[:, n, 0], Vn[:, n, 0], tb[:], ALU.subtract))
                vv(lambda e, n=n: e.tensor_tensor(Vn[:, n, 1], Bb[:, 1], bc(Lp[:, n, 0, :]), ALU.mult))
                vv(lambda e, n=n: e.tensor_tensor(tb[:], Bb[:, 0], bc(Lp[:, n, 1, :]), ALU.mult))
                vv(lambda e, n=n: e.tensor_tensor(Vn[:, n, 1], Vn[:, n, 1], tb[:], ALU.add))

            Cn = sbt("Cn", [128, 2, 8, 64], F32)
            par = sbt("par", [128, 2], F32)
            Xc = sbt("Xc", [128, 2, 128], F32)
            Yc = sbt("Yc", [128, 3, 8, 128], F32)
            S.dma("sp", "Cn0", lambda e: e.dma_start(out=Cn[:, 0, :, :], in_=c_re_d.rearrange("(kt g) h p -> (g h) kt p", g=8)), writes=["Cn0"])
            S.dma("sp", "Cn1", lambda e: e.dma_start(out=Cn[:, 1, :, :], in_=c_im_d.rearrange("(kt g) h p -> (g h) kt p", g=8)), writes=["Cn1"])
            Mp = sbt("Mp", [128, 128], F32)
            S.op("dve", lambda e: e.memset(Mp[:], 0.0), writes=["Mp"])
            for blk_ in range(4):
                S.op("dve", lambda e, blk_=blk_: e.memset(Mp[:, 32 * blk_ + 16:32 * blk_ + 32], 1.0), reads=["Mp"], writes=["Mp"])
            b = nextbank()
            S.op("pe", lambda e: e.transpose(ps[b][:, 0:128], Mp[:, :], ident[:, :]), reads=["Mp", "ident"], writes=[f"ps{b}"])
            S.op("dve", lambda e: e.tensor_copy(par[:, 0:1], ps[b][:, 0:1]), reads=[f"ps{b}"], writes=["par"])
            S.op("dve", lambda e: e.tensor_scalar(par[:, 1:2], par[:, 0:1], -1.0, 1.0, ALU.mult, ALU.add), reads=["par"], writes=["par"])
            for kt in range(8):
                for ri in range(2):
                    S.op("dve", lambda e, kt=kt, ri=ri: e.tensor_scalar(Xc[:, ri, 0:64], Cn[:, ri, kt, :], par[:, 1:2], None, ALU.mult),
                         reads=["Cn0", "Cn1", "par", "Xc%d" % ri], writes=["Xc%d" % ri])
                    S.op("dve", lambda e, kt=kt, ri=ri: e.tensor_scalar(Xc[:, ri, 64:128], Cn[:, ri, kt, :], par[:, 0:1], None, ALU.mult),
                         reads=["Cn0", "Cn1", "par", "Xc%d" % ri], writes=["Xc%d" % ri])
                    b = nextbank()
                    S.op("pe", lambda e, ri=ri, b=b: e.transpose(ps[b][:, 0:128], Xc[:, ri, :], ident[:, :]),
                         reads=["Xc%d" % ri, "ident"], writes=[f"ps{b}"])
                    S.op("act", lambda e, kt=kt, ri=ri, b=b: e.copy(Yc[:, ri, kt, :], ps[b][:, 0:128]),
                         reads=[f"ps{b}"], writes=["Yc"])
                    if ri == 1:
                        S.op("act", lambda e, kt=kt, b=b: e.mul(Yc[:, 2, kt, :], ps[b][:, 0:128], -1.0),
                             reads=[f"ps{b}"], writes=["Yc"])

            W3b = [sbt(f"W3b{i}", [128, 4, 8, 2, 128], BF16) for i in range(2)]
            W1b = [sbt(f"W1b{i}", [128, 4, 8, 2, 128], BF16) for i in range(2)]
            KTb = [sbt(f"KTb{i}", [128, 8, 128], BF16) for i in range(2)]
            Zf = [sbt(f"Zb{i}", [128, 1280], F32) for i in range(2)]
            Zb = [z[:, 0:1024].rearrange("p (q r c) -> p q r c", q=4, r=2) for z in Zf]
            Zd = [z[:, 0:1152].rearrange("p (q x) -> p q x", x=288)[:, :, 0:256].rearrange("p q (r c) -> p q r c", r=2) for z in Zf]
            w3t = sbt("w3t", [128, 32], F32)
            dI = sbt("dI", [128, 128], F32)
            for i in range(2):
                S.op("pool", lambda e, i=i: e.memset(W3b[i][:], 0.0), writes=[f"W3b{i}", f"W3b{i}a"])
                S.op("pool", lambda e, i=i: e.memset(Zf[i][:], 0.0), writes=[f"Zb{i}"])
            zc = 0
            for kt in range(8):
                w3 = W3b[kt % 2]
                w1 = W1b[kt % 2]
                ktb = KTb[kt % 2]
                k3, k1, kk = f"W3b{kt%2}", f"W1b{kt%2}", f"KTb{kt%2}"
                for q in range(4):
                    j = 4 * kt + q
                    cs = slice(32 * q, 32 * q + 32)
                    for tau in range(8):
                        n = tau + 1
                        S.op("dve", lambda e, n=n, j=j, cs=cs, w3=w3, q=q, tau=tau: e.tensor_scalar(
                            w3[:, q, tau, 0, cs], Yc[:, 0, kt, cs], Lp[:, n, 0, j:j + 1], None, ALU.mult), reads=["Yc", "Lp"], writes=[k3 + "a"])
                        S.op("dve", lambda e, n=n, j=j, cs=cs, w3=w3, q=q, tau=tau: e.tensor_scalar(
                            w3[:, q, tau, 1, cs], Yc[:, 0, kt, cs], Lp[:, n, 3, j:j + 1], None, ALU.mult), reads=["Yc", "Lp"], writes=[k3 + "a"])
                for q in range(4):
                    j = 4 * kt + q
                    cs = slice(32 * q, 32 * q + 32)
                    for tau in range(8):
                        n = tau + 1
                        S.op("dve", lambda e, n=n, j=j, cs=cs, w3=w3, q=q, tau=tau: e.scalar_tensor_tensor(
                            w3[:, q, tau, 0, cs], Yc[:, 1, kt, cs], Lp[:, n, 3, j:j + 1], w3[:, q, tau, 0, cs], ALU.mult, ALU.add),
                            reads=["Yc", "Lp", k3 + "a"], writes=[k3])
                        S.op("dve", lambda e, n=n, j=j, cs=cs, w3=w3, q=q, tau=tau: e.scalar_tensor_tensor(
                            w3[:, q, tau, 1, cs], Yc[:, 1, kt, cs], Lp[:, n, 2, j:j + 1], w3[:, q, tau, 1, cs], ALU.mult, ALU.add),
                            reads=["Yc", "Lp", k3 + "a"], writes=[k3])
                S.dma("sp", k3 + "st", lambda e, kt=kt, w3=w3: e.dma_start(out=W3_d[kt], in_=w3[:].rearrange("p a b c d -> p (a b c d)")),
                      reads=[k3, k3 + "a"], writes=[f"W3d{kt}"])
                S.op("dve", lambda e, kt=kt: e.tensor_scalar(dI[:], ident[:], P3[:, 96 + kt:97 + kt], None, ALU.mult),
                     reads=["ident", "P3", "dI"], writes=["dI"])
                for n in range(8):
                    zi = zc % 2
                    zb, zd = Zb[zi], Zd[zi]
                    kz = f"Zb{zi}"
                    zc += 1
                    S.op("pool", lambda e, zd=zd, n=n: e.tensor_copy(
                        zd[0:64, :, :, 0:16], Vn[0:64, n, :, 4 * kt:4 * kt + 4, :].rearrange("p r q h -> p q r h")),
                        reads=["Vn", kz], writes=[kz])
                    S.op("pool", lambda e, zd=zd, n=n: e.tensor_copy(
                        zd[64:128, :, :, 16:32], Vn[64:128, n, :, 4 * kt:4 * kt + 4, :].rearrange("p r q h -> p q r h")),
                        reads=["Vn", kz], writes=[kz])
                    for q in range(4):
                        for ri in range(2):
                            b = nextbank()
                            S.op("pe", lambda e, zb=zb, q=q, ri=ri, b=b: e.transpose(ps[b][:, 0:128], zb[:, q, ri, :], ident[:, :]),
                                 reads=[kz, "ident"], writes=[f"ps{b}"])
                            eng = ev_eng()
                            S.op(eng, copy_op(eng, w1[:, q, 7 - n, ri, :], ps[b][:, 0:128]), reads=[f"ps{b}"], writes=[k1])
                    b = nextbank()
                    for q in range(4):
                        cs = slice(32 * q, 32 * q + 32)
                        S.op("pe", lambda e, zb=zb, q=q, cs=cs, b=b: e.matmul(ps[b][:, cs], zb[:, q, 0, :], Yc[:, 0, kt, cs], start=True, stop=False),
                             reads=[kz, "Yc"], writes=[f"ps{b}"], inc=False)
                        S.op("pe", lambda e, zb=zb, q=q, cs=cs, b=b: e.matmul(ps[b][:, cs], zb[:, q, 1, :], Yc[:, 2, kt, cs], start=False, stop=True),
                             reads=[kz, "Yc"], writes=[f"ps{b}"], inc=(q == 3))
                    if n == 0:
                        S.op("dve", lambda e, ktb=ktb, b=b: e.tensor_tensor(ktb[:, 0, :], ps[b][:, 0:128], dI[:], ALU.add),
                             reads=[f"ps{b}", "dI"], writes=[kk])
                    else:
                        S.op("act", lambda e, ktb=ktb, b=b, n=n: e.copy(ktb[:, n, :], ps[b][:, 0:128]),
                             reads=[f"ps{b}"], writes=[kk])
                S.dma("sp", k1 + "st", lambda e, kt=kt, w1=w1: e.dma_start(out=W1_d[kt], in_=w1[:].rearrange("p a b c d -> p (a b c d)")),
                      reads=[k1], writes=[f"W1d{kt}"])
                S.dma("sp", kk + "st", lambda e, kt=kt, ktb=ktb: e.dma_start(out=KT_d[kt], in_=ktb[:].rearrange("p a b -> p (a b)")),
                      reads=[kk], writes=[f"KTd{kt}"])
        end_phase(m0)
        if stop_after == "p0":
            return nc

        mA = A.mark()
        S.op("dve", lambda e: e.memset(Ecur[:], 0.0), writes=["Ecur"])
        u_holder = {}
        scan_hook = {"f": None}

        def run_segment(seg, mode):
            own = (mode == "own")
            xo = seg * TOK
            if mode in ("rg", "own"):
                S.op("dve", lambda e: e.tensor_scalar(Ecur[:], Ecur[:], segm[:, seg:seg + 1], None, ALU.mult), reads=["Ecur", "segm"], writes=["Ecur"])
            if own:
                S.op("dve", lambda e: e.tensor_copy(hcar[:], Ecur[:]), reads=["Ecur"], writes=["hcar"])
            mA1 = A.mark()
            if mode == "s5prep":
                u_sb = sbt("u_sb", [128, 8, 8, 256], BF16)
            else:
                u_sb = u_holder.get("u")
            mU = A.mark()
            xT = sbt("xT", [128, 16, TOK + HALO], BF16)
            mx = A.mark()
            xin = [sbt(f"xin{i}", [128, D], F32) for i in range(3)]
            tiles = [(0, 3, 0)] + [(3 + 128 * t, 128, 3 + 128 * t) for t in range(16)]
            for ti, (r0, nr, c0) in enumerate(tiles):
                xb = xin[ti % 3]
                key = f"xin{ti%3}"
                S.dma("sp", key, lambda e, xb=xb, r0=r0, nr=nr: e.dma_start(out=xb[0:nr, :], in_=x_d[xo + r0:xo + r0 + nr, :]), writes=[key])
                for b4 in range(4):
                    b = nextbank()
                    for jj in range(4):
                        kt = 4 * b4 + jj
                        S.op("pe", lambda e, xb=xb, nr=nr, kt=kt, jj=jj, b=b: e.transpose(
                            ps[b][:, jj * 128:jj * 128 + nr], xb[0:nr, kt * 128:(kt + 1) * 128], ident[0:nr, 0:nr]),
                            reads=[key, "ident"], writes=[f"ps{b}"], inc=(jj == 3))
                    eng = ev_eng()
                    S.op(eng, copy_op(eng, xT[:, 4 * b4:4 * b4 + 4, c0:c0 + nr],
                                      ps[b][:, 0:512].rearrange("p (j t) -> p j t", j=4)[:, :, 0:nr]),
                         reads=[f"ps{b}"], writes=[f"xT{ti}_{b4}"])
            allx = [f"xT{ti}_{b4}" for ti in range(17) for b4 in range(4)]
            if own:
                S.dma("sp", "xTst", lambda e: e.dma_start(out=xT_d, in_=xT[:]), reads=allx, writes=["xTd"])
            S.barrier()
            A.release(mx)

            def alloc_wb():
                return [sbt(f"wb{i}", [128, 16, 512], BF16) for i in range(2)]

            def do_A1(scan_emit):
                NH = 1024
                bufs = []
                for i in range(2):
                    bufs.append(dict(
                        xr=sbt(f"xr{i}", [128, NH + 3], F32), xc=sbt(f"xc{i}", [128, NH], F32),
                        xcb=sbt(f"xcb{i}", [128, NH], BF16), thr=sbt(f"thr{i}", [128, NH], F32),
                        thi=sbt(f"thi{i}", [128, NH], F32), at=sbt(f"at{i}", [128, NH], F32),
                        tt=sbt(f"tt{i}", [128, NH], F32)))

                def stageA(u):
                    h, hf = u // 2, u % 2
                    B_ = bufs[u % 2]
                    sx = str(u % 2)
                    if hf == 0 and h % 4 == 0 and h // 4 + 1 < 4:
                        load_w((h // 4 + 1) % 2, w_in_v, 512 * (h // 4 + 1))
                    wbi = (h // 4) % 2
                    hh = h % 4
                    cb0 = NH * hf
                    pieces = [(cb0, 3, 0), (cb0 + 3, 512, 3), (cb0 + 515, 512, 515)]
                    for (c0, n, off) in pieces:
                        b = nextbank()
                        for kt in range(16):
                            S.op("pe", lambda e: e.matmul(
                                ps[b][:, 0:n], wb[wbi][:, kt, hh * 128:(hh + 1) * 128], xT[:, kt, c0:c0 + n],
                                start=(kt == 0), stop=(kt == 15)),
                                reads=[f"wb{wbi}"], writes=[f"ps{b}"], inc=(kt == 15))
                        S.op("act", lambda e: e.copy(B_["xr"][:, off:off + n], ps[b][:, 0:n]),
                             reads=[f"ps{b}"], writes=["xr" + sx + "_%d" % off])
                    xrk = ["xr" + sx + "_%d" % o for o in (0, 3, 515)]
                    S.op("dve", lambda e: e.tensor_scalar(B_["xc"][:], B_["xr"][:, 0:NH], P1[:, h:h + 1], P1[:, 64 + h:65 + h], ALU.mult, ALU.add),
                         reads=xrk + ["xc" + sx, "xcb" + sx], writes=["xc" + sx])
                    for k in range(1, 4):
                        S.op("dve", lambda e: e.scalar_tensor_tensor(
                            B_["xc"][:], B_["xr"][:, k:k + NH], P1[:, k * 16 + h:k * 16 + h + 1], B_["xc"][:], ALU.mult, ALU.add),
                            reads=xrk + ["xc" + sx], writes=["xc" + sx])

                def stageA2(u):
                    h, hf = u // 2, u % 2
                    B_ = bufs[u % 2]
                    sx = str(u % 2)
                    S.op("act", lambda e: e.copy(B_["xcb"][:], B_["xc"][:]), reads=["xc" + sx], writes=["xcb" + sx])
                    for g, (wsb, wk, dst) in enumerate(((wa_sb, "wa_sb", "thr"), (wx_sb, "wx_sb", "thi"))):
                        for t2 in range(2):
                            b = nextbank()
                            S.op("pe", lambda e: e.matmul(
                                ps[b][:, :], wsb[:, h, :], B_["xcb"][:, t2 * 512:(t2 + 1) * 512], start=True, stop=True),
                                reads=[wk, "xcb" + sx], writes=[f"ps{b}"])
                            S.op("act", lambda e: e.activation(
                                B_[dst][:, t2 * 512:(t2 + 1) * 512], ps[b][:, :], AF.Tanh, bias=rgc[:, g, h:h + 1], scale=0.5),
                                reads=[f"ps{b}"], writes=[dst + sx + "_%d" % t2])

                def stageB(u):
                    h, hf = u // 2, u % 2
                    B_ = bufs[u % 2]
                    sx = str(u % 2)
                    thrk = ["thr" + sx + "_0", "thr" + sx + "_1"]
                    thik = ["thi" + sx + "_0", "thi" + sx + "_1"]
                    S.op("act", lambda e: e.activation(B_["at"][:], B_["thr"][:], AF.Exp, bias=rgc[:, 3, h:h + 1], scale=rgc[:, 3, h:h + 1]),
                         reads=thrk, writes=["at" + sx])
                    S.op("act", lambda e: e.activation(B_["tt"][:], B_["thr"][:], AF.Tanh, bias=rgc[:, 4, h:h + 1], scale=rgc[:, 4, h:h + 1]),
                         reads=thrk, writes=["tt" + sx])
                    S.op("act", lambda e: e.activation(B_["thr"][:], B_["thr"][:], AF.Exp, bias=rgc[:, 2, h:h + 1], scale=rgc[:, 2, h:h + 1]),
                         reads=thrk, writes=thrk)
                    S.op("dve", lambda e: e.scalar_tensor_tensor(B_["tt"][:], B_["thr"][:], 1.0, B_["tt"][:], ALU.add, ALU.mult),
                         reads=thrk + ["tt" + sx], writes=["tt" + sx])

                def stageB2(u):
                    h, hf = u // 2, u % 2
                    B_ = bufs[u % 2]
                    sx = str(u % 2)
                    thrk = ["thr" + sx + "_0", "thr" + sx + "_1"]
                    thik = ["thi" + sx + "_0", "thi" + sx + "_1"]
                    S.op("act", lambda e: e.activation(B_["tt"][:], B_["tt"][:], AF.Sqrt), reads=["tt" + sx], writes=["tt" + sx])
                    S.op("dve", lambda e: e.scalar_tensor_tensor(B_["thi"][:], B_["thi"][:], 1.0, B_["xc"][:], ALU.add, ALU.mult),
                         reads=thik + ["xc" + sx], writes=thik)
                    S.op("dve", lambda e: e.scalar_tensor_tensor(B_["thi"][:], B_["thi"][:], 0.5, B_["tt"][:], ALU.mult, ALU.mult),
                         reads=thik + ["tt" + sx], writes=thik)
                    if not own:
                        S.op("dve", lambda e: e.tensor_tensor_scan(B_["tt"][:], B_["at"][:], B_["thi"][:], Ecur[:, h:h + 1], ALU.mult, ALU.add),
                             reads=["at" + sx, "Ecur", "tt" + sx] + thik, writes=["tt" + sx])
                        S.op("dve", lambda e: e.tensor_copy(Ecur[:, h:h + 1], B_["tt"][:, NH - 1:NH]), reads=["tt" + sx], writes=["Ecur"])
                    else:
                        S.dma("act", "ast" + sx, lambda e: e.dma_start(out=ab_d[0, h, :, hf * NH:(hf + 1) * NH], in_=B_["at"][:]),
                              reads=["at" + sx], writes=[f"abd0_{h}_{hf}"])
                        S.dma("sp", "bst" + sx, lambda e: e.dma_start(out=ab_d[1, h, :, hf * NH:(hf + 1) * NH], in_=B_["thi"][:]),
                              reads=thik, writes=[f"abd1_{h}_{hf}"])
                    if scan_emit is not None:
                        scan_emit(3)

                load_w(0, w_in_v, 0)
                stageA(0)
                stageA2(0)
                for u in range(32):
                    if u + 1 < 32:
                        stageA(u + 1)
                    stageB(u)
                    if u + 1 < 32:
                        stageA2(u + 1)
                    stageB2(u)

            def do_uproj():
                for g in range(2):
                    load_w(g, w_in_v, 4096 + 512 * g)
                for kt in range(8):
                    wbi, hh = kt // 4, kt % 4
                    for tq in range(4):
                        b = nextbank()
                        c0 = 3 + 512 * tq
                        for k in range(16):
                            S.op("pe", lambda e, b=b, k=k, c0=c0: e.matmul(
                                ps[b][:, :], wb[wbi][:, k, hh * 128:(hh + 1) * 128], xT[:, k, c0:c0 + 512], start=(k == 0), stop=(k == 15)),
                                reads=[f"wb{wbi}"], writes=[f"ps{b}"], inc=(k == 15))
                        eng = ev_eng()
                        S.op(eng, copy_op(eng, u_sb[:, kt, :, 64 * tq:64 * tq + 64], ps[b][:, :].rearrange("p (c s) -> p s c", s=8)),
                             reads=[f"ps{b}"], writes=[f"u{kt}_{tq}"])

            def do_S():
                W1s = [sbt(f"W1s{i}", [128, 8192], BF16) for i in range(2)]
                for kt in range(8):
                    w1 = W1s[kt % 2]
                    k1 = f"W1s{kt%2}"
                    S.dma("sp", k1, lambda e, w1=w1, kt=kt: e.dma_start(out=w1[:], in_=W1_d[kt]), writes=[k1])
                    for q in range(4):
                        j = 4 * kt + q
                        for ri in range(2):
                            b = nextbank()
                            for s in range(8):
                                o = ((q * 8 + s) * 2 + ri) * 128
                                S.op("pe", lambda e, b=b, s=s, o=o, w1=w1: e.matmul(
                                    ps[b][:, 0:256], w1[:, o:o + 128], u_sb[:, kt, s, :], start=(s == 0), stop=(s == 7)),
                                    reads=[k1], writes=[f"ps{b}"], inc=(s == 7))
                            eng = ev_eng()
                            S.op(eng, copy_op(eng, S_sb[:, :, ri, j], ps[b][:, 0:256]), reads=[f"ps{b}"], writes=[f"S_{j}_{ri}"])
                allS = [f"S_{j}_{ri}" for j in range(32) for ri in range(2)]
                return allS

            if own:
                wb = alloc_wb()
                def load_w(i, src_v, c0, ncol=512, nk=16):
                    S.dma("pool", f"wb{i}", lambda e: e.dma_start(out=wb[i][:, 0:nk, 0:ncol], in_=src_v[:, :, c0:c0 + ncol]), writes=[f"wb{i}"])

                do_A1(None)
                do_uproj()
                end_phase(mA1)
                mS = A.mark()
                S_sb = sbt("S_sb", [128, 256, 2, 32], F32)
                u_holder["S_sb"] = S_sb
                mA2 = A.mark()
                do_S()
                end_phase(mA2)
            elif mode == "s5prep":
                mW = A.mark()
                wb = alloc_wb()
                def load_w(i, src_v, c0, ncol=512, nk=16):
                    S.dma("pool", f"wb{i}", lambda e: e.dma_start(out=wb[i][:, 0:nk, 0:ncol], in_=src_v[:, :, c0:c0 + ncol]), writes=[f"wb{i}"])

                do_uproj()
                end_phase(mU)
                S_sb = sbt("S_sb", [128, 256, 2, 32], F32)
                allS = do_S()
                S.dma("sp", "Sspill", lambda e: e.dma_start(out=S_d3[seg], in_=S_sb[:].rearrange("p c r j -> p (c r j)")), reads=allS, writes=[f"S_d3_{seg}"])
                end_phase(mA1)
            else:
                wb = alloc_wb()
                def load_w(i, src_v, c0, ncol=512, nk=16):
                    S.dma("pool", f"wb{i}", lambda e: e.dma_start(out=wb[i][:, 0:nk, 0:ncol], in_=src_v[:, :, c0:c0 + ncol]), writes=[f"wb{i}"])

                do_A1(scan_hook["f"])
                end_phase(mA1)

        for seg in range(3):
            run_segment(seg, "s5prep")
        mW3 = A.mark()
        Sst3 = [sbt(f"Sst3_{i}", [128, 3, 8, 2, 32], F32) for i in range(2)]
        sstate = {"c": 0}
        S_d3v = S_d3.rearrange("s p f -> p s f")

        def load_piece(i):
            S.dma("sp", f"Sst3_{i%2}", lambda e: e.dma_start(out=Sst3[i % 2][:].rearrange("p s c r j -> p s (c r j)"), in_=S_d3v[:, :, 512 * i:512 * (i + 1)]),
                  writes=[f"Sst3_{i%2}"])

        def scan_emit(n):
            for _ in range(n):
                c = sstate["c"]
                if c >= 256:
                    return
                i, cc = c // 8, c % 8
                sk = f"Sst3_{i%2}"
                sc_ap = Sst3[i % 2][:, :, cc, :, :]
                S.op("pool", lambda e: e.tensor_tensor(X3[:], H3[:], LL8[:, 0:2, :].unsqueeze(1).to_broadcast([128, 3, 2, 32]), ALU.mult), reads=["H3"], writes=["X3"])
                S.op("pool", lambda e: e.tensor_tensor(Y3[:, :, 0, :], H3[:, :, 1, :], LL8[:, 2, :].unsqueeze(1).to_broadcast([128, 3, 32]), ALU.mult), reads=["H3"], writes=["Y3"])
                S.op("pool", lambda e: e.tensor_tensor(Y3[:, :, 1, :], H3[:, :, 0, :], LL8[:, 3, :].unsqueeze(1).to_broadcast([128, 3, 32]), ALU.mult), reads=["H3"], writes=["Y3"])
                S.op("pool", lambda e: e.tensor_tensor(X3[:], X3[:], Y3[:], ALU.add), reads=["X3", "Y3"], writes=["X3"])
                S.op("pool", lambda e: e.tensor_tensor(H3[:], X3[:], sc_ap, ALU.add), reads=["X3", sk], writes=["H3"])
                sstate["c"] = c + 1
                if cc == 7 and i + 2 < 32:
                    load_piece(i + 2)
        scan_hook["f"] = scan_emit
        S.op("pool", lambda e: e.memset(H3[:], 0.0), writes=["H3"])
        load_piece(0)
        load_piece(1)
        for seg in range(3):
            run_segment(seg, "rg")
        scan_emit(256)

        def cstep(LLm, prev, add_ap, out_ap, rk, wk):
            S.op("dve", lambda e: e.tensor_tensor(Xd[:], LLm[:, 0:2, :], prev, ALU.mult), reads=rk, writes=["Xd"])
            S.op("dve", lambda e: e.tensor_tensor(Yd[:, 0, :], LLm[:, 2, :], prev[:, 1, :], ALU.mult), reads=rk, writes=["Yd"])
            S.op("dve", lambda e: e.tensor_tensor(Yd[:, 1, :], LLm[:, 3, :], prev[:, 0, :], ALU.mult), reads=rk, writes=["Yd"])
            S.op("dve", lambda e: e.tensor_tensor(Xd[:], Xd[:], Yd[:], ALU.add), reads=["Xd", "Yd"], writes=["Xd"])
            S.op("dve", lambda e: e.tensor_tensor(out_ap, Xd[:], add_ap, ALU.add), reads=["Xd"] + rk, writes=wk)
        cstep(D2kL, H3[:, 0, :, :], H3[:, 1, :, :], Ha[:], ["H3"], ["Ha"])
        cstep(D2kL, Ha[:], H3[:, 2, :, :], Hin[:], ["Ha", "H3"], ["Hin"])
        end_phase(mW3)
        u_holder["u"] = sbt("u_sb", [128, 8, 8, 256], BF16, top_=True)
        run_segment(3, "own")
        u_sb = u_holder["u"]
        S_sb = u_holder["S_sb"]
        if True:
            Hbf = sbt("Hbf", [128, 32, 2, 258], BF16)
            W3s = [sbt(f"W3s{i}", [128, 8192], BF16) for i in range(2)]
            KTs = [sbt(f"KTs{i}", [128, 8, 128], BF16) for i in range(2)]
            ysb = [sbt(f"ysb{i}", [128, TOK], BF16) for i in range(2)]
            Xs = sbt("Xs4", [128, 2, 32], F32)
            Ys = sbt("Ys4", [128, 2, 32], F32)

            def chunk_step4(prev, c):
                S.op("dve", lambda e: e.tensor_tensor(Xs[:], LL8[:, 0:2, :], prev, ALU.mult), reads=["Ssb"], writes=["Xs"])
                S.op("dve", lambda e: e.tensor_tensor(Ys[:, 0, :], LL8[:, 2, :], prev[:, 1, :], ALU.mult), reads=["Ssb"], writes=["Ys"])
                S.op("dve", lambda e: e.tensor_tensor(Ys[:, 1, :], LL8[:, 3, :], prev[:, 0, :], ALU.mult), reads=["Ssb"], writes=["Ys"])
                S.op("dve", lambda e: e.tensor_tensor(Xs[:], Xs[:], Ys[:], ALU.add), reads=["Xs", "Ys"], writes=["Xs"])
                S.op("dve", lambda e: e.tensor_tensor(S_sb[:, c, :, :], Xs[:], S_sb[:, c, :, :], ALU.add), reads=["Xs", "Ssb"], writes=["Ssb"])
            for c in range(256):
                chunk_step4(Hin[:] if c == 0 else S_sb[:, c - 1, :, :], c)
            S.op("act", lambda e: e.copy(Hbf[:, :, :, 0], Hin[:].rearrange("p a b -> p b a")), reads=[], writes=["Hbf_0"])
            S.op("act", lambda e: e.copy(Hbf[:, :, 0, 1:257], S_sb[:, :, 0, :].rearrange("p c j -> p j c")), reads=["Ssb"], writes=["Hbf_1"])
            S.op("dve", lambda e: e.tensor_copy(Hbf[:, :, 1, 1:257], S_sb[:, :, 1, :].rearrange("p c j -> p j c")), reads=["Ssb"], writes=["Hbf_2"])
            hbk = ["Hbf_0", "Hbf_1", "Hbf_2"]
            for kt in range(8):
                w3 = W3s[kt % 2]
                ktb = KTs[kt % 2]
                k3, kk = f"W3s{kt%2}", f"KTs{kt%2}"
                ys = ysb[kt % 2]
                ky = f"ysb{kt%2}"
                S.dma("sp", k3, lambda e, w3=w3, kt=kt: e.dma_start(out=w3[:], in_=W3_d[kt]), writes=[k3])
                S.dma("sp", kk, lambda e, ktb=ktb, kt=kt: e.dma_start(out=ktb[:].rearrange("p a b -> p (a b)"), in_=KT_d[kt]), writes=[kk])
                for tau in range(8):
                    b = nextbank()
                    mm = []
                    for s in range(tau + 1):
                        mm.append((ktb[:, tau - s, :], u_sb[:, kt, s, :], [kk]))
                    for q in range(4):
                        for ri in range(2):
                            o = ((q * 8 + tau) * 2 + ri) * 128
                            mm.append((w3[:, o:o + 128], Hbf[:, 4 * kt + q, ri, 0:256], [k3] + hbk))
                    for i, (l_, r_, rk) in enumerate(mm):
                        S.op("pe", lambda e, b=b, l_=l_, r_=r_, i=i, n=len(mm): e.matmul(
                            ps[b][:, 0:256], l_, r_, start=(i == 0), stop=(i == n - 1)),
                            reads=rk, writes=[f"ps{b}"], inc=(i == len(mm) - 1))
                    S.op("act", lambda e, b=b, ys=ys, tau=tau: e.activation(ys[:, tau::8], ps[b][:, 0:256], AF.Gelu_apprx_tanh),
                         reads=[f"ps{b}"], writes=[ky + "_%d" % tau])
                S.dma("act", ky + "st", lambda e, ys=ys, kt=kt: e.dma_start(out=yS_d[:, kt, :], in_=ys[:]),
                      reads=[ky + "_%d" % t for t in range(8)], writes=[f"ySd{kt}"])
        end_phase(mA)
        if stop_after == "a4":
            return nc

        NB = 1024
        for blk in range(2):
            t0 = blk * NB
            mB = A.mark()
            xTb = sbt("xTb", [128, 16, NB], BF16)
            ySb = sbt("ySb", [128, 8, NB], BF16)
            hg = sbt("hg", [128, 16, NB], BF16)
            S.dma("sp", "xTb", lambda e: e.dma_start(out=xTb[:], in_=xT_d[:, :, 3 + t0:3 + t0 + NB]), writes=["xTb"])
            S.dma("sp", "ySb", lambda e: e.dma_start(out=ySb[:], in_=yS_d[:, :, t0:t0 + NB]), writes=["ySb"])
            mG = A.mark()
            if True:
                wg = [sbt(f"wg{i}", [128, 16, 512], BF16) for i in range(2)]
                gsb = [sbt(f"gsb{i}", [128, NB], F32) for i in range(2)]
                ab = [sbt(f"abl{i}", [128, 2, NB], F32) for i in range(2)]
                hs = [sbt(f"hs{i}", [128, NB], F32) for i in range(2)]
                def load_wg(g):
                    S.dma("pool", f"wg{g%2}", lambda e: e.dma_start(out=wg[g % 2][:], in_=w_in_v[:, :, 2048 + 512 * g:2048 + 512 * g + 512]), writes=[f"wg{g%2}"])
                load_wg(0)
                for h in range(16):
                    if h % 4 == 0 and h // 4 + 1 < 4:
                        load_wg(h // 4 + 1)
                    wgi, hh, sx = (h // 4) % 2, h % 4, str(h % 2)
                    S.dma("sp", "abl" + sx, lambda e, h=h: e.dma_start(out=ab[h % 2][:], in_=ab_d[:, h, :, t0:t0 + NB].rearrange("a p t -> p a t")),
                          writes=["abl" + sx])
                    for t2 in range(2):
                        b = nextbank()
                        for kt in range(16):
                            S.op("pe", lambda e, b=b, kt=kt, t2=t2: e.matmul(
                                ps[b][:, :], wg[wgi][:, kt, hh * 128:(hh + 1) * 128], xTb[:, kt, t2 * 512:(t2 + 1) * 512], start=(kt == 0), stop=(kt == 15)),
                                reads=[f"wg{wgi}", "xTb"], writes=[f"ps{b}"], inc=(kt == 15))
                        S.op("act", lambda e, b=b, t2=t2, h=h: e.activation(gsb[h % 2][:, t2 * 512:(t2 + 1) * 512], ps[b][:, :], AF.Gelu_apprx_tanh),
                             reads=[f"ps{b}"], writes=["gsb" + sx + "_%d" % t2])
                    S.op("dve", lambda e, h=h: e.tensor_tensor_scan(hs[h % 2][:], ab[h % 2][:, 0, :], ab[h % 2][:, 1, :], hcar[:, h:h + 1], ALU.mult, ALU.add),
                         reads=["abl" + sx, "hcar"], writes=["hs" + sx])
                    S.op("dve", lambda e, h=h: e.tensor_copy(hcar[:, h:h + 1], hs[h % 2][:, NB - 1:NB]), reads=["hs" + sx], writes=["hcar"])
                    S.op("dve", lambda e, h=h: e.tensor_tensor(hg[:, h, :], hs[h % 2][:], gsb[h % 2][:], ALU.mult),
                         reads=["hs" + sx, "gsb" + sx + "_0", "gsb" + sx + "_1"], writes=[f"hg{h}"])
            end_phase(mG)
            if True:
                wA = [sbt(f"wA{i}", [128, 16, 256], BF16) for i in range(2)]
                wGa = [sbt(f"wGa{i}", [128, 16, 256], BF16) for i in range(2)]
                wGb = [sbt(f"wGb{i}", [128, 16, 256], BF16) for i in range(2)]
                wLw = [sbt(f"wLw{i}", [128, 8, 256], BF16) for i in range(2)]
                wLv = [sbt(f"wLv{i}", [128, 8, 256], BF16) for i in range(2)]
                tmp = [sbt(f"mt{i}", [128, 4, 512], F32) for i in range(2)]
                mixs = [sbt(f"mixs{i}", [128, 512], BF16) for i in range(2)]
                mc = 0

                def load_mix(jg):
                    i = jg % 2
                    c0 = 256 * jg
                    S.dma("pool", f"wGa{i}", lambda e: e.dma_start(out=wGa[i][:], in_=w_in_v[:, :, 5120 + c0:5120 + c0 + 256]), writes=[f"wGa{i}"])
                    S.dma("pool", f"wGb{i}", lambda e: e.dma_start(out=wGb[i][:], in_=w_in_v[:, :, 7168 + c0:7168 + c0 + 256]), writes=[f"wGb{i}"])
                    S.dma("pool", f"wLv{i}", lambda e: e.dma_start(out=wLv[i][:], in_=glu_v_v[:, :, c0:c0 + 256]), writes=[f"wLv{i}"])
                    S.dma("pool", f"wLw{i}", lambda e: e.dma_start(out=wLw[i][:], in_=glu_w_v[:, :, c0:c0 + 256]), writes=[f"wLw{i}"])
                    S.dma("pool", f"wA{i}", lambda e: e.dma_start(out=wA[i][:], in_=w_a_v[:, :, c0:c0 + 256]), writes=[f"wA{i}"])
                load_mix(0)
                for jg in range(8):
                    i = jg % 2
                    c0 = 256 * jg
                    if jg + 1 < 8:
                        load_mix(jg + 1)
                    for jj in range(2):
                        j = 2 * jg + jj
                        cs = slice(128 * jj, 128 * jj + 128)
                        for t2 in range(2):
                            ts = slice(t2 * 512, (t2 + 1) * 512)
                            T_ = tmp[mc % 2]
                            tk = f"mt{mc%2}"
                            ms = mixs[mc % 2]
                            mk = f"mixs{mc%2}"
                            mc += 1

                            def group(wt, wk, act, ak, nk):
                                b = nextbank()
                                for kt in range(nk):
                                    S.op("pe", lambda e, b=b, kt=kt: e.matmul(ps[b][:, :], wt[:, kt, cs], act[:, kt, ts], start=(kt == 0), stop=(kt == nk - 1)),
                                         reads=[wk] + ak, writes=[f"ps{b}"], inc=(kt == nk - 1))
                                return b
                            bA = group(wGa[i], f"wGa{i}", xTb, ["xTb"], 16)
                            S.op("act", lambda e, b=bA, T_=T_: e.activation(T_[:, 0, :], ps[b][:, :], AF.Sigmoid), reads=[f"ps{bA}"], writes=[tk + "a"])
                            bB = group(wGb[i], f"wGb{i}", xTb, ["xTb"], 16)
                            S.op("act", lambda e, b=bB, T_=T_: e.activation(T_[:, 1, :], ps[b][:, :], AF.Sigmoid), reads=[f"ps{bB}"], writes=[tk + "b"])
                            bV = group(wLv[i], f"wLv{i}", ySb, ["ySb"], 8)
                            S.op("act", lambda e, b=bV, T_=T_: e.activation(T_[:, 2, :], ps[b][:, :], AF.Sigmoid), reads=[f"ps{bV}"], writes=[tk + "v"])
                            bW = group(wLw[i], f"wLw{i}", ySb, ["ySb"], 8)
                            S.op("dve", lambda e, b=bW, T_=T_: e.tensor_tensor(T_[:, 2, :], ps[b][:, :], T_[:, 2, :], ALU.mult), reads=[f"ps{bW}", tk + "v"], writes=[tk + "v"])
                            S.op("dve", lambda e, T_=T_: e.tensor_tensor(T_[:, 2, :], T_[:, 2, :], T_[:, 1, :], ALU.mult), reads=[tk + "v", tk + "b"], writes=[tk + "v"])
                            bY = group(wA[i], f"wA{i}", hg, [], 16)
                            S.op("dve", lambda e, b=bY, T_=T_: e.tensor_tensor(T_[:, 0, :], ps[b][:, :], T_[:, 0, :], ALU.mult), reads=[f"ps{bY}", tk + "a"], writes=[tk + "a"])
                            S.op("dve", lambda e, T_=T_, ms=ms: e.tensor_tensor(ms[:], T_[:, 0, :], T_[:, 2, :], ALU.add), reads=[tk + "a", tk + "v"], writes=[mk])
                            S.dma("sp", mk + "st", lambda e, ms=ms, j=j, t2=t2: e.dma_start(out=mix_d[:, j, t0 + t2 * 512:t0 + (t2 + 1) * 512], in_=ms[:]),
                                  reads=[mk], writes=[f"mixd{j}_{t2}"])
            end_phase(mB)
            if stop_after == "mix":
                continue

            mF = A.mark()
            acc = sbt("acc", [128, 8, D], F32)
            lnp_off = A.lo
            lnp = sbt("lnp", [128, 2, D], F32)
            mO = A.mark()
            if True:
                mixT = sbt("mixT", [128, 16, NB], BF16)
                wo = [sbt(f"wo{i}", [128, 16, 512], BF16) for i in range(2)]
                xres = [sbt(f"xres{i}", [128, 512], F32) for i in range(3)]
                S.dma("sp", "mixT", lambda e: e.dma_start(out=mixT[:], in_=mix_d[:, :, t0:t0 + NB]), writes=["mixT"])
                S.dma("sp", "lnp0", lambda e: e.dma_start(out=lnp[:, 0, :], in_=ln1_g_d.partition_broadcast(128)), writes=["lnp0"])
                S.dma("sp", "lnp1", lambda e: e.dma_start(out=lnp[:, 1, :], in_=ln1_b_d.partition_broadcast(128)), writes=["lnp1"])
                xc_ = 0
                def load_wo(cb):
                    S.dma("pool", f"wo{cb%2}", lambda e: e.dma_start(out=wo[cb % 2][:], in_=w_out_v[:, :, 512 * cb:512 * cb + 512]), writes=[f"wo{cb%2}"])
                load_wo(0)
                for cb in range(4):
                    i = cb % 2
                    if cb + 1 < 4:
                        load_wo(cb + 1)
                    for tt in range(8):
                        xr_ = xres[xc_ % 3]
                        xk = f"xres{xc_%3}"
                        xc_ += 1
                        r0 = 3 * TOK + 3 + t0 + 128 * tt
                        S.dma("sp", xk, lambda e, xr_=xr_, r0=r0, cb=cb: e.dma_start(out=xr_[:], in_=x_d[r0:r0 + 128, 512 * cb:512 * cb + 512]), writes=[xk])
                        b = nextbank()
                        for kt in range(16):
                            S.op("pe", lambda e, b=b, kt=kt, tt=tt, i=i: e.matmul(
                                ps[b][:, :], mixT[:, kt, 128 * tt:128 * tt + 128], wo[i][:, kt, :], start=(kt == 0), stop=(kt == 15)),
                                reads=["mixT", f"wo{i}"], writes=[f"ps{b}"], inc=(kt == 15))
                        S.op("dve", lambda e, b=b, xr_=xr_, tt=tt, cb=cb: e.scalar_tensor_tensor(
                            acc[:, tt, 512 * cb:512 * cb + 512], xr_[:], ALPHA, ps[b][:, :], ALU.mult, ALU.add),
                            reads=[xk, f"ps{b}"], writes=[f"acc{tt}_{cb}"])
                        S.op("dve", lambda e, tt=tt, cb=cb: e.bn_stats(stats[:, tt, cb, :], acc[:, tt, 512 * cb:512 * cb + 512]),
                             reads=[f"acc{tt}_{cb}"], writes=[f"stats{tt}_{cb}"])
            end_phase(mO)
            x1T = sbt("x1T", [128, 16, NB], BF16)

            def layernorm(tt, outk):
                S.op("dve", lambda e: e.bn_aggr(mv[:, tt, 0:2], stats[:, tt, :, :].rearrange("p a b -> p (a b)")),
                     reads=[f"stats{tt}_{c_}" for c_ in range(4)], writes=[f"mv{tt}"])
                S.op("act", lambda e: e.activation(mv[:, tt, 2:3], mv[:, tt, 1:2], AF.Sqrt, bias=EPS), reads=[f"mv{tt}"], writes=[f"mv{tt}"])
                S.op("dve", lambda e: e.reciprocal(mv[:, tt, 2:3], mv[:, tt, 2:3]), reads=[f"mv{tt}"], writes=[f"mv{tt}"])
                S.op("dve", lambda e: e.scalar_tensor_tensor(mv[:, tt, 3:4], mv[:, tt, 0:1], -1.0, mv[:, tt, 2:3], ALU.mult, ALU.mult),
                     reads=[f"mv{tt}"], writes=[f"mv{tt}"])
                S.op("act", lambda e: e.activation(acc[:, tt, :], acc[:, tt, :], AF.Identity, bias=mv[:, tt, 3:4], scale=mv[:, tt, 2:3]),
                     reads=[f"mv{tt}"], writes=[outk])
                S.op("dve", lambda e: e.tensor_tensor(acc[:, tt, :], acc[:, tt, :], lnp[:, 0, :], ALU.mult), reads=[outk, "lnp0"], writes=[outk])
                S.op("dve", lambda e: e.tensor_tensor(acc[:, tt, :], acc[:, tt, :], lnp[:, 1, :], ALU.add), reads=[outk, "lnp1"], writes=[outk])

            for tt in range(8):
                layernorm(tt, f"x1_{tt}")
                for b4 in range(4):
                    b = nextbank()
                    for jj in range(4):
                        kt = 4 * b4 + jj
                        S.op("pe", lambda e, b=b, jj=jj, kt=kt, tt=tt: e.transpose(
                            ps[b][:, jj * 128:(jj + 1) * 128], acc[:, tt, kt * 128:(kt + 1) * 128], ident[:, :]),
                            reads=[f"x1_{tt}"], writes=[f"ps{b}"], inc=(jj == 3))
                    S.op("act", lambda e, b=b, b4=b4, tt=tt: e.copy(x1T[:, 4 * b4:4 * b4 + 4, 128 * tt:128 * tt + 128],
                                                                   ps[b][:, :].rearrange("p (j t) -> p j t", j=4)),
                         reads=[f"ps{b}"], writes=[f"x1T{tt}_{b4}"])
            if dbg and blk == 0:
                S.dma("sp", "dbgx1", lambda e: e.dma_start(out=dbg_x1.rearrange("(t p) c -> p t c", p=128), in_=acc[:]),
                      reads=[f"x1_{tt}" for tt in range(8)])
            S.dma("sp", "lnp0", lambda e: e.dma_start(out=lnp[:, 0, :], in_=b_dn_d.partition_broadcast(128)),
                  reads=[f"x1_{tt}" for tt in range(8)], writes=["lnp0"])
            for tt in range(8):
                S.op("dve", lambda e, tt=tt: e.scalar_tensor_tensor(acc[:, tt, :], acc[:, tt, :], ALPHA, lnp[:, 0, :], ALU.mult, ALU.add),
                     reads=[f"x1_{tt}", "lnp0"], writes=[f"x1_{tt}"])
            S.barrier()

            if True:
                FC = 8
                hTb = [sbt("hT", [128, FC, NB], BF16), A.view(lnp_off, [128, FC, NB], BF16)]
                wu = [sbt(f"wu{i}", [128, 16, 256], BF16) for i in range(2)]
                wd = sbt("wd", [128, FC, D], BF16)
                rl = [sbt(f"rl{i}", [128, 512], F32) for i in range(3)]
                wuc = 0
                rc = 0
                NFC = DFF // 128 // FC
                NG = NFC * (FC // 2)

                def load_wu(g):
                    c0 = g * 256
                    S.dma("pool", f"wu{g%2}", lambda e: e.dma_start(out=wu[g % 2][:], in_=w_up_v[:, :, c0:c0 + 256]), writes=[f"wu{g%2}"])

                def load_wd(fc):
                    for f2 in range(FC // 2):
                        f0 = fc * FC + f2 * 2
                        S.dma("pool", f"wd{f2}", lambda e, f2=f2, f0=f0: e.dma_start(out=wd[:, 2 * f2:2 * f2 + 2, :], in_=w_dn_v[:, f0:f0 + 2, :]), writes=[f"wd{f2}"])
                load_wu(0)
                for fc in range(NFC):
                    hT = hTb[fc % 2]
                    hk = "hT%d_" % (fc % 2)
                    for f2 in range(FC // 2):
                        g = fc * (FC // 2) + f2
                        i = g % 2
                        if g + 1 < NG:
                            load_wu(g + 1)
                        if f2 == 0:
                            load_wd(fc)
                        for f in range(2):
                            fl = f2 * 2 + f
                            ft = fc * FC + fl
                            for t2 in range(2):
                                b = nextbank()
                                for kt in range(16):
                                    S.op("pe", lambda e, b=b, kt=kt, i=i, f=f, t2=t2: e.matmul(
                                        ps[b][:, :], wu[i][:, kt, f * 128:(f + 1) * 128], x1T[:, kt, t2 * 512:(t2 + 1) * 512], start=(kt == 0), stop=(kt == 15)),
                                        reads=[f"wu{i}"], writes=[f"ps{b}"], inc=(kt == 15))
                                r_ = rl[rc % 3]
                                rk = f"rl{rc%3}"
                                rc += 1
                                S.op("act", lambda e, b=b, r_=r_, ft=ft: e.activation(r_[:], ps[b][:, :], AF.Relu, bias=bup[:, ft:ft + 1]),
                                     reads=[f"ps{b}"], writes=[rk])
                                S.op("act", lambda e, r_=r_, fl=fl, t2=t2: e.activation(hT[:, fl, t2 * 512:(t2 + 1) * 512], r_[:], AF.Square),
                                     reads=[rk], writes=[hk + f"{fl}_{t2}"])
                    for tt in range(8):
                        for cb in range(4):
                            b = nextbank()
                            for fl in range(FC):
                                S.op("pe", lambda e, b=b, fl=fl, tt=tt, cb=cb: e.matmul(
                                    ps[b][:, :], hT[:, fl, 128 * tt:128 * tt + 128], wd[:, fl, 512 * cb:512 * cb + 512], start=(fl == 0), stop=(fl == FC - 1)),
                                    reads=[f"wd{fl//2}", hk + f"{fl}_{(128*tt)//512}"], writes=[f"ps{b}"], inc=(fl == FC - 1))
                            S.op("dve", lambda e, b=b, tt=tt, cb=cb: e.tensor_tensor(
                                acc[:, tt, 512 * cb:512 * cb + 512], acc[:, tt, 512 * cb:512 * cb + 512], ps[b][:, :], ALU.add),
                                reads=[f"ps{b}", f"accf{tt}_{cb}"], writes=[f"accf{tt}_{cb}"])
                hk1 = [f"hT1_{fl}_{t2}" for fl in range(FC) for t2 in range(2)]
                S.dma("sp", "lnp0", lambda e: e.dma_start(out=lnp[:, 0, :], in_=ln2_g_d.partition_broadcast(128)), writes=["lnp0"] + hk1)
                S.dma("sp", "lnp1", lambda e: e.dma_start(out=lnp[:, 1, :], in_=ln2_b_d.partition_broadcast(128)), writes=["lnp1"] + hk1)
                for tt in range(8):
                    for cb in range(4):
                        S.op("dve", lambda e, tt=tt, cb=cb: e.bn_stats(stats[:, tt, cb, :], acc[:, tt, 512 * cb:512 * cb + 512]),
                             reads=[f"accf{tt}_{cb}"], writes=[f"stats{tt}_{cb}"])
                    layernorm(tt, f"x2_{tt}")
                    S.dma("sp", f"ost{tt%2}", lambda e, tt=tt: e.dma_start(out=out_d[t0 + 128 * tt:t0 + 128 * tt + 128, :], in_=acc[:, tt, :]),
                          reads=[f"x2_{tt}"], writes=[f"outd{tt}"])
            end_phase(mF)
        S.barrier()
    return nc


_CACHE = {}


def _prep_inputs(inputs, small=False):
    x = np.ascontiguousarray(np.asarray(inputs["x"], dtype=np.float32))
    names = ["w_in", "conv_w", "conv_b", "rg_wa", "rg_ba", "rg_wx", "rg_bx", "rg_lambda", "w_a_out",
             "ssm_a_re", "ssm_a_im", "ssm_log_dt", "ssm_b_re", "ssm_b_im", "ssm_c_re", "ssm_c_im", "ssm_d",
             "glu_w", "glu_v", "w_out", "ln1_g", "ln1_b", "mlp_w_up", "mlp_b_up", "mlp_w_down", "mlp_b_down",
             "ln2_g", "ln2_b"]
    shared = {n: np.ascontiguousarray(np.asarray(inputs[n], dtype=np.float32)[0]) for n in names}
    in_maps = []
    for r in range(NCORE):
        b, k = r // 4, r % 4
        xs = np.zeros((4 * TOK + HALO, D), np.float32)
        n_real = TOK * (k + 1)
        xs[4 * TOK + HALO - n_real:] = x[b, 0:n_real]
        segm = np.ones((128, 4), np.float32)
        segm[:, 3 - k] = 0.0
        m = {"x": xs, "segm": segm}
        m.update(shared)
        if small:
            for n in ("w_a_out", "glu_w", "glu_v", "w_out", "mlp_w_up", "mlp_w_down"):
                m[n] = np.zeros((128, 128), np.float32)
        in_maps.append(m)
    return in_maps


def kernel(**inputs):
    if "nc" not in _CACHE:
        _CACHE["nc"] = build()
    nc = _CACHE["nc"]
    in_maps = _prep_inputs(inputs)
    res = run_bass_kernel_spmd(nc, in_maps, core_ids=list(range(NCORE)))
    out = np.empty((2, 4 * TOK, D), np.float32)
    for r in range(NCORE):
        b, k = r // 4, r % 4
        out[b, TOK * k:TOK * (k + 1)] = res.results[r]["out"]
    return out
```

```python
import numpy as np
from contextlib import ExitStack
import concourse.bass as bass
import concourse.mybir as mybir
from concourse.bass_utils import run_bass_kernel_spmd

F32 = mybir.dt.float32
BF16 = mybir.dt.bfloat16
I32 = mybir.dt.int32
AF = mybir.ActivationFunctionType
ALU = mybir.AluOpType

NCORE = 8
TOK = 2048
HALO = 3
D = 2048
DIN = 9216
DFF = 8192
ALPHA = 2.0 ** 0.25
EPS = 1e-5
TWO_PI = 6.283185307179586
PI = 3.141592653589793


class Sched:
    ENGS = ("pe", "act", "dve", "pool", "sp")

    def __init__(self, nc, stack):
        self.nc = nc
        self.stack = stack
        self.eng = {"pe": nc.tensor, "act": nc.scalar, "dve": nc.vector, "pool": nc.gpsimd, "sp": nc.sync}
        self.sem = {e: stack.enter_context(nc.semaphore("s_" + e)) for e in self.ENGS}
        self.cnt = {e: 0 for e in self.ENGS}
        self.waited = {e: {} for e in self.ENGS}
        self.last_w = {}
        self.readers = {}
        self.dsem = {}
        self.dcnt = {}

    def _h(self, s):
        return self.sem[s[1]] if s[0] == "e" else self.dsem[s[1]]

    def _wait(self, eng, s, v, raw=False):
        if s == ("e", eng) and (eng == "pe" or not raw):
            return
        if self.waited[eng].get(s, 0) >= v:
            return
        self.waited[eng][s] = v
        self.eng[eng].wait_ge(self._h(s), v)

    def _deps(self, eng, reads, writes):
        need = {}
        own = ("e", eng)

        def add(s, v, raw):
            if s == own and not raw:
                return
            if v > need.get(s, 0):
                need[s] = v
        for b in reads:
            w = self.last_w.get(b)
            if w is not None:
                add(w[0], w[1], True)
        for b in writes:
            w = self.last_w.get(b)
            if w is not None:
                add(w[0], w[1], False)
            for s, v in self.readers.get(b, {}).items():
                add(s, v, False)
        for s, v in need.items():
            self._wait(eng, s, v, raw=True)

    def _mark(self, tok, reads, writes):
        for b in writes:
            self.last_w[b] = tok
            self.readers[b] = {}
        for b in reads:
            d = self.readers.setdefault(b, {})
            if tok[1] > d.get(tok[0], 0):
                d[tok[0]] = tok[1]

    def op(self, eng, fn, reads=(), writes=(), inc=True):
        self._deps(eng, reads, writes)
        ins = fn(self.eng[eng])
        if inc:
            self.cnt[eng] += 1
            ins.then_inc(self.sem[eng], 1)
            tok = (("e", eng), self.cnt[eng])
        else:
            tok = (("e", eng), self.cnt[eng] + 1)
        self._mark(tok, reads, writes)
        return tok

    def dma(self, eng, key, fn, reads=(), writes=(), incv=16):
        if key not in self.dsem:
            self.dsem[key] = self.stack.enter_context(self.nc.semaphore("d_" + str(key)))
            self.dcnt[key] = 0
        self._deps(eng, reads, writes)
        self.dcnt[key] += incv
        fn(self.eng[eng]).then_inc(self.dsem[key], incv)
        tok = (("d", key), self.dcnt[key])
        self._mark(tok, reads, writes)
        return tok

    def wait_tok(self, eng, tok):
        self._wait(eng, tok[0], tok[1])

    def barrier(self):
        for e in self.ENGS:
            for e2 in self.ENGS:
                if e2 != e and self.cnt[e2] > 0:
                    self._wait(e, ("e", e2), self.cnt[e2])
            for k, v in self.dcnt.items():
                self._wait(e, ("d", k), v)
        self.last_w.clear()
        self.readers.clear()


_DTSZ = {F32: 4, I32: 4, BF16: 2}
SB_BYTES = 206 * 1024


class Arena:
    def __init__(self, handle, size):
        self.h = handle
        self.lo = 0
        self.hi = size

    def alloc(self, shape, dt, top=False):
        n = 1
        for d in shape[1:]:
            n *= d
        nb = (n * _DTSZ[dt] + 63) // 64 * 64
        if top:
            self.hi -= nb
            off = self.hi
        else:
            off = self.lo
            self.lo += nb
        assert self.lo <= self.hi, ("SBUF arena overflow", self.lo, self.hi)
        v = self.h[:, off:off + n * _DTSZ[dt]].bitcast(dt)
        names = "abcdefg"[:len(shape) - 1]
        if len(shape) > 2:
            v = v.rearrange("p (%s) -> p %s" % (" ".join(names), " ".join(names)),
                            **{k: d for k, d in zip(names[:-1], shape[1:-1])})
        if shape[0] < 128:
            v = v[0:shape[0]]
        return v

    def view(self, off, shape, dt):
        n = 1
        for d in shape[1:]:
            n *= d
        v = self.h[:, off:off + n * _DTSZ[dt]].bitcast(dt)
        names = "abcdefg"[:len(shape) - 1]
        if len(shape) > 2:
            v = v.rearrange("p (%s) -> p %s" % (" ".join(names), " ".join(names)),
                            **{k: d for k, d in zip(names[:-1], shape[1:-1])})
        return v

    def mark(self):
        return (self.lo, self.hi)

    def release(self, m):
        self.lo, self.hi = m


def build(dbg=False, stop_after=None):
    nc = bass.Bass("TRN2", target_bir_lowering=False)

    small = stop_after in ("p0", "a1", "a3", "a4")
    BIG = ("w_a_out", "glu_w", "glu_v", "w_out", "mlp_w_up", "mlp_w_down")

    def din(name, shape):
        if small and name in BIG:
            shape = [128, 128]
        return nc.dram_tensor(name, list(shape), F32, kind="ExternalInput").ap()

    def dscr(name, shape, dt):
        if dbg:
            return nc.dram_tensor(name, list(shape), dt, kind="ExternalOutput").ap()
        return nc.dram_tensor(name, list(shape), dt).ap()

    x_d = din("x", [4 * TOK + HALO, D])
    segm_d = din("segm", [128, 4])
    w_in_d = din("w_in", [D, DIN])
    conv_w_d = din("conv_w", [4, D])
    conv_b_d = din("conv_b", [D])
    rg_wa_d = din("rg_wa", [16, 128, 128])
    rg_ba_d = din("rg_ba", [16, 128])
    rg_wx_d = din("rg_wx", [16, 128, 128])
    rg_bx_d = din("rg_bx", [16, 128])
    rg_lam_d = din("rg_lambda", [D])
    w_a_d = din("w_a_out", [D, D])
    a_re_d = din("ssm_a_re", [64, 64])
    a_im_d = din("ssm_a_im", [64, 64])
    ldt_d = din("ssm_log_dt", [64])
    b_re_d = din("ssm_b_re", [64, 64, 16])
    b_im_d = din("ssm_b_im", [64, 64, 16])
    c_re_d = din("ssm_c_re", [64, 16, 64])
    c_im_d = din("ssm_c_im", [64, 16, 64])
    ssm_d_d = din("ssm_d", [64, 16])
    glu_w_d = din("glu_w", [1024, D])
    glu_v_d = din("glu_v", [1024, D])
    w_out_d = din("w_out", [D, D])
    ln1_g_d = din("ln1_g", [D])
    ln1_b_d = din("ln1_b", [D])
    w_up_d = din("mlp_w_up", [D, DFF])
    b_up_d = din("mlp_b_up", [DFF])
    w_dn_d = din("mlp_w_down", [DFF, D])
    b_dn_d = din("mlp_b_down", [D])
    ln2_g_d = din("ln2_g", [D])
    ln2_b_d = din("ln2_b", [D])
    out_d = nc.dram_tensor("out", [TOK, D], F32, kind="ExternalOutput").ap()

    ab_d = dscr("ab_d", [2, 16, 128, TOK], F32)
    xT_d = dscr("xT_d", [128, 16, TOK + HALO], BF16)
    yS_d = dscr("yS_d", [128, 8, TOK], BF16)
    mix_d = dscr("mix_d", [128, 16, TOK], BF16)
    W1_d = dscr("W1_d", [8, 128, 8192], BF16)
    W3_d = dscr("W3_d", [8, 128, 8192], BF16)
    KT_d = dscr("KT_d", [8, 128, 1024], BF16)
    S_d3 = nc.dram_tensor("S_d3", [3, 128, 16384], F32).ap()
    xTp_d = nc.dram_tensor("xTp_d", [3, 128, 16, TOK + HALO], BF16).ap()
    ccin_d = nc.dram_tensor("ccin_d", [128, 96], F32).ap()
    ccout_d = nc.dram_tensor("ccout_d", [NCORE * 128, 96], F32).ap()
    if dbg:
        dbg_small = nc.dram_tensor("dbg_small", [128, 256], F32, kind="ExternalOutput").ap()
        dbg_x1 = nc.dram_tensor("dbg_x1", [1024, D], F32, kind="ExternalOutput").ap()
        dbg_sc = nc.dram_tensor("dbg_sc", [128, 24 * 32], F32, kind="ExternalOutput").ap()
        dbg_lp = nc.dram_tensor("dbg_lp", [128, 9 * 4 * 32], F32, kind="ExternalOutput").ap()
        dbg_p3 = nc.dram_tensor("dbg_p3", [128, 104], F32, kind="ExternalOutput").ap()
        dbg_p1 = nc.dram_tensor("dbg_p1", [128, 128], F32, kind="ExternalOutput").ap()

    w_in_v = w_in_d.rearrange("(kt p) c -> p kt c", p=128)
    if not small:
        w_a_v = w_a_d.rearrange("(kt p) c -> p kt c", p=128)
        glu_w_v = glu_w_d.rearrange("(kt p) c -> p kt c", p=128)
        glu_v_v = glu_v_d.rearrange("(kt p) c -> p kt c", p=128)
        w_out_v = w_out_d.rearrange("(kt p) c -> p kt c", p=128)
        w_up_v = w_up_d.rearrange("(kt p) c -> p kt c", p=128)
        w_dn_v = w_dn_d.rearrange("(ft p) c -> p ft c", p=128)

    with ExitStack() as top:
        S = Sched(nc, top)
        ccsem = top.enter_context(nc.semaphore("ccsem"))
        arena_t = top.enter_context(nc.sbuf_tensor("arena", [128, SB_BYTES], mybir.dt.uint8))
        A = Arena(arena_t, SB_BYTES)

        def sbt(name, shape, dt, top_=False):
            return A.alloc(list(shape), dt, top=top_)

        def end_phase(m):
            S.barrier()
            A.release(m)

        ps = [top.enter_context(nc.psum_tensor(f"ps{i}", [128, 512], F32)) for i in range(8)]
        bank_ctr = [0]

        def nextbank():
            b = bank_ctr[0] % 8
            bank_ctr[0] += 1
            return b

        ev_ctr = [0]

        def ev_eng():
            ev_ctr[0] += 1
            return "act" if ev_ctr[0] % 2 == 0 else "dve"

        def copy_op(eng, out, in_):
            if eng == "act":
                return lambda e: e.copy(out, in_)
            return lambda e: e.tensor_copy(out, in_)

        ident = sbt("ident", [128, 128], F32)
        P1 = sbt("P1", [128, 128], F32)
        bup = sbt("bup", [128, 64], F32)
        rgc = sbt("rgc", [128, 8, 16], F32)
        wa_sb = sbt("wa_sb", [128, 16, 128], BF16)
        wx_sb = sbt("wx_sb", [128, 16, 128], BF16)
        Ecur = sbt("Ecur", [128, 16], F32)
        sumth = sbt("sumth", [128, 16, 4], F32)
        hcar = sbt("hcar", [128, 16], F32)
        Hin = sbt("Hin", [128, 2, 32], F32)
        LL8 = sbt("LL8", [128, 4, 32], F32)
        D2k = sbt("D2k", [128, 2, 32], F32)
        ccin = sbt("ccin", [128, 96], F32)
        segm = sbt("segm", [128, 4], F32)
        Hs = sbt("Hs", [128, 2, 32], F32)
        Xp = sbt("Xp", [128, 2, 32], F32)
        Yp = sbt("Yp", [128, 2, 32], F32)
        Xd = sbt("Xd", [128, 2, 32], F32)
        Yd = sbt("Yd", [128, 2, 32], F32)
        Ha = sbt("Ha", [128, 2, 32], F32)
        H3 = sbt("H3", [128, 3, 2, 32], F32)
        X3 = sbt("X3", [128, 3, 2, 32], F32)
        Y3 = sbt("Y3", [128, 3, 2, 32], F32)
        D2kL = sbt("D2kL", [128, 4, 32], F32)
        stats = sbt("stats", [128, 8, 4, 6], F32)
        mv = sbt("mv", [128, 8, 4], F32)

        S.op("pool", lambda e: e.memset(ident[:], 1.0), writes=["ident"])
        S.op("pool", lambda e: e.affine_select(out=ident[:], in_=ident[:], pattern=[[-1, 128]],
                                               compare_op=ALU.is_equal, fill=0.0, base=0, channel_multiplier=1),
             reads=["ident"], writes=["ident"])
        S.dma("sp", "segm", lambda e: e.dma_start(out=segm[:], in_=segm_d), writes=["segm"])
        S.dma("pool", "wa_sb", lambda e: e.dma_start(out=wa_sb[:], in_=rg_wa_d.rearrange("h i j -> i h j")), writes=["wa_sb"])
        S.dma("pool", "wx_sb", lambda e: e.dma_start(out=wx_sb[:], in_=rg_wx_d.rearrange("h i j -> i h j")), writes=["wx_sb"])

        m0 = A.mark()
        if True:
            st1 = sbt("st1", [128, 128], F32)
            st2 = sbt("st2", [64, 128], F32)
            st3 = sbt("st3", [104, 128], F32)
            ld2 = sbt("ld2", [32, 2], F32)
            P3 = sbt("P3", [128, 104], F32)
            S.dma("sp", "st1a", lambda e: e.dma_start(out=st1[0:64, :], in_=conv_w_d.rearrange("k (h p) -> (k h) p", p=128)), writes=["st1a"])
            S.dma("sp", "st1b", lambda e: e.dma_start(out=st1[64:80, :], in_=conv_b_d.rearrange("(h p) -> h p", p=128)), writes=["st1b"])
            S.dma("sp", "st1c", lambda e: e.dma_start(out=st1[80:96, :], in_=rg_ba_d), writes=["st1c"])
            S.dma("sp", "st1d", lambda e: e.dma_start(out=st1[96:112, :], in_=rg_bx_d), writes=["st1d"])
            S.dma("sp", "st1e", lambda e: e.dma_start(out=st1[112:128, :], in_=rg_lam_d.rearrange("(h p) -> h p", p=128)), writes=["st1e"])
            S.dma("sp", "st2", lambda e: e.dma_start(out=st2[:], in_=b_up_d.rearrange("(f p) -> f p", p=128)), writes=["st2"])
            S.dma("sp", "ld2", lambda e: e.dma_start(out=ld2[:], in_=ldt_d.rearrange("(j g) -> j g", g=2)), writes=["ld2"])
            S.dma("sp", "st3b", lambda e: e.dma_start(out=st3[32:64, :], in_=a_re_d.rearrange("(j g) p -> j (g p)", g=2)), writes=["st3b"])
            S.dma("sp", "st3c", lambda e: e.dma_start(out=st3[64:96, :], in_=a_im_d.rearrange("(j g) p -> j (g p)", g=2)), writes=["st3c"])
            S.dma("sp", "st3d", lambda e: e.dma_start(out=st3[96:104, :], in_=ssm_d_d.rearrange("(kt g) h -> kt (g h)", g=8)), writes=["st3d"])
            S.op("dve", lambda e: e.tensor_copy(st3[0:32, :].rearrange("j (g p) -> j g p", g=2),
                                                ld2[:, :].unsqueeze(2).to_broadcast([32, 2, 64])),
                 reads=["ld2"], writes=["st3a"])
            b = nextbank()
            S.op("pe", lambda e: e.transpose(ps[b][:, 0:128], st1[:, :], ident[:, :]),
                 reads=["st1a", "st1b", "st1c", "st1d", "st1e", "ident"], writes=[f"ps{b}"])
            S.op("dve", lambda e: e.tensor_copy(P1[:], ps[b][:, 0:128]), reads=[f"ps{b}"], writes=["P1"])
            b = nextbank()
            S.op("pe", lambda e: e.transpose(ps[b][:, 0:64], st2[:, :], ident[0:64, 0:64]),
                 reads=["st2", "ident"], writes=[f"ps{b}"])
            S.op("dve", lambda e: e.tensor_copy(bup[:], ps[b][:, 0:64]), reads=[f"ps{b}"], writes=["bup"])
            b = nextbank()
            S.op("pe", lambda e: e.transpose(ps[b][:, 0:104], st3[:, :], ident[0:104, 0:104]),
                 reads=["st3a", "st3b", "st3c", "st3d", "ident"], writes=[f"ps{b}"])
            S.op("dve", lambda e: e.tensor_copy(P3[:], ps[b][:, 0:104]), reads=[f"ps{b}"], writes=["P3"])

            ba_v = P1[:, 80:96]
            bx_v = P1[:, 96:112]
            lam_v = P1[:, 112:128]
            S.op("dve", lambda e: e.tensor_scalar(rgc[:, 0, :], ba_v, 0.5, None, ALU.mult), reads=["P1"], writes=["rgc0"])
            S.op("dve", lambda e: e.tensor_scalar(rgc[:, 1, :], bx_v, 0.5, None, ALU.mult), reads=["P1"], writes=["rgc1"])
            S.op("act", lambda e: e.activation(rgc[:, 5, :], lam_v, AF.Exp, scale=-1.0), reads=["P1"], writes=["rgc5"])
            S.op("act", lambda e: e.activation(rgc[:, 5, :], rgc[:, 5, :], AF.Ln, bias=1.0), reads=["rgc5"], writes=["rgc5"])
            S.op("dve", lambda e: e.tensor_scalar(rgc[:, 2, :], rgc[:, 5, :], -8.0, None, ALU.mult), reads=["rgc5"], writes=["rgc2"])
            S.op("dve", lambda e: e.tensor_scalar(rgc[:, 3, :], rgc[:, 5, :], -4.0, None, ALU.mult), reads=["rgc5"], writes=["rgc3"])
            S.op("dve", lambda e: e.tensor_scalar(rgc[:, 4, :], rgc[:, 5, :], 4.0, None, ALU.mult), reads=["rgc5"], writes=["rgc4"])

            sc = sbt("sc", [128, 24, 32], F32)
            Lp = sbt("Lp", [128, 9, 4, 32], F32)
            isc = sbt("isc", [128, 32], I32)

            def V(i):
                return sc[:, i, :]

            def dv(fn):
                S.op("dve", fn, reads=["sc", "P3"], writes=["sc"])

            def av(fn):
                S.op("act", fn, reads=["sc", "P3"], writes=["sc"])
            ldt_v, are_v, aim_v = P3[:, 0:32], P3[:, 32:64], P3[:, 64:96]
            av(lambda e: e.activation(V(0), ldt_v, AF.Exp))
            dv(lambda e: e.tensor_scalar(V(1), are_v, -1e-4, None, ALU.min))
            dv(lambda e: e.tensor_tensor(V(2), V(1), V(0), ALU.mult))
            av(lambda e: e.activation(V(3), V(2), AF.Exp))
            dv(lambda e: e.tensor_tensor(V(4), aim_v, V(0), ALU.mult))

            def reduced_sin(dst, shift):
                dv(lambda e: e.tensor_scalar(V(5), V(4), shift, None, ALU.add))
                dv(lambda e: e.tensor_scalar(V(6), V(5), 1.0 / TWO_PI, None, ALU.mult))
                dv(lambda e: e.tensor_copy(isc[:], V(6)))
                dv(lambda e: e.tensor_copy(V(6), isc[:]))
                dv(lambda e: e.scalar_tensor_tensor(V(5), V(6), -TWO_PI, V(5), ALU.mult, ALU.add))
                dv(lambda e: e.tensor_scalar(V(7), V(5), PI, None, ALU.is_gt))
                dv(lambda e: e.scalar_tensor_tensor(V(5), V(7), -TWO_PI, V(5), ALU.mult, ALU.add))
                dv(lambda e: e.tensor_scalar(V(7), V(5), -PI, None, ALU.is_lt))
                dv(lambda e: e.scalar_tensor_tensor(V(5), V(7), TWO_PI, V(5), ALU.mult, ALU.add))
                dv(lambda e: e.tensor_scalar(V(5), V(5), PI, -PI, ALU.min, ALU.max))
                av(lambda e: e.activation(dst, V(5), AF.Sin))
            reduced_sin(V(9), 0.0)
            reduced_sin(V(10), PI / 2)
            dv(lambda e: e.tensor_tensor(V(11), V(3), V(10), ALU.mult))
            dv(lambda e: e.tensor_tensor(V(12), V(3), V(9), ALU.mult))
            dv(lambda e: e.tensor_tensor(V(13), V(1), V(1), ALU.mult))
            dv(lambda e: e.tensor_tensor(V(5), aim_v, aim_v, ALU.mult))
            dv(lambda e: e.tensor_tensor(V(13), V(13), V(5), ALU.add))
            dv(lambda e: e.reciprocal(V(13), V(13)))
            dv(lambda e: e.tensor_scalar(V(14), V(11), -1.0, None, ALU.add))
            dv(lambda e: e.tensor_tensor(V(5), V(14), V(1), ALU.mult))
            dv(lambda e: e.tensor_tensor(V(6), V(12), aim_v, ALU.mult))
            dv(lambda e: e.tensor_tensor(V(5), V(5), V(6), ALU.add))
            dv(lambda e: e.tensor_tensor(V(15), V(5), V(13), ALU.mult))
            dv(lambda e: e.tensor_tensor(V(5), V(12), V(1), ALU.mult))
            dv(lambda e: e.tensor_tensor(V(6), V(14), aim_v, ALU.mult))
            dv(lambda e: e.tensor_tensor(V(5), V(5), V(6), ALU.subtract))
            dv(lambda e: e.tensor_tensor(V(16), V(5), V(13), ALU.mult))

            def lp(fn):
                S.op("dve", fn, reads=["sc", "Lp"], writes=["Lp", "sc"])
            lp(lambda e: e.memset(Lp[:, 0, 0, :], 1.0))
            lp(lambda e: e.memset(Lp[:, 0, 1, :], 0.0))
            lp(lambda e: e.tensor_copy(Lp[:, 1, 0, :], V(11)))
            lp(lambda e: e.tensor_copy(Lp[:, 1, 1, :], V(12)))

            def cmul(o_re, o_im, a_re_, a_im_, b_re_, b_im_, t1, t2, t3):
                lp(lambda e: e.tensor_tensor(t1, a_re_, b_re_, ALU.mult))
                lp(lambda e: e.tensor_tensor(t2, a_im_, b_im_, ALU.mult))
                lp(lambda e: e.tensor_tensor(t1, t1, t2, ALU.subtract))
                lp(lambda e: e.tensor_tensor(t2, a_re_, b_im_, ALU.mult))
                lp(lambda e: e.tensor_tensor(t3, a_im_, b_re_, ALU.mult))
                lp(lambda e: e.tensor_tensor(o_im, t3, t2, ALU.add))
                lp(lambda e: e.tensor_copy(o_re, t1))
            for n in range(1, 8):
                cmul(Lp[:, n + 1, 0, :], Lp[:, n + 1, 1, :], Lp[:, n, 0, :], Lp[:, n, 1, :],
                     Lp[:, 1, 0, :], Lp[:, 1, 1, :], V(17), V(18), V(21))
            for n in range(9):
                lp(lambda e, n=n: e.tensor_scalar(Lp[:, n, 2:4, :], Lp[:, n, 0:2, :], -1.0, None, ALU.mult))
            S.op("dve", lambda e: e.tensor_copy(LL8[:, 0, :], Lp[:, 8, 0, :]), reads=["Lp"], writes=["LL8"])
            S.op("dve", lambda e: e.tensor_copy(LL8[:, 1, :], Lp[:, 8, 0, :]), reads=["Lp"], writes=["LL8"])
            S.op("dve", lambda e: e.tensor_copy(LL8[:, 2, :], Lp[:, 8, 3, :]), reads=["Lp"], writes=["LL8"])
            S.op("dve", lambda e: e.tensor_copy(LL8[:, 3, :], Lp[:, 8, 1, :]), reads=["Lp"], writes=["LL8"])
            lp(lambda e: e.tensor_copy(V(19), Lp[:, 8, 0, :]))
            lp(lambda e: e.tensor_copy(V(20), Lp[:, 8, 1, :]))
            for _ in range(8):
                cmul(V(19), V(20), V(19), V(20), V(19), V(20), V(17), V(18), V(21))
            S.op("dve", lambda e: e.tensor_copy(D2kL[:, 0, :], V(19)), reads=["Lp", "sc"], writes=["D2kL"])
            S.op("dve", lambda e: e.tensor_copy(D2kL[:, 1, :], V(19)), reads=["Lp", "sc"], writes=["D2kL"])
            S.op("dve", lambda e: e.tensor_scalar(D2kL[:, 2, :], V(20), -1.0, None, ALU.mult), reads=["Lp", "sc"], writes=["D2kL"])
            S.op("dve", lambda e: e.tensor_copy(D2kL[:, 3, :], V(20)), reads=["Lp", "sc"], writes=["D2kL"])
            S.op("dve", lambda e: e.tensor_copy(D2k[:, 0, :], V(19)), reads=["Lp", "sc"], writes=["D2k"])
            S.op("dve", lambda e: e.tensor_copy(D2k[:, 1, :], V(20)), reads=["Lp", "sc"], writes=["D2k"])

            if dbg:
                S.dma("sp", "dbgsc", lambda e: e.dma_start(out=dbg_sc, in_=sc[:].rearrange("p a b -> p (a b)")), reads=["sc", "Lp"])
                S.dma("sp", "dbglp", lambda e: e.dma_start(out=dbg_lp, in_=Lp[:].rearrange("p a b c -> p (a b c)")), reads=["sc", "Lp"])
                S.dma("sp", "dbgp3", lambda e: e.dma_start(out=dbg_p3, in_=P3[:]), reads=["P3"])
                S.dma("sp", "dbgp1", lambda e: e.dma_start(out=dbg_p1, in_=P1[:]), reads=["P1"])
            Bn = sbt("Bn", [128, 2, 32, 16], F32)
            Bb = sbt("Bb", [128, 2, 32, 16], F32)
            Vn = sbt("Vn", [128, 8, 2, 32, 16], F32)
            tb = sbt("tb", [128, 32, 16], F32)
            S.dma("sp", "Bn0", lambda e: e.dma_start(out=Bn[:, 0, :, :], in_=b_re_d.rearrange("(j g) p h -> (g p) j h", g=2)), writes=["Bn0"])
            S.dma("sp", "Bn1", lambda e: e.dma_start(out=Bn[:, 1, :, :], in_=b_im_d.rearrange("(j g) p h -> (g p) j h", g=2)), writes=["Bn1"])

            def bc(v):
                return v.unsqueeze(2).to_broadcast([128, 32, 16])

            def bb(fn):
                S.op("dve", fn, reads=["sc", "Lp", "Bn0", "Bn1", "Bb", "tb"], writes=["Bb", "tb"])
            bb(lambda e: e.tensor_tensor(Bb[:, 0], Bn[:, 0], bc(V(15)), ALU.mult))
            bb(lambda e: e.tensor_tensor(tb[:], Bn[:, 1], bc(V(16)), ALU.mult))
            bb(lambda e: e.tensor_tensor(Bb[:, 0], Bb[:, 0], tb[:], ALU.subtract))
            bb(lambda e: e.tensor_tensor(Bb[:, 1], Bn[:, 1], bc(V(15)), ALU.mult))
            bb(lambda e: e.tensor_tensor(tb[:], Bn[:, 0], bc(V(16)), ALU.mult))
            bb(lambda e: e.tensor_tensor(Bb[:, 1], Bb[:, 1], tb[:], ALU.add))

            def vv(fn):
                S.op("dve", fn, reads=["Bb", "Lp", "Vn", "tb"], writes=["Vn", "tb"])
            for n in range(8):
                vv(lambda e, n=n: e.tensor_tensor(Vn[:, n, 0], Bb[:, 0], bc(Lp[:, n, 0, :]), ALU.mult))
                vv(lambda e, n=n: e.tensor_tensor(tb[:], Bb[:, 1], bc(Lp[:, n, 1, :]), ALU.mult))
                vv(lambda e, n=n: e.tensor_tensor(Vn[:, n, 0], Vn[:, n, 0], tb[:], ALU.subtract))
                vv(lambda e, n=n: e.tensor_tensor(Vn[:, n, 1], Bb[:, 1], bc(Lp[:, n, 0, :]), ALU.mult))
                vv(lambda e, n=n: e.tensor_tensor(tb[:], Bb[:, 0], bc(Lp[:, n, 1, :]), ALU.mult))
                vv(lambda e, n=n: e.tensor_tensor(Vn[:, n, 1], Vn[:, n, 1], tb[:], ALU.add))

            Cn = sbt("Cn", [128, 2, 8, 64], F32)
            par = sbt("par", [128, 2], F32)
            Xc = sbt("Xc", [128, 2, 128], F32)
            Yc = sbt("Yc", [128, 3, 8, 128], F32)
            S.dma("sp", "Cn0", lambda e: e.dma_start(out=Cn[:, 0, :, :], in_=c_re_d.rearrange("(kt g) h p -> (g h) kt p", g=8)), writes=["Cn0"])
            S.dma("sp", "Cn1", lambda e: e.dma_start(out=Cn[:, 1, :, :], in_=c_im_d.rearrange("(kt g) h p -> (g h) kt p", g=8)), writes=["Cn1"])
            Mp = sbt("Mp", [128, 128], F32)
            S.op("dve", lambda e: e.memset(Mp[:], 0.0), writes=["Mp"])
            for blk_ in range(4):
                S.op("dve", lambda e, blk_=blk_: e.memset(Mp[:, 32 * blk_ + 16:32 * blk_ + 32], 1.0), reads=["Mp"], writes=["Mp"])
            b = nextbank()
            S.op("pe", lambda e: e.transpose(ps[b][:, 0:128], Mp[:, :], ident[:, :]), reads=["Mp", "ident"], writes=[f"ps{b}"])
            S.op("dve", lambda e: e.tensor_copy(par[:, 0:1], ps[b][:, 0:1]), reads=[f"ps{b}"], writes=["par"])
            S.op("dve", lambda e: e.tensor_scalar(par[:, 1:2], par[:, 0:1], -1.0, 1.0, ALU.mult, ALU.add), reads=["par"], writes=["par"])
            for kt in range(8):
                for ri in range(2):
                    S.op("dve", lambda e, kt=kt, ri=ri: e.tensor_scalar(Xc[:, ri, 0:64], Cn[:, ri, kt, :], par[:, 1:2], None, ALU.mult),
                         reads=["Cn0", "Cn1", "par", "Xc%d" % ri], writes=["Xc%d" % ri])
                    S.op("dve", lambda e, kt=kt, ri=ri: e.tensor_scalar(Xc[:, ri, 64:128], Cn[:, ri, kt, :], par[:, 0:1], None, ALU.mult),
                         reads=["Cn0", "Cn1", "par", "Xc%d" % ri], writes=["Xc%d" % ri])
                    b = nextbank()
                    S.op("pe", lambda e, ri=ri, b=b: e.transpose(ps[b][:, 0:128], Xc[:, ri, :], ident[:, :]),
                         reads=["Xc%d" % ri, "ident"], writes=[f"ps{b}"])
                    S.op("act", lambda e, kt=kt, ri=ri, b=b: e.copy(Yc[:, ri, kt, :], ps[b][:, 0:128]),
                         reads=[f"ps{b}"], writes=["Yc"])
                    if ri == 1:
                        S.op("act", lambda e, kt=kt, b=b: e.mul(Yc[:, 2, kt, :], ps[b][:, 0:128], -1.0),
                             reads=[f"ps{b}"], writes=["Yc"])

            W3b = [sbt(f"W3b{i}", [128, 4, 8, 2, 128], BF16) for i in range(2)]
            W1b = [sbt(f"W1b{i}", [128, 4, 8, 2, 128], BF16) for i in range(2)]
            KTb = [sbt(f"KTb{i}", [128, 8, 128], BF16) for i in range(2)]
            Zf = [sbt(f"Zb{i}", [128, 1280], F32) for i in range(4)]
            Zb = [z[:, 0:1024].rearrange("p (q r c) -> p q r c", q=4, r=2) for z in Zf]
            Zd = [z[:, 0:1152].rearrange("p (q x) -> p q x", x=288)[:, :, 0:256].rearrange("p q (r c) -> p q r c", r=2) for z in Zf]
            w3t = sbt("w3t", [128, 32], F32)
            dI = sbt("dI", [128, 128], F32)
            for i in range(2):
                S.op("pool", lambda e, i=i: e.memset(W3b[i][:], 0.0), writes=[f"W3b{i}", f"W3b{i}a"])
            for i in range(4):
                S.op("pool", lambda e, i=i: e.memset(Zf[i][:], 0.0), writes=[f"Zb{i}"])
            zc = 0
            for kt in range(8):
                w3 = W3b[kt % 2]
                w1 = W1b[kt % 2]
                ktb = KTb[kt % 2]
                k3, k1, kk = f"W3b{kt%2}", f"W1b{kt%2}", f"KTb{kt%2}"
                for q in range(4):
                    j = 4 * kt + q
                    cs = slice(32 * q, 32 * q + 32)
                    for tau in range(8):
                        n = tau + 1
                        S.op("dve", lambda e, n=n, j=j, cs=cs, w3=w3, q=q, tau=tau: e.tensor_scalar(
                            w3[:, q, tau, 0, cs], Yc[:, 0, kt, cs], Lp[:, n, 0, j:j + 1], None, ALU.mult), reads=["Yc", "Lp"], writes=[k3 + "a"])
                        S.op("dve", lambda e, n=n, j=j, cs=cs, w3=w3, q=q, tau=tau: e.tensor_scalar(
                            w3[:, q, tau, 1, cs], Yc[:, 0, kt, cs], Lp[:, n, 3, j:j + 1], None, ALU.mult), reads=["Yc", "Lp"], writes=[k3 + "a"])
                for q in range(4):
                    j = 4 * kt + q
                    cs = slice(32 * q, 32 * q + 32)
                    for tau in range(8):
                        n = tau + 1
                        S.op("dve", lambda e, n=n, j=j, cs=cs, w3=w3, q=q, tau=tau: e.scalar_tensor_tensor(
                            w3[:, q, tau, 0, cs], Yc[:, 1, kt, cs], Lp[:, n, 3, j:j + 1], w3[:, q, tau, 0, cs], ALU.mult, ALU.add),
                            reads=["Yc", "Lp", k3 + "a"], writes=[k3])
                        S.op("dve", lambda e, n=n, j=j, cs=cs, w3=w3, q=q, tau=tau: e.scalar_tensor_tensor(
                            w3[:, q, tau, 1, cs], Yc[:, 1, kt, cs], Lp[:, n, 2, j:j + 1], w3[:, q, tau, 1, cs], ALU.mult, ALU.add),
                            reads=["Yc", "Lp", k3 + "a"], writes=[k3])
                S.dma("sp", k3 + "st", lambda e, kt=kt, w3=w3: e.dma_start(out=W3_d[kt], in_=w3[:].rearrange("p a b c d -> p (a b c d)")),
                      reads=[k3, k3 + "a"], writes=[f"W3d{kt}"])
                S.op("dve", lambda e, kt=kt: e.tensor_scalar(dI[:], ident[:], P3[:, 96 + kt:97 + kt], None, ALU.mult),
                     reads=["ident", "P3", "dI"], writes=["dI"])
                for n in range(8):
                    zi = zc % 4
                    zb, zd = Zb[zi], Zd[zi]
                    kz = f"Zb{zi}"
                    zc += 1
                    S.op("pool", lambda e, zd=zd, n=n: e.tensor_copy(
                        zd[0:64, :, :, 0:16], Vn[0:64, n, :, 4 * kt:4 * kt + 4, :].rearrange("p r q h -> p q r h")),
                        reads=["Vn", kz], writes=[kz])
                    S.op("pool", lambda e, zd=zd, n=n: e.tensor_copy(
                        zd[64:128, :, :, 16:32], Vn[64:128, n, :, 4 * kt:4 * kt + 4, :].rearrange("p r q h -> p q r h")),
                        reads=["Vn", kz], writes=[kz])
                    for q in range(4):
                        for ri in range(2):
                            b = nextbank()
                            S.op("pe", lambda e, zb=zb, q=q, ri=ri, b=b: e.transpose(ps[b][:, 0:128], zb[:, q, ri, :], ident[:, :]),
                                 reads=[kz, "ident"], writes=[f"ps{b}"])
                            eng = ev_eng()
                            S.op(eng, copy_op(eng, w1[:, q, 7 - n, ri, :], ps[b][:, 0:128]), reads=[f"ps{b}"], writes=[k1])
                    b = nextbank()
                    for q in range(4):
                        cs = slice(32 * q, 32 * q + 32)
                        S.op("pe", lambda e, zb=zb, q=q, cs=cs, b=b: e.matmul(ps[b][:, cs], zb[:, q, 0, :], Yc[:, 0, kt, cs], start=True, stop=False),
                             reads=[kz, "Yc"], writes=[f"ps{b}"], inc=False)
                        S.op("pe", lambda e, zb=zb, q=q, cs=cs, b=b: e.matmul(ps[b][:, cs], zb[:, q, 1, :], Yc[:, 2, kt, cs], start=False, stop=True),
                             reads=[kz, "Yc"], writes=[f"ps{b}"], inc=(q == 3))
                    if n == 0:
                        S.op("dve", lambda e, ktb=ktb, b=b: e.tensor_tensor(ktb[:, 0, :], ps[b][:, 0:128], dI[:], ALU.add),
                             reads=[f"ps{b}", "dI"], writes=[kk])
                    else:
                        S.op("act", lambda e, ktb=ktb, b=b, n=n: e.copy(ktb[:, n, :], ps[b][:, 0:128]),
                             reads=[f"ps{b}"], writes=[kk])
                S.dma("sp", k1 + "st", lambda e, kt=kt, w1=w1: e.dma_start(out=W1_d[kt], in_=w1[:].rearrange("p a b c d -> p (a b c d)")),
                      reads=[k1], writes=[f"W1d{kt}"])
                S.dma("sp", kk + "st", lambda e, kt=kt, ktb=ktb: e.dma_start(out=KT_d[kt], in_=ktb[:].rearrange("p a b -> p (a b)")),
                      reads=[kk], writes=[f"KTd{kt}"])
        end_phase(m0)
        if stop_after == "p0":
            return nc

        mA = A.mark()
        S.op("dve", lambda e: e.memset(Ecur[:], 0.0), writes=["Ecur"])
        u_holder = {}
        scan_hook = {"f": None}

        def run_segment(seg, mode):
            own = (mode == "own")
            xo = seg * TOK
            if mode in ("rg", "own"):
                S.op("dve", lambda e: e.tensor_scalar(Ecur[:], Ecur[:], segm[:, seg:seg + 1], None, ALU.mult), reads=["Ecur", "segm"], writes=["Ecur"])
            if own:
                S.op("dve", lambda e: e.tensor_copy(hcar[:], Ecur[:]), reads=["Ecur"], writes=["hcar"])
            mA1 = A.mark()
            if mode == "s5prep":
                u_sb = sbt("u_sb", [128, 8, 8, 256], BF16)
            else:
                u_sb = u_holder.get("u")
            mU = A.mark()
            xT = sbt("xT", [128, 16, TOK + HALO], BF16)
            mx = A.mark()
            if mode == "rg":
                S.dma("sp", "xTld", lambda e: e.dma_start(out=xT[:], in_=xTp_d[seg]), reads=[f"xTp{seg}"], writes=["xTall"])
            else:
                xin = [sbt(f"xin{i}", [128, D], F32) for i in range(3)]
                tiles = [(0, 3, 0)] + [(3 + 128 * t, 128, 3 + 128 * t) for t in range(16)]
                for ti, (r0, nr, c0) in enumerate(tiles):
                    xb = xin[ti % 3]
                    key = f"xin{ti%3}"
                    S.dma("sp", key, lambda e, xb=xb, r0=r0, nr=nr: e.dma_start(out=xb[0:nr, :], in_=x_d[xo + r0:xo + r0 + nr, :]), writes=[key])
                    for b4 in range(4):
                        b = nextbank()
                        for jj in range(4):
                            kt = 4 * b4 + jj
                            S.op("pe", lambda e, xb=xb, nr=nr, kt=kt, jj=jj, b=b: e.transpose(
                                ps[b][:, jj * 128:jj * 128 + nr], xb[0:nr, kt * 128:(kt + 1) * 128], ident[0:nr, 0:nr]),
                                reads=[key, "ident"], writes=[f"ps{b}"], inc=(jj == 3))
                        eng = ev_eng()
                        S.op(eng, copy_op(eng, xT[:, 4 * b4:4 * b4 + 4, c0:c0 + nr],
                                          ps[b][:, 0:512].rearrange("p (j t) -> p j t", j=4)[:, :, 0:nr]),
                             reads=[f"ps{b}"], writes=[f"xT{ti}_{b4}"])
                allx = [f"xT{ti}_{b4}" for ti in range(17) for b4 in range(4)]
                if own:
                    S.dma("sp", "xTst", lambda e: e.dma_start(out=xT_d, in_=xT[:]), reads=allx, writes=["xTd"])

                if mode == "s5prep":
                    S.dma("sp", "xTpst", lambda e: e.dma_start(out=xTp_d[seg], in_=xT[:]), reads=allx, writes=[f"xTp{seg}"])
            S.barrier()
            A.release(mx)

            def alloc_wb():
                return [sbt(f"wb{i}", [128, 16, 512], BF16) for i in range(2)]

            def do_A1(scan_emit):
                NH = 1024
                bufs = []
                for i in range(2):
                    bufs.append(dict(
                        xr=sbt(f"xr{i}", [128, NH + 3], F32), xc=sbt(f"xc{i}", [128, NH], F32),
                        xcb=sbt(f"xcb{i}", [128, NH], BF16), thr=sbt(f"thr{i}", [128, NH], F32),
                        thi=sbt(f"thi{i}", [128, NH], F32), at=sbt(f"at{i}", [128, NH], F32),
                        tt=sbt(f"tt{i}", [128, NH], F32)))

                def stageA(u):
                    h, hf = u // 2, u % 2
                    B_ = bufs[u % 2]
                    sx = str(u % 2)
                    if hf == 0 and h % 4 == 0 and h // 4 + 1 < 4:
                        load_w((h // 4 + 1) % 2, w_in_v, 512 * (h // 4 + 1))
                    wbi = (h // 4) % 2
                    hh = h % 4
                    cb0 = NH * hf
                    pieces = [(cb0, 3, 0), (cb0 + 3, 512, 3), (cb0 + 515, 512, 515)]
                    for (c0, n, off) in pieces:
                        b = nextbank()
                        for kt in range(16):
                            S.op("pe", lambda e: e.matmul(
                                ps[b][:, 0:n], wb[wbi][:, kt, hh * 128:(hh + 1) * 128], xT[:, kt, c0:c0 + n],
                                start=(kt == 0), stop=(kt == 15)),
                                reads=[f"wb{wbi}"], writes=[f"ps{b}"], inc=(kt == 15))
                        S.op("act", lambda e: e.copy(B_["xr"][:, off:off + n], ps[b][:, 0:n]),
                             reads=[f"ps{b}"], writes=["xr" + sx + "_%d" % off])
                    xrk = ["xr" + sx + "_%d" % o for o in (0, 3, 515)]
                    S.op("dve", lambda e: e.tensor_scalar(B_["xc"][:], B_["xr"][:, 0:NH], P1[:, h:h + 1], P1[:, 64 + h:65 + h], ALU.mult, ALU.add),
                         reads=xrk + ["xc" + sx, "xcb" + sx], writes=["xc" + sx])
                    for k in range(1, 4):
                        S.op("dve", lambda e: e.scalar_tensor_tensor(
                            B_["xc"][:], B_["xr"][:, k:k + NH], P1[:, k * 16 + h:k * 16 + h + 1], B_["xc"][:], ALU.mult, ALU.add),
                            reads=xrk + ["xc" + sx], writes=["xc" + sx])

                def stageA2(u):
                    h, hf = u // 2, u % 2
                    B_ = bufs[u % 2]
                    sx = str(u % 2)
                    S.op("act", lambda e: e.copy(B_["xcb"][:], B_["xc"][:]), reads=["xc" + sx], writes=["xcb" + sx])
                    for g, (wsb, wk, dst) in enumerate(((wa_sb, "wa_sb", "thr"), (wx_sb, "wx_sb", "thi"))):
                        for t2 in range(2):
                            b = nextbank()
                            S.op("pe", lambda e: e.matmul(
                                ps[b][:, :], wsb[:, h, :], B_["xcb"][:, t2 * 512:(t2 + 1) * 512], start=True, stop=True),
                                reads=[wk, "xcb" + sx], writes=[f"ps{b}"])
                            S.op("act", lambda e: e.activation(
                                B_[dst][:, t2 * 512:(t2 + 1) * 512], ps[b][:, :], AF.Tanh, bias=rgc[:, g, h:h + 1], scale=0.5),
                                reads=[f"ps{b}"], writes=[dst + sx + "_%d" % t2])

                def stageB(u):
                    h, hf = u // 2, u % 2
                    B_ = bufs[u % 2]
                    sx = str(u % 2)
                    thrk = ["thr" + sx + "_0", "thr" + sx + "_1"]
                    thik = ["thi" + sx + "_0", "thi" + sx + "_1"]
                    S.op("act", lambda e: e.activation(B_["at"][:], B_["thr"][:], AF.Exp, bias=rgc[:, 3, h:h + 1], scale=rgc[:, 3, h:h + 1]),
                         reads=thrk, writes=["at" + sx])
                    S.op("act", lambda e: e.activation(B_["tt"][:], B_["thr"][:], AF.Tanh, bias=rgc[:, 4, h:h + 1], scale=rgc[:, 4, h:h + 1]),
                         reads=thrk, writes=["tt" + sx])
                    S.op("act", lambda e: e.activation(B_["thr"][:], B_["thr"][:], AF.Exp, bias=rgc[:, 2, h:h + 1], scale=rgc[:, 2, h:h + 1]),
                         reads=thrk, writes=thrk)
                    S.op("dve", lambda e: e.scalar_tensor_tensor(B_["tt"][:], B_["thr"][:], 1.0, B_["tt"][:], ALU.add, ALU.mult),
                         reads=thrk + ["tt" + sx], writes=["tt" + sx])

                def stageB2(u):
                    h, hf = u // 2, u % 2
                    B_ = bufs[u % 2]
                    sx = str(u % 2)
                    thrk = ["thr" + sx + "_0", "thr" + sx + "_1"]
                    thik = ["thi" + sx + "_0", "thi" + sx + "_1"]
                    S.op("act", lambda e: e.activation(B_["tt"][:], B_["tt"][:], AF.Sqrt), reads=["tt" + sx], writes=["tt" + sx])
                    S.op("dve", lambda e: e.scalar_tensor_tensor(B_["thi"][:], B_["thi"][:], 1.0, B_["xc"][:], ALU.add, ALU.mult),
                         reads=thik + ["xc" + sx], writes=thik)
                    S.op("dve", lambda e: e.scalar_tensor_tensor(B_["thi"][:], B_["thi"][:], 0.5, B_["tt"][:], ALU.mult, ALU.mult),
                         reads=thik + ["tt" + sx], writes=thik)
                    if not own:
                        S.op("dve", lambda e: e.tensor_tensor_scan(B_["tt"][:], B_["at"][:], B_["thi"][:], Ecur[:, h:h + 1], ALU.mult, ALU.add),
                             reads=["at" + sx, "Ecur", "tt" + sx] + thik, writes=["tt" + sx])
                        S.op("dve", lambda e: e.tensor_copy(Ecur[:, h:h + 1], B_["tt"][:, NH - 1:NH]), reads=["tt" + sx], writes=["Ecur"])
                    else:
                        S.dma("act", "ast" + sx, lambda e: e.dma_start(out=ab_d[0, h, :, hf * NH:(hf + 1) * NH], in_=B_["at"][:]),
                              reads=["at" + sx], writes=[f"abd0_{h}_{hf}"])
                        S.dma("sp", "bst" + sx, lambda e: e.dma_start(out=ab_d[1, h, :, hf * NH:(hf + 1) * NH], in_=B_["thi"][:]),
                              reads=thik, writes=[f"abd1_{h}_{hf}"])
                    if scan_emit is not None:
                        scan_emit(3)

                load_w(0, w_in_v, 0)
                stageA(0)
                stageA2(0)
                for u in range(32):
                    if u + 1 < 32:
                        stageA(u + 1)
                    stageB(u)
                    if u + 1 < 32:
                        stageA2(u + 1)
                    stageB2(u)

            def do_uproj():
                for g in range(2):
                    load_w(g, w_in_v, 4096 + 512 * g)
                for kt in range(8):
                    wbi, hh = kt // 4, kt % 4
                    for tq in range(4):
                        b = nextbank()
                        c0 = 3 + 512 * tq
                        for k in range(16):
                            S.op("pe", lambda e, b=b, k=k, c0=c0: e.matmul(
                                ps[b][:, :], wb[wbi][:, k, hh * 128:(hh + 1) * 128], xT[:, k, c0:c0 + 512], start=(k == 0), stop=(k == 15)),
                                reads=[f"wb{wbi}"], writes=[f"ps{b}"], inc=(k == 15))
                        eng = ev_eng()
                        S.op(eng, copy_op(eng, u_sb[:, kt, :, 64 * tq:64 * tq + 64], ps[b][:, :].rearrange("p (c s) -> p s c", s=8)),
                             reads=[f"ps{b}"], writes=[f"u{kt}_{tq}"])

            def do_S():
                W1s = [sbt(f"W1s{i}", [128, 8192], BF16) for i in range(2)]
                for kt in range(8):
                    w1 = W1s[kt % 2]
                    k1 = f"W1s{kt%2}"
                    S.dma("sp", k1, lambda e, w1=w1, kt=kt: e.dma_start(out=w1[:], in_=W1_d[kt]), writes=[k1])
                    for q in range(4):
                        j = 4 * kt + q
                        for ri in range(2):
                            b = nextbank()
                            for s in range(8):
                                o = ((q * 8 + s) * 2 + ri) * 128
                                S.op("pe", lambda e, b=b, s=s, o=o, w1=w1: e.matmul(
                                    ps[b][:, 0:256], w1[:, o:o + 128], u_sb[:, kt, s, :], start=(s == 0), stop=(s == 7)),
                                    reads=[k1], writes=[f"ps{b}"], inc=(s == 7))
                            eng = ev_eng()
                            S.op(eng, copy_op(eng, S_sb[:, :, ri, j], ps[b][:, 0:256]), reads=[f"ps{b}"], writes=[f"S_{j}_{ri}"])
                allS = [f"S_{j}_{ri}" for j in range(32) for ri in range(2)]
                return allS

            if own:
                wb = alloc_wb()
                def load_w(i, src_v, c0, ncol=512, nk=16):
                    S.dma("pool", f"wb{i}", lambda e: e.dma_start(out=wb[i][:, 0:nk, 0:ncol], in_=src_v[:, :, c0:c0 + ncol]), writes=[f"wb{i}"])

                do_A1(None)
                do_uproj()
                end_phase(mA1)
                mS = A.mark()
                S_sb = sbt("S_sb", [128, 256, 2, 32], F32)
                u_holder["S_sb"] = S_sb
                mA2 = A.mark()
                do_S()
                end_phase(mA2)
            elif mode == "s5prep":
                mW = A.mark()
                wb = alloc_wb()
                def load_w(i, src_v, c0, ncol=512, nk=16):
                    S.dma("pool", f"wb{i}", lambda e: e.dma_start(out=wb[i][:, 0:nk, 0:ncol], in_=src_v[:, :, c0:c0 + ncol]), writes=[f"wb{i}"])

                do_uproj()
                end_phase(mU)
                S_sb = sbt("S_sb", [128, 256, 2, 32], F32)
                allS = do_S()
                S.dma("sp", "Sspill", lambda e: e.dma_start(out=S_d3[seg], in_=S_sb[:].rearrange("p c r j -> p (c r j)")), reads=allS, writes=[f"S_d3_{seg}"])
                end_phase(mA1)
            else:
                wb = alloc_wb()
                def load_w(i, src_v, c0, ncol=512, nk=16):
                    S.dma("pool", f"wb{i}", lambda e: e.dma_start(out=wb[i][:, 0:nk, 0:ncol], in_=src_v[:, :, c0:c0 + ncol]), writes=[f"wb{i}"])

                do_A1(scan_hook["f"])
                end_phase(mA1)

        for seg in range(3):
            run_segment(seg, "s5prep")
        mW3 = A.mark()
        Sst3 = [sbt(f"Sst3_{i}", [128, 3, 8, 2, 32], F32) for i in range(2)]
        sstate = {"c": 0}
        S_d3v = S_d3.rearrange("s p f -> p s f")

        def load_piece(i):
            S.dma("sp", f"Sst3_{i%2}", lambda e: e.dma_start(out=Sst3[i % 2][:].rearrange("p s c r j -> p s (c r j)"), in_=S_d3v[:, :, 512 * i:512 * (i + 1)]),
                  writes=[f"Sst3_{i%2}"])

        def scan_emit(n):
            for _ in range(n):
                c = sstate["c"]
                if c >= 256:
                    return
                i, cc = c // 8, c % 8
                sk = f"Sst3_{i%2}"
                sc_ap = Sst3[i % 2][:, :, cc, :, :]
                S.op("pool", lambda e: e.tensor_tensor(X3[:], H3[:], LL8[:, 0:2, :].unsqueeze(1).to_broadcast([128, 3, 2, 32]), ALU.mult), reads=["H3"], writes=["X3"])
                S.op("pool", lambda e: e.tensor_tensor(Y3[:, :, 0, :], H3[:, :, 1, :], LL8[:, 2, :].unsqueeze(1).to_broadcast([128, 3, 32]), ALU.mult), reads=["H3"], writes=["Y3"])
                S.op("pool", lambda e: e.tensor_tensor(Y3[:, :, 1, :], H3[:, :, 0, :], LL8[:, 3, :].unsqueeze(1).to_broadcast([128, 3, 32]), ALU.mult), reads=["H3"], writes=["Y3"])
                S.op("pool", lambda e: e.tensor_tensor(X3[:], X3[:], Y3[:], ALU.add), reads=["X3", "Y3"], writes=["X3"])
                S.op("pool", lambda e: e.tensor_tensor(H3[:], X3[:], sc_ap, ALU.add), reads=["X3", sk], writes=["H3"])
                sstate["c"] = c + 1
                if cc == 7 and i + 2 < 32:
                    load_piece(i + 2)
        scan_hook["f"] = scan_emit
        S.op("pool", lambda e: e.memset(H3[:], 0.0), writes=["H3"])
        load_piece(0)
        load_piece(1)
        for seg in range(3):
            run_segment(seg, "rg")
        scan_emit(256)

        def cstep(LLm, prev, add_ap, out_ap, rk, wk):
            S.op("dve", lambda e: e.tensor_tensor(Xd[:], LLm[:, 0:2, :], prev, ALU.mult), reads=rk, writes=["Xd"])
            S.op("dve", lambda e: e.tensor_tensor(Yd[:, 0, :], LLm[:, 2, :], prev[:, 1, :], ALU.mult), reads=rk, writes=["Yd"])
            S.op("dve", lambda e: e.tensor_tensor(Yd[:, 1, :], LLm[:, 3, :], prev[:, 0, :], ALU.mult), reads=rk, writes=["Yd"])
            S.op("dve", lambda e: e.tensor_tensor(Xd[:], Xd[:], Yd[:], ALU.add), reads=["Xd", "Yd"], writes=["Xd"])
            S.op("dve", lambda e: e.tensor_tensor(out_ap, Xd[:], add_ap, ALU.add), reads=["Xd"] + rk, writes=wk)
        cstep(D2kL, H3[:, 0, :, :], H3[:, 1, :, :], Ha[:], ["H3"], ["Ha"])
        cstep(D2kL, Ha[:], H3[:, 2, :, :], Hin[:], ["Ha", "H3"], ["Hin"])
        end_phase(mW3)
        u_holder["u"] = sbt("u_sb", [128, 8, 8, 256], BF16, top_=True)
        run_segment(3, "own")
        u_sb = u_holder["u"]
        S_sb = u_holder["S_sb"]
        if True:
            Hbf = sbt("Hbf", [128, 32, 2, 258], BF16)
            W3s = [sbt(f"W3s{i}", [128, 8192], BF16) for i in range(2)]
            KTs = [sbt(f"KTs{i}", [128, 8, 128], BF16) for i in range(2)]
            ysb = [sbt(f"ysb{i}", [128, TOK], BF16) for i in range(2)]
            Xs = sbt("Xs4", [128, 2, 32], F32)
            Ys = sbt("Ys4", [128, 2, 32], F32)

            def chunk_step4(prev, c):
                S.op("dve", lambda e: e.tensor_tensor(Xs[:], LL8[:, 0:2, :], prev, ALU.mult), reads=["Ssb"], writes=["Xs"])
                S.op("dve", lambda e: e.tensor_tensor(Ys[:, 0, :], LL8[:, 2, :], prev[:, 1, :], ALU.mult), reads=["Ssb"], writes=["Ys"])
                S.op("dve", lambda e: e.tensor_tensor(Ys[:, 1, :], LL8[:, 3, :], prev[:, 0, :], ALU.mult), reads=["Ssb"], writes=["Ys"])
                S.op("dve", lambda e: e.tensor_tensor(Xs[:], Xs[:], Ys[:], ALU.add), reads=["Xs", "Ys"], writes=["Xs"])
                S.op("dve", lambda e: e.tensor_tensor(S_sb[:, c, :, :], Xs[:], S_sb[:, c, :, :], ALU.add), reads=["Xs", "Ssb"], writes=["Ssb"])
            for c in range(256):
                chunk_step4(Hin[:] if c == 0 else S_sb[:, c - 1, :, :], c)
            S.op("act", lambda e: e.copy(Hbf[:, :, :, 0], Hin[:].rearrange("p a b -> p b a")), reads=[], writes=["Hbf_0"])
            S.op("act", lambda e: e.copy(Hbf[:, :, 0, 1:257], S_sb[:, :, 0, :].rearrange("p c j -> p j c")), reads=["Ssb"], writes=["Hbf_1"])
            S.op("dve", lambda e: e.tensor_copy(Hbf[:, :, 1, 1:257], S_sb[:, :, 1, :].rearrange("p c j -> p j c")), reads=["Ssb"], writes=["Hbf_2"])
            hbk = ["Hbf_0", "Hbf_1", "Hbf_2"]
            for kt in range(8):
                w3 = W3s[kt % 2]
                ktb = KTs[kt % 2]
                k3, kk = f"W3s{kt%2}", f"KTs{kt%2}"
                ys = ysb[kt % 2]
                ky = f"ysb{kt%2}"
                S.dma("sp", k3, lambda e, w3=w3, kt=kt: e.dma_start(out=w3[:], in_=W3_d[kt]), writes=[k3])
                S.dma("sp", kk, lambda e, ktb=ktb, kt=kt: e.dma_start(out=ktb[:].rearrange("p a b -> p (a b)"), in_=KT_d[kt]), writes=[kk])
                for tau in range(8):
                    b = nextbank()
                    mm = []
                    for s in range(tau + 1):
                        mm.append((ktb[:, tau - s, :], u_sb[:, kt, s, :], [kk]))
                    for q in range(4):
                        for ri in range(2):
                            o = ((q * 8 + tau) * 2 + ri) * 128
                            mm.append((w3[:, o:o + 128], Hbf[:, 4 * kt + q, ri, 0:256], [k3] + hbk))
                    for i, (l_, r_, rk) in enumerate(mm):
                        S.op("pe", lambda e, b=b, l_=l_, r_=r_, i=i, n=len(mm): e.matmul(
                            ps[b][:, 0:256], l_, r_, start=(i == 0), stop=(i == n - 1)),
                            reads=rk, writes=[f"ps{b}"], inc=(i == len(mm) - 1))
                    S.op("act", lambda e, b=b, ys=ys, tau=tau: e.activation(ys[:, tau::8], ps[b][:, 0:256], AF.Gelu_apprx_tanh),
                         reads=[f"ps{b}"], writes=[ky + "_%d" % tau])
                S.dma("act", ky + "st", lambda e, ys=ys, kt=kt: e.dma_start(out=yS_d[:, kt, :], in_=ys[:]),
                      reads=[ky + "_%d" % t for t in range(8)], writes=[f"ySd{kt}"])
        end_phase(mA)
        if stop_after == "a4":
            return nc

        NB = 1024
        for blk in range(2):
            t0 = blk * NB
            mB = A.mark()
            xTb = sbt("xTb", [128, 16, NB], BF16)
            ySb = sbt("ySb", [128, 8, NB], BF16)
            hg = sbt("hg", [128, 16, NB], BF16)
            S.dma("sp", "xTb", lambda e: e.dma_start(out=xTb[:], in_=xT_d[:, :, 3 + t0:3 + t0 + NB]), writes=["xTb"])
            S.dma("sp", "ySb", lambda e: e.dma_start(out=ySb[:], in_=yS_d[:, :, t0:t0 + NB]), writes=["ySb"])
            mG = A.mark()
            if True:
                wg = [sbt(f"wg{i}", [128, 16, 512], BF16) for i in range(2)]
                gsb = [sbt(f"gsb{i}", [128, NB], F32) for i in range(2)]
                ab = [sbt(f"abl{i}", [128, 2, NB], F32) for i in range(2)]
                hs = [sbt(f"hs{i}", [128, NB], F32) for i in range(2)]
                def load_wg(g):
                    S.dma("pool", f"wg{g%2}", lambda e: e.dma_start(out=wg[g % 2][:], in_=w_in_v[:, :, 2048 + 512 * g:2048 + 512 * g + 512]), writes=[f"wg{g%2}"])
                load_wg(0)
                for h in range(16):
                    if h % 4 == 0 and h // 4 + 1 < 4:
                        load_wg(h // 4 + 1)
                    wgi, hh, sx = (h // 4) % 2, h % 4, str(h % 2)
                    S.dma("sp", "abl" + sx, lambda e, h=h: e.dma_start(out=ab[h % 2][:], in_=ab_d[:, h, :, t0:t0 + NB].rearrange("a p t -> p a t")),
                          writes=["abl" + sx])
                    for t2 in range(2):
                        b = nextbank()
                        for kt in range(16):
                            S.op("pe", lambda e, b=b, kt=kt, t2=t2: e.matmul(
                                ps[b][:, :], wg[wgi][:, kt, hh * 128:(hh + 1) * 128], xTb[:, kt, t2 * 512:(t2 + 1) * 512], start=(kt == 0), stop=(kt == 15)),
                                reads=[f"wg{wgi}", "xTb"], writes=[f"ps{b}"], inc=(kt == 15))
                        S.op("act", lambda e, b=b, t2=t2, h=h: e.activation(gsb[h % 2][:, t2 * 512:(t2 + 1) * 512], ps[b][:, :], AF.Gelu_apprx_tanh),
                             reads=[f"ps{b}"], writes=["gsb" + sx + "_%d" % t2])
                    S.op("dve", lambda e, h=h: e.tensor_tensor_scan(hs[h % 2][:], ab[h % 2][:, 0, :], ab[h % 2][:, 1, :], hcar[:, h:h + 1], ALU.mult, ALU.add),
                         reads=["abl" + sx, "hcar"], writes=["hs" + sx])
                    S.op("dve", lambda e, h=h: e.tensor_copy(hcar[:, h:h + 1], hs[h % 2][:, NB - 1:NB]), reads=["hs" + sx], writes=["hcar"])
                    S.op("dve", lambda e, h=h: e.tensor_tensor(hg[:, h, :], hs[h % 2][:], gsb[h % 2][:], ALU.mult),
                         reads=["hs" + sx, "gsb" + sx + "_0", "gsb" + sx + "_1"], writes=[f"hg{h}"])
            end_phase(mG)
            if True:
                wA = [sbt(f"wA{i}", [128, 16, 256], BF16) for i in range(2)]
                wGa = [sbt(f"wGa{i}", [128, 16, 256], BF16) for i in range(2)]
                wGb = [sbt(f"wGb{i}", [128, 16, 256], BF16) for i in range(2)]
                wLw = [sbt(f"wLw{i}", [128, 8, 256], BF16) for i in range(2)]
                wLv = [sbt(f"wLv{i}", [128, 8, 256], BF16) for i in range(2)]
                tmp = [sbt(f"mt{i}", [128, 4, 512], F32) for i in range(2)]
                mixs = [sbt(f"mixs{i}", [128, 512], BF16) for i in range(2)]
                mc = 0

                def load_mix(jg):
                    i = jg % 2
                    c0 = 256 * jg
                    S.dma("pool", f"wGa{i}", lambda e: e.dma_start(out=wGa[i][:], in_=w_in_v[:, :, 5120 + c0:5120 + c0 + 256]), writes=[f"wGa{i}"])
                    S.dma("pool", f"wGb{i}", lambda e: e.dma_start(out=wGb[i][:], in_=w_in_v[:, :, 7168 + c0:7168 + c0 + 256]), writes=[f"wGb{i}"])
                    S.dma("pool", f"wLv{i}", lambda e: e.dma_start(out=wLv[i][:], in_=glu_v_v[:, :, c0:c0 + 256]), writes=[f"wLv{i}"])
                    S.dma("pool", f"wLw{i}", lambda e: e.dma_start(out=wLw[i][:], in_=glu_w_v[:, :, c0:c0 + 256]), writes=[f"wLw{i}"])
                    S.dma("pool", f"wA{i}", lambda e: e.dma_start(out=wA[i][:], in_=w_a_v[:, :, c0:c0 + 256]), writes=[f"wA{i}"])
                load_mix(0)
                for jg in range(8):
                    i = jg % 2
                    c0 = 256 * jg
                    if jg + 1 < 8:
                        load_mix(jg + 1)
                    for jj in range(2):
                        j = 2 * jg + jj
                        cs = slice(128 * jj, 128 * jj + 128)
                        for t2 in range(2):
                            ts = slice(t2 * 512, (t2 + 1) * 512)
                            T_ = tmp[mc % 2]
                            tk = f"mt{mc%2}"
                            ms = mixs[mc % 2]
                            mk = f"mixs{mc%2}"
                            mc += 1

                            def group(wt, wk, act, ak, nk):
                                b = nextbank()
                                for kt in range(nk):
                                    S.op("pe", lambda e, b=b, kt=kt: e.matmul(ps[b][:, :], wt[:, kt, cs], act[:, kt, ts], start=(kt == 0), stop=(kt == nk - 1)),
                                         reads=[wk] + ak, writes=[f"ps{b}"], inc=(kt == nk - 1))
                                return b
                            bA = group(wGa[i], f"wGa{i}", xTb, ["xTb"], 16)
                            S.op("act", lambda e, b=bA, T_=T_: e.activation(T_[:, 0, :], ps[b][:, :], AF.Sigmoid), reads=[f"ps{bA}"], writes=[tk + "a"])
                            bB = group(wGb[i], f"wGb{i}", xTb, ["xTb"], 16)
                            S.op("act", lambda e, b=bB, T_=T_: e.activation(T_[:, 1, :], ps[b][:, :], AF.Sigmoid), reads=[f"ps{bB}"], writes=[tk + "b"])
                            bV = group(wLv[i], f"wLv{i}", ySb, ["ySb"], 8)
                            S.op("act", lambda e, b=bV, T_=T_: e.activation(T_[:, 2, :], ps[b][:, :], AF.Sigmoid), reads=[f"ps{bV}"], writes=[tk + "v"])
                            bW = group(wLw[i], f"wLw{i}", ySb, ["ySb"], 8)
                            S.op("dve", lambda e, b=bW, T_=T_: e.tensor_tensor(T_[:, 2, :], ps[b][:, :], T_[:, 2, :], ALU.mult), reads=[f"ps{bW}", tk + "v"], writes=[tk + "v"])
                            S.op("dve", lambda e, T_=T_: e.tensor_tensor(T_[:, 2, :], T_[:, 2, :], T_[:, 1, :], ALU.mult), reads=[tk + "v", tk + "b"], writes=[tk + "v"])
                            bY = group(wA[i], f"wA{i}", hg, [], 16)
                            S.op("dve", lambda e, b=bY, T_=T_: e.tensor_tensor(T_[:, 0, :], ps[b][:, :], T_[:, 0, :], ALU.mult), reads=[f"ps{bY}", tk + "a"], writes=[tk + "a"])
                            S.op("dve", lambda e, T_=T_, ms=ms: e.tensor_tensor(ms[:], T_[:, 0, :], T_[:, 2, :], ALU.add), reads=[tk + "a", tk + "v"], writes=[mk])
                            S.dma("sp", mk + "st", lambda e, ms=ms, j=j, t2=t2: e.dma_start(out=mix_d[:, j, t0 + t2 * 512:t0 + (t2 + 1) * 512], in_=ms[:]),
                                  reads=[mk], writes=[f"mixd{j}_{t2}"])
            end_phase(mB)
            if stop_after == "mix":
                continue

            mF = A.mark()
            acc = sbt("acc", [128, 8, D], F32)
            lnp_off = A.lo
            lnp = sbt("lnp", [128, 2, D], F32)
            mO = A.mark()
            if True:
                mixT = sbt("mixT", [128, 16, NB], BF16)
                wo = [sbt(f"wo{i}", [128, 16, 512], BF16) for i in range(2)]
                xres = [sbt(f"xres{i}", [128, 512], F32) for i in range(3)]
                S.dma("sp", "mixT", lambda e: e.dma_start(out=mixT[:], in_=mix_d[:, :, t0:t0 + NB]), writes=["mixT"])
                S.dma("sp", "lnp0", lambda e: e.dma_start(out=lnp[:, 0, :], in_=ln1_g_d.partition_broadcast(128)), writes=["lnp0"])
                S.dma("sp", "lnp1", lambda e: e.dma_start(out=lnp[:, 1, :], in_=ln1_b_d.partition_broadcast(128)), writes=["lnp1"])
                xc_ = 0
                def load_wo(cb):
                    S.dma("pool", f"wo{cb%2}", lambda e: e.dma_start(out=wo[cb % 2][:], in_=w_out_v[:, :, 512 * cb:512 * cb + 512]), writes=[f"wo{cb%2}"])
                load_wo(0)
                for cb in range(4):
                    i = cb % 2
                    if cb + 1 < 4:
                        load_wo(cb + 1)
                    for tt in range(8):
                        xr_ = xres[xc_ % 3]
                        xk = f"xres{xc_%3}"
                        xc_ += 1
                        r0 = 3 * TOK + 3 + t0 + 128 * tt
                        S.dma("sp", xk, lambda e, xr_=xr_, r0=r0, cb=cb: e.dma_start(out=xr_[:], in_=x_d[r0:r0 + 128, 512 * cb:512 * cb + 512]), writes=[xk])
                        b = nextbank()
                        for kt in range(16):
                            S.op("pe", lambda e, b=b, kt=kt, tt=tt, i=i: e.matmul(
                                ps[b][:, :], mixT[:, kt, 128 * tt:128 * tt + 128], wo[i][:, kt, :], start=(kt == 0), stop=(kt == 15)),
                                reads=["mixT", f"wo{i}"], writes=[f"ps{b}"], inc=(kt == 15))
                        S.op("dve", lambda e, b=b, xr_=xr_, tt=tt, cb=cb: e.scalar_tensor_tensor(
                            acc[:, tt, 512 * cb:512 * cb + 512], xr_[:], ALPHA, ps[b][:, :], ALU.mult, ALU.add),
                            reads=[xk, f"ps{b}"], writes=[f"acc{tt}_{cb}"])
                        S.op("dve", lambda e, tt=tt, cb=cb: e.bn_stats(stats[:, tt, cb, :], acc[:, tt, 512 * cb:512 * cb + 512]),
                             reads=[f"acc{tt}_{cb}"], writes=[f"stats{tt}_{cb}"])
            end_phase(mO)
            x1T = sbt("x1T", [128, 16, NB], BF16)

            def layernorm(tt, outk):
                S.op("dve", lambda e: e.bn_aggr(mv[:, tt, 0:2], stats[:, tt, :, :].rearrange("p a b -> p (a b)")),
                     reads=[f"stats{tt}_{c_}" for c_ in range(4)], writes=[f"mv{tt}"])
                S.op("act", lambda e: e.activation(mv[:, tt, 2:3], mv[:, tt, 1:2], AF.Sqrt, bias=EPS), reads=[f"mv{tt}"], writes=[f"mv{tt}"])
                S.op("dve", lambda e: e.reciprocal(mv[:, tt, 2:3], mv[:, tt, 2:3]), reads=[f"mv{tt}"], writes=[f"mv{tt}"])
                S.op("dve", lambda e: e.scalar_tensor_tensor(mv[:, tt, 3:4], mv[:, tt, 0:1], -1.0, mv[:, tt, 2:3], ALU.mult, ALU.mult),
                     reads=[f"mv{tt}"], writes=[f"mv{tt}"])
                S.op("act", lambda e: e.activation(acc[:, tt, :], acc[:, tt, :], AF.Identity, bias=mv[:, tt, 3:4], scale=mv[:, tt, 2:3]),
                     reads=[f"mv{tt}"], writes=[outk])
                S.op("dve", lambda e: e.tensor_tensor(acc[:, tt, :], acc[:, tt, :], lnp[:, 0, :], ALU.mult), reads=[outk, "lnp0"], writes=[outk])
                S.op("dve", lambda e: e.tensor_tensor(acc[:, tt, :], acc[:, tt, :], lnp[:, 1, :], ALU.add), reads=[outk, "lnp1"], writes=[outk])

            for tt in range(8):
                layernorm(tt, f"x1_{tt}")
                for b4 in range(4):
                    b = nextbank()
                    for jj in range(4):
                        kt = 4 * b4 + jj
                        S.op("pe", lambda e, b=b, jj=jj, kt=kt, tt=tt: e.transpose(
                            ps[b][:, jj * 128:(jj + 1) * 128], acc[:, tt, kt * 128:(kt + 1) * 128], ident[:, :]),
                            reads=[f"x1_{tt}"], writes=[f"ps{b}"], inc=(jj == 3))
                    S.op("act", lambda e, b=b, b4=b4, tt=tt: e.copy(x1T[:, 4 * b4:4 * b4 + 4, 128 * tt:128 * tt + 128],
                                                                   ps[b][:, :].rearrange("p (j t) -> p j t", j=4)),
                         reads=[f"ps{b}"], writes=[f"x1T{tt}_{b4}"])
            if dbg and blk == 0:
                S.dma("sp", "dbgx1", lambda e: e.dma_start(out=dbg_x1.rearrange("(t p) c -> p t c", p=128), in_=acc[:]),
                      reads=[f"x1_{tt}" for tt in range(8)])
            S.dma("sp", "lnp0", lambda e: e.dma_start(out=lnp[:, 0, :], in_=b_dn_d.partition_broadcast(128)),
                  reads=[f"x1_{tt}" for tt in range(8)], writes=["lnp0"])
            for tt in range(8):
                S.op("dve", lambda e, tt=tt: e.scalar_tensor_tensor(acc[:, tt, :], acc[:, tt, :], ALPHA, lnp[:, 0, :], ALU.mult, ALU.add),
                     reads=[f"x1_{tt}", "lnp0"], writes=[f"x1_{tt}"])
            S.barrier()

            if True:
                FC = 8
                hTb = [sbt("hT", [128, FC, NB], BF16), A.view(lnp_off, [128, FC, NB], BF16)]
                wu = [sbt(f"wu{i}", [128, 16, 256], BF16) for i in range(2)]
                wd = sbt("wd", [128, FC, D], BF16)
                rl = [sbt(f"rl{i}", [128, 512], F32) for i in range(3)]
                wuc = 0
                rc = 0
                NFC = DFF // 128 // FC
                NG = NFC * (FC // 2)

                def load_wu(g):
                    c0 = g * 256
                    S.dma("pool", f"wu{g%2}", lambda e: e.dma_start(out=wu[g % 2][:], in_=w_up_v[:, :, c0:c0 + 256]), writes=[f"wu{g%2}"])

                def load_wd(fc):
                    for f2 in range(FC // 2):
                        f0 = fc * FC + f2 * 2
                        S.dma("pool", f"wd{f2}", lambda e, f2=f2, f0=f0: e.dma_start(out=wd[:, 2 * f2:2 * f2 + 2, :], in_=w_dn_v[:, f0:f0 + 2, :]), writes=[f"wd{f2}"])
                load_wu(0)
                for fc in range(NFC):
                    hT = hTb[fc % 2]
                    hk = "hT%d_" % (fc % 2)
                    for f2 in range(FC // 2):
                        g = fc * (FC // 2) + f2
                        i = g % 2
                        if g + 1 < NG:
                            load_wu(g + 1)
                        if f2 == 0:
                            load_wd(fc)
                        for f in range(2):
                            fl = f2 * 2 + f
                            ft = fc * FC + fl
                            for t2 in range(2):
                                b = nextbank()
                                for kt in range(16):
                                    S.op("pe", lambda e, b=b, kt=kt, i=i, f=f, t2=t2: e.matmul(
                                        ps[b][:, :], wu[i][:, kt, f * 128:(f + 1) * 128], x1T[:, kt, t2 * 512:(t2 + 1) * 512], start=(kt == 0), stop=(kt == 15)),
                                        reads=[f"wu{i}"], writes=[f"ps{b}"], inc=(kt == 15))
                                r_ = rl[rc % 3]
                                rk = f"rl{rc%3}"
                                rc += 1
                                S.op("act", lambda e, b=b, r_=r_, ft=ft: e.activation(r_[:], ps[b][:, :], AF.Relu, bias=bup[:, ft:ft + 1]),
                                     reads=[f"ps{b}"], writes=[rk])
                                S.op("act", lambda e, r_=r_, fl=fl, t2=t2: e.activation(hT[:, fl, t2 * 512:(t2 + 1) * 512], r_[:], AF.Square),
                                     reads=[rk], writes=[hk + f"{fl}_{t2}"])
                    for tt in range(8):
                        for cb in range(4):
                            b = nextbank()
                            for fl in range(FC):
                                S.op("pe", lambda e, b=b, fl=fl, tt=tt, cb=cb: e.matmul(
                                    ps[b][:, :], hT[:, fl, 128 * tt:128 * tt + 128], wd[:, fl, 512 * cb:512 * cb + 512], start=(fl == 0), stop=(fl == FC - 1)),
                                    reads=[f"wd{fl//2}", hk + f"{fl}_{(128*tt)//512}"], writes=[f"ps{b}"], inc=(fl == FC - 1))
                            S.op("dve", lambda e, b=b, tt=tt, cb=cb: e.tensor_tensor(
                                acc[:, tt, 512 * cb:512 * cb + 512], acc[:, tt, 512 * cb:512 * cb + 512], ps[b][:, :], ALU.add),
                                reads=[f"ps{b}", f"accf{tt}_{cb}"], writes=[f"accf{tt}_{cb}"])
                hk1 = [f"hT1_{fl}_{t2}" for fl in range(FC) for t2 in range(2)]
                S.dma("sp", "lnp0", lambda e: e.dma_start(out=lnp[:, 0, :], in_=ln2_g_d.partition_broadcast(128)), writes=["lnp0"] + hk1)
                S.dma("sp", "lnp1", lambda e: e.dma_start(out=lnp[:, 1, :], in_=ln2_b_d.partition_broadcast(128)), writes=["lnp1"] + hk1)
                for tt in range(8):
                    for cb in range(4):
                        S.op("dve", lambda e, tt=tt, cb=cb: e.bn_stats(stats[:, tt, cb, :], acc[:, tt, 512 * cb:512 * cb + 512]),
                             reads=[f"accf{tt}_{cb}"], writes=[f"stats{tt}_{cb}"])
                    layernorm(tt, f"x2_{tt}")
                    S.dma("sp", f"ost{tt%2}", lambda e, tt=tt: e.dma_start(out=out_d[t0 + 128 * tt:t0 + 128 * tt + 128, :], in_=acc[:, tt, :]),
                          reads=[f"x2_{tt}"], writes=[f"outd{tt}"])
            end_phase(mF)
        S.barrier()
    return nc


_CACHE = {}


def _prep_inputs(inputs, small=False):
    x = np.ascontiguousarray(np.asarray(inputs["x"], dtype=np.float32))
    names = ["w_in", "conv_w", "conv_b", "rg_wa", "rg_ba", "rg_wx", "rg_bx", "rg_lambda", "w_a_out",
             "ssm_a_re", "ssm_a_im", "ssm_log_dt", "ssm_b_re", "ssm_b_im", "ssm_c_re", "ssm_c_im", "ssm_d",
             "glu_w", "glu_v", "w_out", "ln1_g", "ln1_b", "mlp_w_up", "mlp_b_up", "mlp_w_down", "mlp_b_down",
             "ln2_g", "ln2_b"]
    shared = {n: np.ascontiguousarray(np.asarray(inputs[n], dtype=np.float32)[0]) for n in names}
    in_maps = []
    for r in range(NCORE):
        b, k = r // 4, r % 4
        xs = np.zeros((4 * TOK + HALO, D), np.float32)
        n_real = TOK * (k + 1)
        xs[4 * TOK + HALO - n_real:] = x[b, 0:n_real]
        segm = np.ones((128, 4), np.float32)
        segm[:, 3 - k] = 0.0
        m = {"x": xs, "segm": segm}
        m.update(shared)
        if small:
            for n in ("w_a_out", "glu_w", "glu_v", "w_out", "mlp_w_up", "mlp_w_down"):
                m[n] = np.zeros((128, 128), np.float32)
        in_maps.append(m)
    return in_maps


def kernel(**inputs):
    if "nc" not in _CACHE:
        _CACHE["nc"] = build()
    nc = _CACHE["nc"]
    in_maps = _prep_inputs(inputs)
    res = run_bass_kernel_spmd(nc, in_maps, core_ids=list(range(NCORE)))
    out = np.empty((2, 4 * TOK, D), np.float32)
    for r in range(NCORE):
        b, k = r // 4, r % 4
        out[b, TOK * k:TOK * (k + 1)] = res.results[r]["out"]
    return out
```
